# Optimizing a Trainium2 kernel written in Bass

```python
import math
import jax
import jax.numpy as jnp
from jax import lax
import numpy as np

D_MODEL = 1024
BATCH = 32
SEQ = 256
DEPTH = 4
DEC_BATCH = 2
DEC_SEQ = 2048
PAST_LEN = 512

GRID_W = 64
ROPE_THETA = 10000.0
Q_BLOCK = 128
EPS = 1e-6
N_BRANCH = 4
BRANCH_W = D_MODEL // 4
FNET_GROUPS = 4
FNET_GW = BRANCH_W // FNET_GROUPS
GQA_HEADS = 4
GQA_KV_HEADS = 2
GQA_HEAD_DIM = BRANCH_W // GQA_HEADS
DIFF_HEADS = 4
DIFF_V_DIM = BRANCH_W // DIFF_HEADS
DIFF_QK_DIM = DIFF_V_DIM // 2
SGU_GROUPS = 4
SGU_CHUNK = 128
SGU_GW = BRANCH_W // SGU_GROUPS
D_FF = 4 * D_MODEL
N_MOD = 6

_IN_SPLITS = (
    BRANCH_W,
    GQA_HEADS * GQA_HEAD_DIM,
    GQA_KV_HEADS * GQA_HEAD_DIM,
    GQA_KV_HEADS * GQA_HEAD_DIM,
    DIFF_HEADS * 2 * DIFF_QK_DIM,
    DIFF_HEADS * 2 * DIFF_QK_DIM,
    DIFF_HEADS * DIFF_V_DIM,
    BRANCH_W,
    BRANCH_W,
)
IN_W = sum(_IN_SPLITS)

kernel_name = "hybrid_diffusion_prefix_trunk_step"


def _rms_norm(x, g):
    xf = x.astype(jnp.float32)
    y = xf * lax.rsqrt(jnp.mean(xf * xf, axis=-1, keepdims=True) + EPS)
    return (y * g.astype(jnp.float32)).astype(x.dtype)


def _split_cols(proj):
    idx = []
    acc = 0
    for s in _IN_SPLITS[:-1]:
        acc += s
        idx.append(acc)
    return jnp.split(proj, idx, axis=-1)


def _axial_rope(length, dim):
    rows = length // GRID_W
    row = jnp.repeat(jnp.arange(rows, dtype=jnp.float32), GRID_W)
    col = jnp.tile(jnp.arange(GRID_W, dtype=jnp.float32), rows)
    quarter = dim // 4
    inv = ROPE_THETA ** (-jnp.arange(quarter, dtype=jnp.float32) / quarter)
    ang = jnp.concatenate([row[:, None] * inv, col[:, None] * inv], axis=-1)
    return jnp.cos(ang), jnp.sin(ang)


def _apply_rope(x, cos, sin):
    half = x.shape[-1] // 2
    shape = (1, x.shape[1]) + (1,) * (x.ndim - 3) + (half,)
    c = cos.reshape(shape)
    s = sin.reshape(shape)
    xf = x.astype(jnp.float32)
    x1, x2 = xf[..., :half], xf[..., half:]
    return jnp.concatenate([x1 * c - x2 * s, x1 * s + x2 * c], axis=-1).astype(x.dtype)


def _gqa_attention(q, k, v):
    b, sq, h, dh = q.shape
    hkv = k.shape[2]
    g = h // hkv
    nb = sq // Q_BLOCK
    scale = 1.0 / math.sqrt(dh)
    qb = q.reshape(b, nb, Q_BLOCK, hkv, g, dh).transpose(1, 0, 2, 3, 4, 5)

    def one_block(qblk):
        s = jnp.einsum("bqkgd,bskd->bkgqs", qblk, k, preferred_element_type=jnp.float32) * scale
        p = jax.nn.softmax(s, axis=-1).astype(v.dtype)
        return jnp.einsum("bkgqs,bskd->bqkgd", p, v)

    o = lax.map(one_block, qb)
    return o.transpose(1, 0, 2, 3, 4, 5).reshape(b, sq, h * dh)


def _diff_attention(q, k, v, lam):
    b, sq, h, _, dq = q.shape
    nb = sq // Q_BLOCK
    scale = 1.0 / math.sqrt(dq)
    qb = q.reshape(b, nb, Q_BLOCK, h, 2, dq).transpose(1, 0, 2, 3, 4, 5)

    def one_block(qblk):
        s = jnp.einsum("bqhmd,bshmd->bhmqs", qblk, k, preferred_element_type=jnp.float32) * scale
        p = jax.nn.softmax(s, axis=-1)
        w = (p[:, :, 0] - lam * p[:, :, 1]).astype(v.dtype)
        return jnp.einsum("bhqs,bshd->bqhd", w, v)

    o = lax.map(one_block, qb)
    return o.transpose(1, 0, 2, 3, 4).reshape(b, sq, h, v.shape[-1])


def _mixer(h, lp, lam, lam_scale, ctx):
    b, length, _ = h.shape
    a, bq, bk, bv, cq, ck, cv, du, dv = _split_cols(h @ lp["w_in"])

    af = a.reshape(b, length, FNET_GROUPS, FNET_GW).astype(jnp.float32)
    af = jnp.fft.fft2(af, axes=(1, 3), norm="ortho").real
    out_a = af.astype(h.dtype).reshape(b, length, BRANCH_W) @ lp["w_fourier"]

    bq = _rms_norm(bq.reshape(b, length, GQA_HEADS, GQA_HEAD_DIM), lp["q_norm_g"])
    bk = _rms_norm(bk.reshape(b, length, GQA_KV_HEADS, GQA_HEAD_DIM), lp["k_norm_g"])
    bv = bv.reshape(b, length, GQA_KV_HEADS, GQA_HEAD_DIM)
    cq = cq.reshape(b, length, DIFF_HEADS, 2, DIFF_QK_DIM)
    ck = ck.reshape(b, length, DIFF_HEADS, 2, DIFF_QK_DIM)
    cv = cv.reshape(b, length, DIFF_HEADS, DIFF_V_DIM)

    if ctx is None:
        new_kv = (bk, bv, ck, cv)
        kb, vb, kc, vc = bk, bv, ck, cv
    else:
        new_kv = None
        cos_b, sin_b = _axial_rope(length, GQA_HEAD_DIM)
        cos_c, sin_c = _axial_rope(length, DIFF_QK_DIM)
        bq = _apply_rope(bq, cos_b, sin_b)
        bk = _apply_rope(bk, cos_b, sin_b)
        cq = _apply_rope(cq, cos_c, sin_c)
        ck = _apply_rope(ck, cos_c, sin_c)
        ctx_bk, ctx_bv, ctx_ck, ctx_cv = ctx
        kb = jnp.concatenate([ctx_bk, bk], axis=1)
        vb = jnp.concatenate([ctx_bv, bv], axis=1)
        kc = jnp.concatenate([ctx_ck, ck], axis=1)
        vc = jnp.concatenate([ctx_cv, cv], axis=1)

    out_b = _gqa_attention(bq, kb, vb)
    oc = _diff_attention(cq, kc, vc, lam)
    out_c = (_rms_norm(oc, lp["diff_norm_g"]) * lam_scale).reshape(b, length, BRANCH_W)

    du = jax.nn.gelu(du)
    dv = _rms_norm(jax.nn.gelu(dv), lp["sgu_norm_g"])
    dv = dv.reshape(b, length // SGU_CHUNK, SGU_CHUNK, SGU_GROUPS, SGU_GW)
    s = jnp.einsum("gqp,bnpgc->bnqgc", lp["w_spatial"], dv) + lp["b_spatial"].T[None, None, :, :, None]
    out_d = du * s.reshape(b, length, BRANCH_W)

    branches = jnp.stack([out_a, out_b, out_c, out_d], axis=2)
    bproj = jnp.einsum("blnc,ncd->blnd", branches, lp["w_branch"])
    gates = jax.nn.sigmoid(h @ lp["w_gate"]).reshape(b, length, N_BRANCH, D_MODEL)
    merged = jnp.sum(gates * bproj, axis=2)
    return merged @ lp["w_out"], new_kv


def _block(x, cond, lp, lam, lam_scale, ctx):
    mod = (jax.nn.silu(cond) @ lp["w_ada"] + lp["b_ada"])[:, None, :]
    sh1, sc1, g1, sh2, sc2, g2 = jnp.split(mod, N_MOD, axis=-1)
    h = _rms_norm(x, lp["norm1_g"]) * (1 + sc1) + sh1
    m, new_kv = _mixer(h, lp, lam, lam_scale, ctx)
    x = x + g1 * m
    h = _rms_norm(x, lp["norm2_g"]) * (1 + sc2) + sh2
    f = jnp.square(jax.nn.relu(h @ lp["w_mlp1"])) @ lp["w_mlp2"]
    return x + g2 * f, new_kv


def setup_inputs(seed: int = 0) -> dict:
    key = jax.random.key(seed)
    ks = jax.random.split(key, 32)
    f32 = jnp.float32

    def nrm(k, shape, scale):
        return jax.random.normal(k, shape, f32) * scale

    def gain(k, shape):
        return 1.0 + 0.01 * jax.random.normal(k, shape, f32)

    return {
        "x_prompt": nrm(ks[0], (BATCH, SEQ, D_MODEL), 1.0),
        "x_sample": nrm(ks[1], (DEC_BATCH, DEC_SEQ, D_MODEL), 1.0),
        "c": nrm(ks[2], (DEC_BATCH, D_MODEL), 1.0),
        "cache_gqa_k": nrm(ks[3], (DEC_BATCH, DEPTH, PAST_LEN, GQA_KV_HEADS, GQA_HEAD_DIM), 1.0),
        "cache_gqa_v": nrm(ks[4], (DEC_BATCH, DEPTH, PAST_LEN, GQA_KV_HEADS, GQA_HEAD_DIM), 1.0),
        "cache_diff_k": nrm(ks[5], (DEC_BATCH, DEPTH, PAST_LEN, DIFF_HEADS, 2, DIFF_QK_DIM), 1.0),
        "cache_diff_v": nrm(ks[6], (DEC_BATCH, DEPTH, PAST_LEN, DIFF_HEADS, DIFF_V_DIM), 1.0),
        "c_ctx": nrm(ks[7], (D_MODEL,), 1.0),
        "w_ada": nrm(ks[8], (DEPTH, D_MODEL, N_MOD * D_MODEL), 0.5 * D_MODEL ** -0.5),
        "b_ada": nrm(ks[9], (DEPTH, N_MOD * D_MODEL), 0.01),
        "norm1_g": gain(ks[10], (DEPTH, D_MODEL)),
        "norm2_g": gain(ks[11], (DEPTH, D_MODEL)),
        "w_in": nrm(ks[12], (DEPTH, D_MODEL, IN_W), D_MODEL ** -0.5),
        "w_fourier": nrm(ks[13], (DEPTH, BRANCH_W, BRANCH_W), BRANCH_W ** -0.5),
        "q_norm_g": gain(ks[14], (DEPTH, GQA_HEAD_DIM)),
        "k_norm_g": gain(ks[15], (DEPTH, GQA_HEAD_DIM)),
        "lambda_q1": nrm(ks[16], (DEPTH, DIFF_QK_DIM), 0.1),
        "lambda_k1": nrm(ks[17], (DEPTH, DIFF_QK_DIM), 0.1),
        "lambda_q2": nrm(ks[18], (DEPTH, DIFF_QK_DIM), 0.1),
        "lambda_k2": nrm(ks[19], (DEPTH, DIFF_QK_DIM), 0.1),
        "diff_norm_g": gain(ks[20], (DEPTH, DIFF_V_DIM)),
        "sgu_norm_g": gain(ks[21], (DEPTH, BRANCH_W)),
        "w_spatial": nrm(ks[22], (DEPTH, SGU_GROUPS, SGU_CHUNK, SGU_CHUNK), SGU_CHUNK ** -0.5),
        "b_spatial": nrm(ks[23], (DEPTH, SGU_GROUPS, SGU_CHUNK), 0.01),
        "w_gate": nrm(ks[24], (DEPTH, D_MODEL, N_BRANCH * D_MODEL), D_MODEL ** -0.5),
        "w_branch": nrm(ks[25], (DEPTH, N_BRANCH, BRANCH_W, D_MODEL), BRANCH_W ** -0.5),
        "w_out": nrm(ks[26], (DEPTH, D_MODEL, D_MODEL), D_MODEL ** -0.5),
        "w_mlp1": nrm(ks[27], (DEPTH, D_MODEL, D_FF), D_MODEL ** -0.5),
        "w_mlp2": nrm(ks[28], (DEPTH, D_FF, D_MODEL), D_FF ** -0.5),
        "final_norm_g": gain(ks[29], (D_MODEL,)),
    }


def reference(x_prompt, x_sample, c, cache_gqa_k, cache_gqa_v, cache_diff_k, cache_diff_v,
              c_ctx, w_ada, b_ada, norm1_g, norm2_g, w_in, w_fourier, q_norm_g, k_norm_g,
              lambda_q1, lambda_k1, lambda_q2, lambda_k2, diff_norm_g, sgu_norm_g,
              w_spatial, b_spatial, w_gate, w_branch, w_out, w_mlp1, w_mlp2, final_norm_g):
    xp = x_prompt
    xs = x_sample
    cond_ctx = c_ctx[None, :]
    gk, gv, dk, dv = [], [], [], []
    for l in range(DEPTH):
        lp = {
            "w_ada": w_ada[l], "b_ada": b_ada[l], "norm1_g": norm1_g[l], "norm2_g": norm2_g[l],
            "w_in": w_in[l], "w_fourier": w_fourier[l], "q_norm_g": q_norm_g[l],
            "k_norm_g": k_norm_g[l], "diff_norm_g": diff_norm_g[l], "sgu_norm_g": sgu_norm_g[l],
            "w_spatial": w_spatial[l], "b_spatial": b_spatial[l], "w_gate": w_gate[l],
            "w_branch": w_branch[l], "w_out": w_out[l], "w_mlp1": w_mlp1[l], "w_mlp2": w_mlp2[l],
        }
        lam_init = 0.8 - 0.6 * math.exp(-0.3 * l)
        lam = (jnp.exp(jnp.sum(lambda_q1[l].astype(jnp.float32) * lambda_k1[l].astype(jnp.float32)))
               - jnp.exp(jnp.sum(lambda_q2[l].astype(jnp.float32) * lambda_k2[l].astype(jnp.float32)))
               + lam_init)
        lam_scale = 1.0 - lam_init
        xp, (bk, bv, ck, cv) = _block(xp, cond_ctx, lp, lam, lam_scale, None)
        gk.append(bk)
        gv.append(bv)
        dk.append(ck)
        dv.append(cv)
        ctx = (cache_gqa_k[:, l], cache_gqa_v[:, l], cache_diff_k[:, l], cache_diff_v[:, l])
        xs, _ = _block(xs, c, lp, lam, lam_scale, ctx)
    y_prompt = _rms_norm(xp, final_norm_g)
    y_sample = _rms_norm(xs, final_norm_g)
    new_gqa_k = jnp.stack(gk, axis=1)
    new_gqa_v = jnp.stack(gv, axis=1)
    new_diff_k = jnp.stack(dk, axis=1)
    new_diff_v = jnp.stack(dv, axis=1)
    return (y_prompt, y_sample, new_gqa_k, new_gqa_v, new_diff_k, new_diff_v)
```

```python
import math
import numpy as np
import concourse.bass as bass
import concourse.mybir as mybir
from concourse.bass_utils import run_bass_kernel_spmd

F32 = mybir.dt.float32
F32R = mybir.dt.float32r
BF16 = mybir.dt.bfloat16
AF = mybir.ActivationFunctionType
ALU = mybir.AluOpType
AX = mybir.AxisListType

D = 1024
DEPTH = 4
T = 512
NU = 3
EPS = 1e-6
GROWS = 1280
RELU2_DVE = False


class Prog:
    def __init__(self, nc):
        self.nc = nc
        self.ops = {e: [] for e in ("pe", "act", "dve", "pool", "sp")}
        self.cnt = {e: 0 for e in self.ops}
        self.esem = {e: nc.alloc_semaphore("sem_" + e) for e in ("pe", "act", "dve", "pool")}
        self.slot_sem = {}
        self.slot_cnt = {}
        self.lastw = {}
        self.readers = {}
        self.waited = {e: {} for e in self.ops}
        self.final = []

    def _deps(self, eng, R, W, pe_acc=False):
        deps = {}

        def add(ev):
            if ev is None:
                return
            k, v, src, _sem = ev
            if pe_acc and src == "pe" and eng == "pe":
                return
            if k not in deps or deps[k][0] < v:
                deps[k] = (v, ev[3])

        for t in list(R) + list(W):
            add(self.lastw.get(t))
        for t in W:
            for ev in self.readers.get(t, []):
                add(ev)
        out = []
        for k, (v, sem) in deps.items():
            if self.waited[eng].get(k, 0) >= v:
                continue
            self.waited[eng][k] = v
            out.append((sem, v))
        return out

    def _commit(self, ev, R, W):
        for t in W:
            self.lastw[t] = ev
            self.readers[t] = []
        for t in R:
            self.readers.setdefault(t, []).append(ev)

    def op(self, eng, fn, R=(), W=()):
        if isinstance(fn, tuple):
            _name, _kw = fn
            fn = (lambda e, _name=_name, _kw=_kw: getattr(e, _name)(**_kw))
        waits = self._deps(eng, R, W, pe_acc=True)
        self.cnt[eng] += 1
        sem = self.esem[eng]
        ev = ("e_" + eng, self.cnt[eng], eng, sem)
        self.ops[eng].append((waits, fn, sem, 1))
        self._commit(ev, R, W)

    def dma(self, q, out, in_, R, W, slot, **kw):
        waits = self._deps(q, R, W)
        if slot not in self.slot_sem:
            self.slot_sem[slot] = self.nc.alloc_semaphore("ds_" + slot)
            self.slot_cnt[slot] = 0
        self.slot_cnt[slot] += 16
        sem = self.slot_sem[slot]
        ev = ("s_" + slot, self.slot_cnt[slot], "dma", sem)
        self.ops[q].append((waits, lambda e: e.dma_start(out=out, in_=in_, **kw), sem, 16))
        self._commit(ev, R, W)

    def coll(self, fn, R, W, slot):
        waits = self._deps("pool", R, W)
        if slot not in self.slot_sem:
            self.slot_sem[slot] = self.nc.alloc_semaphore("ds_" + slot)
            self.slot_cnt[slot] = 0
        self.slot_cnt[slot] += 1
        sem = self.slot_sem[slot]
        ev = ("s_" + slot, self.slot_cnt[slot], "dma", sem)
        self.ops["pool"].append((waits, fn, sem, None))
        self._commit(ev, R, W)

    def emit(self):
        nc = self.nc
        final_waits = [(self.slot_sem[s], self.slot_cnt[s]) for s in self.slot_sem]
        with nc.Block() as block:
            def run(eng_name):
                def body(e):
                    for waits, fn, sem, inc in self.ops[eng_name]:
                        for (s, v) in waits:
                            e.wait_ge(s, v)
                        ins = fn(e)
                        if inc is None:
                            ins.then_inc(sem)
                        else:
                            ins.then_inc(sem, inc)
                    if eng_name == "sp":
                        for (s, v) in final_waits:
                            e.wait_ge(s, v)
                        for en in ("pe", "act", "dve", "pool"):
                            if self.cnt[en]:
                                e.wait_ge(self.esem[en], self.cnt[en])
                return body
            block.sync(run("sp"))
            block.tensor(run("pe"))
            block.scalar(run("act"))
            block.vector(run("dve"))
            block.gpsimd(run("pool"))


def build_nc(depth=DEPTH, units=(0, 1, 2), dbg=False, stop_after=None):
    nc = bass.Bass("TRN2", target_bir_lowering=False)
    P = Prog(nc)

    def din(name, shape, dt=F32):
        return nc.dram_tensor(name, list(shape), dt, kind="ExternalInput").ap()

    def dout(name, shape):
        return nc.dram_tensor(name, list(shape), F32, kind="ExternalOutput").ap()

    xp = din("xp", [4, 256, D])
    xs = din("xs", [512, D])
    cm = din("cm", [D])
    cctx = din("c_ctx", [D])
    cgk = din("cgk", [DEPTH, 512, 128])
    cgv = din("cgv", [DEPTH, 512, 128])
    cdk = din("cdk", [DEPTH, 512, 256])
    cdv = din("cdv", [DEPTH, 512, 256])
    w_ada = din("w_ada", [DEPTH, D, 6 * D])
    b_ada = din("b_ada", [DEPTH, 6 * D])
    norm1_g = din("norm1_g", [DEPTH, D])
    norm2_g = din("norm2_g", [DEPTH, D])
    w_in = din("w_in", [DEPTH, D, 2048])
    w_fourier = din("w_fourier", [DEPTH, 256, 256])
    q_norm_g = din("q_norm_g", [DEPTH, 64])
    k_norm_g = din("k_norm_g", [DEPTH, 64])
    lam_in = [din(n, [DEPTH, 32]) for n in ("lambda_q1", "lambda_k1", "lambda_q2", "lambda_k2")]
    diff_norm_g = din("diff_norm_g", [DEPTH, 64])
    sgu_norm_g = din("sgu_norm_g", [DEPTH, 256])
    w_spatial = din("w_spatial", [DEPTH, 4, 128, 128])
    b_spatial = din("b_spatial", [DEPTH, 4, 128])
    w_gate = din("w_gate", [DEPTH, D, 4 * D])
    w_branch = din("w_branch", [DEPTH, 4, 256, D])
    w_out = din("w_out", [DEPTH, D, D])
    w_mlp1 = din("w_mlp1", [DEPTH, D, 4 * D])
    w_mlp2 = din("w_mlp2", [DEPTH, 4 * D, D])
    final_g = din("final_norm_g", [D])
    k_ident = din("k_ident", [128, 128])
    k_ones = din("k_ones", [128, 128])
    k_blk64 = din("k_blk64", [128, 128])
    k_ccss = din("k_ccss", [128, 256])
    k_rotg = din("k_rotg", [128, 128])
    k_rotd = din("k_rotd", [128, 128])
    k_mask = din("k_mask", [128, 6])
    k_rope = din("k_rope", [4, 128, 512])
    k_dft256 = din("k_dft256", [2, 256, 256])
    k_dftc = din("k_dftc", [2048, 512])
    k_dfts = din("k_dfts", [2048, 512])
    yp = dout("yp", [4, 256, D])
    ys = dout("ys", [512, D])
    ngk = dout("ngk", [4, DEPTH, 256, 128])
    ngv = dout("ngv", [4, DEPTH, 256, 128])
    ndk = dout("ndk", [4, DEPTH, 256, 256])
    ndv = dout("ndv", [4, DEPTH, 256, 256])
    PR = {"g": 256, "d": 512, "x": 512}
    bounce = {p: [nc.dram_tensor("bounce_%s%d" % (p, l), [PR[p], 512], F32).ap() for l in range(DEPTH)] for p in PR}
    gath = {p: [nc.dram_tensor("gath_%s%d" % (p, l), [4 * PR[p], 512], F32).ap() for l in range(DEPTH)] for p in PR}

    dbg_out = dout("dbg", [16, 128, 512]) if dbg else None
    dbg_n = [0]
    dbg_names = []

    def dump(name, ap, R):
        if not dbg:
            return
        i = dbg_n[0]
        dbg_n[0] += 1
        dbg_names.append(name)
        ncol = ap.shape[-1] if len(ap.shape) == 2 else None
        P.dma("sp", dbg_out[i, :, 0:ap.shape[1]], ap, R=R, W=[], slot="dbg")
    nc._dbg_names = dbg_names

    def sb(name, shape, dt=F32):
        return nc.alloc_sbuf_tensor(name, list(shape), dt)

    xT = sb("xT", [128, NU, 8, T])
    hT = sb("hT", [128, 8, T], F32R)
    bigR = sb("bigR", [128, 17, T], F32R)
    TT = sb("TT", [128, 4, 896])
    dft256t = sb("dft256t", [128, 2, 2, 256], F32R)
    slabs = sb("slabs", [128, 4, 4, 512], F32R)
    KT = sb("KT", [128, 2560], F32R)
    VA = sb("VA", [128, 20, 2, 65], BF16)
    PT = sb("PT", [128, 2, 512], BF16)
    tokst = KT[:, 0:1536].rearrange("p (i c) -> p i c", c=384)
    tok = sb("tok", [128, 4, 256])
    sq = sb("sq", [128, 2, T], F32R)
    scr = sb("scr", [128, 2, T])
    cstage = scr[:, 1, :].rearrange("p (i c) -> p i c", c=128)
    DvN = sq[:].rearrange("p a t -> p (a t)").rearrange("p (i c) -> p i c", c=256)
    rstd = sb("rstd", [128, T])
    small = sb("small", [128, 64])
    ident = sb("ident", [128, 128])
    ones_r = sb("ones_r", [128, 128], F32R)
    blk64_r = sb("blk64_r", [128, 128], F32R)
    ccss_r = sb("ccss_r", [128, 256], F32R)
    rotg_r = sb("rotg_r", [128, 128], F32R)
    rotd_r = sb("rotd_r", [128, 128], F32R)
    rope = sb("rope", [128, 4, 512])
    dft256 = dft256t[:]
    xcs = sb("xcs", [128, 4, 512], F32R)
    wf_r = sb("wf_r", [128, 2, 256], F32R)
    wsp = sb("wsp", [128, 4, 128])
    wspT = sb("wspT", [128, 4, 128], F32R)
    bsT = sb("bsT", [128, 4])
    condT = sb("condT", [128, 8, 2])
    scond = sb("scond", [128, 8, 2], F32R)
    modT = sb("modT", [128, 48, 2])
    badaT = sb("badaT", [128, 48])
    A1 = sb("A1", [128, 8, 2])
    A2 = sb("A2", [128, 8, 2])
    n1g = sb("n1g", [128, 8])
    n2g = sb("n2g", [128, 8])
    fng = sb("fng", [128, 8])
    qg = sb("qg", [128, 1])
    kg = sb("kg", [128, 1])
    lamv = sb("lamv", [128, 4, 32])
    neglam = sb("neglam", [128, 1])
    dng = sb("dng", [128, 64])
    sgg = sb("sgg", [128, 256])

    NPS = 8
    psb = [nc.alloc_psum_tensor("ps%d" % i, [128, 512], F32) for i in range(NPS)]
    rot_state = [0]

    def rot():
        i = 4 + rot_state[0] % 4
        rot_state[0] += 1
        return i

    def mm(ps_i, out_ap, lhsT, rhs, start, stop, R, **kw):
        P.op("pe", ("matmul", dict(out=out_ap, lhsT=lhsT, rhs=rhs, start=start, stop=stop, **kw)),
             R=R, W=["ps%d" % ps_i])

    def tr(ps_i, out_ap, in_ap, R):
        P.op("pe", ("transpose", dict(out=out_ap, in_=in_ap, identity=ident[:])), R=list(R) + ["ident"], W=["ps%d" % ps_i])

    def act(out, in_, func, R, W, **kw):
        P.op("act", ("activation", dict(out=out, in_=in_, func=func, **kw)), R=R, W=W)

    def BT(c):
        return "B%d" % c

    def XT(i):
        return ["xcs%da" % i, "xcs%db" % i]

    VAP = ["VAp%d" % i for i in range(8)]

    bigr = bigR[:]
    big = bigR[:].bitcast(F32)
    outst = tok[:].rearrange("p a t -> p (a t)").rearrange("p (b d) -> p b d", d=1024)
    OST = {0: ["tok"], 1: ["tok"]}

    wq = []
    wstate = {"issued": 0, "used": 0}
    PREF = 3

    coll_done = set()
    wneed = {}

    def w_issue():
        i = wstate["issued"]
        if i >= len(wq):
            return False
        need = wneed.get(i)
        if need is not None and need not in coll_done:
            return False
        slot = i % 4
        for (dst_fn, src) in wq[i]:
            P.dma("pool", dst_fn(slot), src, R=(["gath_x"] if need is not None else []), W=["slab%d" % slot], slot="slab%d" % slot)
        wstate["issued"] += 1
        return True

    def w_group(n):
        i = wstate["used"]
        while wstate["issued"] <= min(i + 3, len(wq) - 1):
            if not w_issue():
                break
        assert wstate["issued"] >= i + n, "weight slab not issued"
        wstate["used"] += n
        return [(i + k) % 4 for k in range(n)]

    def w_prefetch():
        i = wstate["used"]
        while wstate["issued"] <= min(i + 3, len(wq) - 1):
            if not w_issue():
                break

    def slab_std(src2d, nk=4):
        ncols = src2d.shape[1]
        return [(lambda s, nk=nk, ncols=ncols: slabs[:, s, 0:nk, 0:ncols],
                 src2d.rearrange("(k p) c -> p k c", p=128))]

    def plan_layer_unit(l, sample):
        for g in ((1, 2, 0, 3) if sample else (0, 1, 2, 3)):
            for kh in range(2):
                rows = slice(kh * 512, kh * 512 + 512)
                if g == 0:
                    ent = [(lambda s: slabs[:, s, :, 0:256],
                            w_in[l, rows, 0:256].rearrange("(k p) c -> p k c", p=128))]
                    for b in range(2):
                        for a in range(2):
                            ent.append((lambda s, b=b, a=a: slabs[:, s, :, 256 + 128 * b + 64 * a:320 + 128 * b + 64 * a],
                                        w_in[l, rows, 256 + 128 * a + 64 * b:320 + 128 * a + 64 * b].rearrange("(k p) d -> p k d", p=128)))
                    wq.append(ent)
                else:
                    wq.append(slab_std(w_in[l, rows, g * 512:(g + 1) * 512]))
        if sample:
            for s4 in range(4):
                wneed[len(wq)] = l
                wq.append([(lambda s: slabs[:, s, :, :],
                            gath["x"][l][s4 * 512:(s4 + 1) * 512, :].rearrange("(k p) c -> p k c", p=128))])
                wq.append(slab_std(k_dftc[s4 * 512:(s4 + 1) * 512, :]))
                wq.append(slab_std(k_dfts[s4 * 512:(s4 + 1) * 512, :]))
        for n in range(4):
            for cg in range(2):
                cols = slice(n * 1024 + cg * 512, n * 1024 + cg * 512 + 512)
                for kh in range(2):
                    wq.append(slab_std(w_gate[l, kh * 512:(kh + 1) * 512, cols]))
                wq.append(slab_std(w_branch[l, n, :, cg * 512:(cg + 1) * 512], nk=2))
        for cg in range(2):
            for kh in range(2):
                wq.append(slab_std(w_out[l, kh * 512:(kh + 1) * 512, cg * 512:(cg + 1) * 512]))
        for half in range(2):
            for g in range(4):
                for kh in range(2):
                    wq.append(slab_std(w_mlp1[l, kh * 512:(kh + 1) * 512, half * 2048 + g * 512: half * 2048 + (g + 1) * 512]))
            for cg in range(2):
                for kq in range(4):
                    r0 = half * 2048 + kq * 512
                    wq.append(slab_std(w_mlp2[l, r0:r0 + 512, cg * 512:(cg + 1) * 512]))

    def plan_ada(l):
        for g in range(12):
            for kh in range(2):
                wq.append(slab_std(w_ada[l, kh * 512:(kh + 1) * 512, g * 512:(g + 1) * 512]))

    for l in range(depth):
        plan_ada(l)
        for u in units:
            plan_layer_unit(l, u == 2)

    def ld(dst, src, W, slot, q="sp", **kw):
        P.dma(q, dst, src, R=[], W=W, slot=slot, **kw)

    ld(ident[:], k_ident, ["ident"], "c0")
    ld(ones_r[:], k_ones, ["ones"], "c1", q="pool")
    ld(blk64_r[:], k_blk64, ["blk64"], "c2", q="pool")
    ld(ccss_r[:], k_ccss, ["ccss"], "c3", q="pool")
    ld(rotg_r[:], k_rotg, ["rotg"], "c4", q="pool")
    ld(rotd_r[:], k_rotd, ["rotd"], "c5", q="pool")
    ld(rope[:], k_rope.rearrange("a p t -> p a t"), ["rope"], "c6")
    for a_ in range(2):
        ld(dft256[:, a_], k_dft256[a_].rearrange("(j p) t -> p j t", p=128), ["dft256"], "c7", q="pool")
    ld(fng[:], final_g.rearrange("(c p) -> p c", p=128), ["fng"], "c8", allow_slow_non_contiguous=True)
    ld(condT[:, :, 0], cctx.rearrange("(c p) -> p c", p=128), ["condT"], "c9", allow_slow_non_contiguous=True)
    ld(condT[:, :, 1], cm.rearrange("(c p) -> p c", p=128), ["condT"], "c9", allow_slow_non_contiguous=True)
    P.op("pool", lambda e: e.memset(VA[:, :, :, 64:65], 1.0), W=["VAones"])
    def x_load():
        for u in range(NU):
            for i in range(4):
                if u < 2:
                    src = xp[2 * u + i // 2, (i % 2) * 128:(i % 2) * 128 + 128, :]
                else:
                    src = xs[i * 128:(i + 1) * 128, :]
                b = 0
                ld(outst[:, b, :], src, OST[b], "ost%d" % b)
                for c in range(8):
                    pi = rot()
                    tr(pi, psb[pi][:, 0:128], outst[:, b, c * 128:(c + 1) * 128], R=OST[b])
                    eng = "dve" if c % 2 == 0 else "act"
                    dst = xT[:, u, c, i * 128:(i + 1) * 128]
                    if eng == "dve":
                        P.op("dve", ("tensor_copy", dict(out=dst, in_=psb[pi][:, 0:128])),
                             R=["ps%d" % pi], W=["x%d_%d" % (u, c)])
                    else:
                        act(dst, psb[pi][:, 0:128], AF.Copy, R=["ps%d" % pi], W=["x%d_%d" % (u, c)])
    act(scond[:], condT[:], AF.Silu, R=["condT"], W=["scond"])

    def norm_mod(u, ci, Amat, shift_c0, gtag):
        pi = rot()
        for c in range(8):
            b = c % 2
            act(sq[:, b, :], xT[:, u, c, :], AF.Square, R=["x%d_%d" % (u, c)], W=["sq%d" % b])
            mm(pi, psb[pi][:], ones_r[:], sq[:, b, :], c == 0, c == 7, R=["ones", "sq%d" % b])
        act(rstd[:], psb[pi][:], AF.Sqrt, R=["ps%d" % pi, "eps"], W=["rstd"], scale=1.0 / D, bias=eps_ap)
        P.op("dve", ("reciprocal", dict(out=rstd[:], in_=rstd[:])), R=["rstd"], W=["rstd"])
        for c in range(8):
            b = c % 2
            P.op("dve", ("scalar_tensor_tensor", dict(
                out=scr[:, b, :], in0=xT[:, u, c, :], scalar=Amat[:, c, ci:ci + 1], in1=rstd[:],
                op0=ALU.mult, op1=ALU.mult)), R=["x%d_%d" % (u, c), gtag, "rstd"], W=["scr%d" % b])
            act(hT[:, c, :], scr[:, b, :], AF.Identity, R=["scr%d" % b, "mod"], W=["h%d" % c],
                bias=modT[:, shift_c0 + c, ci:ci + 1])

    eps_t = sb("eps_t", [128, 1])
    maskc = sb("maskc", [128, 6])
    ld(maskc[:], k_mask, ["maskc"], "c10")
    P.op("pool", lambda e: e.memset(eps_t[:], EPS), W=["eps"])
    eps_ap = eps_t[:]

    def rms_rows(ps_or_ap, R_in, nparts_tag):
        pass

    def final_out(u):
        pi = rot()
        for c in range(8):
            b = c % 2
            act(sq[:, b, :], xT[:, u, c, :], AF.Square, R=["x%d_%d" % (u, c)], W=["sq%d" % b])
            mm(pi, psb[pi][:], ones_r[:], sq[:, b, :], c == 0, c == 7, R=["ones", "sq%d" % b])
        act(rstd[:], psb[pi][:], AF.Sqrt, R=["ps%d" % pi, "eps"], W=["rstd"], scale=1.0 / D, bias=eps_ap)
        P.op("dve", ("reciprocal", dict(out=rstd[:], in_=rstd[:])), R=["rstd"], W=["rstd"])
        scrv = scr[:].rearrange("p a t -> p (a t)").rearrange("p (c q) -> p c q", q=128)
        for i in range(4):
            for c in range(8):
                P.op("dve", ("scalar_tensor_tensor", dict(
                    out=scrv[:, c, :], in0=xT[:, u, c, i * 128:(i + 1) * 128], scalar=fng[:, c:c + 1], in1=rstd[:, i * 128:(i + 1) * 128],
                    op0=ALU.mult, op1=ALU.mult)), R=["x%d_%d" % (u, c), "fng", "rstd"], W=["scr0", "scr1"])
            for c in range(8):
                pi = rot()
                tr(pi, psb[pi][:, 0:128], scrv[:, c, :], R=["scr0", "scr1"])
                if c % 2 == 0:
                    P.op("dve", ("tensor_copy", dict(out=outst[:, 0, c * 128:(c + 1) * 128], in_=psb[pi][:, 0:128])),
                         R=["ps%d" % pi], W=["tok"])
                else:
                    act(outst[:, 0, c * 128:(c + 1) * 128], psb[pi][:, 0:128], AF.Copy, R=["ps%d" % pi], W=["tok"])
            if u < 2:
                dst = yp[2 * u + i // 2, (i % 2) * 128:(i % 2) * 128 + 128, :]
            else:
                dst = ys[i * 128:(i + 1) * 128, :]
            P.dma("sp", dst, outst[:, 0, :], R=["tok"], W=[], slot="oy")


    for l in range(depth):
        lam_init = 0.8 - 0.6 * math.exp(-0.3 * l)
        lam_scale = 1.0 - lam_init
        ld(badaT[:], b_ada[l].rearrange("(c p) -> p c", p=128), ["badaT"], "l0", allow_slow_non_contiguous=True)
        ld(n1g[:], norm1_g[l].rearrange("(c p) -> p c", p=128), ["n1g"], "l1", allow_slow_non_contiguous=True)
        ld(n2g[:], norm2_g[l].rearrange("(c p) -> p c", p=128), ["n2g"], "l2", allow_slow_non_contiguous=True)
        for hh in range(2):
            ld(qg[hh * 64:(hh + 1) * 64, :], q_norm_g[l].rearrange("(p o) -> p o", o=1), ["qg"], "l3", allow_slow_non_contiguous=True)
            ld(kg[hh * 64:(hh + 1) * 64, :], k_norm_g[l].rearrange("(p o) -> p o", o=1), ["kg"], "l4", allow_slow_non_contiguous=True)
        for j in range(4):
            ld(lamv[:, j, :], lam_in[j][l].partition_broadcast(128), ["lamv"], "l5")
        ld(dng[:], diff_norm_g[l].partition_broadcast(128), ["dng"], "l6")
        ld(sgg[:], sgu_norm_g[l].partition_broadcast(128), ["sgg"], "l7")
        ld(wsp[:], w_spatial[l].rearrange("g q p -> q g p"), ["wsp"], "l8")
        ld(bsT[:], b_spatial[l].rearrange("g q -> q g"), ["bsT"], "l9", allow_slow_non_contiguous=True)
        ld(wf_r[:], w_fourier[l].rearrange("(k p) c -> p k c", p=128), ["wf"], "l10", q="pool")
        P.op("dve", ("tensor_tensor", dict(out=small[:, 0:32], in0=lamv[:, 0, :], in1=lamv[:, 1, :], op=ALU.mult)), R=["lamv"], W=["small"])
        P.op("dve", ("tensor_tensor", dict(out=small[:, 32:64], in0=lamv[:, 2, :], in1=lamv[:, 3, :], op=ALU.mult)), R=["lamv", "small"], W=["small"])
        P.op("dve", ("tensor_reduce", dict(out=neglam[:], in_=small[:, 0:32], axis=AX.X, op=ALU.add)), R=["small"], W=["neglam"])
        P.op("dve", ("tensor_reduce", dict(out=small[:, 0:1], in_=small[:, 32:64], axis=AX.X, op=ALU.add)), R=["small", "neglam"], W=["small"])
        act(neglam[:], neglam[:], AF.Exp, R=["neglam"], W=["neglam"])
        act(small[:, 1:2], small[:, 0:1], AF.Exp, R=["small"], W=["small"])
        P.op("dve", ("tensor_tensor", dict(out=neglam[:], in0=small[:, 1:2], in1=neglam[:], op=ALU.subtract)), R=["small", "neglam"], W=["neglam"])
        P.op("dve", ("tensor_scalar_add", dict(out=neglam[:], in0=neglam[:], scalar1=-lam_init)), R=["neglam"], W=["neglam"])
        P.op("dve", ("tensor_scalar_mul", dict(out=dng[:], in0=dng[:], scalar1=lam_scale)), R=["dng"], W=["dng"])
        for g in range(4):
            pi = rot()
            tr(pi, psb[pi][:, 0:128], wsp[:, g, :], R=["wsp"])
            P.op("dve", ("tensor_copy", dict(out=wspT[:, g, :], in_=psb[pi][:, 0:128])), R=["ps%d" % pi], W=["wspT"])
        for g in range(12):
            s0, s1 = w_group(2)
            pi = rot()
            for kc in range(8):
                sl_ = s0 if kc < 4 else s1
                mm(pi, psb[pi][0:2, :], scond[:, kc, :], slabs[:, sl_, kc % 4, :], kc == 0, kc == 7,
                   R=["slab%d" % sl_, "scond"])
            P.op("dve", ("tensor_copy", dict(out=rstd[0:2, :], in_=psb[pi][0:2, :])), R=["ps%d" % pi], W=["rstd"])
            for j in range(4):
                pj = rot()
                P.op("pe", ("transpose", dict(out=psb[pj][:, 0:2], in_=rstd[0:2, j * 128:(j + 1) * 128], identity=ident[0:2, 0:2])),
                     R=["rstd", "ident"], W=["ps%d" % pj])
                cc = g * 4 + j
                P.op("dve", ("tensor_scalar", dict(
                    out=modT[:, cc, :], in0=psb[pj][:, 0:2], scalar1=badaT[:, cc:cc + 1], scalar2=None, op0=ALU.add)),
                    R=["ps%d" % pj, "badaT"], W=["mod"])
        if l == 0:
            x_load()
        for ci in range(2):
            P.op("dve", ("scalar_tensor_tensor", dict(
                out=A1[:, :, ci], in0=modT[:, 8:16, ci], scalar=1.0, in1=n1g[:], op0=ALU.add, op1=ALU.mult)),
                R=["mod", "n1g"], W=["A1"])
            P.op("dve", ("scalar_tensor_tensor", dict(
                out=A2[:, :, ci], in0=modT[:, 32:40, ci], scalar=1.0, in1=n2g[:], op0=ALU.add, op1=ALU.mult)),
                R=["mod", "n2g"], W=["A2"])

        for u in units:
            sample = (u == 2)
            ci = 1 if sample else 0
            nseq = 1 if sample else 2
            sl = T // nseq
            norm_mod(u, ci, A1, 0, "A1")
            hR = ["h%d" % c for c in range(8)]
            TTv = TT[:]
            TT_tags = ["TT"]
            def w_in_groups(gs):
                for g in gs:
                    s0, s1 = w_group(2)

                    def wsl(kc, c0, c1):
                        s = s0 if kc < 4 else s1
                        return slabs[:, s, kc % 4, c0:c1], "slab%d" % s

                    def ftype(c0, dst_chunk, evac):
                        pi = rot()
                        for kc in range(8):
                            w_ap, wt = wsl(kc, c0, c0 + 128)
                            mm(pi, psb[pi][:], w_ap, hT[:, kc, :], kc == 0, kc == 7, R=[wt, "h%d" % kc])
                        evac(pi, dst_chunk)

                    def ttype(c0, ncols, toff, post=None):
                        for i in range(4):
                            pi = rot()
                            for kc in range(8):
                                w_ap, wt = wsl(kc, c0, c0 + ncols)
                                mm(pi, psb[pi][:, 0:ncols], hT[:, kc, i * 128:(i + 1) * 128], w_ap, kc == 0, kc == 7,
                                   R=[wt, "h%d" % kc])
                            dst = TTv[:, i, toff:toff + ncols]
                            if post is None:
                                P.op("dve", ("tensor_copy", dict(out=dst, in_=psb[pi][:, 0:ncols])),
                                     R=["ps%d" % pi], W=TT_tags)
                            else:
                                act(dst, psb[pi][:, 0:ncols], post, R=["ps%d" % pi], W=TT_tags)

                    def ev_copy_r(pi, ch):
                        act(bigr[:, ch, :], psb[pi][:], AF.Copy, R=["ps%d" % pi], W=[BT(ch)])

                    def ev_copy_f(pi, ch):
                        P.op("dve", ("tensor_copy", dict(out=bigr[:, ch, :], in_=psb[pi][:])), R=["ps%d" % pi], W=[BT(ch)])

                    if g == 0:
                        ftype(0, 0, ev_copy_r)
                        ftype(128, 1, ev_copy_r)
                        ftype(256, 2, ev_copy_f)
                        ftype(384, 3, ev_copy_f)
                    elif g == 1:
                        ftype(0, 4, ev_copy_f)
                        ttype(128, 128, 0)
                        ftype(256, 5, ev_copy_f if sample else ev_copy_r)
                        ftype(384, 6, ev_copy_f if sample else ev_copy_r)
                    elif g == 2:
                        ftype(0, 7, ev_copy_f if sample else ev_copy_r)
                        ftype(128, 8, ev_copy_f if sample else ev_copy_r)
                        ttype(256, 256, 128)
                    else:
                        ttype(0, 256, 384, post=AF.Gelu_apprx_tanh)
                        ttype(256, 256, 640, post=AF.Gelu_apprx_tanh)

            def qk_norm(chs):
                for ch in chs:
                    gvec, gt = (kg, "kg") if ch == 4 else (qg, "qg")
                    act(sq[:, 0, :], big[:, ch, :], AF.Square, R=[BT(ch)], W=["sq0"])
                    pi = rot()
                    mm(pi, psb[pi][:], blk64_r[:], sq[:, 0, :], True, True, R=["blk64", "sq0"])
                    act(rstd[:], psb[pi][:], AF.Sqrt, R=["ps%d" % pi, "eps"], W=["rstd"], scale=1.0 / 64, bias=eps_ap)
                    P.op("dve", ("reciprocal", dict(out=rstd[:], in_=rstd[:])), R=["rstd"], W=["rstd"])
                    P.op("dve", ("scalar_tensor_tensor", dict(
                        out=bigr[:, ch, :], in0=big[:, ch, :], scalar=gvec[:, 0:1], in1=rstd[:], op0=ALU.mult, op1=ALU.mult)),
                        R=[BT(ch), gt, "rstd"], W=[BT(ch)])

            def do_rope(chs):
                for ch in chs:
                    isg = ch <= 4
                    rm = rotg_r if isg else rotd_r
                    rt = "rotg" if isg else "rotd"
                    cosi, sini = (0, 1) if isg else (2, 3)
                    act(sq[:, 1, :], big[:, ch, :], AF.Copy, R=[BT(ch)], W=["sq1"])
                    pi = rot()
                    mm(pi, psb[pi][:], rm[:], sq[:, 1, :], True, True, R=[rt, "sq1"])
                    P.op("dve", ("tensor_tensor", dict(out=scr[:, 0, :], in0=psb[pi][:], in1=rope[:, sini, :], op=ALU.mult)),
                         R=["ps%d" % pi, "rope"], W=["scr0"])
                    P.op("pool", ("tensor_tensor", dict(out=scr[:, 1, :], in0=big[:, ch, :], in1=rope[:, cosi, :], op=ALU.mult)),
                         R=[BT(ch), "rope"], W=["scr1"])
                    P.op("dve", ("tensor_tensor", dict(out=bigr[:, ch, :], in0=scr[:, 0, :], in1=scr[:, 1, :], op=ALU.add)),
                         R=["scr0", "scr1"], W=[BT(ch)])

            def emit_kv():
                if True:
                    for i in range(4):
                        for (ch, off) in ((4, 0), (7, 128), (8, 256)):
                            pi = rot()
                            tr(pi, psb[pi][:, 0:128], big[:, ch, i * 128:(i + 1) * 128], R=[BT(ch)])
                            P.op("dve", ("tensor_copy", dict(out=tokst[:, i, off:off + 128], in_=psb[pi][:, 0:128])),
                                 R=["ps%d" % pi], W=["KT"])
                    sq0 = 2 * u
                    for s_ in range(2):
                        P.dma("sp", ngk[sq0 + s_, l].rearrange("(j p) f -> p j f", p=128),
                              tokst[:, 2 * s_:2 * s_ + 2, 0:128].bitcast(F32), R=["KT"], W=[], slot="o_ngk%d" % s_)
                        P.dma("sp", ndk[sq0 + s_, l].rearrange("(j p) f -> p j f", p=128),
                              tokst[:, 2 * s_:2 * s_ + 2, 128:384].bitcast(F32), R=["KT"], W=[], slot="o_ndk%d" % s_)
                        P.dma("sp", ngv[sq0 + s_, l].rearrange("(j p) f -> p j f", p=128),
                              TTv[:, 2 * s_:2 * s_ + 2, 0:128], R=TT_tags, W=[], slot="o_ngv%d" % s_)
                        P.dma("sp", ndv[sq0 + s_, l].rearrange("(j p) f -> p j f", p=128),
                              TTv[:, 2 * s_:2 * s_ + 2, 128:384], R=TT_tags, W=[], slot="o_ndv%d" % s_)

            def dv_rms():
                for i in range(4):
                    dv_ap = TTv[:, i, 640:896]
                    act(scr[:, 0, 0:256], dv_ap, AF.Square, R=TT_tags, W=["scr0"], accum_out=small[:, 8 + i:9 + i])
                P.op("act", ("activation", dict(out=small[:, 12:16], in_=small[:, 8:12], func=AF.Sqrt, scale=1.0 / 256, bias=eps_ap)),
                     R=["scr0", "eps"], W=["small2"])
                P.op("dve", ("reciprocal", dict(out=small[:, 12:16], in_=small[:, 12:16])), R=["small2"], W=["small2"])
                for i in range(4):
                    dv_ap = TTv[:, i, 640:896]
                    P.op("dve", ("scalar_tensor_tensor", dict(
                        out=DvN[:, i, :], in0=dv_ap, scalar=small[:, 12 + i:13 + i], in1=sgg[:], op0=ALU.mult, op1=ALU.mult)),
                        R=TT_tags + ["small2", "sgg"], W=["sq0", "sq1"])


            def chan_dft():
                for i in range(4):
                    pi = rot()
                    for j in range(2):
                        mm(pi, psb[pi][:, j * 256:(j + 1) * 256], bigr[:, j, i * 128:(i + 1) * 128], ccss_r[:], True, True,
                           R=[BT(j), "ccss"])
                    act(xcs[:, i, :], psb[pi][:], AF.Copy, R=["ps%d" % pi], W=XT(i))

            def fourier_prompt():
                for s in range(2):
                    for j in range(2):
                        pi = rot()
                        n = 0
                        for i2 in range(2):
                            i = s * 2 + i2
                            for part in range(2):
                                mm(pi, psb[pi][:, 0:256], xcs[:, i, j * 256 + part * 128: j * 256 + part * 128 + 128],
                                   dft256[:, part, i2, :], n == 0, n == 3, R=XT(i) + ["dft256"])
                                n += 1
                        act(bigr[:, 15 + j, s * 256:(s + 1) * 256], psb[pi][:, 0:256], AF.Copy, R=["ps%d" % pi], W=[BT(15 + j)])

            def exchange_part(p_):
                bz = bounce[p_][l]
                if p_ == "g":
                    P.dma("sp", bz[0:128, :], big[:, 4, :], R=[BT(4)], W=["bounce_g"], slot="bzg")
                    P.dma("sp", bz[128:256, :].rearrange("r (a c) -> (r a) c", c=128).rearrange("(j p) c -> p j c", p=128),
                          TTv[:, :, 0:128], R=TT_tags, W=["bounce_g"], slot="bzg")
                else:
                    P.dma("sp", bz[0:256, :].rearrange("(c p) t -> p c t", p=128), big[:, 7:9, :], R=[BT(7), BT(8)], W=["bounce_d"], slot="bzd")
                    P.dma("sp", bz[256:512, :].rearrange("r (a c) -> (r a) c", c=256).rearrange("(j p) c -> p j c", p=128),
                          TTv[:, :, 128:384], R=TT_tags, W=["bounce_d"], slot="bzd")
                w_prefetch()
                P.coll(lambda e, l=l, p_=p_: e.collective_compute(
                    "AllGather", ALU.bypass, replica_groups=[[0, 1, 2, 3], [4, 5, 6, 7]],
                    ins=[bounce[p_][l].opt()], outs=[gath[p_][l].opt()]), R=["bounce_" + p_], W=["gath_" + p_], slot="cc_" + p_)

            def exchange_x():
                bx_ = bounce["x"][l]
                P.dma("sp", bx_.rearrange("(j p) c -> p j c", p=128), xcs[:].bitcast(F32), R=XT(0) + XT(1) + XT(2) + XT(3), W=["bounce_x"], slot="bzx")
                w_prefetch()
                P.coll(lambda e, l=l: e.collective_compute(
                    "AllGather", ALU.bypass, replica_groups=[[0, 1, 2, 3], [4, 5, 6, 7]],
                    ins=[bounce["x"][l].opt()], outs=[gath["x"][l].opt()]), R=["bounce_x"], W=["gath_x"], slot="cc_x")

            def fourier_sample():
                acc = [rot(), rot()]
                n = 0
                for s4 in range(4):
                    sx, sc, ss = w_group(3)
                    for i in range(4):
                        for j in range(2):
                            mm(acc[j], psb[acc[j]][:], slabs[:, sx, i, j * 256:j * 256 + 128], slabs[:, sc, i, :],
                               n == 0, False, R=["slab%d" % sx, "slab%d" % sc])
                            mm(acc[j], psb[acc[j]][:], slabs[:, sx, i, j * 256 + 128:j * 256 + 256], slabs[:, ss, i, :],
                               False, n == 15, R=["slab%d" % sx, "slab%d" % ss])
                        n += 1
                for j in range(2):
                    act(xcs[:, j, :], psb[acc[j]][:], AF.Copy, R=["ps%d" % acc[j]], W=XT(j))

            def out_a(ysrc, ytags):
                for c in range(2):
                    pi = rot()
                    for kc in range(2):
                        mm(pi, psb[pi][:], wf_r[:, kc, c * 128:(c + 1) * 128], ysrc(kc), kc == 0, kc == 1, R=["wf"] + (ytags[kc] if isinstance(ytags[kc], list) else [ytags[kc]]))
                    act(bigr[:, 9 + c, :], psb[pi][:], AF.Copy, R=["ps%d" % pi], W=[BT(9 + c)])

            def load_keys_sample(kind, hc):
                if kind == "g":
                    gp, gtag, grows = gath["g"][l], "gath_g", 256
                    kcache, kcols, krow0 = cgk[l], slice(0, 128), 0
                    vcache, vcols, vrow0, vw = cgv[l], slice(0, 128), 128, 128
                else:
                    gp, gtag, grows = gath["d"][l], "gath_d", 512
                    kcache, kcols, krow0 = cdk[l], slice(hc * 128, hc * 128 + 128), hc * 128
                    vcache, vcols, vrow0, vw = cdv[l], slice(hc * 128, hc * 128 + 128), 256, 256
                gl = gp.rearrange("(r x) c -> x r c", x=grows)
                w_prefetch()
                P.dma("sp", cstage[:, :, :], kcache[:, kcols].rearrange("(j p) c -> p j c", p=128), R=[], W=["scr1"], slot="cst")
                for j in range(4):
                    pi = rot()
                    tr(pi, psb[pi][:, 0:128], cstage[:, j, :], R=["scr1"])
                    act(KT[:, j * 128:(j + 1) * 128], psb[pi][:, 0:128], AF.Copy, R=["ps%d" % pi], W=["KT"])
                P.dma("pool", KT[:, 512:2560].rearrange("p (r t) -> p r t", r=4), gl[krow0:krow0 + 128, :, :],
                      R=[gtag], W=["KT"], slot="ktl")
                for hh_ in range(2):
                    vc = slice(vcols.start + hh_ * 64, vcols.start + hh_ * 64 + 64)
                    P.dma("pool", VA[:, 0:4, hh_, 0:64], vcache[:, vc].rearrange("(j p) d -> p j d", p=128),
                          R=[], W=["VA"] + VAP, slot="val")
                    for r in range(4):
                        src = gp[r * grows + vrow0: r * grows + vrow0 + vw, :].rearrange("x (a c) -> (x a) c", c=vw)
                        src = src[:, vc].rearrange("(j p) d -> p j d", p=128)
                        P.dma("pool", VA[:, 4 + 4 * r: 8 + 4 * r, hh_, 0:64], src, R=[gtag], W=["VA"] + VAP, slot="val")

            def attention(gqa_preloaded=False):
                nk = 2560 if sample else 256
                nkt = nk // 128
                nq = sl
                nqt = nq // 128

                gctr = [0]

                def attn_group(heads, s):
                    nh = len(heads)
                    g_ = gctr[0]
                    gctr[0] += 1
                    if sample:
                        qoff, vb, vtag = 0, 0, "VA"
                        qtag = lambda hi: XT(hi)
                    else:
                        qoff, vb, vtag = (g_ % 2) * 256, (g_ % 8) * 2, VAP[g_ % 8]
                        qtag = lambda hi: [XT(hi)[g_ % 2]]
                    for hi, hd in enumerate(heads):
                        act(xcs[:, hi, qoff:qoff + nq], big[:, hd["qch"], s * sl: s * sl + nq], AF.Copy,
                            R=[BT(hd["qch"]), "maskc"], W=qtag(hi), scale=maskc[:, hd["mcol"]:hd["mcol"] + 1])
                    steps = [(hi, kt) for hi in range(nh) for kt in range(nkt)]
                    pending = []
                    sbank = {}

                    def issue_s(i):
                        hi, kt = steps[i]
                        hd = heads[hi]
                        pi = rot()
                        mm(pi, psb[pi][:, 0:nq], hd["kfn"](kt), xcs[:, hi, qoff:qoff + nq], True, True, R=[hd["ktag"]] + qtag(hi))
                        sbank[i] = pi

                    issue_s(0)
                    for i in range(len(steps)):
                        if i + 1 < len(steps):
                            issue_s(i + 1)
                        hi, kt = steps[i]
                        hd = heads[hi]
                        pi = sbank[i]
                        b = i % 2
                        act(PT[:, b, 0:nq], psb[pi][:, 0:nq], AF.Exp, R=["ps%d" % pi], W=["PT%d" % b], scale=hd["scale"])
                        ob = hd["obank"]
                        for qt in range(nqt):
                            mm(ob, psb[ob][:, qt * 65:(qt + 1) * 65], PT[:, b, qt * 128:(qt + 1) * 128],
                               VA[:, vb + kt, hd["vslot"], :], kt == 0 and qt == 0, kt == nkt - 1 and qt == nqt - 1,
                               R=["PT%d" % b, vtag, "VAones"], skip_group_check=True)
                        for pd in [p for p in pending if p[0] <= i]:
                            pending.remove(pd)
                            pd[1]()
                        if kt == nkt - 1 and hd["post"] is not None:
                            p1, p2 = hd["post"]
                            p1()
                            if p2 is not None:
                                pending.append((i + 3, p2))
                    for pd in pending:
                        pd[1]()

                def gqa_post(h, s, ob):
                    def p1():
                        ov = psb[ob][:, 0:nqt * 65].rearrange("p (q c) -> p q c", c=65)
                        c0 = 16 + 4 * (h % 2)
                        P.op("dve", ("reciprocal", dict(out=small[:, c0:c0 + nqt], in_=ov[:, :, 64])), R=["ps%d" % ob], W=["small3"])
                        P.op("dve", ("tensor_tensor", dict(
                            out=tok[:, s * nqt:(s + 1) * nqt, h * 64:(h + 1) * 64], in0=ov[:, :, 0:64],
                            in1=small[:, c0:c0 + nqt].unsqueeze(2).broadcast_to([128, nqt, 64]), op=ALU.mult)),
                            R=["ps%d" % ob, "small3"], W=["tok"])
                    return p1, None

                def diff_post(h, s, ob1, ob2):
                    par = h % 2
                    A = scr[:, par, 0:256].rearrange("p (q c) -> p q c", c=64)[:, 0:nqt, :]
                    Bm = scr[:, par, 256:512].rearrange("p (q c) -> p q c", c=64)[:, 0:nqt, :]
                    stag = "scr%d" % par
                    c1, c2, c3 = 16 + 12 * par, 20 + 12 * par, 24 + 12 * par
                    mtag = "smallp%d" % par

                    def p1():
                        o1 = psb[ob1][:, 0:nqt * 65].rearrange("p (q c) -> p q c", c=65)
                        o2 = psb[ob2][:, 0:nqt * 65].rearrange("p (q c) -> p q c", c=65)
                        P.op("dve", ("reciprocal", dict(out=small[:, c1:c1 + nqt], in_=o1[:, :, 64])), R=["ps%d" % ob1], W=[mtag])
                        P.op("dve", ("reciprocal", dict(out=small[:, c2:c2 + nqt], in_=o2[:, :, 64])), R=["ps%d" % ob2, mtag], W=[mtag])
                        P.op("dve", ("tensor_scalar", dict(out=small[:, c2:c2 + nqt], in0=small[:, c2:c2 + nqt], scalar1=neglam[:, 0:1],
                                                              scalar2=None, op0=ALU.mult)), R=[mtag, "neglam"], W=[mtag])
                        P.op("dve", ("tensor_tensor", dict(out=A, in0=o1[:, :, 0:64],
                                                              in1=small[:, c1:c1 + nqt].unsqueeze(2).broadcast_to([128, nqt, 64]), op=ALU.mult)),
                             R=["ps%d" % ob1, mtag], W=[stag])
                        P.op("dve", ("tensor_tensor", dict(out=Bm, in0=o2[:, :, 0:64],
                                                              in1=small[:, c2:c2 + nqt].unsqueeze(2).broadcast_to([128, nqt, 64]), op=ALU.mult)),
                             R=["ps%d" % ob2, mtag, stag], W=[stag])
                        P.op("dve", ("tensor_tensor", dict(out=A, in0=A, in1=Bm, op=ALU.add)), R=[stag], W=[stag])
                        P.op("dve", ("tensor_tensor", dict(out=Bm, in0=A, in1=A, op=ALU.mult)), R=[stag], W=[stag])
                        P.op("dve", ("tensor_reduce", dict(out=small[:, c3:c3 + nqt], in_=Bm, axis=AX.X, op=ALU.add)), R=[stag, mtag], W=[mtag])

                    def p2():
                        P.op("act", ("activation", dict(out=small[:, c3:c3 + nqt], in_=small[:, c3:c3 + nqt], func=AF.Ln, scale=1.0 / 64, bias=eps_ap)),
                             R=[mtag, "eps"], W=[mtag])
                        P.op("act", ("activation", dict(out=small[:, c3:c3 + nqt], in_=small[:, c3:c3 + nqt], func=AF.Exp, scale=-0.5)),
                             R=[mtag], W=[mtag])
                        P.op("dve", ("tensor_tensor", dict(out=A, in0=A,
                                                              in1=small[:, c3:c3 + nqt].unsqueeze(2).broadcast_to([128, nqt, 64]), op=ALU.mult)),
                             R=[stag, mtag], W=[stag])
                        P.op("dve", ("tensor_tensor", dict(out=tok[:, s * nqt:(s + 1) * nqt, h * 64:(h + 1) * 64], in0=A,
                                                              in1=dng[:].unsqueeze(1).broadcast_to([128, nqt, 64]), op=ALU.mult)),
                             R=[stag, "dng"], W=["tok"])
                    return p1, p2

                for s in range(nseq):
                    if sample:
                        if not gqa_preloaded:
                            load_keys_sample("g", 0)
                        kfn = lambda kt: KT[:, kt * 128:(kt + 1) * 128]
                        ktag = "KT"
                    else:
                        for i2 in range(2):
                            i = s * 2 + i2
                            P.op("dve", ("tensor_copy", dict(
                                out=VA[:, (gctr[0] % 8) * 2 + i2, :, 0:64], in_=TTv[:, i, 0:128].rearrange("p (h d) -> p h d", d=64))),
                                R=TT_tags, W=[VAP[gctr[0] % 8], "VA"])
                        kfn = lambda kt, s=s: bigr[:, 4, s * 256 + kt * 128: s * 256 + (kt + 1) * 128]
                        ktag = BT(4)
                    heads = []
                    for h in range(4):
                        heads.append(dict(qch=2 + (h % 2), mcol=h // 2, vslot=h // 2, scale=0.125, obank=h % 2,
                                          kfn=kfn, ktag=ktag, post=gqa_post(h, s, h % 2)))
                    attn_group(heads, s)
                for i in range(4):
                    for c in range(2):
                        pi = rot()
                        tr(pi, psb[pi][:, 0:128], tok[:, i, c * 128:(c + 1) * 128], R=["tok"])
                        act(bigr[:, 11 + c, i * 128:(i + 1) * 128], psb[pi][:, 0:128], AF.Copy, R=["ps%d" % pi], W=[BT(11 + c)])

                dscale = 1.0 / math.sqrt(32.0)
                for s in range(nseq):
                    for hc in range(2):
                        if sample:
                            load_keys_sample("d", hc)
                            kfn = lambda kt: KT[:, kt * 128:(kt + 1) * 128]
                            ktag = "KT"
                        else:
                            for i2 in range(2):
                                i = s * 2 + i2
                                P.op("dve", ("tensor_copy", dict(
                                    out=VA[:, (gctr[0] % 8) * 2 + i2, :, 0:64],
                                    in_=TTv[:, i, 128 + hc * 128: 256 + hc * 128].rearrange("p (h d) -> p h d", d=64))),
                                    R=TT_tags, W=[VAP[gctr[0] % 8], "VA"])
                            kfn = lambda kt, s=s, hc=hc: bigr[:, 7 + hc, s * 256 + kt * 128: s * 256 + (kt + 1) * 128]
                            ktag = BT(7 + hc)
                        heads = []
                        for hh in range(2):
                            h = hc * 2 + hh
                            for m in range(2):
                                heads.append(dict(qch=5 + hc, mcol=2 + hh * 2 + m, vslot=hh, scale=dscale, obank=2 * hh + m,
                                                  kfn=kfn, ktag=ktag,
                                                  post=(diff_post(h, s, 2 * hh, 2 * hh + 1) if m == 1 else None)))
                        attn_group(heads, s)
                for i in range(4):
                    for c in range(2):
                        pi = rot()
                        tr(pi, psb[pi][:, 0:128], tok[:, i, c * 128:(c + 1) * 128], R=["tok"])
                        act(bigr[:, 13 + c, i * 128:(i + 1) * 128], psb[pi][:, 0:128], AF.Copy, R=["ps%d" % pi], W=[BT(13 + c)])


            def d_branch():
                for i in range(4):
                    pi = rot()
                    for g in range(4):
                        mm(pi, psb[pi][:, g * 64:(g + 1) * 64], wspT[:, g, :], DvN[:, i, g * 64:(g + 1) * 64],
                           True, True, R=["wspT", "sq0", "sq1"])
                    for g in range(4):
                        P.op("dve", ("scalar_tensor_tensor", dict(
                            out=tok[:, i, g * 64:(g + 1) * 64], in0=psb[pi][:, g * 64:(g + 1) * 64], scalar=bsT[:, g:g + 1],
                            in1=TTv[:, i, 384 + g * 64: 448 + g * 64], op0=ALU.add, op1=ALU.mult)),
                            R=["ps%d" % pi, "bsT"] + TT_tags, W=["tok"])
                for i in range(4):
                    for c in range(2):
                        pi = rot()
                        tr(pi, psb[pi][:, 0:128], tok[:, i, c * 128:(c + 1) * 128], R=["tok"])
                        act(bigr[:, 15 + c, i * 128:(i + 1) * 128], psb[pi][:, 0:128], AF.Copy, R=["ps%d" % pi], W=[BT(15 + c)])


            if sample:
                w_in_groups((1,))
                qk_norm((4,))
                do_rope((4,))
                exchange_part("g")
                w_in_groups((2,))
                do_rope((7, 8))
                exchange_part("d")
                load_keys_sample("g", 0)
                w_in_groups((0, 3))
                chan_dft()
                exchange_x()
                if stop_after == "coll":
                    break
                qk_norm((2, 3))
                do_rope((2, 3, 5, 6))
                dv_rms()
                d_branch()
                attention(gqa_preloaded=True)
                coll_done.add(l)
                fourier_sample()
                out_a(lambda kc: xcs[:, kc, :], [XT(0), XT(1)])
            else:
                w_in_groups((0, 1, 2, 3))
                qk_norm((2, 3, 4))
                emit_kv()
                dv_rms()
                chan_dft()
                fourier_prompt()
                out_a(lambda kc: bigr[:, 15 + kc, :], [BT(15), BT(16)])
                attention()
                d_branch()

            for cdbg in range(8):
                dump("br%d" % cdbg, big[:, 9 + cdbg, :], [BT(9 + cdbg)])
            for n in range(4):
                for cg in range(2):
                    s0, s1, sbr = w_group(3)
                    for j in range(4):
                        for kc in range(4):
                            mm(j, psb[j][:], slabs[:, s0, kc, j * 128:(j + 1) * 128], hT[:, kc, :], kc == 0, False,
                               R=["slab%d" % s0, "h%d" % kc])
                    for j in range(4):
                        c = cg * 4 + j
                        for kc in range(4, 8):
                            mm(j, psb[j][:], slabs[:, s1, kc - 4, j * 128:(j + 1) * 128], hT[:, kc, :], False, kc == 7,
                               R=["slab%d" % s1, "h%d" % kc])
                        pb = rot()
                        for kc in range(2):
                            mm(pb, psb[pb][:], slabs[:, sbr, kc, j * 128:(j + 1) * 128], bigr[:, 9 + 2 * n + kc, :], kc == 0, kc == 1,
                               R=["slab%d" % sbr, BT(9 + 2 * n + kc)])
                        b = (n * 8 + c) % 2
                        act(scr[:, b, :], psb[j][:], AF.Sigmoid, R=["ps%d" % j], W=["scr%d" % b])
                        if n == 0:
                            P.op("dve", ("tensor_tensor", dict(out=bigr[:, c, :], in0=scr[:, b, :], in1=psb[pb][:], op=ALU.mult)),
                                 R=["scr%d" % b, "ps%d" % pb], W=[BT(c)])
                        else:
                            P.op("dve", ("tensor_tensor", dict(out=scr[:, b, :], in0=scr[:, b, :], in1=psb[pb][:], op=ALU.mult)),
                                 R=["scr%d" % b, "ps%d" % pb], W=["scr%d" % b])
                            P.op("dve", ("tensor_tensor", dict(out=bigr[:, c, :], in0=big[:, c, :], in1=scr[:, b, :], op=ALU.add)),
                                 R=["scr%d" % b, BT(c)], W=[BT(c)])
            dump("merged0", big[:, 0, :], [BT(0)])
            dump("merged7", big[:, 7, :], [BT(7)])
            for cg in range(2):
                s0, s1 = w_group(2)
                for j in range(4):
                    for kc in range(4):
                        mm(4 + j, psb[4 + j][:], slabs[:, s0, kc, j * 128:(j + 1) * 128], bigr[:, kc, :], kc == 0, False,
                           R=["slab%d" % s0, BT(kc)])
                for j in range(4):
                    c = cg * 4 + j
                    pi = 4 + j
                    for kc in range(4, 8):
                        mm(pi, psb[pi][:], slabs[:, s1, kc - 4, j * 128:(j + 1) * 128], bigr[:, kc, :], False, kc == 7,
                           R=["slab%d" % s1, BT(kc)])
                    P.op("dve", ("scalar_tensor_tensor", dict(
                        out=xT[:, u, c, :], in0=psb[pi][:], scalar=modT[:, 16 + c, ci:ci + 1], in1=xT[:, u, c, :],
                        op0=ALU.mult, op1=ALU.add)), R=["ps%d" % pi, "mod", "x%d_%d" % (u, c)], W=["x%d_%d" % (u, c)])
            dump("x1_c0", xT[:, u, 0, :], ["x%d_0" % u])
            norm_mod(u, ci, A2, 24, "A2")
            for half in range(2):
                for g in range(4):
                    s0, s1 = w_group(2)
                    for j in range(4):
                        for kc in range(4):
                            mm(4 + j, psb[4 + j][:], slabs[:, s0, kc, j * 128:(j + 1) * 128], hT[:, kc, :], kc == 0, False,
                               R=["slab%d" % s0, "h%d" % kc])
                    for j in range(4):
                        fc = g * 4 + j
                        pi = 4 + j
                        for kc in range(4, 8):
                            mm(pi, psb[pi][:], slabs[:, s1, kc - 4, j * 128:(j + 1) * 128], hT[:, kc, :], False, kc == 7,
                               R=["slab%d" % s1, "h%d" % kc])
                        if RELU2_DVE:
                            P.op("dve", ("scalar_tensor_tensor", dict(
                                out=bigr[:, fc, :], in0=psb[pi][:], scalar=0.0, in1=psb[pi][:], op0=ALU.max, op1=ALU.mult)),
                                R=["ps%d" % pi], W=[BT(fc)])
                        else:
                            b = fc % 2
                            act(scr[:, b, :], psb[pi][:], AF.Relu, R=["ps%d" % pi], W=["scr%d" % b])
                            P.op("dve", ("tensor_tensor", dict(out=bigr[:, fc, :], in0=scr[:, b, :], in1=scr[:, b, :], op=ALU.mult)),
                                 R=["scr%d" % b], W=[BT(fc)])
                for cg in range(2):
                    for kq in range(4):
                        s, = w_group(1)
                        for j in range(4):
                            for kc in range(4):
                                mm(j, psb[j][:], slabs[:, s, kc, j * 128:(j + 1) * 128], bigr[:, kq * 4 + kc, :],
                                   kq == 0 and kc == 0, kq == 3 and kc == 3, R=["slab%d" % s, BT(kq * 4 + kc)])
                    for j in range(4):
                        c = cg * 4 + j
                        P.op("dve", ("scalar_tensor_tensor", dict(
                            out=xT[:, u, c, :], in0=psb[j][:], scalar=modT[:, 40 + c, ci:ci + 1], in1=xT[:, u, c, :],
                            op0=ALU.mult, op1=ALU.add)), R=["ps%d" % j, "mod", "x%d_%d" % (u, c)], W=["x%d_%d" % (u, c)])
            if l == depth - 1 and stop_after is None:
                final_out(u)

    assert stop_after is not None or wstate["used"] == len(wq), (wstate, len(wq))
    P.emit()
    return nc


def _consts(q):
    c = {}
    c["k_ident"] = np.eye(128, dtype=np.float32)
    c["k_ones"] = np.ones((128, 128), np.float32)
    blk = np.zeros((128, 128), np.float32)
    blk[:64, :64] = 1
    blk[64:, 64:] = 1
    c["k_blk64"] = blk
    n = np.arange(64)
    ang = 2 * np.pi * np.outer(n, n) / 64
    C64 = np.cos(ang) / 8.0
    S64 = np.sin(ang) / 8.0
    cc = np.zeros((128, 256))
    cc[:64, 0:64] = C64
    cc[64:, 64:128] = C64
    cc[:64, 128:192] = S64
    cc[64:, 192:256] = S64
    c["k_ccss"] = cc.astype(np.float32)

    def rotm(hd):
        half = hd // 2
        R = np.zeros((128, 128), np.float32)
        for m in range(128):
            if m % hd < half:
                R[m + half, m] = -1.0
            else:
                R[m - half, m] = 1.0
        return R
    mk = np.zeros((128, 6), np.float32)
    mk[0:64, 0] = 1
    mk[64:128, 1] = 1
    for j in range(4):
        mk[32 * j:32 * j + 32, 2 + j] = 1
    c["k_mask"] = mk
    c["k_rotg"] = rotm(64)
    c["k_rotd"] = rotm(32)
    pos = q * 512 + np.arange(512)
    row = (pos // 64).astype(np.float64)
    col = (pos % 64).astype(np.float64)

    def tab(dim):
        quarter = dim // 4
        inv = 10000.0 ** (-np.arange(quarter, dtype=np.float32) / quarter)
        inv = inv.astype(np.float32)
        ang = np.concatenate([row[:, None].astype(np.float32) * inv, col[:, None].astype(np.float32) * inv], axis=-1)
        ang = ang.astype(np.float32)
        cos = np.cos(ang).astype(np.float32)
        sin = np.sin(ang).astype(np.float32)
        half = dim // 2
        idx = np.arange(128) % half
        return cos[:, idx].T.copy(), sin[:, idx].T.copy()
    cg, sg = tab(64)
    cd, sd = tab(32)
    c["k_rope"] = np.stack([cg, sg, cd, sd]).astype(np.float32)
    t = np.arange(256)
    a = 2 * np.pi * np.outer(t, t) / 256
    c["k_dft256"] = np.stack([np.cos(a) / 16.0, -np.sin(a) / 16.0]).astype(np.float32)
    tt = np.arange(2048, dtype=np.float64)
    a = 2 * np.pi * np.outer(tt, pos.astype(np.float64)) / 2048
    c["k_dftc"] = (np.cos(a) / math.sqrt(2048.0)).astype(np.float32)
    c["k_dfts"] = (-np.sin(a) / math.sqrt(2048.0)).astype(np.float32)
    return c


_NC_CACHE = {}
import os
import json
_DBG = {k: (tuple(v) if isinstance(v, list) else v) for k, v in json.loads(os.environ.get("KDBG", "{}")).items()}
_DBG_RUN = {}


def kernel(**inputs):
    inp = {k: np.ascontiguousarray(np.asarray(v)) for k, v in inputs.items()}
    if "nc" not in _NC_CACHE:
        _NC_CACHE["nc"] = build_nc(**_DBG)
    nc = _NC_CACHE["nc"]
    shared = ["c_ctx", "w_ada", "b_ada", "norm1_g", "norm2_g", "w_in", "w_fourier", "q_norm_g", "k_norm_g",
              "lambda_q1", "lambda_k1", "lambda_q2", "lambda_k2", "diff_norm_g", "sgu_norm_g", "w_spatial",
              "b_spatial", "w_gate", "w_branch", "w_out", "w_mlp1", "w_mlp2", "final_norm_g"]
    in_maps = []
    for core in range(8):
        b, q = core // 4, core % 4
        m = {k: inp[k] for k in shared}
        m["xp"] = inp["x_prompt"][4 * core:4 * core + 4]
        m["xs"] = inp["x_sample"][b, q * 512:(q + 1) * 512]
        m["cm"] = inp["c"][b]
        m["cgk"] = inp["cache_gqa_k"][b].reshape(DEPTH, 512, 128)
        m["cgv"] = inp["cache_gqa_v"][b].reshape(DEPTH, 512, 128)
        m["cdk"] = inp["cache_diff_k"][b].reshape(DEPTH, 512, 256)
        m["cdv"] = inp["cache_diff_v"][b].reshape(DEPTH, 512, 256)
        m.update(_consts(q))
        in_maps.append({k: np.ascontiguousarray(v) for k, v in m.items()})
    ncores = _DBG_RUN.get("cores", 8)
    res = run_bass_kernel_spmd(nc, in_maps[:ncores], core_ids=list(range(ncores)))
    R = list(res.results)
    _DBG_RUN["results"] = R
    _DBG_RUN["names"] = getattr(nc, "_dbg_names", [])
    while len(R) < 8:
        R.append(R[0])
    y_prompt = np.concatenate([R[c]["yp"] for c in range(8)], axis=0)
    y_sample = np.stack([np.concatenate([R[b * 4 + q]["ys"] for q in range(4)], axis=0) for b in range(2)], axis=0)
    ngk = np.concatenate([R[c]["ngk"] for c in range(8)], axis=0).reshape(32, DEPTH, 256, 2, 64)
    ngv = np.concatenate([R[c]["ngv"] for c in range(8)], axis=0).reshape(32, DEPTH, 256, 2, 64)
    ndk = np.concatenate([R[c]["ndk"] for c in range(8)], axis=0).reshape(32, DEPTH, 256, 4, 2, 32)
    ndv = np.concatenate([R[c]["ndv"] for c in range(8)], axis=0).reshape(32, DEPTH, 256, 4, 64)
    return (y_prompt.astype(np.float32), y_sample.astype(np.float32), ngk.astype(np.float32),
            ngv.astype(np.float32), ndk.astype(np.float32), ndv.astype(np.float32))
```

```python
import math
import numpy as np
import concourse.bass as bass
import concourse.mybir as mybir
from concourse.bass_utils import run_bass_kernel_spmd

F32 = mybir.dt.float32
F32R = mybir.dt.float32r
BF16 = mybir.dt.bfloat16
AF = mybir.ActivationFunctionType
ALU = mybir.AluOpType
AX = mybir.AxisListType

D = 1024
DEPTH = 4
T = 512
NU = 3
EPS = 1e-6
GROWS = 1280
RELU2_DVE = False


class Prog:
    def __init__(self, nc):
        self.nc = nc
        self.ops = {e: [] for e in ("pe", "act", "dve", "pool", "sp")}
        self.cnt = {e: 0 for e in self.ops}
        self.esem = {e: nc.alloc_semaphore("sem_" + e) for e in ("pe", "act", "dve", "pool")}
        self.slot_sem = {}
        self.slot_cnt = {}
        self.lastw = {}
        self.readers = {}
        self.waited = {e: {} for e in self.ops}
        self.final = []

    def _deps(self, eng, R, W, pe_acc=False):
        deps = {}

        def add(ev):
            if ev is None:
                return
            k, v, src, _sem = ev
            if pe_acc and src == "pe" and eng == "pe":
                return
            if k not in deps or deps[k][0] < v:
                deps[k] = (v, ev[3])

        for t in list(R) + list(W):
            add(self.lastw.get(t))
        for t in W:
            for ev in self.readers.get(t, []):
                add(ev)
        out = []
        for k, (v, sem) in deps.items():
            if self.waited[eng].get(k, 0) >= v:
                continue
            self.waited[eng][k] = v
            out.append((sem, v))
        return out

    def _commit(self, ev, R, W):
        for t in W:
            self.lastw[t] = ev
            self.readers[t] = []
        for t in R:
            self.readers.setdefault(t, []).append(ev)

    def op(self, eng, fn, R=(), W=()):
        if isinstance(fn, tuple):
            _name, _kw = fn
            fn = (lambda e, _name=_name, _kw=_kw: getattr(e, _name)(**_kw))
        waits = self._deps(eng, R, W, pe_acc=True)
        self.cnt[eng] += 1
        sem = self.esem[eng]
        ev = ("e_" + eng, self.cnt[eng], eng, sem)
        self.ops[eng].append((waits, fn, sem, 1))
        self._commit(ev, R, W)

    def dma(self, q, out, in_, R, W, slot, **kw):
        waits = self._deps(q, R, W)
        if slot not in self.slot_sem:
            self.slot_sem[slot] = self.nc.alloc_semaphore("ds_" + slot)
            self.slot_cnt[slot] = 0
        self.slot_cnt[slot] += 16
        sem = self.slot_sem[slot]
        ev = ("s_" + slot, self.slot_cnt[slot], "dma", sem)
        self.ops[q].append((waits, lambda e: e.dma_start(out=out, in_=in_, **kw), sem, 16))
        self._commit(ev, R, W)

    def coll(self, fn, R, W, slot):
        waits = self._deps("pool", R, W)
        if slot not in self.slot_sem:
            self.slot_sem[slot] = self.nc.alloc_semaphore("ds_" + slot)
            self.slot_cnt[slot] = 0
        self.slot_cnt[slot] += 1
        sem = self.slot_sem[slot]
        ev = ("s_" + slot, self.slot_cnt[slot], "dma", sem)
        self.ops["pool"].append((waits, fn, sem, None))
        self._commit(ev, R, W)

    def emit(self):
        nc = self.nc
        final_waits = [(self.slot_sem[s], self.slot_cnt[s]) for s in self.slot_sem]
        with nc.Block() as block:
            def run(eng_name):
                def body(e):
                    for waits, fn, sem, inc in self.ops[eng_name]:
                        for (s, v) in waits:
                            e.wait_ge(s, v)
                        ins = fn(e)
                        if inc is None:
                            ins.then_inc(sem)
                        else:
                            ins.then_inc(sem, inc)
                    if eng_name == "sp":
                        for (s, v) in final_waits:
                            e.wait_ge(s, v)
                        for en in ("pe", "act", "dve", "pool"):
                            if self.cnt[en]:
                                e.wait_ge(self.esem[en], self.cnt[en])
                return body
            block.sync(run("sp"))
            block.tensor(run("pe"))
            block.scalar(run("act"))
            block.vector(run("dve"))
            block.gpsimd(run("pool"))


def build_nc(depth=DEPTH, units=(0, 1, 2), dbg=False, stop_after=None):
    nc = bass.Bass("TRN2", target_bir_lowering=False)
    P = Prog(nc)

    def din(name, shape, dt=F32):
        return nc.dram_tensor(name, list(shape), dt, kind="ExternalInput").ap()

    def dout(name, shape):
        return nc.dram_tensor(name, list(shape), F32, kind="ExternalOutput").ap()

    xp = din("xp", [4, 256, D])
    xs = din("xs", [512, D])
    cm = din("cm", [D])
    cctx = din("c_ctx", [D])
    cgk = din("cgk", [DEPTH, 512, 128])
    cgv = din("cgv", [DEPTH, 512, 128])
    cdk = din("cdk", [DEPTH, 512, 256])
    cdv = din("cdv", [DEPTH, 512, 256])
    w_ada = din("w_ada", [DEPTH, D, 6 * D])
    b_ada = din("b_ada", [DEPTH, 6 * D])
    norm1_g = din("norm1_g", [DEPTH, D])
    norm2_g = din("norm2_g", [DEPTH, D])
    w_in = din("w_in", [DEPTH, D, 2048])
    w_fourier = din("w_fourier", [DEPTH, 256, 256])
    q_norm_g = din("q_norm_g", [DEPTH, 64])
    k_norm_g = din("k_norm_g", [DEPTH, 64])
    lam_in = [din(n, [DEPTH, 32]) for n in ("lambda_q1", "lambda_k1", "lambda_q2", "lambda_k2")]
    diff_norm_g = din("diff_norm_g", [DEPTH, 64])
    sgu_norm_g = din("sgu_norm_g", [DEPTH, 256])
    w_spatial = din("w_spatial", [DEPTH, 4, 128, 128])
    b_spatial = din("b_spatial", [DEPTH, 4, 128])
    w_gate = din("w_gate", [DEPTH, D, 4 * D])
    w_branch = din("w_branch", [DEPTH, 4, 256, D])
    w_out = din("w_out", [DEPTH, D, D])
    w_mlp1 = din("w_mlp1", [DEPTH, D, 4 * D])
    w_mlp2 = din("w_mlp2", [DEPTH, 4 * D, D])
    final_g = din("final_norm_g", [D])
    k_ident = din("k_ident", [128, 128])
    k_ones = din("k_ones", [128, 128])
    k_blk64 = din("k_blk64", [128, 128])
    k_ccss = din("k_ccss", [128, 256])
    k_rotg = din("k_rotg", [128, 128])
    k_rotd = din("k_rotd", [128, 128])
    k_mask = din("k_mask", [128, 6])
    k_rope = din("k_rope", [4, 128, 512])
    k_dft256 = din("k_dft256", [2, 256, 256])
    k_dftc = din("k_dftc", [2048, 512])
    k_dfts = din("k_dfts", [2048, 512])
    yp = dout("yp", [4, 256, D])
    ys = dout("ys", [512, D])
    ngk = dout("ngk", [4, DEPTH, 256, 128])
    ngv = dout("ngv", [4, DEPTH, 256, 128])
    ndk = dout("ndk", [4, DEPTH, 256, 256])
    ndv = dout("ndv", [4, DEPTH, 256, 256])
    PR = {"g": 256, "d": 512, "x": 512}
    bounce = {p: [nc.dram_tensor("bounce_%s%d" % (p, l), [PR[p], 512], F32).ap() for l in range(DEPTH)] for p in PR}
    gath = {p: [nc.dram_tensor("gath_%s%d" % (p, l), [4 * PR[p], 512], F32).ap() for l in range(DEPTH)] for p in PR}

    dbg_out = dout("dbg", [16, 128, 512]) if dbg else None
    dbg_n = [0]
    dbg_names = []

    def dump(name, ap, R):
        if not dbg:
            return
        i = dbg_n[0]
        dbg_n[0] += 1
        dbg_names.append(name)
        ncol = ap.shape[-1] if len(ap.shape) == 2 else None
        P.dma("sp", dbg_out[i, :, 0:ap.shape[1]], ap, R=R, W=[], slot="dbg")
    nc._dbg_names = dbg_names

    def sb(name, shape, dt=F32):
        return nc.alloc_sbuf_tensor(name, list(shape), dt)

    xT = sb("xT", [128, NU, 8, T])
    hT = sb("hT", [128, 8, T], F32R)
    bigR = sb("bigR", [128, 17, T], F32R)
    TT = sb("TT", [128, 4, 896])
    dft256t = sb("dft256t", [128, 2, 2, 256], F32R)
    slabs = sb("slabs", [128, 4, 4, 512], F32R)
    KT = sb("KT", [128, 2560], F32R)
    VA = sb("VA", [128, 20, 2, 65], BF16)
    PT = sb("PT", [128, 2, 512], BF16)
    tokst = KT[:, 0:1536].rearrange("p (i c) -> p i c", c=384)
    tok = sb("tok", [128, 4, 256])
    sq = sb("sq", [128, 2, T], F32R)
    scr = sb("scr", [128, 2, T])
    cstage = scr[:, 1, :].rearrange("p (i c) -> p i c", c=128)
    DvN = sq[:].rearrange("p a t -> p (a t)").rearrange("p (i c) -> p i c", c=256)
    rstd = sb("rstd", [128, T])
    small = sb("small", [128, 64])
    ident = sb("ident", [128, 128])
    ones_r = sb("ones_r", [128, 128], F32R)
    blk64_r = sb("blk64_r", [128, 128], F32R)
    ccss_r = sb("ccss_r", [128, 256], F32R)
    rotg_r = sb("rotg_r", [128, 128], F32R)
    rotd_r = sb("rotd_r", [128, 128], F32R)
    rope = sb("rope", [128, 4, 512])
    dft256 = dft256t[:]
    xcs = sb("xcs", [128, 4, 512], F32R)
    wf_r = sb("wf_r", [128, 2, 256], F32R)
    wsp = sb("wsp", [128, 4, 128])
    wspT = sb("wspT", [128, 4, 128], F32R)
    bsT = sb("bsT", [128, 4])
    condT = sb("condT", [128, 8, 2])
    scond = sb("scond", [128, 8, 2], F32R)
    modT = sb("modT", [128, 48, 2])
    badaT = sb("badaT", [128, 48])
    A1 = sb("A1", [128, 8, 2])
    A2 = sb("A2", [128, 8, 2])
    n1g = sb("n1g", [128, 8])
    n2g = sb("n2g", [128, 8])
    fng = sb("fng", [128, 8])
    qg = sb("qg", [128, 1])
    kg = sb("kg", [128, 1])
    lamv = sb("lamv", [128, 4, 32])
    neglam = sb("neglam", [128, 1])
    dng = sb("dng", [128, 64])
    sgg = sb("sgg", [128, 256])

    NPS = 8
    psb = [nc.alloc_psum_tensor("ps%d" % i, [128, 512], F32) for i in range(NPS)]
    rot_state = [0]

    def rot():
        i = 4 + rot_state[0] % 4
        rot_state[0] += 1
        return i

    def mm(ps_i, out_ap, lhsT, rhs, start, stop, R, **kw):
        P.op("pe", ("matmul", dict(out=out_ap, lhsT=lhsT, rhs=rhs, start=start, stop=stop, **kw)),
             R=R, W=["ps%d" % ps_i])

    def tr(ps_i, out_ap, in_ap, R):
        P.op("pe", ("transpose", dict(out=out_ap, in_=in_ap, identity=ident[:])), R=list(R) + ["ident"], W=["ps%d" % ps_i])

    def act(out, in_, func, R, W, **kw):
        P.op("act", ("activation", dict(out=out, in_=in_, func=func, **kw)), R=R, W=W)

    def BT(c):
        return "B%d" % c

    def XT(i):
        return ["xcs%da" % i, "xcs%db" % i]

    VAP = ["VAp%d" % i for i in range(8)]

    bigr = bigR[:]
    big = bigR[:].bitcast(F32)
    outst = tok[:].rearrange("p a t -> p (a t)").rearrange("p (b d) -> p b d", d=1024)
    OST = {0: ["tok"], 1: ["tok"]}

    wq = []
    wstate = {"issued": 0, "used": 0}
    PREF = 3

    coll_done = set()
    wneed = {}

    def w_issue():
        i = wstate["issued"]
        if i >= len(wq):
            return False
        need = wneed.get(i)
        if need is not None and need not in coll_done:
            return False
        slot = i % 4
        for (dst_fn, src) in wq[i]:
            P.dma("pool", dst_fn(slot), src, R=(["gath_x"] if need is not None else []), W=["slab%d" % slot], slot="slab%d" % slot)
        wstate["issued"] += 1
        return True

    def w_group(n):
        i = wstate["used"]
        while wstate["issued"] <= min(i + 3, len(wq) - 1):
            if not w_issue():
                break
        assert wstate["issued"] >= i + n, "weight slab not issued"
        wstate["used"] += n
        return [(i + k) % 4 for k in range(n)]

    def w_prefetch():
        i = wstate["used"]
        while wstate["issued"] <= min(i + 3, len(wq) - 1):
            if not w_issue():
                break

    def slab_std(src2d, nk=4):
        ncols = src2d.shape[1]
        return [(lambda s, nk=nk, ncols=ncols: slabs[:, s, 0:nk, 0:ncols],
                 src2d.rearrange("(k p) c -> p k c", p=128))]

    def plan_layer_unit(l, sample):
        for g in ((1, 0, 2, 3) if sample else (0, 1, 2, 3)):
            for kh in range(2):
                rows = slice(kh * 512, kh * 512 + 512)
                if g == 0:
                    ent = [(lambda s: slabs[:, s, :, 0:256],
                            w_in[l, rows, 0:256].rearrange("(k p) c -> p k c", p=128))]
                    for b in range(2):
                        for a in range(2):
                            ent.append((lambda s, b=b, a=a: slabs[:, s, :, 256 + 128 * b + 64 * a:320 + 128 * b + 64 * a],
                                        w_in[l, rows, 256 + 128 * a + 64 * b:320 + 128 * a + 64 * b].rearrange("(k p) d -> p k d", p=128)))
                    wq.append(ent)
                else:
                    wq.append(slab_std(w_in[l, rows, g * 512:(g + 1) * 512]))
        if sample:
            for s4 in range(4):
                wneed[len(wq)] = l
                wq.append([(lambda s: slabs[:, s, :, :],
                            gath["x"][l][s4 * 512:(s4 + 1) * 512, :].rearrange("(k p) c -> p k c", p=128))])
                wq.append(slab_std(k_dftc[s4 * 512:(s4 + 1) * 512, :]))
                wq.append(slab_std(k_dfts[s4 * 512:(s4 + 1) * 512, :]))
        for n in range(4):
            for cg in range(2):
                cols = slice(n * 1024 + cg * 512, n * 1024 + cg * 512 + 512)
                for kh in range(2):
                    wq.append(slab_std(w_gate[l, kh * 512:(kh + 1) * 512, cols]))
                wq.append(slab_std(w_branch[l, n, :, cg * 512:(cg + 1) * 512], nk=2))
        for cg in range(2):
            for kh in range(2):
                wq.append(slab_std(w_out[l, kh * 512:(kh + 1) * 512, cg * 512:(cg + 1) * 512]))
        for half in range(2):
            for g in range(4):
                for kh in range(2):
                    wq.append(slab_std(w_mlp1[l, kh * 512:(kh + 1) * 512, half * 2048 + g * 512: half * 2048 + (g + 1) * 512]))
            for cg in range(2):
                for kq in range(4):
                    r0 = half * 2048 + kq * 512
                    wq.append(slab_std(w_mlp2[l, r0:r0 + 512, cg * 512:(cg + 1) * 512]))

    def plan_ada(l):
        for g in range(12):
            for kh in range(2):
                wq.append(slab_std(w_ada[l, kh * 512:(kh + 1) * 512, g * 512:(g + 1) * 512]))

    for l in range(depth):
        plan_ada(l)
        for u in units:
            plan_layer_unit(l, u == 2)

    def ld(dst, src, W, slot, q="sp", **kw):
        P.dma(q, dst, src, R=[], W=W, slot=slot, **kw)

    ld(ident[:], k_ident, ["ident"], "c0")
    ld(ones_r[:], k_ones, ["ones"], "c1", q="pool")
    ld(blk64_r[:], k_blk64, ["blk64"], "c2", q="pool")
    ld(ccss_r[:], k_ccss, ["ccss"], "c3", q="pool")
    ld(rotg_r[:], k_rotg, ["rotg"], "c4", q="pool")
    ld(rotd_r[:], k_rotd, ["rotd"], "c5", q="pool")
    ld(rope[:], k_rope.rearrange("a p t -> p a t"), ["rope"], "c6")
    for a_ in range(2):
        ld(dft256[:, a_], k_dft256[a_].rearrange("(j p) t -> p j t", p=128), ["dft256"], "c7", q="pool")
    ld(fng[:], final_g.rearrange("(c p) -> p c", p=128), ["fng"], "c8", allow_slow_non_contiguous=True)
    ld(condT[:, :, 0], cctx.rearrange("(c p) -> p c", p=128), ["condT"], "c9", allow_slow_non_contiguous=True)
    ld(condT[:, :, 1], cm.rearrange("(c p) -> p c", p=128), ["condT"], "c9", allow_slow_non_contiguous=True)
    P.op("pool", lambda e: e.memset(VA[:, :, :, 64:65], 1.0), W=["VAones"])
    def x_load():
        for u in range(NU):
            for i in range(4):
                if u < 2:
                    src = xp[2 * u + i // 2, (i % 2) * 128:(i % 2) * 128 + 128, :]
                else:
                    src = xs[i * 128:(i + 1) * 128, :]
                b = 0
                ld(outst[:, b, :], src, OST[b], "ost%d" % b)
                for c in range(8):
                    pi = rot()
                    tr(pi, psb[pi][:, 0:128], outst[:, b, c * 128:(c + 1) * 128], R=OST[b])
                    eng = "dve" if c % 2 == 0 else "act"
                    dst = xT[:, u, c, i * 128:(i + 1) * 128]
                    if eng == "dve":
                        P.op("dve", ("tensor_copy", dict(out=dst, in_=psb[pi][:, 0:128])),
                             R=["ps%d" % pi], W=["x%d_%d" % (u, c)])
                    else:
                        act(dst, psb[pi][:, 0:128], AF.Copy, R=["ps%d" % pi], W=["x%d_%d" % (u, c)])
    act(scond[:], condT[:], AF.Silu, R=["condT"], W=["scond"])

    def norm_mod(u, ci, Amat, shift_c0, gtag):
        pi = rot()
        for c in range(8):
            b = c % 2
            act(sq[:, b, :], xT[:, u, c, :], AF.Square, R=["x%d_%d" % (u, c)], W=["sq%d" % b])
            mm(pi, psb[pi][:], ones_r[:], sq[:, b, :], c == 0, c == 7, R=["ones", "sq%d" % b])
        act(rstd[:], psb[pi][:], AF.Sqrt, R=["ps%d" % pi, "eps"], W=["rstd"], scale=1.0 / D, bias=eps_ap)
        P.op("dve", ("reciprocal", dict(out=rstd[:], in_=rstd[:])), R=["rstd"], W=["rstd"])
        for c in range(8):
            b = c % 2
            P.op("dve", ("scalar_tensor_tensor", dict(
                out=scr[:, b, :], in0=xT[:, u, c, :], scalar=Amat[:, c, ci:ci + 1], in1=rstd[:],
                op0=ALU.mult, op1=ALU.mult)), R=["x%d_%d" % (u, c), gtag, "rstd"], W=["scr%d" % b])
            act(hT[:, c, :], scr[:, b, :], AF.Identity, R=["scr%d" % b, "mod"], W=["h%d" % c],
                bias=modT[:, shift_c0 + c, ci:ci + 1])

    eps_t = sb("eps_t", [128, 1])
    maskc = sb("maskc", [128, 6])
    ld(maskc[:], k_mask, ["maskc"], "c10")
    P.op("pool", lambda e: e.memset(eps_t[:], EPS), W=["eps"])
    eps_ap = eps_t[:]

    def rms_rows(ps_or_ap, R_in, nparts_tag):
        pass

    def final_out(u):
        pi = rot()
        for c in range(8):
            b = c % 2
            act(sq[:, b, :], xT[:, u, c, :], AF.Square, R=["x%d_%d" % (u, c)], W=["sq%d" % b])
            mm(pi, psb[pi][:], ones_r[:], sq[:, b, :], c == 0, c == 7, R=["ones", "sq%d" % b])
        act(rstd[:], psb[pi][:], AF.Sqrt, R=["ps%d" % pi, "eps"], W=["rstd"], scale=1.0 / D, bias=eps_ap)
        P.op("dve", ("reciprocal", dict(out=rstd[:], in_=rstd[:])), R=["rstd"], W=["rstd"])
        scrv = scr[:].rearrange("p a t -> p (a t)").rearrange("p (c q) -> p c q", q=128)
        for i in range(4):
            for c in range(8):
                P.op("dve", ("scalar_tensor_tensor", dict(
                    out=scrv[:, c, :], in0=xT[:, u, c, i * 128:(i + 1) * 128], scalar=fng[:, c:c + 1], in1=rstd[:, i * 128:(i + 1) * 128],
                    op0=ALU.mult, op1=ALU.mult)), R=["x%d_%d" % (u, c), "fng", "rstd"], W=["scr0", "scr1"])
            for c in range(8):
                pi = rot()
                tr(pi, psb[pi][:, 0:128], scrv[:, c, :], R=["scr0", "scr1"])
                if c % 2 == 0:
                    P.op("dve", ("tensor_copy", dict(out=outst[:, 0, c * 128:(c + 1) * 128], in_=psb[pi][:, 0:128])),
                         R=["ps%d" % pi], W=["tok"])
                else:
                    act(outst[:, 0, c * 128:(c + 1) * 128], psb[pi][:, 0:128], AF.Copy, R=["ps%d" % pi], W=["tok"])
            if u < 2:
                dst = yp[2 * u + i // 2, (i % 2) * 128:(i % 2) * 128 + 128, :]
            else:
                dst = ys[i * 128:(i + 1) * 128, :]
            P.dma("sp", dst, outst[:, 0, :], R=["tok"], W=[], slot="oy")


    for l in range(depth):
        lam_init = 0.8 - 0.6 * math.exp(-0.3 * l)
        lam_scale = 1.0 - lam_init
        ld(badaT[:], b_ada[l].rearrange("(c p) -> p c", p=128), ["badaT"], "l0", allow_slow_non_contiguous=True)
        ld(n1g[:], norm1_g[l].rearrange("(c p) -> p c", p=128), ["n1g"], "l1", allow_slow_non_contiguous=True)
        ld(n2g[:], norm2_g[l].rearrange("(c p) -> p c", p=128), ["n2g"], "l2", allow_slow_non_contiguous=True)
        for hh in range(2):
            ld(qg[hh * 64:(hh + 1) * 64, :], q_norm_g[l].rearrange("(p o) -> p o", o=1), ["qg"], "l3", allow_slow_non_contiguous=True)
            ld(kg[hh * 64:(hh + 1) * 64, :], k_norm_g[l].rearrange("(p o) -> p o", o=1), ["kg"], "l4", allow_slow_non_contiguous=True)
        for j in range(4):
            ld(lamv[:, j, :], lam_in[j][l].partition_broadcast(128), ["lamv"], "l5")
        ld(dng[:], diff_norm_g[l].partition_broadcast(128), ["dng"], "l6")
        ld(sgg[:], sgu_norm_g[l].partition_broadcast(128), ["sgg"], "l7")
        ld(wsp[:], w_spatial[l].rearrange("g q p -> q g p"), ["wsp"], "l8")
        ld(bsT[:], b_spatial[l].rearrange("g q -> q g"), ["bsT"], "l9", allow_slow_non_contiguous=True)
        ld(wf_r[:], w_fourier[l].rearrange("(k p) c -> p k c", p=128), ["wf"], "l10", q="pool")
        P.op("dve", ("tensor_tensor", dict(out=small[:, 0:32], in0=lamv[:, 0, :], in1=lamv[:, 1, :], op=ALU.mult)), R=["lamv"], W=["small"])
        P.op("dve", ("tensor_tensor", dict(out=small[:, 32:64], in0=lamv[:, 2, :], in1=lamv[:, 3, :], op=ALU.mult)), R=["lamv", "small"], W=["small"])
        P.op("dve", ("tensor_reduce", dict(out=neglam[:], in_=small[:, 0:32], axis=AX.X, op=ALU.add)), R=["small"], W=["neglam"])
        P.op("dve", ("tensor_reduce", dict(out=small[:, 0:1], in_=small[:, 32:64], axis=AX.X, op=ALU.add)), R=["small", "neglam"], W=["small"])
        act(neglam[:], neglam[:], AF.Exp, R=["neglam"], W=["neglam"])
        act(small[:, 1:2], small[:, 0:1], AF.Exp, R=["small"], W=["small"])
        P.op("dve", ("tensor_tensor", dict(out=neglam[:], in0=small[:, 1:2], in1=neglam[:], op=ALU.subtract)), R=["small", "neglam"], W=["neglam"])
        P.op("dve", ("tensor_scalar_add", dict(out=neglam[:], in0=neglam[:], scalar1=-lam_init)), R=["neglam"], W=["neglam"])
        P.op("dve", ("tensor_scalar_mul", dict(out=dng[:], in0=dng[:], scalar1=lam_scale)), R=["dng"], W=["dng"])
        for g in range(4):
            pi = rot()
            tr(pi, psb[pi][:, 0:128], wsp[:, g, :], R=["wsp"])
            P.op("dve", ("tensor_copy", dict(out=wspT[:, g, :], in_=psb[pi][:, 0:128])), R=["ps%d" % pi], W=["wspT"])
        for g in range(12):
            s0, s1 = w_group(2)
            pi = rot()
            for kc in range(8):
                sl_ = s0 if kc < 4 else s1
                mm(pi, psb[pi][0:2, :], scond[:, kc, :], slabs[:, sl_, kc % 4, :], kc == 0, kc == 7,
                   R=["slab%d" % sl_, "scond"])
            P.op("dve", ("tensor_copy", dict(out=rstd[0:2, :], in_=psb[pi][0:2, :])), R=["ps%d" % pi], W=["rstd"])
            for j in range(4):
                pj = rot()
                P.op("pe", ("transpose", dict(out=psb[pj][:, 0:2], in_=rstd[0:2, j * 128:(j + 1) * 128], identity=ident[0:2, 0:2])),
                     R=["rstd", "ident"], W=["ps%d" % pj])
                cc = g * 4 + j
                P.op("dve", ("tensor_scalar", dict(
                    out=modT[:, cc, :], in0=psb[pj][:, 0:2], scalar1=badaT[:, cc:cc + 1], scalar2=None, op0=ALU.add)),
                    R=["ps%d" % pj, "badaT"], W=["mod"])
        if l == 0:
            x_load()
        for ci in range(2):
            P.op("dve", ("scalar_tensor_tensor", dict(
                out=A1[:, :, ci], in0=modT[:, 8:16, ci], scalar=1.0, in1=n1g[:], op0=ALU.add, op1=ALU.mult)),
                R=["mod", "n1g"], W=["A1"])
            P.op("dve", ("scalar_tensor_tensor", dict(
                out=A2[:, :, ci], in0=modT[:, 32:40, ci], scalar=1.0, in1=n2g[:], op0=ALU.add, op1=ALU.mult)),
                R=["mod", "n2g"], W=["A2"])

        for u in units:
            sample = (u == 2)
            ci = 1 if sample else 0
            nseq = 1 if sample else 2
            sl = T // nseq
            norm_mod(u, ci, A1, 0, "A1")
            hR = ["h%d" % c for c in range(8)]
            TTv = TT[:]
            TT_tags = ["TT"]
            def w_in_groups(gs):
                for g in gs:
                    s0, s1 = w_group(2)

                    def wsl(kc, c0, c1):
                        s = s0 if kc < 4 else s1
                        return slabs[:, s, kc % 4, c0:c1], "slab%d" % s

                    def ftype(c0, dst_chunk, evac):
                        pi = rot()
                        for kc in range(8):
                            w_ap, wt = wsl(kc, c0, c0 + 128)
                            mm(pi, psb[pi][:], w_ap, hT[:, kc, :], kc == 0, kc == 7, R=[wt, "h%d" % kc])
                        evac(pi, dst_chunk)

                    def ttype(c0, ncols, toff, post=None):
                        for i in range(4):
                            pi = rot()
                            for kc in range(8):
                                w_ap, wt = wsl(kc, c0, c0 + ncols)
                                mm(pi, psb[pi][:, 0:ncols], hT[:, kc, i * 128:(i + 1) * 128], w_ap, kc == 0, kc == 7,
                                   R=[wt, "h%d" % kc])
                            dst = TTv[:, i, toff:toff + ncols]
                            if post is None:
                                P.op("dve", ("tensor_copy", dict(out=dst, in_=psb[pi][:, 0:ncols])),
                                     R=["ps%d" % pi], W=TT_tags)
                            else:
                                act(dst, psb[pi][:, 0:ncols], post, R=["ps%d" % pi], W=TT_tags)

                    def ev_copy_r(pi, ch):
                        act(bigr[:, ch, :], psb[pi][:], AF.Copy, R=["ps%d" % pi], W=[BT(ch)])

                    def ev_copy_f(pi, ch):
                        P.op("dve", ("tensor_copy", dict(out=bigr[:, ch, :], in_=psb[pi][:])), R=["ps%d" % pi], W=[BT(ch)])

                    if g == 0:
                        ftype(0, 0, ev_copy_r)
                        ftype(128, 1, ev_copy_r)
                        ftype(256, 2, ev_copy_f)
                        ftype(384, 3, ev_copy_f)
                    elif g == 1:
                        ftype(0, 4, ev_copy_f)
                        ttype(128, 128, 0)
                        ftype(256, 5, ev_copy_f if sample else ev_copy_r)
                        ftype(384, 6, ev_copy_f if sample else ev_copy_r)
                    elif g == 2:
                        ftype(0, 7, ev_copy_f if sample else ev_copy_r)
                        ftype(128, 8, ev_copy_f if sample else ev_copy_r)
                        ttype(256, 256, 128)
                    else:
                        ttype(0, 256, 384, post=AF.Gelu_apprx_tanh)
                        ttype(256, 256, 640, post=AF.Gelu_apprx_tanh)

            def qk_norm(chs):
                for ch in chs:
                    gvec, gt = (kg, "kg") if ch == 4 else (qg, "qg")
                    act(sq[:, 0, :], big[:, ch, :], AF.Square, R=[BT(ch)], W=["sq0"])
                    pi = rot()
                    mm(pi, psb[pi][:], blk64_r[:], sq[:, 0, :], True, True, R=["blk64", "sq0"])
                    act(rstd[:], psb[pi][:], AF.Sqrt, R=["ps%d" % pi, "eps"], W=["rstd"], scale=1.0 / 64, bias=eps_ap)
                    P.op("dve", ("reciprocal", dict(out=rstd[:], in_=rstd[:])), R=["rstd"], W=["rstd"])
                    P.op("dve", ("scalar_tensor_tensor", dict(
                        out=bigr[:, ch, :], in0=big[:, ch, :], scalar=gvec[:, 0:1], in1=rstd[:], op0=ALU.mult, op1=ALU.mult)),
                        R=[BT(ch), gt, "rstd"], W=[BT(ch)])

            def do_rope(chs):
                for ch in chs:
                    isg = ch <= 4
                    rm = rotg_r if isg else rotd_r
                    rt = "rotg" if isg else "rotd"
                    cosi, sini = (0, 1) if isg else (2, 3)
                    act(sq[:, 1, :], big[:, ch, :], AF.Copy, R=[BT(ch)], W=["sq1"])
                    pi = rot()
                    mm(pi, psb[pi][:], rm[:], sq[:, 1, :], True, True, R=[rt, "sq1"])
                    P.op("dve", ("tensor_tensor", dict(out=scr[:, 0, :], in0=psb[pi][:], in1=rope[:, sini, :], op=ALU.mult)),
                         R=["ps%d" % pi, "rope"], W=["scr0"])
                    P.op("pool", ("tensor_tensor", dict(out=scr[:, 1, :], in0=big[:, ch, :], in1=rope[:, cosi, :], op=ALU.mult)),
                         R=[BT(ch), "rope"], W=["scr1"])
                    P.op("dve", ("tensor_tensor", dict(out=bigr[:, ch, :], in0=scr[:, 0, :], in1=scr[:, 1, :], op=ALU.add)),
                         R=["scr0", "scr1"], W=[BT(ch)])

            def emit_kv():
                if True:
                    for i in range(4):
                        for (ch, off) in ((4, 0), (7, 128), (8, 256)):
                            pi = rot()
                            tr(pi, psb[pi][:, 0:128], big[:, ch, i * 128:(i + 1) * 128], R=[BT(ch)])
                            P.op("dve", ("tensor_copy", dict(out=tokst[:, i, off:off + 128], in_=psb[pi][:, 0:128])),
                                 R=["ps%d" % pi], W=["KT"])
                    sq0 = 2 * u
                    for s_ in range(2):
                        P.dma("sp", ngk[sq0 + s_, l].rearrange("(j p) f -> p j f", p=128),
                              tokst[:, 2 * s_:2 * s_ + 2, 0:128].bitcast(F32), R=["KT"], W=[], slot="o_ngk%d" % s_)
                        P.dma("sp", ndk[sq0 + s_, l].rearrange("(j p) f -> p j f", p=128),
                              tokst[:, 2 * s_:2 * s_ + 2, 128:384].bitcast(F32), R=["KT"], W=[], slot="o_ndk%d" % s_)
                        P.dma("sp", ngv[sq0 + s_, l].rearrange("(j p) f -> p j f", p=128),
                              TTv[:, 2 * s_:2 * s_ + 2, 0:128], R=TT_tags, W=[], slot="o_ngv%d" % s_)
                        P.dma("sp", ndv[sq0 + s_, l].rearrange("(j p) f -> p j f", p=128),
                              TTv[:, 2 * s_:2 * s_ + 2, 128:384], R=TT_tags, W=[], slot="o_ndv%d" % s_)

            def dv_rms():
                for i in range(4):
                    dv_ap = TTv[:, i, 640:896]
                    act(scr[:, 0, 0:256], dv_ap, AF.Square, R=TT_tags, W=["scr0"], accum_out=small[:, 8 + i:9 + i])
                P.op("act", ("activation", dict(out=small[:, 12:16], in_=small[:, 8:12], func=AF.Sqrt, scale=1.0 / 256, bias=eps_ap)),
                     R=["scr0", "eps"], W=["small2"])
                P.op("dve", ("reciprocal", dict(out=small[:, 12:16], in_=small[:, 12:16])), R=["small2"], W=["small2"])
                for i in range(4):
                    dv_ap = TTv[:, i, 640:896]
                    P.op("dve", ("scalar_tensor_tensor", dict(
                        out=DvN[:, i, :], in0=dv_ap, scalar=small[:, 12 + i:13 + i], in1=sgg[:], op0=ALU.mult, op1=ALU.mult)),
                        R=TT_tags + ["small2", "sgg"], W=["sq0", "sq1"])


            def chan_dft():
                for i in range(4):
                    pi = rot()
                    for j in range(2):
                        mm(pi, psb[pi][:, j * 256:(j + 1) * 256], bigr[:, j, i * 128:(i + 1) * 128], ccss_r[:], True, True,
                           R=[BT(j), "ccss"])
                    act(xcs[:, i, :], psb[pi][:], AF.Copy, R=["ps%d" % pi], W=XT(i))

            def fourier_prompt():
                for s in range(2):
                    for j in range(2):
                        pi = rot()
                        n = 0
                        for i2 in range(2):
                            i = s * 2 + i2
                            for part in range(2):
                                mm(pi, psb[pi][:, 0:256], xcs[:, i, j * 256 + part * 128: j * 256 + part * 128 + 128],
                                   dft256[:, part, i2, :], n == 0, n == 3, R=XT(i) + ["dft256"])
                                n += 1
                        act(bigr[:, 15 + j, s * 256:(s + 1) * 256], psb[pi][:, 0:256], AF.Copy, R=["ps%d" % pi], W=[BT(15 + j)])

            def exchange_part(p_):
                bz = bounce[p_][l]
                if p_ == "g":
                    P.dma("sp", bz[0:128, :], big[:, 4, :], R=[BT(4)], W=["bounce_g"], slot="bzg")
                    P.dma("sp", bz[128:256, :].rearrange("r (a c) -> (r a) c", c=128).rearrange("(j p) c -> p j c", p=128),
                          TTv[:, :, 0:128], R=TT_tags, W=["bounce_g"], slot="bzg")
                else:
                    P.dma("sp", bz[0:256, :].rearrange("(c p) t -> p c t", p=128), big[:, 7:9, :], R=[BT(7), BT(8)], W=["bounce_d"], slot="bzd")
                    P.dma("sp", bz[256:512, :].rearrange("r (a c) -> (r a) c", c=256).rearrange("(j p) c -> p j c", p=128),
                          TTv[:, :, 128:384], R=TT_tags, W=["bounce_d"], slot="bzd")
                w_prefetch()
                P.coll(lambda e, l=l, p_=p_: e.collective_compute(
                    "AllGather", ALU.bypass, replica_groups=[[0, 1, 2, 3], [4, 5, 6, 7]],
                    ins=[bounce[p_][l].opt()], outs=[gath[p_][l].opt()]), R=["bounce_" + p_], W=["gath_" + p_], slot="cc_" + p_)

            def exchange_x():
                bx_ = bounce["x"][l]
                P.dma("sp", bx_.rearrange("(j p) c -> p j c", p=128), xcs[:].bitcast(F32), R=XT(0) + XT(1) + XT(2) + XT(3), W=["bounce_x"], slot="bzx")
                w_prefetch()
                P.coll(lambda e, l=l: e.collective_compute(
                    "AllGather", ALU.bypass, replica_groups=[[0, 1, 2, 3], [4, 5, 6, 7]],
                    ins=[bounce["x"][l].opt()], outs=[gath["x"][l].opt()]), R=["bounce_x"], W=["gath_x"], slot="cc_x")

            def fourier_sample():
                acc = [rot(), rot()]
                n = 0
                for s4 in range(4):
                    sx, sc, ss = w_group(3)
                    for i in range(4):
                        for j in range(2):
                            mm(acc[j], psb[acc[j]][:], slabs[:, sx, i, j * 256:j * 256 + 128], slabs[:, sc, i, :],
                               n == 0, False, R=["slab%d" % sx, "slab%d" % sc])
                            mm(acc[j], psb[acc[j]][:], slabs[:, sx, i, j * 256 + 128:j * 256 + 256], slabs[:, ss, i, :],
                               False, n == 15, R=["slab%d" % sx, "slab%d" % ss])
                        n += 1
                for j in range(2):
                    act(xcs[:, j, :], psb[acc[j]][:], AF.Copy, R=["ps%d" % acc[j]], W=XT(j))

            def out_a(ysrc, ytags):
                for c in range(2):
                    pi = rot()
                    for kc in range(2):
                        mm(pi, psb[pi][:], wf_r[:, kc, c * 128:(c + 1) * 128], ysrc(kc), kc == 0, kc == 1, R=["wf"] + (ytags[kc] if isinstance(ytags[kc], list) else [ytags[kc]]))
                    act(bigr[:, 9 + c, :], psb[pi][:], AF.Copy, R=["ps%d" % pi], W=[BT(9 + c)])

            def load_keys_sample(kind, hc):
                if kind == "g":
                    gp, gtag, grows = gath["g"][l], "gath_g", 256
                    kcache, kcols, krow0 = cgk[l], slice(0, 128), 0
                    vcache, vcols, vrow0, vw = cgv[l], slice(0, 128), 128, 128
                else:
                    gp, gtag, grows = gath["d"][l], "gath_d", 512
                    kcache, kcols, krow0 = cdk[l], slice(hc * 128, hc * 128 + 128), hc * 128
                    vcache, vcols, vrow0, vw = cdv[l], slice(hc * 128, hc * 128 + 128), 256, 256
                gl = gp.rearrange("(r x) c -> x r c", x=grows)
                w_prefetch()
                P.dma("sp", cstage[:, :, :], kcache[:, kcols].rearrange("(j p) c -> p j c", p=128), R=[], W=["scr1"], slot="cst")
                for j in range(4):
                    pi = rot()
                    tr(pi, psb[pi][:, 0:128], cstage[:, j, :], R=["scr1"])
                    act(KT[:, j * 128:(j + 1) * 128], psb[pi][:, 0:128], AF.Copy, R=["ps%d" % pi], W=["KT"])
                P.dma("pool", KT[:, 512:2560].rearrange("p (r t) -> p r t", r=4), gl[krow0:krow0 + 128, :, :],
                      R=[gtag], W=["KT"], slot="ktl")
                for hh_ in range(2):
                    vc = slice(vcols.start + hh_ * 64, vcols.start + hh_ * 64 + 64)
                    P.dma("pool", VA[:, 0:4, hh_, 0:64], vcache[:, vc].rearrange("(j p) d -> p j d", p=128),
                          R=[], W=["VA"] + VAP, slot="val")
                    for r in range(4):
                        src = gp[r * grows + vrow0: r * grows + vrow0 + vw, :].rearrange("x (a c) -> (x a) c", c=vw)
                        src = src[:, vc].rearrange("(j p) d -> p j d", p=128)
                        P.dma("pool", VA[:, 4 + 4 * r: 8 + 4 * r, hh_, 0:64], src, R=[gtag], W=["VA"] + VAP, slot="val")

            def attention(gqa_preloaded=False, parts=("g", "d")):
                nk = 2560 if sample else 256
                nkt = nk // 128
                nq = sl
                nqt = nq // 128

                gctr = [0]

                def attn_group(heads, s):
                    nh = len(heads)
                    g_ = gctr[0]
                    gctr[0] += 1
                    if sample:
                        qoff, vb, vtag = 0, 0, "VA"
                        qtag = lambda hi: XT(hi)
                    else:
                        qoff, vb, vtag = (g_ % 2) * 256, (g_ % 8) * 2, VAP[g_ % 8]
                        qtag = lambda hi: [XT(hi)[g_ % 2]]
                    for hi, hd in enumerate(heads):
                        act(xcs[:, hi, qoff:qoff + nq], big[:, hd["qch"], s * sl: s * sl + nq], AF.Copy,
                            R=[BT(hd["qch"]), "maskc"], W=qtag(hi), scale=maskc[:, hd["mcol"]:hd["mcol"] + 1])
                    steps = [(hi, kt) for hi in range(nh) for kt in range(nkt)]
                    pending = []
                    sbank = {}

                    def issue_s(i):
                        hi, kt = steps[i]
                        hd = heads[hi]
                        pi = rot()
                        mm(pi, psb[pi][:, 0:nq], hd["kfn"](kt), xcs[:, hi, qoff:qoff + nq], True, True, R=[hd["ktag"]] + qtag(hi))
                        sbank[i] = pi

                    issue_s(0)
                    for i in range(len(steps)):
                        if i + 1 < len(steps):
                            issue_s(i + 1)
                        hi, kt = steps[i]
                        hd = heads[hi]
                        pi = sbank[i]
                        b = i % 2
                        act(PT[:, b, 0:nq], psb[pi][:, 0:nq], AF.Exp, R=["ps%d" % pi], W=["PT%d" % b], scale=hd["scale"])
                        ob = hd["obank"]
                        for qt in range(nqt):
                            mm(ob, psb[ob][:, qt * 65:(qt + 1) * 65], PT[:, b, qt * 128:(qt + 1) * 128],
                               VA[:, vb + kt, hd["vslot"], :], kt == 0 and qt == 0, kt == nkt - 1 and qt == nqt - 1,
                               R=["PT%d" % b, vtag, "VAones"], skip_group_check=True)
                        for pd in [p for p in pending if p[0] <= i]:
                            pending.remove(pd)
                            pd[1]()
                        if kt == nkt - 1 and hd["post"] is not None:
                            p1, p2 = hd["post"]
                            p1()
                            if p2 is not None:
                                pending.append((i + 3, p2))
                    for pd in pending:
                        pd[1]()

                def gqa_post(h, s, ob):
                    def p1():
                        ov = psb[ob][:, 0:nqt * 65].rearrange("p (q c) -> p q c", c=65)
                        c0 = 16 + 4 * (h % 2)
                        P.op("dve", ("reciprocal", dict(out=small[:, c0:c0 + nqt], in_=ov[:, :, 64])), R=["ps%d" % ob], W=["small3"])
                        P.op("dve", ("tensor_tensor", dict(
                            out=tok[:, s * nqt:(s + 1) * nqt, h * 64:(h + 1) * 64], in0=ov[:, :, 0:64],
                            in1=small[:, c0:c0 + nqt].unsqueeze(2).broadcast_to([128, nqt, 64]), op=ALU.mult)),
                            R=["ps%d" % ob, "small3"], W=["tok"])
                    return p1, None

                def diff_post(h, s, ob1, ob2):
                    par = h % 2
                    A = scr[:, par, 0:256].rearrange("p (q c) -> p q c", c=64)[:, 0:nqt, :]
                    Bm = scr[:, par, 256:512].rearrange("p (q c) -> p q c", c=64)[:, 0:nqt, :]
                    stag = "scr%d" % par
                    c1, c2, c3 = 16 + 12 * par, 20 + 12 * par, 24 + 12 * par
                    mtag = "smallp%d" % par

                    def p1():
                        o1 = psb[ob1][:, 0:nqt * 65].rearrange("p (q c) -> p q c", c=65)
                        o2 = psb[ob2][:, 0:nqt * 65].rearrange("p (q c) -> p q c", c=65)
                        P.op("dve", ("reciprocal", dict(out=small[:, c1:c1 + nqt], in_=o1[:, :, 64])), R=["ps%d" % ob1], W=[mtag])
                        P.op("dve", ("reciprocal", dict(out=small[:, c2:c2 + nqt], in_=o2[:, :, 64])), R=["ps%d" % ob2, mtag], W=[mtag])
                        P.op("dve", ("tensor_scalar", dict(out=small[:, c2:c2 + nqt], in0=small[:, c2:c2 + nqt], scalar1=neglam[:, 0:1],
                                                              scalar2=None, op0=ALU.mult)), R=[mtag, "neglam"], W=[mtag])
                        P.op("dve", ("tensor_tensor", dict(out=A, in0=o1[:, :, 0:64],
                                                              in1=small[:, c1:c1 + nqt].unsqueeze(2).broadcast_to([128, nqt, 64]), op=ALU.mult)),
                             R=["ps%d" % ob1, mtag], W=[stag])
                        P.op("dve", ("tensor_tensor", dict(out=Bm, in0=o2[:, :, 0:64],
                                                              in1=small[:, c2:c2 + nqt].unsqueeze(2).broadcast_to([128, nqt, 64]), op=ALU.mult)),
                             R=["ps%d" % ob2, mtag, stag], W=[stag])
                        P.op("dve", ("tensor_tensor", dict(out=A, in0=A, in1=Bm, op=ALU.add)), R=[stag], W=[stag])
                        P.op("dve", ("tensor_tensor", dict(out=Bm, in0=A, in1=A, op=ALU.mult)), R=[stag], W=[stag])
                        P.op("dve", ("tensor_reduce", dict(out=small[:, c3:c3 + nqt], in_=Bm, axis=AX.X, op=ALU.add)), R=[stag, mtag], W=[mtag])

                    def p2():
                        P.op("act", ("activation", dict(out=small[:, c3:c3 + nqt], in_=small[:, c3:c3 + nqt], func=AF.Ln, scale=1.0 / 64, bias=eps_ap)),
                             R=[mtag, "eps"], W=[mtag])
                        P.op("act", ("activation", dict(out=small[:, c3:c3 + nqt], in_=small[:, c3:c3 + nqt], func=AF.Exp, scale=-0.5)),
                             R=[mtag], W=[mtag])
                        P.op("dve", ("tensor_tensor", dict(out=A, in0=A,
                                                              in1=small[:, c3:c3 + nqt].unsqueeze(2).broadcast_to([128, nqt, 64]), op=ALU.mult)),
                             R=[stag, mtag], W=[stag])
                        P.op("dve", ("tensor_tensor", dict(out=tok[:, s * nqt:(s + 1) * nqt, h * 64:(h + 1) * 64], in0=A,
                                                              in1=dng[:].unsqueeze(1).broadcast_to([128, nqt, 64]), op=ALU.mult)),
                             R=[stag, "dng"], W=["tok"])
                    return p1, p2

                if "g" in parts:
                    for s in range(nseq):
                        if sample:
                            if not gqa_preloaded:
                                load_keys_sample("g", 0)
                            kfn = lambda kt: KT[:, kt * 128:(kt + 1) * 128]
                            ktag = "KT"
                        else:
                            for i2 in range(2):
                                i = s * 2 + i2
                                P.op("dve", ("tensor_copy", dict(
                                    out=VA[:, (gctr[0] % 8) * 2 + i2, :, 0:64], in_=TTv[:, i, 0:128].rearrange("p (h d) -> p h d", d=64))),
                                    R=TT_tags, W=[VAP[gctr[0] % 8], "VA"])
                            kfn = lambda kt, s=s: bigr[:, 4, s * 256 + kt * 128: s * 256 + (kt + 1) * 128]
                            ktag = BT(4)
                        heads = []
                        for h in range(4):
                            heads.append(dict(qch=2 + (h % 2), mcol=h // 2, vslot=h // 2, scale=0.125, obank=h % 2,
                                              kfn=kfn, ktag=ktag, post=gqa_post(h, s, h % 2)))
                        attn_group(heads, s)
                    for i in range(4):
                        for c in range(2):
                            pi = rot()
                            tr(pi, psb[pi][:, 0:128], tok[:, i, c * 128:(c + 1) * 128], R=["tok"])
                            act(bigr[:, 11 + c, i * 128:(i + 1) * 128], psb[pi][:, 0:128], AF.Copy, R=["ps%d" % pi], W=[BT(11 + c)])

                if "d" in parts:
                    dscale = 1.0 / math.sqrt(32.0)
                    for s in range(nseq):
                        for hc in range(2):
                            if sample:
                                load_keys_sample("d", hc)
                                kfn = lambda kt: KT[:, kt * 128:(kt + 1) * 128]
                                ktag = "KT"
                            else:
                                for i2 in range(2):
                                    i = s * 2 + i2
                                    P.op("dve", ("tensor_copy", dict(
                                        out=VA[:, (gctr[0] % 8) * 2 + i2, :, 0:64],
                                        in_=TTv[:, i, 128 + hc * 128: 256 + hc * 128].rearrange("p (h d) -> p h d", d=64))),
                                        R=TT_tags, W=[VAP[gctr[0] % 8], "VA"])
                                kfn = lambda kt, s=s, hc=hc: bigr[:, 7 + hc, s * 256 + kt * 128: s * 256 + (kt + 1) * 128]
                                ktag = BT(7 + hc)
                            heads = []
                            for hh in range(2):
                                h = hc * 2 + hh
                                for m in range(2):
                                    heads.append(dict(qch=5 + hc, mcol=2 + hh * 2 + m, vslot=hh, scale=dscale, obank=2 * hh + m,
                                                      kfn=kfn, ktag=ktag,
                                                      post=(diff_post(h, s, 2 * hh, 2 * hh + 1) if m == 1 else None)))
                            attn_group(heads, s)
                    for i in range(4):
                        for c in range(2):
                            pi = rot()
                            tr(pi, psb[pi][:, 0:128], tok[:, i, c * 128:(c + 1) * 128], R=["tok"])
                            act(bigr[:, 13 + c, i * 128:(i + 1) * 128], psb[pi][:, 0:128], AF.Copy, R=["ps%d" % pi], W=[BT(13 + c)])


            def d_branch():
                for i in range(4):
                    pi = rot()
                    for g in range(4):
                        mm(pi, psb[pi][:, g * 64:(g + 1) * 64], wspT[:, g, :], DvN[:, i, g * 64:(g + 1) * 64],
                           True, True, R=["wspT", "sq0", "sq1"])
                    for g in range(4):
                        P.op("dve", ("scalar_tensor_tensor", dict(
                            out=tok[:, i, g * 64:(g + 1) * 64], in0=psb[pi][:, g * 64:(g + 1) * 64], scalar=bsT[:, g:g + 1],
                            in1=TTv[:, i, 384 + g * 64: 448 + g * 64], op0=ALU.add, op1=ALU.mult)),
                            R=["ps%d" % pi, "bsT"] + TT_tags, W=["tok"])
                for i in range(4):
                    for c in range(2):
                        pi = rot()
                        tr(pi, psb[pi][:, 0:128], tok[:, i, c * 128:(c + 1) * 128], R=["tok"])
                        act(bigr[:, 15 + c, i * 128:(i + 1) * 128], psb[pi][:, 0:128], AF.Copy, R=["ps%d" % pi], W=[BT(15 + c)])


            if sample:
                w_in_groups((1, 0))
                qk_norm((4, 2, 3))
                do_rope((4, 2, 3))
                exchange_part("g")
                w_in_groups((2,))
                do_rope((7, 8, 5, 6))
                load_keys_sample("g", 0)
                exchange_part("d")
                if stop_after == "coll":
                    break
                attention(gqa_preloaded=True, parts=("g",))
                w_in_groups((3,))
                chan_dft()
                exchange_x()
                dv_rms()
                d_branch()
                attention(parts=("d",))
                coll_done.add(l)
                fourier_sample()
                out_a(lambda kc: xcs[:, kc, :], [XT(0), XT(1)])
            else:
                w_in_groups((0, 1, 2, 3))
                qk_norm((2, 3, 4))
                emit_kv()
                dv_rms()
                chan_dft()
                fourier_prompt()
                out_a(lambda kc: bigr[:, 15 + kc, :], [BT(15), BT(16)])
                attention()
                d_branch()

            for cdbg in range(8):
                dump("br%d" % cdbg, big[:, 9 + cdbg, :], [BT(9 + cdbg)])
            for n in range(4):
                for cg in range(2):
                    s0, s1, sbr = w_group(3)
                    for j in range(4):
                        for kc in range(4):
                            mm(j, psb[j][:], slabs[:, s0, kc, j * 128:(j + 1) * 128], hT[:, kc, :], kc == 0, False,
                               R=["slab%d" % s0, "h%d" % kc])
                    for j in range(4):
                        c = cg * 4 + j
                        for kc in range(4, 8):
                            mm(j, psb[j][:], slabs[:, s1, kc - 4, j * 128:(j + 1) * 128], hT[:, kc, :], False, kc == 7,
                               R=["slab%d" % s1, "h%d" % kc])
                        pb = rot()
                        for kc in range(2):
                            mm(pb, psb[pb][:], slabs[:, sbr, kc, j * 128:(j + 1) * 128], bigr[:, 9 + 2 * n + kc, :], kc == 0, kc == 1,
                               R=["slab%d" % sbr, BT(9 + 2 * n + kc)])
                        b = (n * 8 + c) % 2
                        act(scr[:, b, :], psb[j][:], AF.Sigmoid, R=["ps%d" % j], W=["scr%d" % b])
                        if n == 0:
                            P.op("dve", ("tensor_tensor", dict(out=bigr[:, c, :], in0=scr[:, b, :], in1=psb[pb][:], op=ALU.mult)),
                                 R=["scr%d" % b, "ps%d" % pb], W=[BT(c)])
                        else:
                            P.op("dve", ("tensor_tensor", dict(out=scr[:, b, :], in0=scr[:, b, :], in1=psb[pb][:], op=ALU.mult)),
                                 R=["scr%d" % b, "ps%d" % pb], W=["scr%d" % b])
                            P.op("dve", ("tensor_tensor", dict(out=bigr[:, c, :], in0=big[:, c, :], in1=scr[:, b, :], op=ALU.add)),
                                 R=["scr%d" % b, BT(c)], W=[BT(c)])
            dump("merged0", big[:, 0, :], [BT(0)])
            dump("merged7", big[:, 7, :], [BT(7)])
            for cg in range(2):
                s0, s1 = w_group(2)
                for j in range(4):
                    for kc in range(4):
                        mm(4 + j, psb[4 + j][:], slabs[:, s0, kc, j * 128:(j + 1) * 128], bigr[:, kc, :], kc == 0, False,
                           R=["slab%d" % s0, BT(kc)])
                for j in range(4):
                    c = cg * 4 + j
                    pi = 4 + j
                    for kc in range(4, 8):
                        mm(pi, psb[pi][:], slabs[:, s1, kc - 4, j * 128:(j + 1) * 128], bigr[:, kc, :], False, kc == 7,
                           R=["slab%d" % s1, BT(kc)])
                    P.op("dve", ("scalar_tensor_tensor", dict(
                        out=xT[:, u, c, :], in0=psb[pi][:], scalar=modT[:, 16 + c, ci:ci + 1], in1=xT[:, u, c, :],
                        op0=ALU.mult, op1=ALU.add)), R=["ps%d" % pi, "mod", "x%d_%d" % (u, c)], W=["x%d_%d" % (u, c)])
            dump("x1_c0", xT[:, u, 0, :], ["x%d_0" % u])
            norm_mod(u, ci, A2, 24, "A2")
            for half in range(2):
                for g in range(4):
                    s0, s1 = w_group(2)
                    for j in range(4):
                        for kc in range(4):
                            mm(4 + j, psb[4 + j][:], slabs[:, s0, kc, j * 128:(j + 1) * 128], hT[:, kc, :], kc == 0, False,
                               R=["slab%d" % s0, "h%d" % kc])
                    for j in range(4):
                        fc = g * 4 + j
                        pi = 4 + j
                        for kc in range(4, 8):
                            mm(pi, psb[pi][:], slabs[:, s1, kc - 4, j * 128:(j + 1) * 128], hT[:, kc, :], False, kc == 7,
                               R=["slab%d" % s1, "h%d" % kc])
                        if RELU2_DVE:
                            P.op("dve", ("scalar_tensor_tensor", dict(
                                out=bigr[:, fc, :], in0=psb[pi][:], scalar=0.0, in1=psb[pi][:], op0=ALU.max, op1=ALU.mult)),
                                R=["ps%d" % pi], W=[BT(fc)])
                        else:
                            b = fc % 2
                            act(scr[:, b, :], psb[pi][:], AF.Relu, R=["ps%d" % pi], W=["scr%d" % b])
                            P.op("dve", ("tensor_tensor", dict(out=bigr[:, fc, :], in0=scr[:, b, :], in1=scr[:, b, :], op=ALU.mult)),
                                 R=["scr%d" % b], W=[BT(fc)])
                for cg in range(2):
                    for kq in range(4):
                        s, = w_group(1)
                        for j in range(4):
                            for kc in range(4):
                                mm(j, psb[j][:], slabs[:, s, kc, j * 128:(j + 1) * 128], bigr[:, kq * 4 + kc, :],
                                   kq == 0 and kc == 0, kq == 3 and kc == 3, R=["slab%d" % s, BT(kq * 4 + kc)])
                    for j in range(4):
                        c = cg * 4 + j
                        P.op("dve", ("scalar_tensor_tensor", dict(
                            out=xT[:, u, c, :], in0=psb[j][:], scalar=modT[:, 40 + c, ci:ci + 1], in1=xT[:, u, c, :],
                            op0=ALU.mult, op1=ALU.add)), R=["ps%d" % j, "mod", "x%d_%d" % (u, c)], W=["x%d_%d" % (u, c)])
            if l == depth - 1 and stop_after is None:
                final_out(u)

    assert stop_after is not None or wstate["used"] == len(wq), (wstate, len(wq))
    P.emit()
    return nc


def _consts(q):
    c = {}
    c["k_ident"] = np.eye(128, dtype=np.float32)
    c["k_ones"] = np.ones((128, 128), np.float32)
    blk = np.zeros((128, 128), np.float32)
    blk[:64, :64] = 1
    blk[64:, 64:] = 1
    c["k_blk64"] = blk
    n = np.arange(64)
    ang = 2 * np.pi * np.outer(n, n) / 64
    C64 = np.cos(ang) / 8.0
    S64 = np.sin(ang) / 8.0
    cc = np.zeros((128, 256))
    cc[:64, 0:64] = C64
    cc[64:, 64:128] = C64
    cc[:64, 128:192] = S64
    cc[64:, 192:256] = S64
    c["k_ccss"] = cc.astype(np.float32)

    def rotm(hd):
        half = hd // 2
        R = np.zeros((128, 128), np.float32)
        for m in range(128):
            if m % hd < half:
                R[m + half, m] = -1.0
            else:
                R[m - half, m] = 1.0
        return R
    mk = np.zeros((128, 6), np.float32)
    mk[0:64, 0] = 1
    mk[64:128, 1] = 1
    for j in range(4):
        mk[32 * j:32 * j + 32, 2 + j] = 1
    c["k_mask"] = mk
    c["k_rotg"] = rotm(64)
    c["k_rotd"] = rotm(32)
    pos = q * 512 + np.arange(512)
    row = (pos // 64).astype(np.float64)
    col = (pos % 64).astype(np.float64)

    def tab(dim):
        quarter = dim // 4
        inv = 10000.0 ** (-np.arange(quarter, dtype=np.float32) / quarter)
        inv = inv.astype(np.float32)
        ang = np.concatenate([row[:, None].astype(np.float32) * inv, col[:, None].astype(np.float32) * inv], axis=-1)
        ang = ang.astype(np.float32)
        cos = np.cos(ang).astype(np.float32)
        sin = np.sin(ang).astype(np.float32)
        half = dim // 2
        idx = np.arange(128) % half
        return cos[:, idx].T.copy(), sin[:, idx].T.copy()
    cg, sg = tab(64)
    cd, sd = tab(32)
    c["k_rope"] = np.stack([cg, sg, cd, sd]).astype(np.float32)
    t = np.arange(256)
    a = 2 * np.pi * np.outer(t, t) / 256
    c["k_dft256"] = np.stack([np.cos(a) / 16.0, -np.sin(a) / 16.0]).astype(np.float32)
    tt = np.arange(2048, dtype=np.float64)
    a = 2 * np.pi * np.outer(tt, pos.astype(np.float64)) / 2048
    c["k_dftc"] = (np.cos(a) / math.sqrt(2048.0)).astype(np.float32)
    c["k_dfts"] = (-np.sin(a) / math.sqrt(2048.0)).astype(np.float32)
    return c


_NC_CACHE = {}
import os
import json
_DBG = {k: (tuple(v) if isinstance(v, list) else v) for k, v in json.loads(os.environ.get("KDBG", "{}")).items()}
_DBG_RUN = {}


def kernel(**inputs):
    inp = {k: np.ascontiguousarray(np.asarray(v)) for k, v in inputs.items()}
    if "nc" not in _NC_CACHE:
        _NC_CACHE["nc"] = build_nc(**_DBG)
    nc = _NC_CACHE["nc"]
    shared = ["c_ctx", "w_ada", "b_ada", "norm1_g", "norm2_g", "w_in", "w_fourier", "q_norm_g", "k_norm_g",
              "lambda_q1", "lambda_k1", "lambda_q2", "lambda_k2", "diff_norm_g", "sgu_norm_g", "w_spatial",
              "b_spatial", "w_gate", "w_branch", "w_out", "w_mlp1", "w_mlp2", "final_norm_g"]
    in_maps = []
    for core in range(8):
        b, q = core // 4, core % 4
        m = {k: inp[k] for k in shared}
        m["xp"] = inp["x_prompt"][4 * core:4 * core + 4]
        m["xs"] = inp["x_sample"][b, q * 512:(q + 1) * 512]
        m["cm"] = inp["c"][b]
        m["cgk"] = inp["cache_gqa_k"][b].reshape(DEPTH, 512, 128)
        m["cgv"] = inp["cache_gqa_v"][b].reshape(DEPTH, 512, 128)
        m["cdk"] = inp["cache_diff_k"][b].reshape(DEPTH, 512, 256)
        m["cdv"] = inp["cache_diff_v"][b].reshape(DEPTH, 512, 256)
        m.update(_consts(q))
        in_maps.append({k: np.ascontiguousarray(v) for k, v in m.items()})
    ncores = _DBG_RUN.get("cores", 8)
    res = run_bass_kernel_spmd(nc, in_maps[:ncores], core_ids=list(range(ncores)))
    R = list(res.results)
    _DBG_RUN["results"] = R
    _DBG_RUN["names"] = getattr(nc, "_dbg_names", [])
    while len(R) < 8:
        R.append(R[0])
    y_prompt = np.concatenate([R[c]["yp"] for c in range(8)], axis=0)
    y_sample = np.stack([np.concatenate([R[b * 4 + q]["ys"] for q in range(4)], axis=0) for b in range(2)], axis=0)
    ngk = np.concatenate([R[c]["ngk"] for c in range(8)], axis=0).reshape(32, DEPTH, 256, 2, 64)
    ngv = np.concatenate([R[c]["ngv"] for c in range(8)], axis=0).reshape(32, DEPTH, 256, 2, 64)
    ndk = np.concatenate([R[c]["ndk"] for c in range(8)], axis=0).reshape(32, DEPTH, 256, 4, 2, 32)
    ndv = np.concatenate([R[c]["ndv"] for c in range(8)], axis=0).reshape(32, DEPTH, 256, 4, 64)
    return (y_prompt.astype(np.float32), y_sample.astype(np.float32), ngk.astype(np.float32),
            ngv.astype(np.float32), ndk.astype(np.float32), ndv.astype(np.float32))
```

```python
import math
import numpy as np
import concourse.bass as bass
import concourse.mybir as mybir
from concourse.bass_utils import run_bass_kernel_spmd

F32 = mybir.dt.float32
F32R = mybir.dt.float32r
BF16 = mybir.dt.bfloat16
AF = mybir.ActivationFunctionType
ALU = mybir.AluOpType
AX = mybir.AxisListType

D = 1024
DEPTH = 4
T = 512
NU = 3
EPS = 1e-6
GROWS = 1280
RELU2_DVE = False


class Prog:
    def __init__(self, nc):
        self.nc = nc
        self.ops = {e: [] for e in ("pe", "act", "dve", "pool", "sp")}
        self.cnt = {e: 0 for e in self.ops}
        self.esem = {e: nc.alloc_semaphore("sem_" + e) for e in ("pe", "act", "dve", "pool")}
        self.slot_sem = {}
        self.slot_cnt = {}
        self.lastw = {}
        self.readers = {}
        self.waited = {e: {} for e in self.ops}
        self.final = []

    def _deps(self, eng, R, W, pe_acc=False):
        deps = {}

        def add(ev):
            if ev is None:
                return
            k, v, src, _sem = ev
            if pe_acc and src == "pe" and eng == "pe":
                return
            if k not in deps or deps[k][0] < v:
                deps[k] = (v, ev[3])

        for t in list(R) + list(W):
            add(self.lastw.get(t))
        for t in W:
            for ev in self.readers.get(t, []):
                add(ev)
        out = []
        for k, (v, sem) in deps.items():
            if self.waited[eng].get(k, 0) >= v:
                continue
            self.waited[eng][k] = v
            out.append((sem, v))
        return out

    def _commit(self, ev, R, W):
        for t in W:
            self.lastw[t] = ev
            self.readers[t] = []
        for t in R:
            self.readers.setdefault(t, []).append(ev)

    def op(self, eng, fn, R=(), W=()):
        if isinstance(fn, tuple):
            _name, _kw = fn
            fn = (lambda e, _name=_name, _kw=_kw: getattr(e, _name)(**_kw))
        waits = self._deps(eng, R, W, pe_acc=True)
        self.cnt[eng] += 1
        sem = self.esem[eng]
        ev = ("e_" + eng, self.cnt[eng], eng, sem)
        self.ops[eng].append((waits, fn, sem, 1))
        self._commit(ev, R, W)

    def dma(self, q, out, in_, R, W, slot, **kw):
        waits = self._deps(q, R, W)
        if slot not in self.slot_sem:
            self.slot_sem[slot] = self.nc.alloc_semaphore("ds_" + slot)
            self.slot_cnt[slot] = 0
        self.slot_cnt[slot] += 16
        sem = self.slot_sem[slot]
        ev = ("s_" + slot, self.slot_cnt[slot], "dma", sem)
        self.ops[q].append((waits, lambda e: e.dma_start(out=out, in_=in_, **kw), sem, 16))
        self._commit(ev, R, W)

    def coll(self, fn, R, W, slot):
        waits = self._deps("pool", R, W)
        if slot not in self.slot_sem:
            self.slot_sem[slot] = self.nc.alloc_semaphore("ds_" + slot)
            self.slot_cnt[slot] = 0
        self.slot_cnt[slot] += 1
        sem = self.slot_sem[slot]
        ev = ("s_" + slot, self.slot_cnt[slot], "dma", sem)
        self.ops["pool"].append((waits, fn, sem, None))
        self._commit(ev, R, W)

    def emit(self):
        nc = self.nc
        final_waits = [(self.slot_sem[s], self.slot_cnt[s]) for s in self.slot_sem]
        with nc.Block() as block:
            def run(eng_name):
                def body(e):
                    for waits, fn, sem, inc in self.ops[eng_name]:
                        for (s, v) in waits:
                            e.wait_ge(s, v)
                        ins = fn(e)
                        if inc is None:
                            ins.then_inc(sem)
                        else:
                            ins.then_inc(sem, inc)
                    if eng_name == "sp":
                        for (s, v) in final_waits:
                            e.wait_ge(s, v)
                        for en in ("pe", "act", "dve", "pool"):
                            if self.cnt[en]:
                                e.wait_ge(self.esem[en], self.cnt[en])
                return body
            block.sync(run("sp"))
            block.tensor(run("pe"))
            block.scalar(run("act"))
            block.vector(run("dve"))
            block.gpsimd(run("pool"))


def build_nc(depth=DEPTH, units=(0, 1, 2), dbg=False, stop_after=None):
    nc = bass.Bass("TRN2", target_bir_lowering=False)
    P = Prog(nc)

    def din(name, shape, dt=F32):
        return nc.dram_tensor(name, list(shape), dt, kind="ExternalInput").ap()

    def dout(name, shape):
        return nc.dram_tensor(name, list(shape), F32, kind="ExternalOutput").ap()

    xp = din("xp", [4, 256, D])
    xs = din("xs", [512, D])
    cm = din("cm", [D])
    cctx = din("c_ctx", [D])
    cgk = din("cgk", [DEPTH, 512, 128])
    cgv = din("cgv", [DEPTH, 512, 128])
    cdk = din("cdk", [DEPTH, 512, 256])
    cdv = din("cdv", [DEPTH, 512, 256])
    w_ada = din("w_ada", [DEPTH, D, 6 * D])
    b_ada = din("b_ada", [DEPTH, 6 * D])
    norm1_g = din("norm1_g", [DEPTH, D])
    norm2_g = din("norm2_g", [DEPTH, D])
    w_in = din("w_in", [DEPTH, D, 2048])
    w_fourier = din("w_fourier", [DEPTH, 256, 256])
    q_norm_g = din("q_norm_g", [DEPTH, 64])
    k_norm_g = din("k_norm_g", [DEPTH, 64])
    lam_in = [din(n, [DEPTH, 32]) for n in ("lambda_q1", "lambda_k1", "lambda_q2", "lambda_k2")]
    diff_norm_g = din("diff_norm_g", [DEPTH, 64])
    sgu_norm_g = din("sgu_norm_g", [DEPTH, 256])
    w_spatial = din("w_spatial", [DEPTH, 4, 128, 128])
    b_spatial = din("b_spatial", [DEPTH, 4, 128])
    w_gate = din("w_gate", [DEPTH, D, 4 * D])
    w_branch = din("w_branch", [DEPTH, 4, 256, D])
    w_out = din("w_out", [DEPTH, D, D])
    w_mlp1 = din("w_mlp1", [DEPTH, D, 4 * D])
    w_mlp2 = din("w_mlp2", [DEPTH, 4 * D, D])
    final_g = din("final_norm_g", [D])
    k_ident = din("k_ident", [128, 128])
    k_ones = din("k_ones", [128, 128])
    k_blk64 = din("k_blk64", [128, 128])
    k_ccss = din("k_ccss", [128, 256])
    k_rotg = din("k_rotg", [128, 128])
    k_rotd = din("k_rotd", [128, 128])
    k_mask = din("k_mask", [128, 6])
    k_rope = din("k_rope", [4, 128, 512])
    k_dft256 = din("k_dft256", [2, 256, 256])
    k_dftc = din("k_dftc", [2048, 512])
    k_dfts = din("k_dfts", [2048, 512])
    yp = dout("yp", [4, 256, D])
    ys = dout("ys", [512, D])
    ngk = dout("ngk", [4, DEPTH, 256, 128])
    ngv = dout("ngv", [4, DEPTH, 256, 128])
    ndk = dout("ndk", [4, DEPTH, 256, 256])
    ndv = dout("ndv", [4, DEPTH, 256, 256])
    PR = {"g": 256, "d": 512, "x": 512}
    bounce = {p: [nc.dram_tensor("bounce_%s%d" % (p, l), [PR[p], 512], F32).ap() for l in range(DEPTH)] for p in PR}
    gath = {p: [nc.dram_tensor("gath_%s%d" % (p, l), [4 * PR[p], 512], F32).ap() for l in range(DEPTH)] for p in PR}

    dbg_out = dout("dbg", [16, 128, 512]) if dbg else None
    dbg_n = [0]
    dbg_names = []

    def dump(name, ap, R):
        if not dbg:
            return
        i = dbg_n[0]
        dbg_n[0] += 1
        dbg_names.append(name)
        ncol = ap.shape[-1] if len(ap.shape) == 2 else None
        P.dma("sp", dbg_out[i, :, 0:ap.shape[1]], ap, R=R, W=[], slot="dbg")
    nc._dbg_names = dbg_names

    def sb(name, shape, dt=F32):
        return nc.alloc_sbuf_tensor(name, list(shape), dt)

    xT = sb("xT", [128, NU, 8, T])
    hT = sb("hT", [128, 8, T], F32R)
    bigR = sb("bigR", [128, 17, T], F32R)
    TT = sb("TT", [128, 4, 896])
    dft256t = sb("dft256t", [128, 2, 2, 256], F32R)
    slabs = sb("slabs", [128, 4, 4, 512], F32R)
    KT = sb("KT", [128, 2560], F32R)
    VA = sb("VA", [128, 20, 2, 65], BF16)
    PT = sb("PT", [128, 2, 512], BF16)
    tokst = KT[:, 0:1536].rearrange("p (i c) -> p i c", c=384)
    tok = sb("tok", [128, 4, 256])
    sq = sb("sq", [128, 2, T], F32R)
    scr = sb("scr", [128, 2, T])
    cstage = scr[:, 1, :].rearrange("p (i c) -> p i c", c=128)
    DvN = sq[:].rearrange("p a t -> p (a t)").rearrange("p (i c) -> p i c", c=256)
    rstd = sb("rstd", [128, T])
    small = sb("small", [128, 64])
    ident = sb("ident", [128, 128])
    ones_r = sb("ones_r", [128, 128], F32R)
    blk64_r = sb("blk64_r", [128, 128], F32R)
    ccss_r = sb("ccss_r", [128, 256], F32R)
    rotg_r = sb("rotg_r", [128, 128], F32R)
    rotd_r = sb("rotd_r", [128, 128], F32R)
    rope = sb("rope", [128, 4, 512])
    dft256 = dft256t[:]
    xcs = sb("xcs", [128, 4, 512], F32R)
    wf_r = sb("wf_r", [128, 2, 256], F32R)
    wsp = sb("wsp", [128, 4, 128])
    wspT = sb("wspT", [128, 4, 128], F32R)
    bsT = sb("bsT", [128, 4])
    condT = sb("condT", [128, 8, 2])
    scond = sb("scond", [128, 8, 2], F32R)
    modT = sb("modT", [128, 48, 2])
    badaT = sb("badaT", [128, 48])
    A1 = sb("A1", [128, 8, 2])
    A2 = sb("A2", [128, 8, 2])
    n1g = sb("n1g", [128, 8])
    n2g = sb("n2g", [128, 8])
    fng = sb("fng", [128, 8])
    qg = sb("qg", [128, 1])
    kg = sb("kg", [128, 1])
    lamv = sb("lamv", [128, 4, 32])
    neglam = sb("neglam", [128, 1])
    dng = sb("dng", [128, 64])
    sgg = sb("sgg", [128, 256])

    NPS = 8
    psb = [nc.alloc_psum_tensor("ps%d" % i, [128, 512], F32) for i in range(NPS)]
    rot_state = [0]

    def rot():
        i = 4 + rot_state[0] % 4
        rot_state[0] += 1
        return i

    def mm(ps_i, out_ap, lhsT, rhs, start, stop, R, **kw):
        P.op("pe", ("matmul", dict(out=out_ap, lhsT=lhsT, rhs=rhs, start=start, stop=stop, **kw)),
             R=R, W=["ps%d" % ps_i])

    def tr(ps_i, out_ap, in_ap, R):
        P.op("pe", ("transpose", dict(out=out_ap, in_=in_ap, identity=ident[:])), R=list(R) + ["ident"], W=["ps%d" % ps_i])

    def act(out, in_, func, R, W, **kw):
        P.op("act", ("activation", dict(out=out, in_=in_, func=func, **kw)), R=R, W=W)

    def BT(c):
        return "B%d" % c

    def XT(i):
        return ["xcs%da" % i, "xcs%db" % i]

    VAP = ["VAp%d" % i for i in range(8)]

    bigr = bigR[:]
    big = bigR[:].bitcast(F32)
    outst = tok[:].rearrange("p a t -> p (a t)").rearrange("p (b d) -> p b d", d=1024)
    OST = {0: ["tok"], 1: ["tok"]}

    wq = []
    wstate = {"issued": 0, "used": 0}
    PREF = 3

    coll_done = set()
    wneed = {}

    def w_issue():
        i = wstate["issued"]
        if i >= len(wq):
            return False
        need = wneed.get(i)
        if need is not None and need not in coll_done:
            return False
        slot = i % 4
        for (dst_fn, src) in wq[i]:
            P.dma("pool", dst_fn(slot), src, R=(["gath_x"] if need is not None else []), W=["slab%d" % slot], slot="slab%d" % slot)
        wstate["issued"] += 1
        return True

    def w_group(n):
        i = wstate["used"]
        while wstate["issued"] <= min(i + 3, len(wq) - 1):
            if not w_issue():
                break
        assert wstate["issued"] >= i + n, "weight slab not issued"
        wstate["used"] += n
        return [(i + k) % 4 for k in range(n)]

    def w_prefetch():
        i = wstate["used"]
        while wstate["issued"] <= min(i + 3, len(wq) - 1):
            if not w_issue():
                break

    def slab_std(src2d, nk=4):
        ncols = src2d.shape[1]
        return [(lambda s, nk=nk, ncols=ncols: slabs[:, s, 0:nk, 0:ncols],
                 src2d.rearrange("(k p) c -> p k c", p=128))]

    def plan_layer_unit(l, sample):
        for g in ((1, 0, 2, 3) if sample else (0, 1, 2, 3)):
            for kh in range(2):
                rows = slice(kh * 512, kh * 512 + 512)
                if g == 0:
                    ent = [(lambda s: slabs[:, s, :, 0:256],
                            w_in[l, rows, 0:256].rearrange("(k p) c -> p k c", p=128))]
                    for b in range(2):
                        for a in range(2):
                            ent.append((lambda s, b=b, a=a: slabs[:, s, :, 256 + 128 * b + 64 * a:320 + 128 * b + 64 * a],
                                        w_in[l, rows, 256 + 128 * a + 64 * b:320 + 128 * a + 64 * b].rearrange("(k p) d -> p k d", p=128)))
                    wq.append(ent)
                else:
                    wq.append(slab_std(w_in[l, rows, g * 512:(g + 1) * 512]))
        if sample:
            for s4 in range(4):
                wneed[len(wq)] = l
                wq.append([(lambda s: slabs[:, s, :, :],
                            gath["x"][l][s4 * 512:(s4 + 1) * 512, :].rearrange("(k p) c -> p k c", p=128))])
                wq.append(slab_std(k_dftc[s4 * 512:(s4 + 1) * 512, :]))
                wq.append(slab_std(k_dfts[s4 * 512:(s4 + 1) * 512, :]))
        for n in range(4):
            for cg in range(2):
                cols = slice(n * 1024 + cg * 512, n * 1024 + cg * 512 + 512)
                for kh in range(2):
                    wq.append(slab_std(w_gate[l, kh * 512:(kh + 1) * 512, cols]))
                wq.append(slab_std(w_branch[l, n, :, cg * 512:(cg + 1) * 512], nk=2))
        for cg in range(2):
            for kh in range(2):
                wq.append(slab_std(w_out[l, kh * 512:(kh + 1) * 512, cg * 512:(cg + 1) * 512]))
        for half in range(2):
            for g in range(4):
                for kh in range(2):
                    wq.append(slab_std(w_mlp1[l, kh * 512:(kh + 1) * 512, half * 2048 + g * 512: half * 2048 + (g + 1) * 512]))
            for cg in range(2):
                for kq in range(4):
                    r0 = half * 2048 + kq * 512
                    wq.append(slab_std(w_mlp2[l, r0:r0 + 512, cg * 512:(cg + 1) * 512]))

    def plan_ada(l):
        for g in range(12):
            for kh in range(2):
                wq.append(slab_std(w_ada[l, kh * 512:(kh + 1) * 512, g * 512:(g + 1) * 512]))

    for l in range(depth):
        plan_ada(l)
        for u in units:
            plan_layer_unit(l, u == 2)

    def ld(dst, src, W, slot, q="sp", **kw):
        P.dma(q, dst, src, R=[], W=W, slot=slot, **kw)

    ld(ident[:], k_ident, ["ident"], "c0")
    ld(ones_r[:], k_ones, ["ones"], "c1", q="pool")
    ld(blk64_r[:], k_blk64, ["blk64"], "c2", q="pool")
    ld(ccss_r[:], k_ccss, ["ccss"], "c3", q="pool")
    ld(rotg_r[:], k_rotg, ["rotg"], "c4", q="pool")
    ld(rotd_r[:], k_rotd, ["rotd"], "c5", q="pool")
    ld(rope[:], k_rope.rearrange("a p t -> p a t"), ["rope"], "c6")
    for a_ in range(2):
        ld(dft256[:, a_], k_dft256[a_].rearrange("(j p) t -> p j t", p=128), ["dft256"], "c7", q="pool")
    ld(fng[:], final_g.rearrange("(c p) -> p c", p=128), ["fng"], "c8", allow_slow_non_contiguous=True)
    ld(condT[:, :, 0], cctx.rearrange("(c p) -> p c", p=128), ["condT"], "c9", allow_slow_non_contiguous=True)
    ld(condT[:, :, 1], cm.rearrange("(c p) -> p c", p=128), ["condT"], "c9", allow_slow_non_contiguous=True)
    P.op("pool", lambda e: e.memset(VA[:, :, :, 64:65], 1.0), W=["VAones"])
    def x_load():
        for u in range(NU):
            for i in range(4):
                if u < 2:
                    src = xp[2 * u + i // 2, (i % 2) * 128:(i % 2) * 128 + 128, :]
                else:
                    src = xs[i * 128:(i + 1) * 128, :]
                b = 0
                ld(outst[:, b, :], src, OST[b], "ost%d" % b)
                for c in range(8):
                    pi = rot()
                    tr(pi, psb[pi][:, 0:128], outst[:, b, c * 128:(c + 1) * 128], R=OST[b])
                    eng = "dve" if c % 2 == 0 else "act"
                    dst = xT[:, u, c, i * 128:(i + 1) * 128]
                    if eng == "dve":
                        P.op("dve", ("tensor_copy", dict(out=dst, in_=psb[pi][:, 0:128])),
                             R=["ps%d" % pi], W=["x%d_%d" % (u, c)])
                    else:
                        act(dst, psb[pi][:, 0:128], AF.Copy, R=["ps%d" % pi], W=["x%d_%d" % (u, c)])
    act(scond[:], condT[:], AF.Silu, R=["condT"], W=["scond"])

    def norm_mod(u, ci, Amat, shift_c0, gtag):
        pi = rot()
        for c in range(8):
            b = c % 2
            act(sq[:, b, :], xT[:, u, c, :], AF.Square, R=["x%d_%d" % (u, c)], W=["sq%d" % b])
            mm(pi, psb[pi][:], ones_r[:], sq[:, b, :], c == 0, c == 7, R=["ones", "sq%d" % b])
        act(rstd[:], psb[pi][:], AF.Sqrt, R=["ps%d" % pi, "eps"], W=["rstd"], scale=1.0 / D, bias=eps_ap)
        P.op("dve", ("reciprocal", dict(out=rstd[:], in_=rstd[:])), R=["rstd"], W=["rstd"])
        for c in range(8):
            b = c % 2
            P.op("dve", ("scalar_tensor_tensor", dict(
                out=scr[:, b, :], in0=xT[:, u, c, :], scalar=Amat[:, c, ci:ci + 1], in1=rstd[:],
                op0=ALU.mult, op1=ALU.mult)), R=["x%d_%d" % (u, c), gtag, "rstd"], W=["scr%d" % b])
            act(hT[:, c, :], scr[:, b, :], AF.Identity, R=["scr%d" % b, "mod"], W=["h%d" % c],
                bias=modT[:, shift_c0 + c, ci:ci + 1])

    eps_t = sb("eps_t", [128, 1])
    maskc = sb("maskc", [128, 6])
    ld(maskc[:], k_mask, ["maskc"], "c10")
    P.op("pool", lambda e: e.memset(eps_t[:], EPS), W=["eps"])
    eps_ap = eps_t[:]

    def rms_rows(ps_or_ap, R_in, nparts_tag):
        pass

    def final_out(u):
        pi = rot()
        for c in range(8):
            b = c % 2
            act(sq[:, b, :], xT[:, u, c, :], AF.Square, R=["x%d_%d" % (u, c)], W=["sq%d" % b])
            mm(pi, psb[pi][:], ones_r[:], sq[:, b, :], c == 0, c == 7, R=["ones", "sq%d" % b])
        act(rstd[:], psb[pi][:], AF.Sqrt, R=["ps%d" % pi, "eps"], W=["rstd"], scale=1.0 / D, bias=eps_ap)
        P.op("dve", ("reciprocal", dict(out=rstd[:], in_=rstd[:])), R=["rstd"], W=["rstd"])
        scrv = scr[:].rearrange("p a t -> p (a t)").rearrange("p (c q) -> p c q", q=128)
        for i in range(4):
            for c in range(8):
                P.op("dve", ("scalar_tensor_tensor", dict(
                    out=scrv[:, c, :], in0=xT[:, u, c, i * 128:(i + 1) * 128], scalar=fng[:, c:c + 1], in1=rstd[:, i * 128:(i + 1) * 128],
                    op0=ALU.mult, op1=ALU.mult)), R=["x%d_%d" % (u, c), "fng", "rstd"], W=["scr0", "scr1"])
            for c in range(8):
                pi = rot()
                tr(pi, psb[pi][:, 0:128], scrv[:, c, :], R=["scr0", "scr1"])
                if c % 2 == 0:
                    P.op("dve", ("tensor_copy", dict(out=outst[:, 0, c * 128:(c + 1) * 128], in_=psb[pi][:, 0:128])),
                         R=["ps%d" % pi], W=["tok"])
                else:
                    act(outst[:, 0, c * 128:(c + 1) * 128], psb[pi][:, 0:128], AF.Copy, R=["ps%d" % pi], W=["tok"])
            if u < 2:
                dst = yp[2 * u + i // 2, (i % 2) * 128:(i % 2) * 128 + 128, :]
            else:
                dst = ys[i * 128:(i + 1) * 128, :]
            P.dma("sp", dst, outst[:, 0, :], R=["tok"], W=[], slot="oy")


    for l in range(depth):
        lam_init = 0.8 - 0.6 * math.exp(-0.3 * l)
        lam_scale = 1.0 - lam_init
        ld(badaT[:], b_ada[l].rearrange("(c p) -> p c", p=128), ["badaT"], "l0", allow_slow_non_contiguous=True)
        ld(n1g[:], norm1_g[l].rearrange("(c p) -> p c", p=128), ["n1g"], "l1", allow_slow_non_contiguous=True)
        ld(n2g[:], norm2_g[l].rearrange("(c p) -> p c", p=128), ["n2g"], "l2", allow_slow_non_contiguous=True)
        for hh in range(2):
            ld(qg[hh * 64:(hh + 1) * 64, :], q_norm_g[l].rearrange("(p o) -> p o", o=1), ["qg"], "l3", allow_slow_non_contiguous=True)
            ld(kg[hh * 64:(hh + 1) * 64, :], k_norm_g[l].rearrange("(p o) -> p o", o=1), ["kg"], "l4", allow_slow_non_contiguous=True)
        for j in range(4):
            ld(lamv[:, j, :], lam_in[j][l].partition_broadcast(128), ["lamv"], "l5")
        ld(dng[:], diff_norm_g[l].partition_broadcast(128), ["dng"], "l6")
        ld(sgg[:], sgu_norm_g[l].partition_broadcast(128), ["sgg"], "l7")
        ld(wsp[:], w_spatial[l].rearrange("g q p -> q g p"), ["wsp"], "l8")
        ld(bsT[:], b_spatial[l].rearrange("g q -> q g"), ["bsT"], "l9", allow_slow_non_contiguous=True)
        ld(wf_r[:], w_fourier[l].rearrange("(k p) c -> p k c", p=128), ["wf"], "l10", q="pool")
        P.op("dve", ("tensor_tensor", dict(out=small[:, 0:32], in0=lamv[:, 0, :], in1=lamv[:, 1, :], op=ALU.mult)), R=["lamv"], W=["small"])
        P.op("dve", ("tensor_tensor", dict(out=small[:, 32:64], in0=lamv[:, 2, :], in1=lamv[:, 3, :], op=ALU.mult)), R=["lamv", "small"], W=["small"])
        P.op("dve", ("tensor_reduce", dict(out=neglam[:], in_=small[:, 0:32], axis=AX.X, op=ALU.add)), R=["small"], W=["neglam"])
        P.op("dve", ("tensor_reduce", dict(out=small[:, 0:1], in_=small[:, 32:64], axis=AX.X, op=ALU.add)), R=["small", "neglam"], W=["small"])
        act(neglam[:], neglam[:], AF.Exp, R=["neglam"], W=["neglam"])
        act(small[:, 1:2], small[:, 0:1], AF.Exp, R=["small"], W=["small"])
        P.op("dve", ("tensor_tensor", dict(out=neglam[:], in0=small[:, 1:2], in1=neglam[:], op=ALU.subtract)), R=["small", "neglam"], W=["neglam"])
        P.op("dve", ("tensor_scalar_add", dict(out=neglam[:], in0=neglam[:], scalar1=-lam_init)), R=["neglam"], W=["neglam"])
        P.op("dve", ("tensor_scalar_mul", dict(out=dng[:], in0=dng[:], scalar1=lam_scale)), R=["dng"], W=["dng"])
        for g in range(4):
            pi = rot()
            tr(pi, psb[pi][:, 0:128], wsp[:, g, :], R=["wsp"])
            P.op("dve", ("tensor_copy", dict(out=wspT[:, g, :], in_=psb[pi][:, 0:128])), R=["ps%d" % pi], W=["wspT"])
        for g in range(12):
            s0, s1 = w_group(2)
            pi = rot()
            for kc in range(8):
                sl_ = s0 if kc < 4 else s1
                mm(pi, psb[pi][0:2, :], scond[:, kc, :], slabs[:, sl_, kc % 4, :], kc == 0, kc == 7,
                   R=["slab%d" % sl_, "scond"])
            P.op("dve", ("tensor_copy", dict(out=rstd[0:2, :], in_=psb[pi][0:2, :])), R=["ps%d" % pi], W=["rstd"])
            for j in range(4):
                pj = rot()
                P.op("pe", ("transpose", dict(out=psb[pj][:, 0:2], in_=rstd[0:2, j * 128:(j + 1) * 128], identity=ident[0:2, 0:2])),
                     R=["rstd", "ident"], W=["ps%d" % pj])
                cc = g * 4 + j
                P.op("dve", ("tensor_scalar", dict(
                    out=modT[:, cc, :], in0=psb[pj][:, 0:2], scalar1=badaT[:, cc:cc + 1], scalar2=None, op0=ALU.add)),
                    R=["ps%d" % pj, "badaT"], W=["mod"])
        if l == 0:
            x_load()
        for ci in range(2):
            P.op("dve", ("scalar_tensor_tensor", dict(
                out=A1[:, :, ci], in0=modT[:, 8:16, ci], scalar=1.0, in1=n1g[:], op0=ALU.add, op1=ALU.mult)),
                R=["mod", "n1g"], W=["A1"])
            P.op("dve", ("scalar_tensor_tensor", dict(
                out=A2[:, :, ci], in0=modT[:, 32:40, ci], scalar=1.0, in1=n2g[:], op0=ALU.add, op1=ALU.mult)),
                R=["mod", "n2g"], W=["A2"])

        for u in units:
            sample = (u == 2)
            ci = 1 if sample else 0
            nseq = 1 if sample else 2
            sl = T // nseq
            norm_mod(u, ci, A1, 0, "A1")
            hR = ["h%d" % c for c in range(8)]
            TTv = TT[:]
            TT_tags = ["TT"]
            def w_in_groups(gs):
                for g in gs:
                    s0, s1 = w_group(2)

                    def wsl(kc, c0, c1):
                        s = s0 if kc < 4 else s1
                        return slabs[:, s, kc % 4, c0:c1], "slab%d" % s

                    def ftype(c0, dst_chunk, evac):
                        pi = rot()
                        for kc in range(8):
                            w_ap, wt = wsl(kc, c0, c0 + 128)
                            mm(pi, psb[pi][:], w_ap, hT[:, kc, :], kc == 0, kc == 7, R=[wt, "h%d" % kc])
                        evac(pi, dst_chunk)

                    def ttype(c0, ncols, toff, post=None):
                        for i in range(4):
                            pi = rot()
                            for kc in range(8):
                                w_ap, wt = wsl(kc, c0, c0 + ncols)
                                mm(pi, psb[pi][:, 0:ncols], hT[:, kc, i * 128:(i + 1) * 128], w_ap, kc == 0, kc == 7,
                                   R=[wt, "h%d" % kc])
                            dst = TTv[:, i, toff:toff + ncols]
                            if post is None:
                                P.op("dve", ("tensor_copy", dict(out=dst, in_=psb[pi][:, 0:ncols])),
                                     R=["ps%d" % pi], W=TT_tags)
                            else:
                                act(dst, psb[pi][:, 0:ncols], post, R=["ps%d" % pi], W=TT_tags)

                    def ev_copy_r(pi, ch):
                        act(bigr[:, ch, :], psb[pi][:], AF.Copy, R=["ps%d" % pi], W=[BT(ch)])

                    def ev_copy_f(pi, ch):
                        P.op("dve", ("tensor_copy", dict(out=bigr[:, ch, :], in_=psb[pi][:])), R=["ps%d" % pi], W=[BT(ch)])

                    if g == 0:
                        ftype(0, 0, ev_copy_r)
                        ftype(128, 1, ev_copy_r)
                        ftype(256, 2, ev_copy_f)
                        ftype(384, 3, ev_copy_f)
                    elif g == 1:
                        ftype(0, 4, ev_copy_f)
                        ttype(128, 128, 0)
                        ftype(256, 5, ev_copy_f if sample else ev_copy_r)
                        ftype(384, 6, ev_copy_f if sample else ev_copy_r)
                    elif g == 2:
                        ftype(0, 7, ev_copy_f if sample else ev_copy_r)
                        ftype(128, 8, ev_copy_f if sample else ev_copy_r)
                        ttype(256, 256, 128)
                    else:
                        ttype(0, 256, 384, post=AF.Gelu_apprx_tanh)
                        ttype(256, 256, 640, post=AF.Gelu_apprx_tanh)

            def qk_norm(chs):
                for ch in chs:
                    gvec, gt = (kg, "kg") if ch == 4 else (qg, "qg")
                    act(sq[:, 0, :], big[:, ch, :], AF.Square, R=[BT(ch)], W=["sq0"])
                    pi = rot()
                    mm(pi, psb[pi][:], blk64_r[:], sq[:, 0, :], True, True, R=["blk64", "sq0"])
                    act(rstd[:], psb[pi][:], AF.Sqrt, R=["ps%d" % pi, "eps"], W=["rstd"], scale=1.0 / 64, bias=eps_ap)
                    P.op("dve", ("reciprocal", dict(out=rstd[:], in_=rstd[:])), R=["rstd"], W=["rstd"])
                    P.op("dve", ("scalar_tensor_tensor", dict(
                        out=bigr[:, ch, :], in0=big[:, ch, :], scalar=gvec[:, 0:1], in1=rstd[:], op0=ALU.mult, op1=ALU.mult)),
                        R=[BT(ch), gt, "rstd"], W=[BT(ch)])

            def do_rope(chs):
                for ch in chs:
                    isg = ch <= 4
                    rm = rotg_r if isg else rotd_r
                    rt = "rotg" if isg else "rotd"
                    cosi, sini = (0, 1) if isg else (2, 3)
                    act(sq[:, 1, :], big[:, ch, :], AF.Copy, R=[BT(ch)], W=["sq1"])
                    pi = rot()
                    mm(pi, psb[pi][:], rm[:], sq[:, 1, :], True, True, R=[rt, "sq1"])
                    P.op("dve", ("tensor_tensor", dict(out=scr[:, 0, :], in0=psb[pi][:], in1=rope[:, sini, :], op=ALU.mult)),
                         R=["ps%d" % pi, "rope"], W=["scr0"])
                    P.op("pool", ("tensor_tensor", dict(out=scr[:, 1, :], in0=big[:, ch, :], in1=rope[:, cosi, :], op=ALU.mult)),
                         R=[BT(ch), "rope"], W=["scr1"])
                    P.op("dve", ("tensor_tensor", dict(out=bigr[:, ch, :], in0=scr[:, 0, :], in1=scr[:, 1, :], op=ALU.add)),
                         R=["scr0", "scr1"], W=[BT(ch)])

            def emit_kv():
                if True:
                    for i in range(4):
                        for (ch, off) in ((4, 0), (7, 128), (8, 256)):
                            pi = rot()
                            tr(pi, psb[pi][:, 0:128], big[:, ch, i * 128:(i + 1) * 128], R=[BT(ch)])
                            P.op("dve", ("tensor_copy", dict(out=tokst[:, i, off:off + 128], in_=psb[pi][:, 0:128])),
                                 R=["ps%d" % pi], W=["KT"])
                    sq0 = 2 * u
                    for s_ in range(2):
                        P.dma("sp", ngk[sq0 + s_, l].rearrange("(j p) f -> p j f", p=128),
                              tokst[:, 2 * s_:2 * s_ + 2, 0:128].bitcast(F32), R=["KT"], W=[], slot="o_ngk%d" % s_)
                        P.dma("sp", ndk[sq0 + s_, l].rearrange("(j p) f -> p j f", p=128),
                              tokst[:, 2 * s_:2 * s_ + 2, 128:384].bitcast(F32), R=["KT"], W=[], slot="o_ndk%d" % s_)
                        P.dma("sp", ngv[sq0 + s_, l].rearrange("(j p) f -> p j f", p=128),
                              TTv[:, 2 * s_:2 * s_ + 2, 0:128], R=TT_tags, W=[], slot="o_ngv%d" % s_)
                        P.dma("sp", ndv[sq0 + s_, l].rearrange("(j p) f -> p j f", p=128),
                              TTv[:, 2 * s_:2 * s_ + 2, 128:384], R=TT_tags, W=[], slot="o_ndv%d" % s_)

            def dv_rms():
                for i in range(4):
                    dv_ap = TTv[:, i, 640:896]
                    act(scr[:, 0, 0:256], dv_ap, AF.Square, R=TT_tags, W=["scr0"], accum_out=small[:, 8 + i:9 + i])
                P.op("act", ("activation", dict(out=small[:, 12:16], in_=small[:, 8:12], func=AF.Sqrt, scale=1.0 / 256, bias=eps_ap)),
                     R=["scr0", "eps"], W=["small2"])
                P.op("dve", ("reciprocal", dict(out=small[:, 12:16], in_=small[:, 12:16])), R=["small2"], W=["small2"])
                for i in range(4):
                    dv_ap = TTv[:, i, 640:896]
                    P.op("dve", ("scalar_tensor_tensor", dict(
                        out=DvN[:, i, :], in0=dv_ap, scalar=small[:, 12 + i:13 + i], in1=sgg[:], op0=ALU.mult, op1=ALU.mult)),
                        R=TT_tags + ["small2", "sgg"], W=["sq0", "sq1"])


            def chan_dft():
                for i in range(4):
                    pi = rot()
                    for j in range(2):
                        mm(pi, psb[pi][:, j * 256:(j + 1) * 256], bigr[:, j, i * 128:(i + 1) * 128], ccss_r[:], True, True,
                           R=[BT(j), "ccss"])
                    act(xcs[:, i, :], psb[pi][:], AF.Copy, R=["ps%d" % pi], W=XT(i))

            def fourier_prompt():
                for s in range(2):
                    for j in range(2):
                        pi = rot()
                        n = 0
                        for i2 in range(2):
                            i = s * 2 + i2
                            for part in range(2):
                                mm(pi, psb[pi][:, 0:256], xcs[:, i, j * 256 + part * 128: j * 256 + part * 128 + 128],
                                   dft256[:, part, i2, :], n == 0, n == 3, R=XT(i) + ["dft256"])
                                n += 1
                        act(bigr[:, 15 + j, s * 256:(s + 1) * 256], psb[pi][:, 0:256], AF.Copy, R=["ps%d" % pi], W=[BT(15 + j)])

            def exchange_part(p_):
                bz = bounce[p_][l]
                if p_ == "g":
                    P.dma("sp", bz[0:128, :], big[:, 4, :], R=[BT(4)], W=["bounce_g"], slot="bzg")
                    P.dma("sp", bz[128:256, :].rearrange("r (a c) -> (r a) c", c=128).rearrange("(j p) c -> p j c", p=128),
                          TTv[:, :, 0:128], R=TT_tags, W=["bounce_g"], slot="bzg")
                else:
                    P.dma("sp", bz[0:256, :].rearrange("(c p) t -> p c t", p=128), big[:, 7:9, :], R=[BT(7), BT(8)], W=["bounce_d"], slot="bzd")
                    P.dma("sp", bz[256:512, :].rearrange("r (a c) -> (r a) c", c=256).rearrange("(j p) c -> p j c", p=128),
                          TTv[:, :, 128:384], R=TT_tags, W=["bounce_d"], slot="bzd")
                w_prefetch()
                P.coll(lambda e, l=l, p_=p_: e.collective_compute(
                    "AllGather", ALU.bypass, replica_groups=[[0, 1, 2, 3], [4, 5, 6, 7]],
                    ins=[bounce[p_][l].opt()], outs=[gath[p_][l].opt()]), R=["bounce_" + p_], W=["gath_" + p_], slot="cc_" + p_)

            def exchange_x():
                bx_ = bounce["x"][l]
                P.dma("sp", bx_.rearrange("(j p) c -> p j c", p=128), xcs[:].bitcast(F32), R=XT(0) + XT(1) + XT(2) + XT(3), W=["bounce_x"], slot="bzx")
                w_prefetch()
                P.coll(lambda e, l=l: e.collective_compute(
                    "AllGather", ALU.bypass, replica_groups=[[0, 1, 2, 3], [4, 5, 6, 7]],
                    ins=[bounce["x"][l].opt()], outs=[gath["x"][l].opt()]), R=["bounce_x"], W=["gath_x"], slot="cc_x")

            def fourier_sample():
                acc = [rot(), rot()]
                n = 0
                for s4 in range(4):
                    sx, sc, ss = w_group(3)
                    for i in range(4):
                        for j in range(2):
                            mm(acc[j], psb[acc[j]][:], slabs[:, sx, i, j * 256:j * 256 + 128], slabs[:, sc, i, :],
                               n == 0, False, R=["slab%d" % sx, "slab%d" % sc])
                            mm(acc[j], psb[acc[j]][:], slabs[:, sx, i, j * 256 + 128:j * 256 + 256], slabs[:, ss, i, :],
                               False, n == 15, R=["slab%d" % sx, "slab%d" % ss])
                        n += 1
                for j in range(2):
                    act(xcs[:, j, :], psb[acc[j]][:], AF.Copy, R=["ps%d" % acc[j]], W=XT(j))

            def out_a(ysrc, ytags):
                for c in range(2):
                    pi = rot()
                    for kc in range(2):
                        mm(pi, psb[pi][:], wf_r[:, kc, c * 128:(c + 1) * 128], ysrc(kc), kc == 0, kc == 1, R=["wf"] + (ytags[kc] if isinstance(ytags[kc], list) else [ytags[kc]]))
                    act(bigr[:, 9 + c, :], psb[pi][:], AF.Copy, R=["ps%d" % pi], W=[BT(9 + c)])

            def load_keys_sample(kind, hc):
                if kind == "g":
                    gp, gtag, grows = gath["g"][l], "gath_g", 256
                    kcache, kcols, krow0 = cgk[l], slice(0, 128), 0
                    vcache, vcols, vrow0, vw = cgv[l], slice(0, 128), 128, 128
                else:
                    gp, gtag, grows = gath["d"][l], "gath_d", 512
                    kcache, kcols, krow0 = cdk[l], slice(hc * 128, hc * 128 + 128), hc * 128
                    vcache, vcols, vrow0, vw = cdv[l], slice(hc * 128, hc * 128 + 128), 256, 256
                gl = gp.rearrange("(r x) c -> x r c", x=grows)
                w_prefetch()
                P.dma("sp", cstage[:, :, :], kcache[:, kcols].rearrange("(j p) c -> p j c", p=128), R=[], W=["scr1"], slot="cst")
                for j in range(4):
                    pi = rot()
                    tr(pi, psb[pi][:, 0:128], cstage[:, j, :], R=["scr1"])
                    act(KT[:, j * 128:(j + 1) * 128], psb[pi][:, 0:128], AF.Copy, R=["ps%d" % pi], W=["KT"])
                P.dma("pool", KT[:, 512:2560].rearrange("p (r t) -> p r t", r=4), gl[krow0:krow0 + 128, :, :],
                      R=[gtag], W=["KT"], slot="ktl")
                for hh_ in range(2):
                    vc = slice(vcols.start + hh_ * 64, vcols.start + hh_ * 64 + 64)
                    P.dma("pool", VA[:, 0:4, hh_, 0:64], vcache[:, vc].rearrange("(j p) d -> p j d", p=128),
                          R=[], W=["VA"] + VAP, slot="val")
                    for r in range(4):
                        src = gp[r * grows + vrow0: r * grows + vrow0 + vw, :].rearrange("x (a c) -> (x a) c", c=vw)
                        src = src[:, vc].rearrange("(j p) d -> p j d", p=128)
                        P.dma("pool", VA[:, 4 + 4 * r: 8 + 4 * r, hh_, 0:64], src, R=[gtag], W=["VA"] + VAP, slot="val")

            def attention(gqa_preloaded=False, parts=("g", "d")):
                nk = 2560 if sample else 256
                nkt = nk // 128
                nq = sl
                nqt = nq // 128

                gctr = [0]

                def attn_group(heads, s):
                    nh = len(heads)
                    g_ = gctr[0]
                    gctr[0] += 1
                    if sample:
                        qoff, vb, vtag = 0, 0, "VA"
                        qtag = lambda hi: XT(hi)
                    else:
                        qoff, vb, vtag = (g_ % 2) * 256, (g_ % 8) * 2, VAP[g_ % 8]
                        qtag = lambda hi: [XT(hi)[g_ % 2]]
                    for hi, hd in enumerate(heads):
                        act(xcs[:, hi, qoff:qoff + nq], big[:, hd["qch"], s * sl: s * sl + nq], AF.Copy,
                            R=[BT(hd["qch"]), "maskc"], W=qtag(hi), scale=maskc[:, hd["mcol"]:hd["mcol"] + 1])
                    steps = [(hi, kt) for hi in range(nh) for kt in range(nkt)]
                    pending = []
                    sbank = {}

                    def issue_s(i):
                        hi, kt = steps[i]
                        hd = heads[hi]
                        pi = rot()
                        mm(pi, psb[pi][:, 0:nq], hd["kfn"](kt), xcs[:, hi, qoff:qoff + nq], True, True, R=[hd["ktag"]] + qtag(hi))
                        sbank[i] = pi

                    issue_s(0)
                    for i in range(len(steps)):
                        if i + 1 < len(steps):
                            issue_s(i + 1)
                        hi, kt = steps[i]
                        hd = heads[hi]
                        pi = sbank[i]
                        b = i % 2
                        act(PT[:, b, 0:nq], psb[pi][:, 0:nq], AF.Exp, R=["ps%d" % pi], W=["PT%d" % b], scale=hd["scale"])
                        ob = hd["obank"]
                        for qt in range(nqt):
                            mm(ob, psb[ob][:, qt * 65:(qt + 1) * 65], PT[:, b, qt * 128:(qt + 1) * 128],
                               VA[:, vb + kt, hd["vslot"], :], kt == 0 and qt == 0, kt == nkt - 1 and qt == nqt - 1,
                               R=["PT%d" % b, vtag, "VAones"], skip_group_check=True)
                        for pd in [p for p in pending if p[0] <= i]:
                            pending.remove(pd)
                            pd[1]()
                        if kt == nkt - 1 and hd["post"] is not None:
                            p1, p2 = hd["post"]
                            p1()
                            if p2 is not None:
                                pending.append((i + 3, p2))
                    for pd in pending:
                        pd[1]()

                def gqa_post(h, s, ob):
                    def p1():
                        ov = psb[ob][:, 0:nqt * 65].rearrange("p (q c) -> p q c", c=65)
                        c0 = 16 + 4 * (h % 2)
                        P.op("dve", ("reciprocal", dict(out=small[:, c0:c0 + nqt], in_=ov[:, :, 64])), R=["ps%d" % ob], W=["small3"])
                        P.op("dve", ("tensor_tensor", dict(
                            out=tok[:, s * nqt:(s + 1) * nqt, h * 64:(h + 1) * 64], in0=ov[:, :, 0:64],
                            in1=small[:, c0:c0 + nqt].unsqueeze(2).broadcast_to([128, nqt, 64]), op=ALU.mult)),
                            R=["ps%d" % ob, "small3"], W=["tok"])
                    return p1, None

                def diff_post(h, s, ob1, ob2):
                    par = h % 2
                    A = scr[:, par, 0:256].rearrange("p (q c) -> p q c", c=64)[:, 0:nqt, :]
                    Bm = scr[:, par, 256:512].rearrange("p (q c) -> p q c", c=64)[:, 0:nqt, :]
                    stag = "scr%d" % par
                    c1, c2, c3 = 16 + 12 * par, 20 + 12 * par, 24 + 12 * par
                    mtag = "smallp%d" % par

                    def p1():
                        o1 = psb[ob1][:, 0:nqt * 65].rearrange("p (q c) -> p q c", c=65)
                        o2 = psb[ob2][:, 0:nqt * 65].rearrange("p (q c) -> p q c", c=65)
                        P.op("dve", ("reciprocal", dict(out=small[:, c1:c1 + nqt], in_=o1[:, :, 64])), R=["ps%d" % ob1], W=[mtag])
                        P.op("dve", ("reciprocal", dict(out=small[:, c2:c2 + nqt], in_=o2[:, :, 64])), R=["ps%d" % ob2, mtag], W=[mtag])
                        P.op("dve", ("tensor_scalar", dict(out=small[:, c2:c2 + nqt], in0=small[:, c2:c2 + nqt], scalar1=neglam[:, 0:1],
                                                              scalar2=None, op0=ALU.mult)), R=[mtag, "neglam"], W=[mtag])
                        P.op("dve", ("tensor_tensor", dict(out=A, in0=o1[:, :, 0:64],
                                                              in1=small[:, c1:c1 + nqt].unsqueeze(2).broadcast_to([128, nqt, 64]), op=ALU.mult)),
                             R=["ps%d" % ob1, mtag], W=[stag])
                        P.op("dve", ("tensor_tensor", dict(out=Bm, in0=o2[:, :, 0:64],
                                                              in1=small[:, c2:c2 + nqt].unsqueeze(2).broadcast_to([128, nqt, 64]), op=ALU.mult)),
                             R=["ps%d" % ob2, mtag, stag], W=[stag])
                        P.op("dve", ("tensor_tensor", dict(out=A, in0=A, in1=Bm, op=ALU.add)), R=[stag], W=[stag])
                        P.op("dve", ("tensor_tensor", dict(out=Bm, in0=A, in1=A, op=ALU.mult)), R=[stag], W=[stag])
                        P.op("dve", ("tensor_reduce", dict(out=small[:, c3:c3 + nqt], in_=Bm, axis=AX.X, op=ALU.add)), R=[stag, mtag], W=[mtag])

                    def p2():
                        P.op("act", ("activation", dict(out=small[:, c3:c3 + nqt], in_=small[:, c3:c3 + nqt], func=AF.Ln, scale=1.0 / 64, bias=eps_ap)),
                             R=[mtag, "eps"], W=[mtag])
                        P.op("act", ("activation", dict(out=small[:, c3:c3 + nqt], in_=small[:, c3:c3 + nqt], func=AF.Exp, scale=-0.5)),
                             R=[mtag], W=[mtag])
                        P.op("dve", ("tensor_tensor", dict(out=A, in0=A,
                                                              in1=small[:, c3:c3 + nqt].unsqueeze(2).broadcast_to([128, nqt, 64]), op=ALU.mult)),
                             R=[stag, mtag], W=[stag])
                        P.op("dve", ("tensor_tensor", dict(out=tok[:, s * nqt:(s + 1) * nqt, h * 64:(h + 1) * 64], in0=A,
                                                              in1=dng[:].unsqueeze(1).broadcast_to([128, nqt, 64]), op=ALU.mult)),
                             R=[stag, "dng"], W=["tok"])
                    return p1, p2

                if "g" in parts:
                    for s in range(nseq):
                        if sample:
                            if not gqa_preloaded:
                                load_keys_sample("g", 0)
                            kfn = lambda kt: KT[:, kt * 128:(kt + 1) * 128]
                            ktag = "KT"
                        else:
                            for i2 in range(2):
                                i = s * 2 + i2
                                P.op("dve", ("tensor_copy", dict(
                                    out=VA[:, (gctr[0] % 8) * 2 + i2, :, 0:64], in_=TTv[:, i, 0:128].rearrange("p (h d) -> p h d", d=64))),
                                    R=TT_tags, W=[VAP[gctr[0] % 8], "VA"])
                            kfn = lambda kt, s=s: bigr[:, 4, s * 256 + kt * 128: s * 256 + (kt + 1) * 128]
                            ktag = BT(4)
                        heads = []
                        for h in range(4):
                            heads.append(dict(qch=2 + (h % 2), mcol=h // 2, vslot=h // 2, scale=0.125, obank=h % 2,
                                              kfn=kfn, ktag=ktag, post=gqa_post(h, s, h % 2)))
                        attn_group(heads, s)
                    for i in range(4):
                        for c in range(2):
                            pi = rot()
                            tr(pi, psb[pi][:, 0:128], tok[:, i, c * 128:(c + 1) * 128], R=["tok"])
                            act(bigr[:, 11 + c, i * 128:(i + 1) * 128], psb[pi][:, 0:128], AF.Copy, R=["ps%d" % pi], W=[BT(11 + c)])

                if "d" in parts:
                    dscale = 1.0 / math.sqrt(32.0)
                    for s in range(nseq):
                        for hc in range(2):
                            if sample:
                                load_keys_sample("d", hc)
                                kfn = lambda kt: KT[:, kt * 128:(kt + 1) * 128]
                                ktag = "KT"
                            else:
                                for i2 in range(2):
                                    i = s * 2 + i2
                                    P.op("dve", ("tensor_copy", dict(
                                        out=VA[:, (gctr[0] % 8) * 2 + i2, :, 0:64],
                                        in_=TTv[:, i, 128 + hc * 128: 256 + hc * 128].rearrange("p (h d) -> p h d", d=64))),
                                        R=TT_tags, W=[VAP[gctr[0] % 8], "VA"])
                                kfn = lambda kt, s=s, hc=hc: bigr[:, 7 + hc, s * 256 + kt * 128: s * 256 + (kt + 1) * 128]
                                ktag = BT(7 + hc)
                            heads = []
                            for hh in range(2):
                                h = hc * 2 + hh
                                for m in range(2):
                                    heads.append(dict(qch=5 + hc, mcol=2 + hh * 2 + m, vslot=hh, scale=dscale, obank=2 * hh + m,
                                                      kfn=kfn, ktag=ktag,
                                                      post=(diff_post(h, s, 2 * hh, 2 * hh + 1) if m == 1 else None)))
                            attn_group(heads, s)
                    for i in range(4):
                        for c in range(2):
                            pi = rot()
                            tr(pi, psb[pi][:, 0:128], tok[:, i, c * 128:(c + 1) * 128], R=["tok"])
                            act(bigr[:, 13 + c, i * 128:(i + 1) * 128], psb[pi][:, 0:128], AF.Copy, R=["ps%d" % pi], W=[BT(13 + c)])


            def d_branch():
                for i in range(4):
                    pi = rot()
                    for g in range(4):
                        mm(pi, psb[pi][:, g * 64:(g + 1) * 64], wspT[:, g, :], DvN[:, i, g * 64:(g + 1) * 64],
                           True, True, R=["wspT", "sq0", "sq1"])
                    for g in range(4):
                        P.op("dve", ("scalar_tensor_tensor", dict(
                            out=tok[:, i, g * 64:(g + 1) * 64], in0=psb[pi][:, g * 64:(g + 1) * 64], scalar=bsT[:, g:g + 1],
                            in1=TTv[:, i, 384 + g * 64: 448 + g * 64], op0=ALU.add, op1=ALU.mult)),
                            R=["ps%d" % pi, "bsT"] + TT_tags, W=["tok"])
                for i in range(4):
                    for c in range(2):
                        pi = rot()
                        tr(pi, psb[pi][:, 0:128], tok[:, i, c * 128:(c + 1) * 128], R=["tok"])
                        act(bigr[:, 15 + c, i * 128:(i + 1) * 128], psb[pi][:, 0:128], AF.Copy, R=["ps%d" % pi], W=[BT(15 + c)])


            if sample:
                w_in_groups((1, 0))
                qk_norm((4, 2, 3))
                do_rope((4, 2, 3))
                exchange_part("g")
                w_in_groups((2,))
                do_rope((7, 8, 5, 6))
                load_keys_sample("g", 0)
                exchange_part("d")
                if stop_after == "coll":
                    break
                w_in_groups((3,))
                chan_dft()
                exchange_x()
                dv_rms()
                d_branch()
                attention(gqa_preloaded=True, parts=("g",))
                attention(parts=("d",))
                coll_done.add(l)
                fourier_sample()
                out_a(lambda kc: xcs[:, kc, :], [XT(0), XT(1)])
            else:
                w_in_groups((0, 1, 2, 3))
                qk_norm((2, 3, 4))
                emit_kv()
                dv_rms()
                chan_dft()
                fourier_prompt()
                out_a(lambda kc: bigr[:, 15 + kc, :], [BT(15), BT(16)])
                attention()
                d_branch()

            for cdbg in range(8):
                dump("br%d" % cdbg, big[:, 9 + cdbg, :], [BT(9 + cdbg)])
            for n in range(4):
                for cg in range(2):
                    s0, s1, sbr = w_group(3)
                    for j in range(4):
                        for kc in range(4):
                            mm(j, psb[j][:], slabs[:, s0, kc, j * 128:(j + 1) * 128], hT[:, kc, :], kc == 0, False,
                               R=["slab%d" % s0, "h%d" % kc])
                    for j in range(4):
                        c = cg * 4 + j
                        for kc in range(4, 8):
                            mm(j, psb[j][:], slabs[:, s1, kc - 4, j * 128:(j + 1) * 128], hT[:, kc, :], False, kc == 7,
                               R=["slab%d" % s1, "h%d" % kc])
                        pb = rot()
                        for kc in range(2):
                            mm(pb, psb[pb][:], slabs[:, sbr, kc, j * 128:(j + 1) * 128], bigr[:, 9 + 2 * n + kc, :], kc == 0, kc == 1,
                               R=["slab%d" % sbr, BT(9 + 2 * n + kc)])
                        b = (n * 8 + c) % 2
                        act(scr[:, b, :], psb[j][:], AF.Sigmoid, R=["ps%d" % j], W=["scr%d" % b])
                        if n == 0:
                            P.op("dve", ("tensor_tensor", dict(out=bigr[:, c, :], in0=scr[:, b, :], in1=psb[pb][:], op=ALU.mult)),
                                 R=["scr%d" % b, "ps%d" % pb], W=[BT(c)])
                        else:
                            P.op("dve", ("tensor_tensor", dict(out=scr[:, b, :], in0=scr[:, b, :], in1=psb[pb][:], op=ALU.mult)),
                                 R=["scr%d" % b, "ps%d" % pb], W=["scr%d" % b])
                            P.op("dve", ("tensor_tensor", dict(out=bigr[:, c, :], in0=big[:, c, :], in1=scr[:, b, :], op=ALU.add)),
                                 R=["scr%d" % b, BT(c)], W=[BT(c)])
            dump("merged0", big[:, 0, :], [BT(0)])
            dump("merged7", big[:, 7, :], [BT(7)])
            for cg in range(2):
                s0, s1 = w_group(2)
                for j in range(4):
                    for kc in range(4):
                        mm(4 + j, psb[4 + j][:], slabs[:, s0, kc, j * 128:(j + 1) * 128], bigr[:, kc, :], kc == 0, False,
                           R=["slab%d" % s0, BT(kc)])
                for j in range(4):
                    c = cg * 4 + j
                    pi = 4 + j
                    for kc in range(4, 8):
                        mm(pi, psb[pi][:], slabs[:, s1, kc - 4, j * 128:(j + 1) * 128], bigr[:, kc, :], False, kc == 7,
                           R=["slab%d" % s1, BT(kc)])
                    P.op("dve", ("scalar_tensor_tensor", dict(
                        out=xT[:, u, c, :], in0=psb[pi][:], scalar=modT[:, 16 + c, ci:ci + 1], in1=xT[:, u, c, :],
                        op0=ALU.mult, op1=ALU.add)), R=["ps%d" % pi, "mod", "x%d_%d" % (u, c)], W=["x%d_%d" % (u, c)])
            dump("x1_c0", xT[:, u, 0, :], ["x%d_0" % u])
            norm_mod(u, ci, A2, 24, "A2")
            for half in range(2):
                for g in range(4):
                    s0, s1 = w_group(2)
                    for j in range(4):
                        for kc in range(4):
                            mm(4 + j, psb[4 + j][:], slabs[:, s0, kc, j * 128:(j + 1) * 128], hT[:, kc, :], kc == 0, False,
                               R=["slab%d" % s0, "h%d" % kc])
                    for j in range(4):
                        fc = g * 4 + j
                        pi = 4 + j
                        for kc in range(4, 8):
                            mm(pi, psb[pi][:], slabs[:, s1, kc - 4, j * 128:(j + 1) * 128], hT[:, kc, :], False, kc == 7,
                               R=["slab%d" % s1, "h%d" % kc])
                        if RELU2_DVE:
                            P.op("dve", ("scalar_tensor_tensor", dict(
                                out=bigr[:, fc, :], in0=psb[pi][:], scalar=0.0, in1=psb[pi][:], op0=ALU.max, op1=ALU.mult)),
                                R=["ps%d" % pi], W=[BT(fc)])
                        else:
                            b = fc % 2
                            act(scr[:, b, :], psb[pi][:], AF.Relu, R=["ps%d" % pi], W=["scr%d" % b])
                            P.op("dve", ("tensor_tensor", dict(out=bigr[:, fc, :], in0=scr[:, b, :], in1=scr[:, b, :], op=ALU.mult)),
                                 R=["scr%d" % b], W=[BT(fc)])
                for cg in range(2):
                    for kq in range(4):
                        s, = w_group(1)
                        for j in range(4):
                            for kc in range(4):
                                mm(j, psb[j][:], slabs[:, s, kc, j * 128:(j + 1) * 128], bigr[:, kq * 4 + kc, :],
                                   kq == 0 and kc == 0, kq == 3 and kc == 3, R=["slab%d" % s, BT(kq * 4 + kc)])
                    for j in range(4):
                        c = cg * 4 + j
                        P.op("dve", ("scalar_tensor_tensor", dict(
                            out=xT[:, u, c, :], in0=psb[j][:], scalar=modT[:, 40 + c, ci:ci + 1], in1=xT[:, u, c, :],
                            op0=ALU.mult, op1=ALU.add)), R=["ps%d" % j, "mod", "x%d_%d" % (u, c)], W=["x%d_%d" % (u, c)])
            if l == depth - 1 and stop_after is None:
                final_out(u)

    assert stop_after is not None or wstate["used"] == len(wq), (wstate, len(wq))
    P.emit()
    return nc


def _consts(q):
    c = {}
    c["k_ident"] = np.eye(128, dtype=np.float32)
    c["k_ones"] = np.ones((128, 128), np.float32)
    blk = np.zeros((128, 128), np.float32)
    blk[:64, :64] = 1
    blk[64:, 64:] = 1
    c["k_blk64"] = blk
    n = np.arange(64)
    ang = 2 * np.pi * np.outer(n, n) / 64
    C64 = np.cos(ang) / 8.0
    S64 = np.sin(ang) / 8.0
    cc = np.zeros((128, 256))
    cc[:64, 0:64] = C64
    cc[64:, 64:128] = C64
    cc[:64, 128:192] = S64
    cc[64:, 192:256] = S64
    c["k_ccss"] = cc.astype(np.float32)

    def rotm(hd):
        half = hd // 2
        R = np.zeros((128, 128), np.float32)
        for m in range(128):
            if m % hd < half:
                R[m + half, m] = -1.0
            else:
                R[m - half, m] = 1.0
        return R
    mk = np.zeros((128, 6), np.float32)
    mk[0:64, 0] = 1
    mk[64:128, 1] = 1
    for j in range(4):
        mk[32 * j:32 * j + 32, 2 + j] = 1
    c["k_mask"] = mk
    c["k_rotg"] = rotm(64)
    c["k_rotd"] = rotm(32)
    pos = q * 512 + np.arange(512)
    row = (pos // 64).astype(np.float64)
    col = (pos % 64).astype(np.float64)

    def tab(dim):
        quarter = dim // 4
        inv = 10000.0 ** (-np.arange(quarter, dtype=np.float32) / quarter)
        inv = inv.astype(np.float32)
        ang = np.concatenate([row[:, None].astype(np.float32) * inv, col[:, None].astype(np.float32) * inv], axis=-1)
        ang = ang.astype(np.float32)
        cos = np.cos(ang).astype(np.float32)
        sin = np.sin(ang).astype(np.float32)
        half = dim // 2
        idx = np.arange(128) % half
        return cos[:, idx].T.copy(), sin[:, idx].T.copy()
    cg, sg = tab(64)
    cd, sd = tab(32)
    c["k_rope"] = np.stack([cg, sg, cd, sd]).astype(np.float32)
    t = np.arange(256)
    a = 2 * np.pi * np.outer(t, t) / 256
    c["k_dft256"] = np.stack([np.cos(a) / 16.0, -np.sin(a) / 16.0]).astype(np.float32)
    tt = np.arange(2048, dtype=np.float64)
    a = 2 * np.pi * np.outer(tt, pos.astype(np.float64)) / 2048
    c["k_dftc"] = (np.cos(a) / math.sqrt(2048.0)).astype(np.float32)
    c["k_dfts"] = (-np.sin(a) / math.sqrt(2048.0)).astype(np.float32)
    return c


_NC_CACHE = {}
import os
import json
_DBG = {k: (tuple(v) if isinstance(v, list) else v) for k, v in json.loads(os.environ.get("KDBG", "{}")).items()}
_DBG_RUN = {}


def kernel(**inputs):
    inp = {k: np.ascontiguousarray(np.asarray(v)) for k, v in inputs.items()}
    if "nc" not in _NC_CACHE:
        _NC_CACHE["nc"] = build_nc(**_DBG)
    nc = _NC_CACHE["nc"]
    shared = ["c_ctx", "w_ada", "b_ada", "norm1_g", "norm2_g", "w_in", "w_fourier", "q_norm_g", "k_norm_g",
              "lambda_q1", "lambda_k1", "lambda_q2", "lambda_k2", "diff_norm_g", "sgu_norm_g", "w_spatial",
              "b_spatial", "w_gate", "w_branch", "w_out", "w_mlp1", "w_mlp2", "final_norm_g"]
    in_maps = []
    for core in range(8):
        b, q = core // 4, core % 4
        m = {k: inp[k] for k in shared}
        m["xp"] = inp["x_prompt"][4 * core:4 * core + 4]
        m["xs"] = inp["x_sample"][b, q * 512:(q + 1) * 512]
        m["cm"] = inp["c"][b]
        m["cgk"] = inp["cache_gqa_k"][b].reshape(DEPTH, 512, 128)
        m["cgv"] = inp["cache_gqa_v"][b].reshape(DEPTH, 512, 128)
        m["cdk"] = inp["cache_diff_k"][b].reshape(DEPTH, 512, 256)
        m["cdv"] = inp["cache_diff_v"][b].reshape(DEPTH, 512, 256)
        m.update(_consts(q))
        in_maps.append({k: np.ascontiguousarray(v) for k, v in m.items()})
    ncores = _DBG_RUN.get("cores", 8)
    res = run_bass_kernel_spmd(nc, in_maps[:ncores], core_ids=list(range(ncores)))
    R = list(res.results)
    _DBG_RUN["results"] = R
    _DBG_RUN["names"] = getattr(nc, "_dbg_names", [])
    while len(R) < 8:
        R.append(R[0])
    y_prompt = np.concatenate([R[c]["yp"] for c in range(8)], axis=0)
    y_sample = np.stack([np.concatenate([R[b * 4 + q]["ys"] for q in range(4)], axis=0) for b in range(2)], axis=0)
    ngk = np.concatenate([R[c]["ngk"] for c in range(8)], axis=0).reshape(32, DEPTH, 256, 2, 64)
    ngv = np.concatenate([R[c]["ngv"] for c in range(8)], axis=0).reshape(32, DEPTH, 256, 2, 64)
    ndk = np.concatenate([R[c]["ndk"] for c in range(8)], axis=0).reshape(32, DEPTH, 256, 4, 2, 32)
    ndv = np.concatenate([R[c]["ndv"] for c in range(8)], axis=0).reshape(32, DEPTH, 256, 4, 64)
    return (y_prompt.astype(np.float32), y_sample.astype(np.float32), ngk.astype(np.float32),
            ngv.astype(np.float32), ndk.astype(np.float32), ndv.astype(np.float32))
```

```python
import math
import numpy as np
import concourse.bass as bass
import concourse.mybir as mybir
from concourse.bass_utils import run_bass_kernel_spmd

F32 = mybir.dt.float32
F32R = mybir.dt.float32r
BF16 = mybir.dt.bfloat16
AF = mybir.ActivationFunctionType
ALU = mybir.AluOpType
AX = mybir.AxisListType

D = 1024
DEPTH = 4
T = 512
NU = 3
EPS = 1e-6
GROWS = 1280
RELU2_DVE = False


class Prog:
    def __init__(self, nc):
        self.nc = nc
        self.ops = {e: [] for e in ("pe", "act", "dve", "pool", "sp")}
        self.cnt = {e: 0 for e in self.ops}
        self.esem = {e: nc.alloc_semaphore("sem_" + e) for e in ("pe", "act", "dve", "pool")}
        self.slot_sem = {}
        self.slot_cnt = {}
        self.lastw = {}
        self.readers = {}
        self.waited = {e: {} for e in self.ops}
        self.final = []

    def _deps(self, eng, R, W, pe_acc=False):
        deps = {}

        def add(ev):
            if ev is None:
                return
            k, v, src, _sem = ev
            if pe_acc and src == "pe" and eng == "pe":
                return
            if k not in deps or deps[k][0] < v:
                deps[k] = (v, ev[3])

        for t in list(R) + list(W):
            add(self.lastw.get(t))
        for t in W:
            for ev in self.readers.get(t, []):
                add(ev)
        out = []
        for k, (v, sem) in deps.items():
            if self.waited[eng].get(k, 0) >= v:
                continue
            self.waited[eng][k] = v
            out.append((sem, v))
        return out

    def _commit(self, ev, R, W):
        for t in W:
            self.lastw[t] = ev
            self.readers[t] = []
        for t in R:
            self.readers.setdefault(t, []).append(ev)

    def op(self, eng, fn, R=(), W=()):
        if isinstance(fn, tuple):
            _name, _kw = fn
            fn = (lambda e, _name=_name, _kw=_kw: getattr(e, _name)(**_kw))
        waits = self._deps(eng, R, W, pe_acc=True)
        self.cnt[eng] += 1
        sem = self.esem[eng]
        ev = ("e_" + eng, self.cnt[eng], eng, sem)
        self.ops[eng].append((waits, fn, sem, 1))
        self._commit(ev, R, W)

    def dma(self, q, out, in_, R, W, slot, **kw):
        waits = self._deps(q, R, W)
        if slot not in self.slot_sem:
            self.slot_sem[slot] = self.nc.alloc_semaphore("ds_" + slot)
            self.slot_cnt[slot] = 0
        self.slot_cnt[slot] += 16
        sem = self.slot_sem[slot]
        ev = ("s_" + slot, self.slot_cnt[slot], "dma", sem)
        self.ops[q].append((waits, lambda e: e.dma_start(out=out, in_=in_, **kw), sem, 16))
        self._commit(ev, R, W)

    def coll(self, fn, R, W, slot):
        waits = self._deps("pool", R, W)
        if slot not in self.slot_sem:
            self.slot_sem[slot] = self.nc.alloc_semaphore("ds_" + slot)
            self.slot_cnt[slot] = 0
        self.slot_cnt[slot] += 1
        sem = self.slot_sem[slot]
        ev = ("s_" + slot, self.slot_cnt[slot], "dma", sem)
        self.ops["pool"].append((waits, fn, sem, None))
        self._commit(ev, R, W)

    def emit(self):
        nc = self.nc
        final_waits = [(self.slot_sem[s], self.slot_cnt[s]) for s in self.slot_sem]
        with nc.Block() as block:
            def run(eng_name):
                def body(e):
                    for waits, fn, sem, inc in self.ops[eng_name]:
                        for (s, v) in waits:
                            e.wait_ge(s, v)
                        ins = fn(e)
                        if inc is None:
                            ins.then_inc(sem)
                        else:
                            ins.then_inc(sem, inc)
                    if eng_name == "sp":
                        for (s, v) in final_waits:
                            e.wait_ge(s, v)
                        for en in ("pe", "act", "dve", "pool"):
                            if self.cnt[en]:
                                e.wait_ge(self.esem[en], self.cnt[en])
                return body
            block.sync(run("sp"))
            block.tensor(run("pe"))
            block.scalar(run("act"))
            block.vector(run("dve"))
            block.gpsimd(run("pool"))


def build_nc(depth=DEPTH, units=(0, 1, 2), dbg=False, stop_after=None):
    nc = bass.Bass("TRN2", target_bir_lowering=False)
    P = Prog(nc)

    def din(name, shape, dt=F32):
        return nc.dram_tensor(name, list(shape), dt, kind="ExternalInput").ap()

    def dout(name, shape):
        return nc.dram_tensor(name, list(shape), F32, kind="ExternalOutput").ap()

    xp = din("xp", [4, 256, D])
    xs = din("xs", [512, D])
    cm = din("cm", [D])
    cctx = din("c_ctx", [D])
    cgk = din("cgk", [DEPTH, 512, 128])
    cgv = din("cgv", [DEPTH, 512, 128])
    cdk = din("cdk", [DEPTH, 512, 256])
    cdv = din("cdv", [DEPTH, 512, 256])
    w_ada = din("w_ada", [DEPTH, D, 6 * D])
    b_ada = din("b_ada", [DEPTH, 6 * D])
    norm1_g = din("norm1_g", [DEPTH, D])
    norm2_g = din("norm2_g", [DEPTH, D])
    w_in = din("w_in", [DEPTH, D, 2048])
    w_fourier = din("w_fourier", [DEPTH, 256, 256])
    q_norm_g = din("q_norm_g", [DEPTH, 64])
    k_norm_g = din("k_norm_g", [DEPTH, 64])
    lam_in = [din(n, [DEPTH, 32]) for n in ("lambda_q1", "lambda_k1", "lambda_q2", "lambda_k2")]
    diff_norm_g = din("diff_norm_g", [DEPTH, 64])
    sgu_norm_g = din("sgu_norm_g", [DEPTH, 256])
    w_spatial = din("w_spatial", [DEPTH, 4, 128, 128])
    b_spatial = din("b_spatial", [DEPTH, 4, 128])
    w_gate = din("w_gate", [DEPTH, D, 4 * D])
    w_branch = din("w_branch", [DEPTH, 4, 256, D])
    w_out = din("w_out", [DEPTH, D, D])
    w_mlp1 = din("w_mlp1", [DEPTH, D, 4 * D])
    w_mlp2 = din("w_mlp2", [DEPTH, 4 * D, D])
    final_g = din("final_norm_g", [D])
    k_ident = din("k_ident", [128, 128])
    k_ones = din("k_ones", [128, 128])
    k_blk64 = din("k_blk64", [128, 128])
    k_ccss = din("k_ccss", [128, 256])
    k_rotg = din("k_rotg", [128, 128])
    k_rotd = din("k_rotd", [128, 128])
    k_mask = din("k_mask", [128, 6])
    k_rope = din("k_rope", [4, 128, 512])
    k_dft256 = din("k_dft256", [2, 256, 256])
    k_dftc = din("k_dftc", [2048, 512])
    k_dfts = din("k_dfts", [2048, 512])
    yp = dout("yp", [4, 256, D])
    ys = dout("ys", [512, D])
    ngk = dout("ngk", [4, DEPTH, 256, 128])
    ngv = dout("ngv", [4, DEPTH, 256, 128])
    ndk = dout("ndk", [4, DEPTH, 256, 256])
    ndv = dout("ndv", [4, DEPTH, 256, 256])
    PR = {"g": 256, "d": 512, "x": 512}
    bounce = {p: [nc.dram_tensor("bounce_%s%d" % (p, l), [PR[p], 512], F32).ap() for l in range(DEPTH)] for p in PR}
    gath = {p: [nc.dram_tensor("gath_%s%d" % (p, l), [4 * PR[p], 512], F32).ap() for l in range(DEPTH)] for p in PR}

    dbg_out = dout("dbg", [16, 128, 512]) if dbg else None
    dbg_n = [0]
    dbg_names = []

    def dump(name, ap, R):
        if not dbg:
            return
        i = dbg_n[0]
        dbg_n[0] += 1
        dbg_names.append(name)
        ncol = ap.shape[-1] if len(ap.shape) == 2 else None
        P.dma("sp", dbg_out[i, :, 0:ap.shape[1]], ap, R=R, W=[], slot="dbg")
    nc._dbg_names = dbg_names

    def sb(name, shape, dt=F32):
        return nc.alloc_sbuf_tensor(name, list(shape), dt)

    xT = sb("xT", [128, NU, 8, T])
    hT = sb("hT", [128, 8, T], F32R)
    bigR = sb("bigR", [128, 17, T], F32R)
    TT = sb("TT", [128, 4, 896])
    dft256t = sb("dft256t", [128, 2, 2, 256], F32R)
    slabs = sb("slabs", [128, 4, 4, 512], F32R)
    KT = sb("KT", [128, 2560], F32R)
    VA = sb("VA", [128, 20, 2, 65], BF16)
    PT = sb("PT", [128, 2, 512], BF16)
    tokst = KT[:, 0:1536].rearrange("p (i c) -> p i c", c=384)
    tok = sb("tok", [128, 4, 256])
    sq = sb("sq", [128, 2, T], F32R)
    scr = sb("scr", [128, 2, T])
    cstage = scr[:, 1, :].rearrange("p (i c) -> p i c", c=128)
    DvN = sq[:].rearrange("p a t -> p (a t)").rearrange("p (i c) -> p i c", c=256)
    rstd = sb("rstd", [128, T])
    small = sb("small", [128, 64])
    ident = sb("ident", [128, 128])
    ones_r = sb("ones_r", [128, 128], F32R)
    blk64_r = sb("blk64_r", [128, 128], F32R)
    ccss_r = sb("ccss_r", [128, 256], F32R)
    rotg_r = sb("rotg_r", [128, 128], F32R)
    rotd_r = sb("rotd_r", [128, 128], F32R)
    rope = sb("rope", [128, 4, 512])
    dft256 = dft256t[:]
    xcs = sb("xcs", [128, 4, 512], F32R)
    wf_r = sb("wf_r", [128, 2, 256], F32R)
    wsp = sb("wsp", [128, 4, 128])
    wspT = sb("wspT", [128, 4, 128], F32R)
    bsT = sb("bsT", [128, 4])
    condT = sb("condT", [128, 8, 2])
    scond = sb("scond", [128, 8, 2], F32R)
    modT = sb("modT", [128, 48, 2])
    badaT = sb("badaT", [128, 48])
    A1 = sb("A1", [128, 8, 2])
    A2 = sb("A2", [128, 8, 2])
    n1g = sb("n1g", [128, 8])
    n2g = sb("n2g", [128, 8])
    fng = sb("fng", [128, 8])
    qg = sb("qg", [128, 1])
    kg = sb("kg", [128, 1])
    lamv = sb("lamv", [128, 4, 32])
    neglam = sb("neglam", [128, 1])
    dng = sb("dng", [128, 64])
    sgg = sb("sgg", [128, 256])

    NPS = 8
    psb = [nc.alloc_psum_tensor("ps%d" % i, [128, 512], F32) for i in range(NPS)]
    rot_state = [0]

    def rot():
        i = 4 + rot_state[0] % 4
        rot_state[0] += 1
        return i

    def mm(ps_i, out_ap, lhsT, rhs, start, stop, R, **kw):
        P.op("pe", ("matmul", dict(out=out_ap, lhsT=lhsT, rhs=rhs, start=start, stop=stop, **kw)),
             R=R, W=["ps%d" % ps_i])

    def tr(ps_i, out_ap, in_ap, R):
        P.op("pe", ("transpose", dict(out=out_ap, in_=in_ap, identity=ident[:])), R=list(R) + ["ident"], W=["ps%d" % ps_i])

    def act(out, in_, func, R, W, **kw):
        P.op("act", ("activation", dict(out=out, in_=in_, func=func, **kw)), R=R, W=W)

    def BT(c):
        return "B%d" % c

    def XT(i):
        return ["xcs%da" % i, "xcs%db" % i]

    VAP = ["VAp%d" % i for i in range(8)]

    bigr = bigR[:]
    big = bigR[:].bitcast(F32)
    outst = tok[:].rearrange("p a t -> p (a t)").rearrange("p (b d) -> p b d", d=1024)
    OST = {0: ["tok"], 1: ["tok"]}

    wq = []
    wstate = {"issued": 0, "used": 0}
    PREF = 3

    coll_done = set()
    wneed = {}

    def w_issue():
        i = wstate["issued"]
        if i >= len(wq):
            return False
        need = wneed.get(i)
        if need is not None and need not in coll_done:
            return False
        slot = i % 4
        for (dst_fn, src) in wq[i]:
            P.dma("pool", dst_fn(slot), src, R=(["gath_x"] if need is not None else []), W=["slab%d" % slot], slot="slab%d" % slot)
        wstate["issued"] += 1
        return True

    def w_group(n):
        i = wstate["used"]
        while wstate["issued"] <= min(i + 3, len(wq) - 1):
            if not w_issue():
                break
        assert wstate["issued"] >= i + n, "weight slab not issued"
        wstate["used"] += n
        return [(i + k) % 4 for k in range(n)]

    def w_prefetch():
        i = wstate["used"]
        while wstate["issued"] <= min(i + 3, len(wq) - 1):
            if not w_issue():
                break

    def slab_std(src2d, nk=4):
        ncols = src2d.shape[1]
        return [(lambda s, nk=nk, ncols=ncols: slabs[:, s, 0:nk, 0:ncols],
                 src2d.rearrange("(k p) c -> p k c", p=128))]

    def plan_layer_unit(l, sample):
        for g in ((1, 0, 2, 3) if sample else (0, 1, 2, 3)):
            for kh in range(2):
                rows = slice(kh * 512, kh * 512 + 512)
                if g == 0:
                    ent = [(lambda s: slabs[:, s, :, 0:256],
                            w_in[l, rows, 0:256].rearrange("(k p) c -> p k c", p=128))]
                    for b in range(2):
                        for a in range(2):
                            ent.append((lambda s, b=b, a=a: slabs[:, s, :, 256 + 128 * b + 64 * a:320 + 128 * b + 64 * a],
                                        w_in[l, rows, 256 + 128 * a + 64 * b:320 + 128 * a + 64 * b].rearrange("(k p) d -> p k d", p=128)))
                    wq.append(ent)
                else:
                    wq.append(slab_std(w_in[l, rows, g * 512:(g + 1) * 512]))
        if sample:
            for s4 in range(4):
                wneed[len(wq)] = l
                wq.append([(lambda s: slabs[:, s, :, :],
                            gath["x"][l][s4 * 512:(s4 + 1) * 512, :].rearrange("(k p) c -> p k c", p=128))])
                wq.append(slab_std(k_dftc[s4 * 512:(s4 + 1) * 512, :]))
                wq.append(slab_std(k_dfts[s4 * 512:(s4 + 1) * 512, :]))
        for n in range(4):
            for cg in range(2):
                cols = slice(n * 1024 + cg * 512, n * 1024 + cg * 512 + 512)
                for kh in range(2):
                    wq.append(slab_std(w_gate[l, kh * 512:(kh + 1) * 512, cols]))
                wq.append(slab_std(w_branch[l, n, :, cg * 512:(cg + 1) * 512], nk=2))
        for cg in range(2):
            for kh in range(2):
                wq.append(slab_std(w_out[l, kh * 512:(kh + 1) * 512, cg * 512:(cg + 1) * 512]))
        for half in range(2):
            for g in range(4):
                for kh in range(2):
                    wq.append(slab_std(w_mlp1[l, kh * 512:(kh + 1) * 512, half * 2048 + g * 512: half * 2048 + (g + 1) * 512]))
            for cg in range(2):
                for kq in range(4):
                    r0 = half * 2048 + kq * 512
                    wq.append(slab_std(w_mlp2[l, r0:r0 + 512, cg * 512:(cg + 1) * 512]))

    def plan_ada(l):
        for g in range(12):
            for kh in range(2):
                wq.append(slab_std(w_ada[l, kh * 512:(kh + 1) * 512, g * 512:(g + 1) * 512]))

    for l in range(depth):
        plan_ada(l)
        for u in units:
            plan_layer_unit(l, u == 2)

    def ld(dst, src, W, slot, q="sp", **kw):
        P.dma(q, dst, src, R=[], W=W, slot=slot, **kw)

    ld(ident[:], k_ident, ["ident"], "c0")
    ld(ones_r[:], k_ones, ["ones"], "c1", q="pool")
    ld(blk64_r[:], k_blk64, ["blk64"], "c2", q="pool")
    ld(ccss_r[:], k_ccss, ["ccss"], "c3", q="pool")
    ld(rotg_r[:], k_rotg, ["rotg"], "c4", q="pool")
    ld(rotd_r[:], k_rotd, ["rotd"], "c5", q="pool")
    ld(rope[:], k_rope.rearrange("a p t -> p a t"), ["rope"], "c6")
    for a_ in range(2):
        ld(dft256[:, a_], k_dft256[a_].rearrange("(j p) t -> p j t", p=128), ["dft256"], "c7", q="pool")
    ld(fng[:], final_g.rearrange("(c p) -> p c", p=128), ["fng"], "c8", allow_slow_non_contiguous=True)
    ld(condT[:, :, 0], cctx.rearrange("(c p) -> p c", p=128), ["condT"], "c9", allow_slow_non_contiguous=True)
    ld(condT[:, :, 1], cm.rearrange("(c p) -> p c", p=128), ["condT"], "c9", allow_slow_non_contiguous=True)
    P.op("pool", lambda e: e.memset(VA[:, :, :, 64:65], 1.0), W=["VAones"])
    def x_load():
        for u in range(NU):
            for i in range(4):
                if u < 2:
                    src = xp[2 * u + i // 2, (i % 2) * 128:(i % 2) * 128 + 128, :]
                else:
                    src = xs[i * 128:(i + 1) * 128, :]
                b = 0
                ld(outst[:, b, :], src, OST[b], "ost%d" % b)
                for c in range(8):
                    pi = rot()
                    tr(pi, psb[pi][:, 0:128], outst[:, b, c * 128:(c + 1) * 128], R=OST[b])
                    eng = "dve" if c % 2 == 0 else "act"
                    dst = xT[:, u, c, i * 128:(i + 1) * 128]
                    if eng == "dve":
                        P.op("dve", ("tensor_copy", dict(out=dst, in_=psb[pi][:, 0:128])),
                             R=["ps%d" % pi], W=["x%d_%d" % (u, c)])
                    else:
                        act(dst, psb[pi][:, 0:128], AF.Copy, R=["ps%d" % pi], W=["x%d_%d" % (u, c)])
    act(scond[:], condT[:], AF.Silu, R=["condT"], W=["scond"])

    def norm_mod(u, ci, Amat, shift_c0, gtag):
        pi = rot()
        for c in range(8):
            b = c % 2
            act(sq[:, b, :], xT[:, u, c, :], AF.Square, R=["x%d_%d" % (u, c)], W=["sq%d" % b])
            mm(pi, psb[pi][:], ones_r[:], sq[:, b, :], c == 0, c == 7, R=["ones", "sq%d" % b])
        act(rstd[:], psb[pi][:], AF.Sqrt, R=["ps%d" % pi, "eps"], W=["rstd"], scale=1.0 / D, bias=eps_ap)
        P.op("dve", ("reciprocal", dict(out=rstd[:], in_=rstd[:])), R=["rstd"], W=["rstd"])
        for c in range(8):
            b = c % 2
            P.op("dve", ("scalar_tensor_tensor", dict(
                out=scr[:, b, :], in0=xT[:, u, c, :], scalar=Amat[:, c, ci:ci + 1], in1=rstd[:],
                op0=ALU.mult, op1=ALU.mult)), R=["x%d_%d" % (u, c), gtag, "rstd"], W=["scr%d" % b])
            act(hT[:, c, :], scr[:, b, :], AF.Identity, R=["scr%d" % b, "mod"], W=["h%d" % c],
                bias=modT[:, shift_c0 + c, ci:ci + 1])

    eps_t = sb("eps_t", [128, 1])
    maskc = sb("maskc", [128, 6])
    ld(maskc[:], k_mask, ["maskc"], "c10")
    P.op("pool", lambda e: e.memset(eps_t[:], EPS), W=["eps"])
    eps_ap = eps_t[:]

    def rms_rows(ps_or_ap, R_in, nparts_tag):
        pass

    def final_out(u):
        pi = rot()
        for c in range(8):
            b = c % 2
            act(sq[:, b, :], xT[:, u, c, :], AF.Square, R=["x%d_%d" % (u, c)], W=["sq%d" % b])
            mm(pi, psb[pi][:], ones_r[:], sq[:, b, :], c == 0, c == 7, R=["ones", "sq%d" % b])
        act(rstd[:], psb[pi][:], AF.Sqrt, R=["ps%d" % pi, "eps"], W=["rstd"], scale=1.0 / D, bias=eps_ap)
        P.op("dve", ("reciprocal", dict(out=rstd[:], in_=rstd[:])), R=["rstd"], W=["rstd"])
        scrv = scr[:].rearrange("p a t -> p (a t)").rearrange("p (c q) -> p c q", q=128)
        for i in range(4):
            for c in range(8):
                P.op("dve", ("scalar_tensor_tensor", dict(
                    out=scrv[:, c, :], in0=xT[:, u, c, i * 128:(i + 1) * 128], scalar=fng[:, c:c + 1], in1=rstd[:, i * 128:(i + 1) * 128],
                    op0=ALU.mult, op1=ALU.mult)), R=["x%d_%d" % (u, c), "fng", "rstd"], W=["scr0", "scr1"])
            for c in range(8):
                pi = rot()
                tr(pi, psb[pi][:, 0:128], scrv[:, c, :], R=["scr0", "scr1"])
                if c % 2 == 0:
                    P.op("dve", ("tensor_copy", dict(out=outst[:, 0, c * 128:(c + 1) * 128], in_=psb[pi][:, 0:128])),
                         R=["ps%d" % pi], W=["tok"])
                else:
                    act(outst[:, 0, c * 128:(c + 1) * 128], psb[pi][:, 0:128], AF.Copy, R=["ps%d" % pi], W=["tok"])
            if u < 2:
                dst = yp[2 * u + i // 2, (i % 2) * 128:(i % 2) * 128 + 128, :]
            else:
                dst = ys[i * 128:(i + 1) * 128, :]
            P.dma("sp", dst, outst[:, 0, :], R=["tok"], W=[], slot="oy")


    for l in range(depth):
        lam_init = 0.8 - 0.6 * math.exp(-0.3 * l)
        lam_scale = 1.0 - lam_init
        ld(badaT[:], b_ada[l].rearrange("(c p) -> p c", p=128), ["badaT"], "l0", allow_slow_non_contiguous=True)
        ld(n1g[:], norm1_g[l].rearrange("(c p) -> p c", p=128), ["n1g"], "l1", allow_slow_non_contiguous=True)
        ld(n2g[:], norm2_g[l].rearrange("(c p) -> p c", p=128), ["n2g"], "l2", allow_slow_non_contiguous=True)
        for hh in range(2):
            ld(qg[hh * 64:(hh + 1) * 64, :], q_norm_g[l].rearrange("(p o) -> p o", o=1), ["qg"], "l3", allow_slow_non_contiguous=True)
            ld(kg[hh * 64:(hh + 1) * 64, :], k_norm_g[l].rearrange("(p o) -> p o", o=1), ["kg"], "l4", allow_slow_non_contiguous=True)
        for j in range(4):
            ld(lamv[:, j, :], lam_in[j][l].partition_broadcast(128), ["lamv"], "l5")
        ld(dng[:], diff_norm_g[l].partition_broadcast(128), ["dng"], "l6")
        ld(sgg[:], sgu_norm_g[l].partition_broadcast(128), ["sgg"], "l7")
        ld(wsp[:], w_spatial[l].rearrange("g q p -> q g p"), ["wsp"], "l8")
        ld(bsT[:], b_spatial[l].rearrange("g q -> q g"), ["bsT"], "l9", allow_slow_non_contiguous=True)
        ld(wf_r[:], w_fourier[l].rearrange("(k p) c -> p k c", p=128), ["wf"], "l10", q="pool")
        P.op("dve", ("tensor_tensor", dict(out=small[:, 0:32], in0=lamv[:, 0, :], in1=lamv[:, 1, :], op=ALU.mult)), R=["lamv"], W=["small"])
        P.op("dve", ("tensor_tensor", dict(out=small[:, 32:64], in0=lamv[:, 2, :], in1=lamv[:, 3, :], op=ALU.mult)), R=["lamv", "small"], W=["small"])
        P.op("dve", ("tensor_reduce", dict(out=neglam[:], in_=small[:, 0:32], axis=AX.X, op=ALU.add)), R=["small"], W=["neglam"])
        P.op("dve", ("tensor_reduce", dict(out=small[:, 0:1], in_=small[:, 32:64], axis=AX.X, op=ALU.add)), R=["small", "neglam"], W=["small"])
        act(neglam[:], neglam[:], AF.Exp, R=["neglam"], W=["neglam"])
        act(small[:, 1:2], small[:, 0:1], AF.Exp, R=["small"], W=["small"])
        P.op("dve", ("tensor_tensor", dict(out=neglam[:], in0=small[:, 1:2], in1=neglam[:], op=ALU.subtract)), R=["small", "neglam"], W=["neglam"])
        P.op("dve", ("tensor_scalar_add", dict(out=neglam[:], in0=neglam[:], scalar1=-lam_init)), R=["neglam"], W=["neglam"])
        P.op("dve", ("tensor_scalar_mul", dict(out=dng[:], in0=dng[:], scalar1=lam_scale)), R=["dng"], W=["dng"])
        for g in range(4):
            pi = rot()
            tr(pi, psb[pi][:, 0:128], wsp[:, g, :], R=["wsp"])
            P.op("dve", ("tensor_copy", dict(out=wspT[:, g, :], in_=psb[pi][:, 0:128])), R=["ps%d" % pi], W=["wspT"])
        for g in range(12):
            s0, s1 = w_group(2)
            pi = rot()
            for kc in range(8):
                sl_ = s0 if kc < 4 else s1
                mm(pi, psb[pi][0:2, :], scond[:, kc, :], slabs[:, sl_, kc % 4, :], kc == 0, kc == 7,
                   R=["slab%d" % sl_, "scond"])
            P.op("dve", ("tensor_copy", dict(out=rstd[0:2, :], in_=psb[pi][0:2, :])), R=["ps%d" % pi], W=["rstd"])
            for j in range(4):
                pj = rot()
                P.op("pe", ("transpose", dict(out=psb[pj][:, 0:2], in_=rstd[0:2, j * 128:(j + 1) * 128], identity=ident[0:2, 0:2])),
                     R=["rstd", "ident"], W=["ps%d" % pj])
                cc = g * 4 + j
                P.op("dve", ("tensor_scalar", dict(
                    out=modT[:, cc, :], in0=psb[pj][:, 0:2], scalar1=badaT[:, cc:cc + 1], scalar2=None, op0=ALU.add)),
                    R=["ps%d" % pj, "badaT"], W=["mod"])
        if l == 0:
            x_load()
        for ci in range(2):
            P.op("dve", ("scalar_tensor_tensor", dict(
                out=A1[:, :, ci], in0=modT[:, 8:16, ci], scalar=1.0, in1=n1g[:], op0=ALU.add, op1=ALU.mult)),
                R=["mod", "n1g"], W=["A1"])
            P.op("dve", ("scalar_tensor_tensor", dict(
                out=A2[:, :, ci], in0=modT[:, 32:40, ci], scalar=1.0, in1=n2g[:], op0=ALU.add, op1=ALU.mult)),
                R=["mod", "n2g"], W=["A2"])

        for u in units:
            sample = (u == 2)
            ci = 1 if sample else 0
            nseq = 1 if sample else 2
            sl = T // nseq
            norm_mod(u, ci, A1, 0, "A1")
            hR = ["h%d" % c for c in range(8)]
            TTv = TT[:]
            TT_tags = ["TT"]
            def w_in_groups(gs):
                for g in gs:
                    s0, s1 = w_group(2)

                    def wsl(kc, c0, c1):
                        s = s0 if kc < 4 else s1
                        return slabs[:, s, kc % 4, c0:c1], "slab%d" % s

                    def ftype(c0, dst_chunk, evac):
                        pi = rot()
                        for kc in range(8):
                            w_ap, wt = wsl(kc, c0, c0 + 128)
                            mm(pi, psb[pi][:], w_ap, hT[:, kc, :], kc == 0, kc == 7, R=[wt, "h%d" % kc])
                        evac(pi, dst_chunk)

                    def ttype(c0, ncols, toff, post=None):
                        for i in range(4):
                            pi = rot()
                            for kc in range(8):
                                w_ap, wt = wsl(kc, c0, c0 + ncols)
                                mm(pi, psb[pi][:, 0:ncols], hT[:, kc, i * 128:(i + 1) * 128], w_ap, kc == 0, kc == 7,
                                   R=[wt, "h%d" % kc])
                            dst = TTv[:, i, toff:toff + ncols]
                            if post is None:
                                P.op("dve", ("tensor_copy", dict(out=dst, in_=psb[pi][:, 0:ncols])),
                                     R=["ps%d" % pi], W=TT_tags)
                            else:
                                act(dst, psb[pi][:, 0:ncols], post, R=["ps%d" % pi], W=TT_tags)

                    def ev_copy_r(pi, ch):
                        act(bigr[:, ch, :], psb[pi][:], AF.Copy, R=["ps%d" % pi], W=[BT(ch)])

                    def ev_copy_f(pi, ch):
                        P.op("dve", ("tensor_copy", dict(out=bigr[:, ch, :], in_=psb[pi][:])), R=["ps%d" % pi], W=[BT(ch)])

                    if g == 0:
                        ftype(0, 0, ev_copy_r)
                        ftype(128, 1, ev_copy_r)
                        ftype(256, 2, ev_copy_f)
                        ftype(384, 3, ev_copy_f)
                    elif g == 1:
                        ftype(0, 4, ev_copy_f)
                        ttype(128, 128, 0)
                        ftype(256, 5, ev_copy_f if sample else ev_copy_r)
                        ftype(384, 6, ev_copy_f if sample else ev_copy_r)
                    elif g == 2:
                        ftype(0, 7, ev_copy_f if sample else ev_copy_r)
                        ftype(128, 8, ev_copy_f if sample else ev_copy_r)
                        ttype(256, 256, 128)
                    else:
                        ttype(0, 256, 384, post=AF.Gelu_apprx_tanh)
                        ttype(256, 256, 640, post=AF.Gelu_apprx_tanh)

            def qk_norm(chs):
                for ch in chs:
                    gvec, gt = (kg, "kg") if ch == 4 else (qg, "qg")
                    act(sq[:, 0, :], big[:, ch, :], AF.Square, R=[BT(ch)], W=["sq0"])
                    pi = rot()
                    mm(pi, psb[pi][:], blk64_r[:], sq[:, 0, :], True, True, R=["blk64", "sq0"])
                    act(rstd[:], psb[pi][:], AF.Sqrt, R=["ps%d" % pi, "eps"], W=["rstd"], scale=1.0 / 64, bias=eps_ap)
                    P.op("dve", ("reciprocal", dict(out=rstd[:], in_=rstd[:])), R=["rstd"], W=["rstd"])
                    P.op("dve", ("scalar_tensor_tensor", dict(
                        out=bigr[:, ch, :], in0=big[:, ch, :], scalar=gvec[:, 0:1], in1=rstd[:], op0=ALU.mult, op1=ALU.mult)),
                        R=[BT(ch), gt, "rstd"], W=[BT(ch)])

            def do_rope(chs):
                for ch in chs:
                    isg = ch <= 4
                    rm = rotg_r if isg else rotd_r
                    rt = "rotg" if isg else "rotd"
                    cosi, sini = (0, 1) if isg else (2, 3)
                    act(sq[:, 1, :], big[:, ch, :], AF.Copy, R=[BT(ch)], W=["sq1"])
                    pi = rot()
                    mm(pi, psb[pi][:], rm[:], sq[:, 1, :], True, True, R=[rt, "sq1"])
                    P.op("dve", ("tensor_tensor", dict(out=scr[:, 0, :], in0=psb[pi][:], in1=rope[:, sini, :], op=ALU.mult)),
                         R=["ps%d" % pi, "rope"], W=["scr0"])
                    P.op("pool", ("tensor_tensor", dict(out=scr[:, 1, :], in0=big[:, ch, :], in1=rope[:, cosi, :], op=ALU.mult)),
                         R=[BT(ch), "rope"], W=["scr1"])
                    P.op("dve", ("tensor_tensor", dict(out=bigr[:, ch, :], in0=scr[:, 0, :], in1=scr[:, 1, :], op=ALU.add)),
                         R=["scr0", "scr1"], W=[BT(ch)])

            def emit_kv():
                if True:
                    for i in range(4):
                        for (ch, off) in ((4, 0), (7, 128), (8, 256)):
                            pi = rot()
                            tr(pi, psb[pi][:, 0:128], big[:, ch, i * 128:(i + 1) * 128], R=[BT(ch)])
                            P.op("dve", ("tensor_copy", dict(out=tokst[:, i, off:off + 128], in_=psb[pi][:, 0:128])),
                                 R=["ps%d" % pi], W=["KT"])
                    sq0 = 2 * u
                    for s_ in range(2):
                        P.dma("sp", ngk[sq0 + s_, l].rearrange("(j p) f -> p j f", p=128),
                              tokst[:, 2 * s_:2 * s_ + 2, 0:128].bitcast(F32), R=["KT"], W=[], slot="o_ngk%d" % s_)
                        P.dma("sp", ndk[sq0 + s_, l].rearrange("(j p) f -> p j f", p=128),
                              tokst[:, 2 * s_:2 * s_ + 2, 128:384].bitcast(F32), R=["KT"], W=[], slot="o_ndk%d" % s_)
                        P.dma("sp", ngv[sq0 + s_, l].rearrange("(j p) f -> p j f", p=128),
                              TTv[:, 2 * s_:2 * s_ + 2, 0:128], R=TT_tags, W=[], slot="o_ngv%d" % s_)
                        P.dma("sp", ndv[sq0 + s_, l].rearrange("(j p) f -> p j f", p=128),
                              TTv[:, 2 * s_:2 * s_ + 2, 128:384], R=TT_tags, W=[], slot="o_ndv%d" % s_)

            def dv_rms():
                for i in range(4):
                    dv_ap = TTv[:, i, 640:896]
                    act(scr[:, 0, 0:256], dv_ap, AF.Square, R=TT_tags, W=["scr0"], accum_out=small[:, 8 + i:9 + i])
                P.op("act", ("activation", dict(out=small[:, 12:16], in_=small[:, 8:12], func=AF.Sqrt, scale=1.0 / 256, bias=eps_ap)),
                     R=["scr0", "eps"], W=["small2"])
                P.op("dve", ("reciprocal", dict(out=small[:, 12:16], in_=small[:, 12:16])), R=["small2"], W=["small2"])
                for i in range(4):
                    dv_ap = TTv[:, i, 640:896]
                    P.op("dve", ("scalar_tensor_tensor", dict(
                        out=DvN[:, i, :], in0=dv_ap, scalar=small[:, 12 + i:13 + i], in1=sgg[:], op0=ALU.mult, op1=ALU.mult)),
                        R=TT_tags + ["small2", "sgg"], W=["sq0", "sq1"])


            def chan_dft():
                for i in range(4):
                    pi = rot()
                    for j in range(2):
                        mm(pi, psb[pi][:, j * 256:(j + 1) * 256], bigr[:, j, i * 128:(i + 1) * 128], ccss_r[:], True, True,
                           R=[BT(j), "ccss"])
                    act(xcs[:, i, :], psb[pi][:], AF.Copy, R=["ps%d" % pi], W=XT(i))

            def fourier_prompt():
                for s in range(2):
                    for j in range(2):
                        pi = rot()
                        n = 0
                        for i2 in range(2):
                            i = s * 2 + i2
                            for part in range(2):
                                mm(pi, psb[pi][:, 0:256], xcs[:, i, j * 256 + part * 128: j * 256 + part * 128 + 128],
                                   dft256[:, part, i2, :], n == 0, n == 3, R=XT(i) + ["dft256"])
                                n += 1
                        act(bigr[:, 15 + j, s * 256:(s + 1) * 256], psb[pi][:, 0:256], AF.Copy, R=["ps%d" % pi], W=[BT(15 + j)])

            def exchange_part(p_):
                bz = bounce[p_][l]
                if p_ == "g":
                    P.dma("sp", bz[0:128, :], big[:, 4, :], R=[BT(4)], W=["bounce_g"], slot="bzg")
                    P.dma("sp", bz[128:256, :].rearrange("r (a c) -> (r a) c", c=128).rearrange("(j p) c -> p j c", p=128),
                          TTv[:, :, 0:128], R=TT_tags, W=["bounce_g"], slot="bzg")
                else:
                    P.dma("sp", bz[0:256, :].rearrange("(c p) t -> p c t", p=128), big[:, 7:9, :], R=[BT(7), BT(8)], W=["bounce_d"], slot="bzd")
                    P.dma("sp", bz[256:512, :].rearrange("r (a c) -> (r a) c", c=256).rearrange("(j p) c -> p j c", p=128),
                          TTv[:, :, 128:384], R=TT_tags, W=["bounce_d"], slot="bzd")
                w_prefetch()
                P.coll(lambda e, l=l, p_=p_: e.collective_compute(
                    "AllGather", ALU.bypass, replica_groups=[[0, 1, 2, 3], [4, 5, 6, 7]],
                    ins=[bounce[p_][l].opt()], outs=[gath[p_][l].opt()]), R=["bounce_" + p_], W=["gath_" + p_], slot="cc_" + p_)

            def exchange_x():
                bx_ = bounce["x"][l]
                P.dma("sp", bx_.rearrange("(j p) c -> p j c", p=128), xcs[:].bitcast(F32), R=XT(0) + XT(1) + XT(2) + XT(3), W=["bounce_x"], slot="bzx")
                w_prefetch()
                P.coll(lambda e, l=l: e.collective_compute(
                    "AllGather", ALU.bypass, replica_groups=[[0, 1, 2, 3], [4, 5, 6, 7]],
                    ins=[bounce["x"][l].opt()], outs=[gath["x"][l].opt()]), R=["bounce_x"], W=["gath_x"], slot="cc_x")

            def fourier_sample():
                acc = [rot(), rot()]
                n = 0
                for s4 in range(4):
                    sx, sc, ss = w_group(3)
                    for i in range(4):
                        for j in range(2):
                            mm(acc[j], psb[acc[j]][:], slabs[:, sx, i, j * 256:j * 256 + 128], slabs[:, sc, i, :],
                               n == 0, False, R=["slab%d" % sx, "slab%d" % sc])
                            mm(acc[j], psb[acc[j]][:], slabs[:, sx, i, j * 256 + 128:j * 256 + 256], slabs[:, ss, i, :],
                               False, n == 15, R=["slab%d" % sx, "slab%d" % ss])
                        n += 1
                for j in range(2):
                    act(xcs[:, j, :], psb[acc[j]][:], AF.Copy, R=["ps%d" % acc[j]], W=XT(j))

            def out_a(ysrc, ytags):
                for c in range(2):
                    pi = rot()
                    for kc in range(2):
                        mm(pi, psb[pi][:], wf_r[:, kc, c * 128:(c + 1) * 128], ysrc(kc), kc == 0, kc == 1, R=["wf"] + (ytags[kc] if isinstance(ytags[kc], list) else [ytags[kc]]))
                    act(bigr[:, 9 + c, :], psb[pi][:], AF.Copy, R=["ps%d" % pi], W=[BT(9 + c)])

            def load_keys_sample(kind, hc):
                if kind == "g":
                    gp, gtag, grows = gath["g"][l], "gath_g", 256
                    kcache, kcols, krow0 = cgk[l], slice(0, 128), 0
                    vcache, vcols, vrow0, vw = cgv[l], slice(0, 128), 128, 128
                else:
                    gp, gtag, grows = gath["d"][l], "gath_d", 512
                    kcache, kcols, krow0 = cdk[l], slice(hc * 128, hc * 128 + 128), hc * 128
                    vcache, vcols, vrow0, vw = cdv[l], slice(hc * 128, hc * 128 + 128), 256, 256
                gl = gp.rearrange("(r x) c -> x r c", x=grows)
                w_prefetch()
                P.dma("sp", cstage[:, :, :], kcache[:, kcols].rearrange("(j p) c -> p j c", p=128), R=[], W=["scr1"], slot="cst")
                for j in range(4):
                    pi = rot()
                    tr(pi, psb[pi][:, 0:128], cstage[:, j, :], R=["scr1"])
                    act(KT[:, j * 128:(j + 1) * 128], psb[pi][:, 0:128], AF.Copy, R=["ps%d" % pi], W=["KT"])
                P.dma("pool", KT[:, 512:2560].rearrange("p (r t) -> p r t", r=4), gl[krow0:krow0 + 128, :, :],
                      R=[gtag], W=["KT"], slot="ktl")
                for hh_ in range(2):
                    vc = slice(vcols.start + hh_ * 64, vcols.start + hh_ * 64 + 64)
                    P.dma("pool", VA[:, 0:4, hh_, 0:64], vcache[:, vc].rearrange("(j p) d -> p j d", p=128),
                          R=[], W=["VA"] + VAP, slot="val")
                    for r in range(4):
                        src = gp[r * grows + vrow0: r * grows + vrow0 + vw, :].rearrange("x (a c) -> (x a) c", c=vw)
                        src = src[:, vc].rearrange("(j p) d -> p j d", p=128)
                        P.dma("pool", VA[:, 4 + 4 * r: 8 + 4 * r, hh_, 0:64], src, R=[gtag], W=["VA"] + VAP, slot="val")

            def attention(gqa_preloaded=False, parts=("g", "d")):
                nk = 2560 if sample else 256
                nkt = nk // 128
                nq = sl
                nqt = nq // 128

                gctr = [0]

                def attn_group(heads, s):
                    nh = len(heads)
                    g_ = gctr[0]
                    gctr[0] += 1
                    if sample:
                        qoff, vb, vtag = 0, 0, "VA"
                        qtag = lambda hi: XT(hi)
                    else:
                        qoff, vb, vtag = (g_ % 2) * 256, (g_ % 8) * 2, VAP[g_ % 8]
                        qtag = lambda hi: [XT(hi)[g_ % 2]]
                    for hi, hd in enumerate(heads):
                        act(xcs[:, hi, qoff:qoff + nq], big[:, hd["qch"], s * sl: s * sl + nq], AF.Copy,
                            R=[BT(hd["qch"]), "maskc"], W=qtag(hi), scale=maskc[:, hd["mcol"]:hd["mcol"] + 1])
                    kpb = 2 if (nkt == 2 and nq * 2 <= 512) else 1
                    steps = [(hi, kt) for hi in range(nh) for kt in range(0, nkt, kpb)]
                    pending = []
                    sbank = {}

                    def issue_s(i):
                        hi, kt = steps[i]
                        hd = heads[hi]
                        pi = rot()
                        for kk in range(kpb):
                            mm(pi, psb[pi][:, kk * nq:(kk + 1) * nq], hd["kfn"](kt + kk), xcs[:, hi, qoff:qoff + nq], True, True,
                               R=[hd["ktag"]] + qtag(hi), skip_group_check=True)
                        sbank[i] = pi

                    issue_s(0)
                    for i in range(len(steps)):
                        if i + 1 < len(steps):
                            issue_s(i + 1)
                        hi, kt = steps[i]
                        hd = heads[hi]
                        pi = sbank[i]
                        b = i % 2
                        act(PT[:, b, 0:kpb * nq], psb[pi][:, 0:kpb * nq], AF.Exp, R=["ps%d" % pi], W=["PT%d" % b], scale=hd["scale"])
                        ob = hd["obank"]
                        for kk in range(kpb):
                            for qt in range(nqt):
                                ktt = kt + kk
                                mm(ob, psb[ob][:, qt * 65:(qt + 1) * 65], PT[:, b, kk * nq + qt * 128: kk * nq + (qt + 1) * 128],
                                   VA[:, vb + ktt, hd["vslot"], :], ktt == 0 and qt == 0, ktt == nkt - 1 and qt == nqt - 1,
                                   R=["PT%d" % b, vtag, "VAones"], skip_group_check=True)
                        for pd in [p for p in pending if p[0] <= i]:
                            pending.remove(pd)
                            pd[1]()
                        if kt + kpb - 1 == nkt - 1 and hd["post"] is not None:
                            p1, p2 = hd["post"]
                            p1()
                            if p2 is not None:
                                pending.append((i + 3, p2))
                    for pd in pending:
                        pd[1]()

                def gqa_post(h, s, ob):
                    def p1():
                        ov = psb[ob][:, 0:nqt * 65].rearrange("p (q c) -> p q c", c=65)
                        c0 = 16 + 4 * (h % 2)
                        P.op("dve", ("reciprocal", dict(out=small[:, c0:c0 + nqt], in_=ov[:, :, 64])), R=["ps%d" % ob], W=["small3"])
                        P.op("dve", ("tensor_tensor", dict(
                            out=tok[:, s * nqt:(s + 1) * nqt, h * 64:(h + 1) * 64], in0=ov[:, :, 0:64],
                            in1=small[:, c0:c0 + nqt].unsqueeze(2).broadcast_to([128, nqt, 64]), op=ALU.mult)),
                            R=["ps%d" % ob, "small3"], W=["tok"])
                    return p1, None

                def diff_post(h, s, ob1, ob2):
                    par = h % 2
                    A = scr[:, par, 0:256].rearrange("p (q c) -> p q c", c=64)[:, 0:nqt, :]
                    Bm = scr[:, par, 256:512].rearrange("p (q c) -> p q c", c=64)[:, 0:nqt, :]
                    stag = "scr%d" % par
                    c1, c2, c3 = 16 + 12 * par, 20 + 12 * par, 24 + 12 * par
                    mtag = "smallp%d" % par

                    def p1():
                        o1 = psb[ob1][:, 0:nqt * 65].rearrange("p (q c) -> p q c", c=65)
                        o2 = psb[ob2][:, 0:nqt * 65].rearrange("p (q c) -> p q c", c=65)
                        P.op("dve", ("reciprocal", dict(out=small[:, c1:c1 + nqt], in_=o1[:, :, 64])), R=["ps%d" % ob1], W=[mtag])
                        P.op("dve", ("reciprocal", dict(out=small[:, c2:c2 + nqt], in_=o2[:, :, 64])), R=["ps%d" % ob2, mtag], W=[mtag])
                        P.op("dve", ("tensor_scalar", dict(out=small[:, c2:c2 + nqt], in0=small[:, c2:c2 + nqt], scalar1=neglam[:, 0:1],
                                                              scalar2=None, op0=ALU.mult)), R=[mtag, "neglam"], W=[mtag])
                        P.op("dve", ("tensor_tensor", dict(out=A, in0=o1[:, :, 0:64],
                                                              in1=small[:, c1:c1 + nqt].unsqueeze(2).broadcast_to([128, nqt, 64]), op=ALU.mult)),
                             R=["ps%d" % ob1, mtag], W=[stag])
                        P.op("dve", ("tensor_tensor", dict(out=Bm, in0=o2[:, :, 0:64],
                                                              in1=small[:, c2:c2 + nqt].unsqueeze(2).broadcast_to([128, nqt, 64]), op=ALU.mult)),
                             R=["ps%d" % ob2, mtag, stag], W=[stag])
                        P.op("dve", ("tensor_tensor", dict(out=A, in0=A, in1=Bm, op=ALU.add)), R=[stag], W=[stag])
                        P.op("dve", ("tensor_tensor", dict(out=Bm, in0=A, in1=A, op=ALU.mult)), R=[stag], W=[stag])
                        P.op("dve", ("tensor_reduce", dict(out=small[:, c3:c3 + nqt], in_=Bm, axis=AX.X, op=ALU.add)), R=[stag, mtag], W=[mtag])

                    def p2():
                        P.op("act", ("activation", dict(out=small[:, c3:c3 + nqt], in_=small[:, c3:c3 + nqt], func=AF.Ln, scale=1.0 / 64, bias=eps_ap)),
                             R=[mtag, "eps"], W=[mtag])
                        P.op("act", ("activation", dict(out=small[:, c3:c3 + nqt], in_=small[:, c3:c3 + nqt], func=AF.Exp, scale=-0.5)),
                             R=[mtag], W=[mtag])
                        P.op("dve", ("tensor_tensor", dict(out=A, in0=A,
                                                              in1=small[:, c3:c3 + nqt].unsqueeze(2).broadcast_to([128, nqt, 64]), op=ALU.mult)),
                             R=[stag, mtag], W=[stag])
                        P.op("dve", ("tensor_tensor", dict(out=tok[:, s * nqt:(s + 1) * nqt, h * 64:(h + 1) * 64], in0=A,
                                                              in1=dng[:].unsqueeze(1).broadcast_to([128, nqt, 64]), op=ALU.mult)),
                             R=[stag, "dng"], W=["tok"])
                    return p1, p2

                if "g" in parts:
                    for s in range(nseq):
                        if sample:
                            if not gqa_preloaded:
                                load_keys_sample("g", 0)
                            kfn = lambda kt: KT[:, kt * 128:(kt + 1) * 128]
                            ktag = "KT"
                        else:
                            for i2 in range(2):
                                i = s * 2 + i2
                                P.op("dve", ("tensor_copy", dict(
                                    out=VA[:, (gctr[0] % 8) * 2 + i2, :, 0:64], in_=TTv[:, i, 0:128].rearrange("p (h d) -> p h d", d=64))),
                                    R=TT_tags, W=[VAP[gctr[0] % 8], "VA"])
                            kfn = lambda kt, s=s: bigr[:, 4, s * 256 + kt * 128: s * 256 + (kt + 1) * 128]
                            ktag = BT(4)
                        heads = []
                        for h in range(4):
                            heads.append(dict(qch=2 + (h % 2), mcol=h // 2, vslot=h // 2, scale=0.125, obank=h % 2,
                                              kfn=kfn, ktag=ktag, post=gqa_post(h, s, h % 2)))
                        attn_group(heads, s)
                    for i in range(4):
                        for c in range(2):
                            pi = rot()
                            tr(pi, psb[pi][:, 0:128], tok[:, i, c * 128:(c + 1) * 128], R=["tok"])
                            act(bigr[:, 11 + c, i * 128:(i + 1) * 128], psb[pi][:, 0:128], AF.Copy, R=["ps%d" % pi], W=[BT(11 + c)])

                if "d" in parts:
                    dscale = 1.0 / math.sqrt(32.0)
                    for s in range(nseq):
                        for hc in range(2):
                            if sample:
                                load_keys_sample("d", hc)
                                kfn = lambda kt: KT[:, kt * 128:(kt + 1) * 128]
                                ktag = "KT"
                            else:
                                for i2 in range(2):
                                    i = s * 2 + i2
                                    P.op("dve", ("tensor_copy", dict(
                                        out=VA[:, (gctr[0] % 8) * 2 + i2, :, 0:64],
                                        in_=TTv[:, i, 128 + hc * 128: 256 + hc * 128].rearrange("p (h d) -> p h d", d=64))),
                                        R=TT_tags, W=[VAP[gctr[0] % 8], "VA"])
                                kfn = lambda kt, s=s, hc=hc: bigr[:, 7 + hc, s * 256 + kt * 128: s * 256 + (kt + 1) * 128]
                                ktag = BT(7 + hc)
                            heads = []
                            for hh in range(2):
                                h = hc * 2 + hh
                                for m in range(2):
                                    heads.append(dict(qch=5 + hc, mcol=2 + hh * 2 + m, vslot=hh, scale=dscale, obank=2 * hh + m,
                                                      kfn=kfn, ktag=ktag,
                                                      post=(diff_post(h, s, 2 * hh, 2 * hh + 1) if m == 1 else None)))
                            attn_group(heads, s)
                    for i in range(4):
                        for c in range(2):
                            pi = rot()
                            tr(pi, psb[pi][:, 0:128], tok[:, i, c * 128:(c + 1) * 128], R=["tok"])
                            act(bigr[:, 13 + c, i * 128:(i + 1) * 128], psb[pi][:, 0:128], AF.Copy, R=["ps%d" % pi], W=[BT(13 + c)])


            def d_branch():
                for i in range(4):
                    pi = rot()
                    for g in range(4):
                        mm(pi, psb[pi][:, g * 64:(g + 1) * 64], wspT[:, g, :], DvN[:, i, g * 64:(g + 1) * 64],
                           True, True, R=["wspT", "sq0", "sq1"])
                    for g in range(4):
                        P.op("dve", ("scalar_tensor_tensor", dict(
                            out=tok[:, i, g * 64:(g + 1) * 64], in0=psb[pi][:, g * 64:(g + 1) * 64], scalar=bsT[:, g:g + 1],
                            in1=TTv[:, i, 384 + g * 64: 448 + g * 64], op0=ALU.add, op1=ALU.mult)),
                            R=["ps%d" % pi, "bsT"] + TT_tags, W=["tok"])
                for i in range(4):
                    for c in range(2):
                        pi = rot()
                        tr(pi, psb[pi][:, 0:128], tok[:, i, c * 128:(c + 1) * 128], R=["tok"])
                        act(bigr[:, 15 + c, i * 128:(i + 1) * 128], psb[pi][:, 0:128], AF.Copy, R=["ps%d" % pi], W=[BT(15 + c)])


            if sample:
                w_in_groups((1, 0))
                qk_norm((4, 2, 3))
                do_rope((4, 2, 3))
                exchange_part("g")
                w_in_groups((2,))
                do_rope((7, 8, 5, 6))
                load_keys_sample("g", 0)
                exchange_part("d")
                if stop_after == "coll":
                    break
                w_in_groups((3,))
                chan_dft()
                exchange_x()
                dv_rms()
                d_branch()
                attention(gqa_preloaded=True, parts=("g",))
                attention(parts=("d",))
                coll_done.add(l)
                fourier_sample()
                out_a(lambda kc: xcs[:, kc, :], [XT(0), XT(1)])
            else:
                w_in_groups((0, 1, 2, 3))
                qk_norm((2, 3, 4))
                emit_kv()
                dv_rms()
                chan_dft()
                fourier_prompt()
                out_a(lambda kc: bigr[:, 15 + kc, :], [BT(15), BT(16)])
                attention()
                d_branch()

            for cdbg in range(8):
                dump("br%d" % cdbg, big[:, 9 + cdbg, :], [BT(9 + cdbg)])
            for n in range(4):
                for cg in range(2):
                    s0, s1, sbr = w_group(3)
                    for j in range(4):
                        for kc in range(4):
                            mm(j, psb[j][:], slabs[:, s0, kc, j * 128:(j + 1) * 128], hT[:, kc, :], kc == 0, False,
                               R=["slab%d" % s0, "h%d" % kc])
                    for j in range(4):
                        c = cg * 4 + j
                        for kc in range(4, 8):
                            mm(j, psb[j][:], slabs[:, s1, kc - 4, j * 128:(j + 1) * 128], hT[:, kc, :], False, kc == 7,
                               R=["slab%d" % s1, "h%d" % kc])
                        pb = rot()
                        for kc in range(2):
                            mm(pb, psb[pb][:], slabs[:, sbr, kc, j * 128:(j + 1) * 128], bigr[:, 9 + 2 * n + kc, :], kc == 0, kc == 1,
                               R=["slab%d" % sbr, BT(9 + 2 * n + kc)])
                        b = (n * 8 + c) % 2
                        act(scr[:, b, :], psb[j][:], AF.Sigmoid, R=["ps%d" % j], W=["scr%d" % b])
                        if n == 0:
                            P.op("dve", ("tensor_tensor", dict(out=bigr[:, c, :], in0=scr[:, b, :], in1=psb[pb][:], op=ALU.mult)),
                                 R=["scr%d" % b, "ps%d" % pb], W=[BT(c)])
                        else:
                            P.op("dve", ("tensor_tensor", dict(out=scr[:, b, :], in0=scr[:, b, :], in1=psb[pb][:], op=ALU.mult)),
                                 R=["scr%d" % b, "ps%d" % pb], W=["scr%d" % b])
                            P.op("dve", ("tensor_tensor", dict(out=bigr[:, c, :], in0=big[:, c, :], in1=scr[:, b, :], op=ALU.add)),
                                 R=["scr%d" % b, BT(c)], W=[BT(c)])
            dump("merged0", big[:, 0, :], [BT(0)])
            dump("merged7", big[:, 7, :], [BT(7)])
            for cg in range(2):
                s0, s1 = w_group(2)
                for j in range(4):
                    for kc in range(4):
                        mm(4 + j, psb[4 + j][:], slabs[:, s0, kc, j * 128:(j + 1) * 128], bigr[:, kc, :], kc == 0, False,
                           R=["slab%d" % s0, BT(kc)])
                for j in range(4):
                    c = cg * 4 + j
                    pi = 4 + j
                    for kc in range(4, 8):
                        mm(pi, psb[pi][:], slabs[:, s1, kc - 4, j * 128:(j + 1) * 128], bigr[:, kc, :], False, kc == 7,
                           R=["slab%d" % s1, BT(kc)])
                    P.op("dve", ("scalar_tensor_tensor", dict(
                        out=xT[:, u, c, :], in0=psb[pi][:], scalar=modT[:, 16 + c, ci:ci + 1], in1=xT[:, u, c, :],
                        op0=ALU.mult, op1=ALU.add)), R=["ps%d" % pi, "mod", "x%d_%d" % (u, c)], W=["x%d_%d" % (u, c)])
            dump("x1_c0", xT[:, u, 0, :], ["x%d_0" % u])
            norm_mod(u, ci, A2, 24, "A2")
            for half in range(2):
                for g in range(4):
                    s0, s1 = w_group(2)
                    for j in range(4):
                        for kc in range(4):
                            mm(4 + j, psb[4 + j][:], slabs[:, s0, kc, j * 128:(j + 1) * 128], hT[:, kc, :], kc == 0, False,
                               R=["slab%d" % s0, "h%d" % kc])
                    for j in range(4):
                        fc = g * 4 + j
                        pi = 4 + j
                        for kc in range(4, 8):
                            mm(pi, psb[pi][:], slabs[:, s1, kc - 4, j * 128:(j + 1) * 128], hT[:, kc, :], False, kc == 7,
                               R=["slab%d" % s1, "h%d" % kc])
                        if RELU2_DVE:
                            P.op("dve", ("scalar_tensor_tensor", dict(
                                out=bigr[:, fc, :], in0=psb[pi][:], scalar=0.0, in1=psb[pi][:], op0=ALU.max, op1=ALU.mult)),
                                R=["ps%d" % pi], W=[BT(fc)])
                        else:
                            b = fc % 2
                            act(scr[:, b, :], psb[pi][:], AF.Relu, R=["ps%d" % pi], W=["scr%d" % b])
                            P.op("dve", ("tensor_tensor", dict(out=bigr[:, fc, :], in0=scr[:, b, :], in1=scr[:, b, :], op=ALU.mult)),
                                 R=["scr%d" % b], W=[BT(fc)])
                for cg in range(2):
                    for kq in range(4):
                        s, = w_group(1)
                        for j in range(4):
                            for kc in range(4):
                                mm(j, psb[j][:], slabs[:, s, kc, j * 128:(j + 1) * 128], bigr[:, kq * 4 + kc, :],
                                   kq == 0 and kc == 0, kq == 3 and kc == 3, R=["slab%d" % s, BT(kq * 4 + kc)])
                    for j in range(4):
                        c = cg * 4 + j
                        P.op("dve", ("scalar_tensor_tensor", dict(
                            out=xT[:, u, c, :], in0=psb[j][:], scalar=modT[:, 40 + c, ci:ci + 1], in1=xT[:, u, c, :],
                            op0=ALU.mult, op1=ALU.add)), R=["ps%d" % j, "mod", "x%d_%d" % (u, c)], W=["x%d_%d" % (u, c)])
            if l == depth - 1 and stop_after is None:
                final_out(u)

    assert stop_after is not None or wstate["used"] == len(wq), (wstate, len(wq))
    P.emit()
    return nc


def _consts(q):
    c = {}
    c["k_ident"] = np.eye(128, dtype=np.float32)
    c["k_ones"] = np.ones((128, 128), np.float32)
    blk = np.zeros((128, 128), np.float32)
    blk[:64, :64] = 1
    blk[64:, 64:] = 1
    c["k_blk64"] = blk
    n = np.arange(64)
    ang = 2 * np.pi * np.outer(n, n) / 64
    C64 = np.cos(ang) / 8.0
    S64 = np.sin(ang) / 8.0
    cc = np.zeros((128, 256))
    cc[:64, 0:64] = C64
    cc[64:, 64:128] = C64
    cc[:64, 128:192] = S64
    cc[64:, 192:256] = S64
    c["k_ccss"] = cc.astype(np.float32)

    def rotm(hd):
        half = hd // 2
        R = np.zeros((128, 128), np.float32)
        for m in range(128):
            if m % hd < half:
                R[m + half, m] = -1.0
            else:
                R[m - half, m] = 1.0
        return R
    mk = np.zeros((128, 6), np.float32)
    mk[0:64, 0] = 1
    mk[64:128, 1] = 1
    for j in range(4):
        mk[32 * j:32 * j + 32, 2 + j] = 1
    c["k_mask"] = mk
    c["k_rotg"] = rotm(64)
    c["k_rotd"] = rotm(32)
    pos = q * 512 + np.arange(512)
    row = (pos // 64).astype(np.float64)
    col = (pos % 64).astype(np.float64)

    def tab(dim):
        quarter = dim // 4
        inv = 10000.0 ** (-np.arange(quarter, dtype=np.float32) / quarter)
        inv = inv.astype(np.float32)
        ang = np.concatenate([row[:, None].astype(np.float32) * inv, col[:, None].astype(np.float32) * inv], axis=-1)
        ang = ang.astype(np.float32)
        cos = np.cos(ang).astype(np.float32)
        sin = np.sin(ang).astype(np.float32)
        half = dim // 2
        idx = np.arange(128) % half
        return cos[:, idx].T.copy(), sin[:, idx].T.copy()
    cg, sg = tab(64)
    cd, sd = tab(32)
    c["k_rope"] = np.stack([cg, sg, cd, sd]).astype(np.float32)
    t = np.arange(256)
    a = 2 * np.pi * np.outer(t, t) / 256
    c["k_dft256"] = np.stack([np.cos(a) / 16.0, -np.sin(a) / 16.0]).astype(np.float32)
    tt = np.arange(2048, dtype=np.float64)
    a = 2 * np.pi * np.outer(tt, pos.astype(np.float64)) / 2048
    c["k_dftc"] = (np.cos(a) / math.sqrt(2048.0)).astype(np.float32)
    c["k_dfts"] = (-np.sin(a) / math.sqrt(2048.0)).astype(np.float32)
    return c


_NC_CACHE = {}
import os
import json
_DBG = {k: (tuple(v) if isinstance(v, list) else v) for k, v in json.loads(os.environ.get("KDBG", "{}")).items()}
_DBG_RUN = {}


def kernel(**inputs):
    inp = {k: np.ascontiguousarray(np.asarray(v)) for k, v in inputs.items()}
    if "nc" not in _NC_CACHE:
        _NC_CACHE["nc"] = build_nc(**_DBG)
    nc = _NC_CACHE["nc"]
    shared = ["c_ctx", "w_ada", "b_ada", "norm1_g", "norm2_g", "w_in", "w_fourier", "q_norm_g", "k_norm_g",
              "lambda_q1", "lambda_k1", "lambda_q2", "lambda_k2", "diff_norm_g", "sgu_norm_g", "w_spatial",
              "b_spatial", "w_gate", "w_branch", "w_out", "w_mlp1", "w_mlp2", "final_norm_g"]
    in_maps = []
    for core in range(8):
        b, q = core // 4, core % 4
        m = {k: inp[k] for k in shared}
        m["xp"] = inp["x_prompt"][4 * core:4 * core + 4]
        m["xs"] = inp["x_sample"][b, q * 512:(q + 1) * 512]
        m["cm"] = inp["c"][b]
        m["cgk"] = inp["cache_gqa_k"][b].reshape(DEPTH, 512, 128)
        m["cgv"] = inp["cache_gqa_v"][b].reshape(DEPTH, 512, 128)
        m["cdk"] = inp["cache_diff_k"][b].reshape(DEPTH, 512, 256)
        m["cdv"] = inp["cache_diff_v"][b].reshape(DEPTH, 512, 256)
        m.update(_consts(q))
        in_maps.append({k: np.ascontiguousarray(v) for k, v in m.items()})
    ncores = _DBG_RUN.get("cores", 8)
    res = run_bass_kernel_spmd(nc, in_maps[:ncores], core_ids=list(range(ncores)))
    R = list(res.results)
    _DBG_RUN["results"] = R
    _DBG_RUN["names"] = getattr(nc, "_dbg_names", [])
    while len(R) < 8:
        R.append(R[0])
    y_prompt = np.concatenate([R[c]["yp"] for c in range(8)], axis=0)
    y_sample = np.stack([np.concatenate([R[b * 4 + q]["ys"] for q in range(4)], axis=0) for b in range(2)], axis=0)
    ngk = np.concatenate([R[c]["ngk"] for c in range(8)], axis=0).reshape(32, DEPTH, 256, 2, 64)
    ngv = np.concatenate([R[c]["ngv"] for c in range(8)], axis=0).reshape(32, DEPTH, 256, 2, 64)
    ndk = np.concatenate([R[c]["ndk"] for c in range(8)], axis=0).reshape(32, DEPTH, 256, 4, 2, 32)
    ndv = np.concatenate([R[c]["ndv"] for c in range(8)], axis=0).reshape(32, DEPTH, 256, 4, 64)
    return (y_prompt.astype(np.float32), y_sample.astype(np.float32), ngk.astype(np.float32),
            ngv.astype(np.float32), ndk.astype(np.float32), ndv.astype(np.float32))
```

```python
import math
import numpy as np
import concourse.bass as bass
import concourse.mybir as mybir
from concourse.bass_utils import run_bass_kernel_spmd

F32 = mybir.dt.float32
F32R = mybir.dt.float32r
BF16 = mybir.dt.bfloat16
AF = mybir.ActivationFunctionType
ALU = mybir.AluOpType
AX = mybir.AxisListType

D = 1024
DEPTH = 4
T = 512
NU = 3
EPS = 1e-6
GROWS = 1280
RELU2_DVE = False


class Prog:
    def __init__(self, nc):
        self.nc = nc
        self.ops = {e: [] for e in ("pe", "act", "dve", "pool", "sp")}
        self.cnt = {e: 0 for e in self.ops}
        self.esem = {e: nc.alloc_semaphore("sem_" + e) for e in ("pe", "act", "dve", "pool")}
        self.slot_sem = {}
        self.slot_cnt = {}
        self.lastw = {}
        self.readers = {}
        self.waited = {e: {} for e in self.ops}
        self.final = []

    def _deps(self, eng, R, W, pe_acc=False):
        deps = {}

        def add(ev):
            if ev is None:
                return
            k, v, src, _sem = ev
            if pe_acc and src == "pe" and eng == "pe":
                return
            if k not in deps or deps[k][0] < v:
                deps[k] = (v, ev[3])

        for t in list(R) + list(W):
            add(self.lastw.get(t))
        for t in W:
            for ev in self.readers.get(t, []):
                add(ev)
        out = []
        for k, (v, sem) in deps.items():
            if self.waited[eng].get(k, 0) >= v:
                continue
            self.waited[eng][k] = v
            out.append((sem, v))
        return out

    def _commit(self, ev, R, W):
        for t in W:
            self.lastw[t] = ev
            self.readers[t] = []
        for t in R:
            self.readers.setdefault(t, []).append(ev)

    def op(self, eng, fn, R=(), W=()):
        if isinstance(fn, tuple):
            _name, _kw = fn
            fn = (lambda e, _name=_name, _kw=_kw: getattr(e, _name)(**_kw))
        waits = self._deps(eng, R, W, pe_acc=True)
        self.cnt[eng] += 1
        sem = self.esem[eng]
        ev = ("e_" + eng, self.cnt[eng], eng, sem)
        self.ops[eng].append((waits, fn, sem, 1))
        self._commit(ev, R, W)

    def dma(self, q, out, in_, R, W, slot, **kw):
        waits = self._deps(q, R, W)
        if slot not in self.slot_sem:
            self.slot_sem[slot] = self.nc.alloc_semaphore("ds_" + slot)
            self.slot_cnt[slot] = 0
        self.slot_cnt[slot] += 16
        sem = self.slot_sem[slot]
        ev = ("s_" + slot, self.slot_cnt[slot], "dma", sem)
        self.ops[q].append((waits, lambda e: e.dma_start(out=out, in_=in_, **kw), sem, 16))
        self._commit(ev, R, W)

    def coll(self, fn, R, W, slot):
        waits = self._deps("pool", R, W)
        if slot not in self.slot_sem:
            self.slot_sem[slot] = self.nc.alloc_semaphore("ds_" + slot)
            self.slot_cnt[slot] = 0
        self.slot_cnt[slot] += 1
        sem = self.slot_sem[slot]
        ev = ("s_" + slot, self.slot_cnt[slot], "dma", sem)
        self.ops["pool"].append((waits, fn, sem, None))
        self._commit(ev, R, W)

    def emit(self):
        nc = self.nc
        final_waits = [(self.slot_sem[s], self.slot_cnt[s]) for s in self.slot_sem]
        with nc.Block() as block:
            def run(eng_name):
                def body(e):
                    for waits, fn, sem, inc in self.ops[eng_name]:
                        for (s, v) in waits:
                            e.wait_ge(s, v)
                        ins = fn(e)
                        if inc is None:
                            ins.then_inc(sem)
                        else:
                            ins.then_inc(sem, inc)
                    if eng_name == "sp":
                        for (s, v) in final_waits:
                            e.wait_ge(s, v)
                        for en in ("pe", "act", "dve", "pool"):
                            if self.cnt[en]:
                                e.wait_ge(self.esem[en], self.cnt[en])
                return body
            block.sync(run("sp"))
            block.tensor(run("pe"))
            block.scalar(run("act"))
            block.vector(run("dve"))
            block.gpsimd(run("pool"))


def build_nc(depth=DEPTH, units=(0, 1, 2), dbg=False, stop_after=None):
    nc = bass.Bass("TRN2", target_bir_lowering=False)
    P = Prog(nc)

    def din(name, shape, dt=F32):
        return nc.dram_tensor(name, list(shape), dt, kind="ExternalInput").ap()

    def dout(name, shape):
        return nc.dram_tensor(name, list(shape), F32, kind="ExternalOutput").ap()

    xp = din("xp", [4, 256, D])
    xs = din("xs", [512, D])
    cm = din("cm", [D])
    cctx = din("c_ctx", [D])
    cgk = din("cgk", [DEPTH, 512, 128])
    cgv = din("cgv", [DEPTH, 512, 128])
    cdk = din("cdk", [DEPTH, 512, 256])
    cdv = din("cdv", [DEPTH, 512, 256])
    w_ada = din("w_ada", [DEPTH, D, 6 * D])
    b_ada = din("b_ada", [DEPTH, 6 * D])
    norm1_g = din("norm1_g", [DEPTH, D])
    norm2_g = din("norm2_g", [DEPTH, D])
    w_in = din("w_in", [DEPTH, D, 2048])
    w_fourier = din("w_fourier", [DEPTH, 256, 256])
    q_norm_g = din("q_norm_g", [DEPTH, 64])
    k_norm_g = din("k_norm_g", [DEPTH, 64])
    lam_in = [din(n, [DEPTH, 32]) for n in ("lambda_q1", "lambda_k1", "lambda_q2", "lambda_k2")]
    diff_norm_g = din("diff_norm_g", [DEPTH, 64])
    sgu_norm_g = din("sgu_norm_g", [DEPTH, 256])
    w_spatial = din("w_spatial", [DEPTH, 4, 128, 128])
    b_spatial = din("b_spatial", [DEPTH, 4, 128])
    w_gate = din("w_gate", [DEPTH, D, 4 * D])
    w_branch = din("w_branch", [DEPTH, 4, 256, D])
    w_out = din("w_out", [DEPTH, D, D])
    w_mlp1 = din("w_mlp1", [DEPTH, D, 4 * D])
    w_mlp2 = din("w_mlp2", [DEPTH, 4 * D, D])
    final_g = din("final_norm_g", [D])
    k_ident = din("k_ident", [128, 128])
    k_ones = din("k_ones", [128, 128])
    k_blk64 = din("k_blk64", [128, 128])
    k_ccss = din("k_ccss", [128, 256])
    k_rotg = din("k_rotg", [128, 128])
    k_rotd = din("k_rotd", [128, 128])
    k_mask = din("k_mask", [128, 6])
    k_rope = din("k_rope", [4, 128, 512])
    k_dft256 = din("k_dft256", [2, 256, 256])
    k_dftc = din("k_dftc", [2048, 512])
    k_dfts = din("k_dfts", [2048, 512])
    yp = dout("yp", [4, 256, D])
    ys = dout("ys", [512, D])
    ngk = dout("ngk", [4, DEPTH, 256, 128])
    ngv = dout("ngv", [4, DEPTH, 256, 128])
    ndk = dout("ndk", [4, DEPTH, 256, 256])
    ndv = dout("ndv", [4, DEPTH, 256, 256])
    PR = {"g": 256, "d": 512, "x": 512}
    bounce = {p: [nc.dram_tensor("bounce_%s%d" % (p, l), [PR[p], 512], F32).ap() for l in range(DEPTH)] for p in PR}
    gath = {p: [nc.dram_tensor("gath_%s%d" % (p, l), [4 * PR[p], 512], F32).ap() for l in range(DEPTH)] for p in PR}

    dbg_out = dout("dbg", [16, 128, 512]) if dbg else None
    dbg_n = [0]
    dbg_names = []

    def dump(name, ap, R):
        if not dbg:
            return
        i = dbg_n[0]
        dbg_n[0] += 1
        dbg_names.append(name)
        ncol = ap.shape[-1] if len(ap.shape) == 2 else None
        P.dma("sp", dbg_out[i, :, 0:ap.shape[1]], ap, R=R, W=[], slot="dbg")
    nc._dbg_names = dbg_names

    def sb(name, shape, dt=F32):
        return nc.alloc_sbuf_tensor(name, list(shape), dt)

    xT = sb("xT", [128, NU, 8, T])
    hT = sb("hT", [128, 8, T], F32R)
    bigR = sb("bigR", [128, 17, T], F32R)
    TT = sb("TT", [128, 4, 896])
    dft256t = sb("dft256t", [128, 2, 2, 256], F32R)
    slabs = sb("slabs", [128, 4, 4, 512], F32R)
    KT = sb("KT", [128, 2560], F32R)
    VA = sb("VA", [128, 20, 2, 65], BF16)
    PT = sb("PT", [128, 2, 512], BF16)
    tokst = KT[:, 0:1536].rearrange("p (i c) -> p i c", c=384)
    tok = sb("tok", [128, 4, 256])
    sq = sb("sq", [128, 2, T], F32R)
    scr = sb("scr", [128, 2, T])
    cstage = scr[:, 1, :].rearrange("p (i c) -> p i c", c=128)
    DvN = sq[:].rearrange("p a t -> p (a t)").rearrange("p (i c) -> p i c", c=256)
    rstd = sb("rstd", [128, T])
    small = sb("small", [128, 64])
    ident = sb("ident", [128, 128])
    ones_r = sb("ones_r", [128, 128], F32R)
    blk64_r = sb("blk64_r", [128, 128], F32R)
    ccss_r = sb("ccss_r", [128, 256], F32R)
    rotg_r = sb("rotg_r", [128, 128], F32R)
    rotd_r = sb("rotd_r", [128, 128], F32R)
    rope = sb("rope", [128, 4, 512])
    dft256 = dft256t[:]
    xcs = sb("xcs", [128, 4, 512], F32R)
    wf_r = sb("wf_r", [128, 2, 256], F32R)
    wsp = sb("wsp", [128, 4, 128])
    wspT = sb("wspT", [128, 4, 128], F32R)
    bsT = sb("bsT", [128, 4])
    condT = sb("condT", [128, 8, 2])
    scond = sb("scond", [128, 8, 2], F32R)
    modT = sb("modT", [128, 48, 2])
    badaT = sb("badaT", [128, 48])
    A1 = sb("A1", [128, 8, 2])
    A2 = sb("A2", [128, 8, 2])
    n1g = sb("n1g", [128, 8])
    n2g = sb("n2g", [128, 8])
    fng = sb("fng", [128, 8])
    qg = sb("qg", [128, 1])
    kg = sb("kg", [128, 1])
    lamv = sb("lamv", [128, 4, 32])
    neglam = sb("neglam", [128, 1])
    dng = sb("dng", [128, 64])
    sgg = sb("sgg", [128, 256])

    NPS = 8
    psb = [nc.alloc_psum_tensor("ps%d" % i, [128, 512], F32) for i in range(NPS)]
    rot_state = [0]

    def rot():
        i = 4 + rot_state[0] % 4
        rot_state[0] += 1
        return i

    def mm(ps_i, out_ap, lhsT, rhs, start, stop, R, **kw):
        P.op("pe", ("matmul", dict(out=out_ap, lhsT=lhsT, rhs=rhs, start=start, stop=stop, **kw)),
             R=R, W=["ps%d" % ps_i])

    def tr(ps_i, out_ap, in_ap, R):
        P.op("pe", ("transpose", dict(out=out_ap, in_=in_ap, identity=ident[:])), R=list(R) + ["ident"], W=["ps%d" % ps_i])

    def act(out, in_, func, R, W, **kw):
        P.op("act", ("activation", dict(out=out, in_=in_, func=func, **kw)), R=R, W=W)

    def BT(c):
        return "B%d" % c

    def XT(i):
        return ["xcs%da" % i, "xcs%db" % i]

    VAP = ["VAp%d" % i for i in range(8)]

    bigr = bigR[:]
    big = bigR[:].bitcast(F32)
    outst = tok[:].rearrange("p a t -> p (a t)").rearrange("p (b d) -> p b d", d=1024)
    OST = {0: ["tok"], 1: ["tok"]}

    wq = []
    wstate = {"issued": 0, "used": 0}
    PREF = 3

    coll_done = set()
    wneed = {}

    def w_issue():
        i = wstate["issued"]
        if i >= len(wq):
            return False
        need = wneed.get(i)
        if need is not None and need not in coll_done:
            return False
        slot = i % 4
        for (dst_fn, src) in wq[i]:
            P.dma("pool", dst_fn(slot), src, R=(["gath_x"] if need is not None else []), W=["slab%d" % slot], slot="slab%d" % slot)
        wstate["issued"] += 1
        return True

    def w_group(n):
        i = wstate["used"]
        while wstate["issued"] <= min(i + 3, len(wq) - 1):
            if not w_issue():
                break
        assert wstate["issued"] >= i + n, "weight slab not issued"
        wstate["used"] += n
        return [(i + k) % 4 for k in range(n)]

    def w_prefetch():
        i = wstate["used"]
        while wstate["issued"] <= min(i + 3, len(wq) - 1):
            if not w_issue():
                break

    def slab_std(src2d, nk=4):
        ncols = src2d.shape[1]
        return [(lambda s, nk=nk, ncols=ncols: slabs[:, s, 0:nk, 0:ncols],
                 src2d.rearrange("(k p) c -> p k c", p=128))]

    def plan_layer_unit(l, sample):
        for g in ((1, 0, 2, 3) if sample else (0, 1, 2, 3)):
            for kh in range(2):
                rows = slice(kh * 512, kh * 512 + 512)
                if g == 0:
                    ent = [(lambda s: slabs[:, s, :, 0:256],
                            w_in[l, rows, 0:256].rearrange("(k p) c -> p k c", p=128))]
                    for b in range(2):
                        for a in range(2):
                            ent.append((lambda s, b=b, a=a: slabs[:, s, :, 256 + 128 * b + 64 * a:320 + 128 * b + 64 * a],
                                        w_in[l, rows, 256 + 128 * a + 64 * b:320 + 128 * a + 64 * b].rearrange("(k p) d -> p k d", p=128)))
                    wq.append(ent)
                else:
                    wq.append(slab_std(w_in[l, rows, g * 512:(g + 1) * 512]))
        if sample:
            for s4 in range(4):
                wneed[len(wq)] = l
                wq.append([(lambda s: slabs[:, s, :, :],
                            gath["x"][l][s4 * 512:(s4 + 1) * 512, :].rearrange("(k p) c -> p k c", p=128))])
                wq.append(slab_std(k_dftc[s4 * 512:(s4 + 1) * 512, :]))
                wq.append(slab_std(k_dfts[s4 * 512:(s4 + 1) * 512, :]))
        for n in range(4):
            for cg in range(2):
                cols = slice(n * 1024 + cg * 512, n * 1024 + cg * 512 + 512)
                for kh in range(2):
                    wq.append(slab_std(w_gate[l, kh * 512:(kh + 1) * 512, cols]))
                wq.append(slab_std(w_branch[l, n, :, cg * 512:(cg + 1) * 512], nk=2))
        for cg in range(2):
            for kh in range(2):
                wq.append(slab_std(w_out[l, kh * 512:(kh + 1) * 512, cg * 512:(cg + 1) * 512]))
        for half in range(2):
            for g in range(4):
                for kh in range(2):
                    wq.append(slab_std(w_mlp1[l, kh * 512:(kh + 1) * 512, half * 2048 + g * 512: half * 2048 + (g + 1) * 512]))
            for cg in range(2):
                for kq in range(4):
                    r0 = half * 2048 + kq * 512
                    wq.append(slab_std(w_mlp2[l, r0:r0 + 512, cg * 512:(cg + 1) * 512]))

    def plan_ada(l):
        for g in range(12):
            for kh in range(2):
                wq.append(slab_std(w_ada[l, kh * 512:(kh + 1) * 512, g * 512:(g + 1) * 512]))

    for l in range(depth):
        plan_ada(l)
        for u in units:
            plan_layer_unit(l, u == 2)

    def ld(dst, src, W, slot, q="sp", **kw):
        P.dma(q, dst, src, R=[], W=W, slot=slot, **kw)

    ld(ident[:], k_ident, ["ident"], "c0")
    ld(ones_r[:], k_ones, ["ones"], "c1", q="pool")
    ld(blk64_r[:], k_blk64, ["blk64"], "c2", q="pool")
    ld(ccss_r[:], k_ccss, ["ccss"], "c3", q="pool")
    ld(rotg_r[:], k_rotg, ["rotg"], "c4", q="pool")
    ld(rotd_r[:], k_rotd, ["rotd"], "c5", q="pool")
    ld(rope[:], k_rope.rearrange("a p t -> p a t"), ["rope"], "c6")
    for a_ in range(2):
        ld(dft256[:, a_], k_dft256[a_].rearrange("(j p) t -> p j t", p=128), ["dft256"], "c7", q="pool")
    ld(fng[:], final_g.rearrange("(c p) -> p c", p=128), ["fng"], "c8", allow_slow_non_contiguous=True)
    ld(condT[:, :, 0], cctx.rearrange("(c p) -> p c", p=128), ["condT"], "c9", allow_slow_non_contiguous=True)
    ld(condT[:, :, 1], cm.rearrange("(c p) -> p c", p=128), ["condT"], "c9", allow_slow_non_contiguous=True)
    P.op("pool", lambda e: e.memset(VA[:, :, :, 64:65], 1.0), W=["VAones"])
    def x_load():
        for u in range(NU):
            for i in range(4):
                if u < 2:
                    src = xp[2 * u + i // 2, (i % 2) * 128:(i % 2) * 128 + 128, :]
                else:
                    src = xs[i * 128:(i + 1) * 128, :]
                b = (u * 4 + i) % 2
                stg = outst[:, 0, :] if b == 0 else scr[:].rearrange("p a t -> p (a t)")
                stags = OST[0] if b == 0 else ["scr0", "scr1"]
                ld(stg, src, stags, "ost%d" % b)
                for c in range(8):
                    pi = rot()
                    tr(pi, psb[pi][:, 0:128], stg[:, c * 128:(c + 1) * 128], R=stags)
                    eng = "dve" if c % 2 == 0 else "act"
                    dst = xT[:, u, c, i * 128:(i + 1) * 128]
                    if eng == "dve":
                        P.op("dve", ("tensor_copy", dict(out=dst, in_=psb[pi][:, 0:128])),
                             R=["ps%d" % pi], W=["x%d_%d" % (u, c)])
                    else:
                        act(dst, psb[pi][:, 0:128], AF.Copy, R=["ps%d" % pi], W=["x%d_%d" % (u, c)])
    act(scond[:], condT[:], AF.Silu, R=["condT"], W=["scond"])

    def norm_mod(u, ci, Amat, shift_c0, gtag):
        pi = rot()
        for c in range(8):
            b = c % 2
            act(sq[:, b, :], xT[:, u, c, :], AF.Square, R=["x%d_%d" % (u, c)], W=["sq%d" % b])
            mm(pi, psb[pi][:], ones_r[:], sq[:, b, :], c == 0, c == 7, R=["ones", "sq%d" % b])
        act(rstd[:], psb[pi][:], AF.Sqrt, R=["ps%d" % pi, "eps"], W=["rstd"], scale=1.0 / D, bias=eps_ap)
        P.op("dve", ("reciprocal", dict(out=rstd[:], in_=rstd[:])), R=["rstd"], W=["rstd"])
        for c in range(8):
            b = c % 2
            P.op("dve", ("scalar_tensor_tensor", dict(
                out=scr[:, b, :], in0=xT[:, u, c, :], scalar=Amat[:, c, ci:ci + 1], in1=rstd[:],
                op0=ALU.mult, op1=ALU.mult)), R=["x%d_%d" % (u, c), gtag, "rstd"], W=["scr%d" % b])
            act(hT[:, c, :], scr[:, b, :], AF.Identity, R=["scr%d" % b, "mod"], W=["h%d" % c],
                bias=modT[:, shift_c0 + c, ci:ci + 1])

    eps_t = sb("eps_t", [128, 1])
    maskc = sb("maskc", [128, 6])
    ld(maskc[:], k_mask, ["maskc"], "c10")
    P.op("pool", lambda e: e.memset(eps_t[:], EPS), W=["eps"])
    eps_ap = eps_t[:]

    def rms_rows(ps_or_ap, R_in, nparts_tag):
        pass

    def final_out(u):
        pi = rot()
        for c in range(8):
            b = c % 2
            act(sq[:, b, :], xT[:, u, c, :], AF.Square, R=["x%d_%d" % (u, c)], W=["sq%d" % b])
            mm(pi, psb[pi][:], ones_r[:], sq[:, b, :], c == 0, c == 7, R=["ones", "sq%d" % b])
        act(rstd[:], psb[pi][:], AF.Sqrt, R=["ps%d" % pi, "eps"], W=["rstd"], scale=1.0 / D, bias=eps_ap)
        P.op("dve", ("reciprocal", dict(out=rstd[:], in_=rstd[:])), R=["rstd"], W=["rstd"])
        scrv = scr[:].rearrange("p a t -> p (a t)").rearrange("p (c q) -> p c q", q=128)
        for i in range(4):
            for c in range(8):
                P.op("dve", ("scalar_tensor_tensor", dict(
                    out=scrv[:, c, :], in0=xT[:, u, c, i * 128:(i + 1) * 128], scalar=fng[:, c:c + 1], in1=rstd[:, i * 128:(i + 1) * 128],
                    op0=ALU.mult, op1=ALU.mult)), R=["x%d_%d" % (u, c), "fng", "rstd"], W=["scr0", "scr1"])
            for c in range(8):
                pi = rot()
                tr(pi, psb[pi][:, 0:128], scrv[:, c, :], R=["scr0", "scr1"])
                if c % 2 == 0:
                    P.op("dve", ("tensor_copy", dict(out=outst[:, 0, c * 128:(c + 1) * 128], in_=psb[pi][:, 0:128])),
                         R=["ps%d" % pi], W=["tok"])
                else:
                    act(outst[:, 0, c * 128:(c + 1) * 128], psb[pi][:, 0:128], AF.Copy, R=["ps%d" % pi], W=["tok"])
            if u < 2:
                dst = yp[2 * u + i // 2, (i % 2) * 128:(i % 2) * 128 + 128, :]
            else:
                dst = ys[i * 128:(i + 1) * 128, :]
            P.dma("sp", dst, outst[:, 0, :], R=["tok"], W=[], slot="oy")


    for l in range(depth):
        lam_init = 0.8 - 0.6 * math.exp(-0.3 * l)
        lam_scale = 1.0 - lam_init
        ld(badaT[:], b_ada[l].rearrange("(c p) -> p c", p=128), ["badaT"], "l0", allow_slow_non_contiguous=True)
        ld(n1g[:], norm1_g[l].rearrange("(c p) -> p c", p=128), ["n1g"], "l1", allow_slow_non_contiguous=True)
        ld(n2g[:], norm2_g[l].rearrange("(c p) -> p c", p=128), ["n2g"], "l2", allow_slow_non_contiguous=True)
        for hh in range(2):
            ld(qg[hh * 64:(hh + 1) * 64, :], q_norm_g[l].rearrange("(p o) -> p o", o=1), ["qg"], "l3", allow_slow_non_contiguous=True)
            ld(kg[hh * 64:(hh + 1) * 64, :], k_norm_g[l].rearrange("(p o) -> p o", o=1), ["kg"], "l4", allow_slow_non_contiguous=True)
        for j in range(4):
            ld(lamv[:, j, :], lam_in[j][l].partition_broadcast(128), ["lamv"], "l5")
        ld(dng[:], diff_norm_g[l].partition_broadcast(128), ["dng"], "l6")
        ld(sgg[:], sgu_norm_g[l].partition_broadcast(128), ["sgg"], "l7")
        ld(wsp[:], w_spatial[l].rearrange("g q p -> q g p"), ["wsp"], "l8")
        ld(bsT[:], b_spatial[l].rearrange("g q -> q g"), ["bsT"], "l9", allow_slow_non_contiguous=True)
        ld(wf_r[:], w_fourier[l].rearrange("(k p) c -> p k c", p=128), ["wf"], "l10", q="pool")
        P.op("dve", ("tensor_tensor", dict(out=small[:, 0:32], in0=lamv[:, 0, :], in1=lamv[:, 1, :], op=ALU.mult)), R=["lamv"], W=["small"])
        P.op("dve", ("tensor_tensor", dict(out=small[:, 32:64], in0=lamv[:, 2, :], in1=lamv[:, 3, :], op=ALU.mult)), R=["lamv", "small"], W=["small"])
        P.op("dve", ("tensor_reduce", dict(out=neglam[:], in_=small[:, 0:32], axis=AX.X, op=ALU.add)), R=["small"], W=["neglam"])
        P.op("dve", ("tensor_reduce", dict(out=small[:, 0:1], in_=small[:, 32:64], axis=AX.X, op=ALU.add)), R=["small", "neglam"], W=["small"])
        act(neglam[:], neglam[:], AF.Exp, R=["neglam"], W=["neglam"])
        act(small[:, 1:2], small[:, 0:1], AF.Exp, R=["small"], W=["small"])
        P.op("dve", ("tensor_tensor", dict(out=neglam[:], in0=small[:, 1:2], in1=neglam[:], op=ALU.subtract)), R=["small", "neglam"], W=["neglam"])
        P.op("dve", ("tensor_scalar_add", dict(out=neglam[:], in0=neglam[:], scalar1=-lam_init)), R=["neglam"], W=["neglam"])
        P.op("dve", ("tensor_scalar_mul", dict(out=dng[:], in0=dng[:], scalar1=lam_scale)), R=["dng"], W=["dng"])
        for g in range(4):
            pi = rot()
            tr(pi, psb[pi][:, 0:128], wsp[:, g, :], R=["wsp"])
            P.op("dve", ("tensor_copy", dict(out=wspT[:, g, :], in_=psb[pi][:, 0:128])), R=["ps%d" % pi], W=["wspT"])
        for g in range(12):
            s0, s1 = w_group(2)
            pi = rot()
            for kc in range(8):
                sl_ = s0 if kc < 4 else s1
                mm(pi, psb[pi][0:2, :], scond[:, kc, :], slabs[:, sl_, kc % 4, :], kc == 0, kc == 7,
                   R=["slab%d" % sl_, "scond"])
            P.op("dve", ("tensor_copy", dict(out=rstd[0:2, :], in_=psb[pi][0:2, :])), R=["ps%d" % pi], W=["rstd"])
            for j in range(4):
                pj = rot()
                P.op("pe", ("transpose", dict(out=psb[pj][:, 0:2], in_=rstd[0:2, j * 128:(j + 1) * 128], identity=ident[0:2, 0:2])),
                     R=["rstd", "ident"], W=["ps%d" % pj])
                cc = g * 4 + j
                P.op("dve", ("tensor_scalar", dict(
                    out=modT[:, cc, :], in0=psb[pj][:, 0:2], scalar1=badaT[:, cc:cc + 1], scalar2=None, op0=ALU.add)),
                    R=["ps%d" % pj, "badaT"], W=["mod"])
        if l == 0:
            x_load()
        for ci in range(2):
            P.op("dve", ("scalar_tensor_tensor", dict(
                out=A1[:, :, ci], in0=modT[:, 8:16, ci], scalar=1.0, in1=n1g[:], op0=ALU.add, op1=ALU.mult)),
                R=["mod", "n1g"], W=["A1"])
            P.op("dve", ("scalar_tensor_tensor", dict(
                out=A2[:, :, ci], in0=modT[:, 32:40, ci], scalar=1.0, in1=n2g[:], op0=ALU.add, op1=ALU.mult)),
                R=["mod", "n2g"], W=["A2"])

        for u in units:
            sample = (u == 2)
            ci = 1 if sample else 0
            nseq = 1 if sample else 2
            sl = T // nseq
            norm_mod(u, ci, A1, 0, "A1")
            hR = ["h%d" % c for c in range(8)]
            TTv = TT[:]
            TT_tags = ["TT"]
            def w_in_groups(gs):
                for g in gs:
                    s0, s1 = w_group(2)

                    def wsl(kc, c0, c1):
                        s = s0 if kc < 4 else s1
                        return slabs[:, s, kc % 4, c0:c1], "slab%d" % s

                    def ftype(c0, dst_chunk, evac):
                        pi = rot()
                        for kc in range(8):
                            w_ap, wt = wsl(kc, c0, c0 + 128)
                            mm(pi, psb[pi][:], w_ap, hT[:, kc, :], kc == 0, kc == 7, R=[wt, "h%d" % kc])
                        evac(pi, dst_chunk)

                    def ttype(c0, ncols, toff, post=None):
                        for i in range(4):
                            pi = rot()
                            for kc in range(8):
                                w_ap, wt = wsl(kc, c0, c0 + ncols)
                                mm(pi, psb[pi][:, 0:ncols], hT[:, kc, i * 128:(i + 1) * 128], w_ap, kc == 0, kc == 7,
                                   R=[wt, "h%d" % kc])
                            dst = TTv[:, i, toff:toff + ncols]
                            if post is None:
                                P.op("dve", ("tensor_copy", dict(out=dst, in_=psb[pi][:, 0:ncols])),
                                     R=["ps%d" % pi], W=TT_tags)
                            else:
                                act(dst, psb[pi][:, 0:ncols], post, R=["ps%d" % pi], W=TT_tags)

                    def ev_copy_r(pi, ch):
                        act(bigr[:, ch, :], psb[pi][:], AF.Copy, R=["ps%d" % pi], W=[BT(ch)])

                    def ev_copy_f(pi, ch):
                        P.op("dve", ("tensor_copy", dict(out=bigr[:, ch, :], in_=psb[pi][:])), R=["ps%d" % pi], W=[BT(ch)])

                    if g == 0:
                        ftype(0, 0, ev_copy_r)
                        ftype(128, 1, ev_copy_r)
                        ftype(256, 2, ev_copy_f)
                        ftype(384, 3, ev_copy_f)
                    elif g == 1:
                        ftype(0, 4, ev_copy_f)
                        ttype(128, 128, 0)
                        ftype(256, 5, ev_copy_f if sample else ev_copy_r)
                        ftype(384, 6, ev_copy_f if sample else ev_copy_r)
                    elif g == 2:
                        ftype(0, 7, ev_copy_f if sample else ev_copy_r)
                        ftype(128, 8, ev_copy_f if sample else ev_copy_r)
                        ttype(256, 256, 128)
                    else:
                        ttype(0, 256, 384, post=AF.Gelu_apprx_tanh)
                        ttype(256, 256, 640, post=AF.Gelu_apprx_tanh)

            def qk_norm(chs):
                for ch in chs:
                    gvec, gt = (kg, "kg") if ch == 4 else (qg, "qg")
                    act(sq[:, 0, :], big[:, ch, :], AF.Square, R=[BT(ch)], W=["sq0"])
                    pi = rot()
                    mm(pi, psb[pi][:], blk64_r[:], sq[:, 0, :], True, True, R=["blk64", "sq0"])
                    act(rstd[:], psb[pi][:], AF.Sqrt, R=["ps%d" % pi, "eps"], W=["rstd"], scale=1.0 / 64, bias=eps_ap)
                    P.op("dve", ("reciprocal", dict(out=rstd[:], in_=rstd[:])), R=["rstd"], W=["rstd"])
                    P.op("dve", ("scalar_tensor_tensor", dict(
                        out=bigr[:, ch, :], in0=big[:, ch, :], scalar=gvec[:, 0:1], in1=rstd[:], op0=ALU.mult, op1=ALU.mult)),
                        R=[BT(ch), gt, "rstd"], W=[BT(ch)])

            def do_rope(chs):
                for ch in chs:
                    isg = ch <= 4
                    rm = rotg_r if isg else rotd_r
                    rt = "rotg" if isg else "rotd"
                    cosi, sini = (0, 1) if isg else (2, 3)
                    act(sq[:, 1, :], big[:, ch, :], AF.Copy, R=[BT(ch)], W=["sq1"])
                    pi = rot()
                    mm(pi, psb[pi][:], rm[:], sq[:, 1, :], True, True, R=[rt, "sq1"])
                    P.op("dve", ("tensor_tensor", dict(out=scr[:, 0, :], in0=psb[pi][:], in1=rope[:, sini, :], op=ALU.mult)),
                         R=["ps%d" % pi, "rope"], W=["scr0"])
                    P.op("pool", ("tensor_tensor", dict(out=scr[:, 1, :], in0=big[:, ch, :], in1=rope[:, cosi, :], op=ALU.mult)),
                         R=[BT(ch), "rope"], W=["scr1"])
                    P.op("dve", ("tensor_tensor", dict(out=bigr[:, ch, :], in0=scr[:, 0, :], in1=scr[:, 1, :], op=ALU.add)),
                         R=["scr0", "scr1"], W=[BT(ch)])

            def emit_kv():
                if True:
                    for i in range(4):
                        for (ch, off) in ((4, 0), (7, 128), (8, 256)):
                            pi = rot()
                            tr(pi, psb[pi][:, 0:128], big[:, ch, i * 128:(i + 1) * 128], R=[BT(ch)])
                            P.op("dve", ("tensor_copy", dict(out=tokst[:, i, off:off + 128], in_=psb[pi][:, 0:128])),
                                 R=["ps%d" % pi], W=["KT"])
                    sq0 = 2 * u
                    for s_ in range(2):
                        P.dma("sp", ngk[sq0 + s_, l].rearrange("(j p) f -> p j f", p=128),
                              tokst[:, 2 * s_:2 * s_ + 2, 0:128].bitcast(F32), R=["KT"], W=[], slot="o_ngk%d" % s_)
                        P.dma("sp", ndk[sq0 + s_, l].rearrange("(j p) f -> p j f", p=128),
                              tokst[:, 2 * s_:2 * s_ + 2, 128:384].bitcast(F32), R=["KT"], W=[], slot="o_ndk%d" % s_)
                        P.dma("sp", ngv[sq0 + s_, l].rearrange("(j p) f -> p j f", p=128),
                              TTv[:, 2 * s_:2 * s_ + 2, 0:128], R=TT_tags, W=[], slot="o_ngv%d" % s_)
                        P.dma("sp", ndv[sq0 + s_, l].rearrange("(j p) f -> p j f", p=128),
                              TTv[:, 2 * s_:2 * s_ + 2, 128:384], R=TT_tags, W=[], slot="o_ndv%d" % s_)

            def dv_rms():
                for i in range(4):
                    dv_ap = TTv[:, i, 640:896]
                    act(scr[:, 0, 0:256], dv_ap, AF.Square, R=TT_tags, W=["scr0"], accum_out=small[:, 8 + i:9 + i])
                P.op("act", ("activation", dict(out=small[:, 12:16], in_=small[:, 8:12], func=AF.Sqrt, scale=1.0 / 256, bias=eps_ap)),
                     R=["scr0", "eps"], W=["small2"])
                P.op("dve", ("reciprocal", dict(out=small[:, 12:16], in_=small[:, 12:16])), R=["small2"], W=["small2"])
                for i in range(4):
                    dv_ap = TTv[:, i, 640:896]
                    P.op("dve", ("scalar_tensor_tensor", dict(
                        out=DvN[:, i, :], in0=dv_ap, scalar=small[:, 12 + i:13 + i], in1=sgg[:], op0=ALU.mult, op1=ALU.mult)),
                        R=TT_tags + ["small2", "sgg"], W=["sq0", "sq1"])


            def chan_dft():
                for i in range(4):
                    pi = rot()
                    for j in range(2):
                        mm(pi, psb[pi][:, j * 256:(j + 1) * 256], bigr[:, j, i * 128:(i + 1) * 128], ccss_r[:], True, True,
                           R=[BT(j), "ccss"])
                    act(xcs[:, i, :], psb[pi][:], AF.Copy, R=["ps%d" % pi], W=XT(i))

            def fourier_prompt():
                for s in range(2):
                    for j in range(2):
                        pi = rot()
                        n = 0
                        for i2 in range(2):
                            i = s * 2 + i2
                            for part in range(2):
                                mm(pi, psb[pi][:, 0:256], xcs[:, i, j * 256 + part * 128: j * 256 + part * 128 + 128],
                                   dft256[:, part, i2, :], n == 0, n == 3, R=XT(i) + ["dft256"])
                                n += 1
                        act(bigr[:, 15 + j, s * 256:(s + 1) * 256], psb[pi][:, 0:256], AF.Copy, R=["ps%d" % pi], W=[BT(15 + j)])

            def exchange_part(p_):
                bz = bounce[p_][l]
                if p_ == "g":
                    P.dma("sp", bz[0:128, :], big[:, 4, :], R=[BT(4)], W=["bounce_g"], slot="bzg")
                    P.dma("sp", bz[128:256, :].rearrange("r (a c) -> (r a) c", c=128).rearrange("(j p) c -> p j c", p=128),
                          TTv[:, :, 0:128], R=TT_tags, W=["bounce_g"], slot="bzg")
                else:
                    P.dma("sp", bz[0:256, :].rearrange("(c p) t -> p c t", p=128), big[:, 7:9, :], R=[BT(7), BT(8)], W=["bounce_d"], slot="bzd")
                    P.dma("sp", bz[256:512, :].rearrange("r (a c) -> (r a) c", c=256).rearrange("(j p) c -> p j c", p=128),
                          TTv[:, :, 128:384], R=TT_tags, W=["bounce_d"], slot="bzd")
                w_prefetch()
                P.coll(lambda e, l=l, p_=p_: e.collective_compute(
                    "AllGather", ALU.bypass, replica_groups=[[0, 1, 2, 3], [4, 5, 6, 7]],
                    ins=[bounce[p_][l].opt()], outs=[gath[p_][l].opt()]), R=["bounce_" + p_], W=["gath_" + p_], slot="cc_" + p_)

            def exchange_x():
                bx_ = bounce["x"][l]
                P.dma("sp", bx_.rearrange("(j p) c -> p j c", p=128), xcs[:].bitcast(F32), R=XT(0) + XT(1) + XT(2) + XT(3), W=["bounce_x"], slot="bzx")
                w_prefetch()
                P.coll(lambda e, l=l: e.collective_compute(
                    "AllGather", ALU.bypass, replica_groups=[[0, 1, 2, 3], [4, 5, 6, 7]],
                    ins=[bounce["x"][l].opt()], outs=[gath["x"][l].opt()]), R=["bounce_x"], W=["gath_x"], slot="cc_x")

            def fourier_sample():
                acc = [rot(), rot()]
                n = 0
                for s4 in range(4):
                    sx, sc, ss = w_group(3)
                    for i in range(4):
                        for j in range(2):
                            mm(acc[j], psb[acc[j]][:], slabs[:, sx, i, j * 256:j * 256 + 128], slabs[:, sc, i, :],
                               n == 0, False, R=["slab%d" % sx, "slab%d" % sc])
                            mm(acc[j], psb[acc[j]][:], slabs[:, sx, i, j * 256 + 128:j * 256 + 256], slabs[:, ss, i, :],
                               False, n == 15, R=["slab%d" % sx, "slab%d" % ss])
                        n += 1
                for j in range(2):
                    act(xcs[:, j, :], psb[acc[j]][:], AF.Copy, R=["ps%d" % acc[j]], W=XT(j))

            def out_a(ysrc, ytags):
                for c in range(2):
                    pi = rot()
                    for kc in range(2):
                        mm(pi, psb[pi][:], wf_r[:, kc, c * 128:(c + 1) * 128], ysrc(kc), kc == 0, kc == 1, R=["wf"] + (ytags[kc] if isinstance(ytags[kc], list) else [ytags[kc]]))
                    act(bigr[:, 9 + c, :], psb[pi][:], AF.Copy, R=["ps%d" % pi], W=[BT(9 + c)])

            def load_keys_sample(kind, hc):
                if kind == "g":
                    gp, gtag, grows = gath["g"][l], "gath_g", 256
                    kcache, kcols, krow0 = cgk[l], slice(0, 128), 0
                    vcache, vcols, vrow0, vw = cgv[l], slice(0, 128), 128, 128
                else:
                    gp, gtag, grows = gath["d"][l], "gath_d", 512
                    kcache, kcols, krow0 = cdk[l], slice(hc * 128, hc * 128 + 128), hc * 128
                    vcache, vcols, vrow0, vw = cdv[l], slice(hc * 128, hc * 128 + 128), 256, 256
                gl = gp.rearrange("(r x) c -> x r c", x=grows)
                w_prefetch()
                P.dma("sp", cstage[:, :, :], kcache[:, kcols].rearrange("(j p) c -> p j c", p=128), R=[], W=["scr1"], slot="cst")
                for j in range(4):
                    pi = rot()
                    tr(pi, psb[pi][:, 0:128], cstage[:, j, :], R=["scr1"])
                    act(KT[:, j * 128:(j + 1) * 128], psb[pi][:, 0:128], AF.Copy, R=["ps%d" % pi], W=["KT"])
                P.dma("pool", KT[:, 512:2560].rearrange("p (r t) -> p r t", r=4), gl[krow0:krow0 + 128, :, :],
                      R=[gtag], W=["KT"], slot="ktl")
                for hh_ in range(2):
                    vc = slice(vcols.start + hh_ * 64, vcols.start + hh_ * 64 + 64)
                    P.dma("pool", VA[:, 0:4, hh_, 0:64], vcache[:, vc].rearrange("(j p) d -> p j d", p=128),
                          R=[], W=["VA"] + VAP, slot="val")
                    for r in range(4):
                        src = gp[r * grows + vrow0: r * grows + vrow0 + vw, :].rearrange("x (a c) -> (x a) c", c=vw)
                        src = src[:, vc].rearrange("(j p) d -> p j d", p=128)
                        P.dma("pool", VA[:, 4 + 4 * r: 8 + 4 * r, hh_, 0:64], src, R=[gtag], W=["VA"] + VAP, slot="val")

            def attention(gqa_preloaded=False, parts=("g", "d")):
                nk = 2560 if sample else 256
                nkt = nk // 128
                nq = sl
                nqt = nq // 128

                gctr = [0]

                def attn_group(heads, s):
                    nh = len(heads)
                    g_ = gctr[0]
                    gctr[0] += 1
                    if sample:
                        qoff, vb, vtag = 0, 0, "VA"
                        qtag = lambda hi: XT(hi)
                    else:
                        qoff, vb, vtag = (g_ % 2) * 256, (g_ % 8) * 2, VAP[g_ % 8]
                        qtag = lambda hi: [XT(hi)[g_ % 2]]
                    for hi, hd in enumerate(heads):
                        act(xcs[:, hi, qoff:qoff + nq], big[:, hd["qch"], s * sl: s * sl + nq], AF.Copy,
                            R=[BT(hd["qch"]), "maskc"], W=qtag(hi), scale=maskc[:, hd["mcol"]:hd["mcol"] + 1])
                    kpb = 2 if (nkt == 2 and nq * 2 <= 512) else 1
                    steps = [(hi, kt) for hi in range(nh) for kt in range(0, nkt, kpb)]
                    pending = []
                    sbank = {}

                    def issue_s(i):
                        hi, kt = steps[i]
                        hd = heads[hi]
                        pi = rot()
                        for kk in range(kpb):
                            mm(pi, psb[pi][:, kk * nq:(kk + 1) * nq], hd["kfn"](kt + kk), xcs[:, hi, qoff:qoff + nq], True, True,
                               R=[hd["ktag"]] + qtag(hi), skip_group_check=True)
                        sbank[i] = pi

                    issue_s(0)
                    for i in range(len(steps)):
                        if i + 1 < len(steps):
                            issue_s(i + 1)
                        hi, kt = steps[i]
                        hd = heads[hi]
                        pi = sbank[i]
                        b = i % 2
                        act(PT[:, b, 0:kpb * nq], psb[pi][:, 0:kpb * nq], AF.Exp, R=["ps%d" % pi], W=["PT%d" % b], scale=hd["scale"])
                        ob = hd["obank"]
                        for kk in range(kpb):
                            for qt in range(nqt):
                                ktt = kt + kk
                                mm(ob, psb[ob][:, qt * 65:(qt + 1) * 65], PT[:, b, kk * nq + qt * 128: kk * nq + (qt + 1) * 128],
                                   VA[:, vb + ktt, hd["vslot"], :], ktt == 0 and qt == 0, ktt == nkt - 1 and qt == nqt - 1,
                                   R=["PT%d" % b, vtag, "VAones"], skip_group_check=True)
                        for pd in [p for p in pending if p[0] <= i]:
                            pending.remove(pd)
                            pd[1]()
                        if kt + kpb - 1 == nkt - 1 and hd["post"] is not None:
                            p1, p2 = hd["post"]
                            p1()
                            if p2 is not None:
                                pending.append((i + 3, p2))
                    for pd in pending:
                        pd[1]()

                def gqa_post(h, s, ob):
                    def p1():
                        ov = psb[ob][:, 0:nqt * 65].rearrange("p (q c) -> p q c", c=65)
                        c0 = 16 + 4 * (h % 2)
                        P.op("dve", ("reciprocal", dict(out=small[:, c0:c0 + nqt], in_=ov[:, :, 64])), R=["ps%d" % ob], W=["small3"])
                        P.op("dve", ("tensor_tensor", dict(
                            out=tok[:, s * nqt:(s + 1) * nqt, h * 64:(h + 1) * 64], in0=ov[:, :, 0:64],
                            in1=small[:, c0:c0 + nqt].unsqueeze(2).broadcast_to([128, nqt, 64]), op=ALU.mult)),
                            R=["ps%d" % ob, "small3"], W=["tok"])
                    return p1, None

                def diff_post(h, s, ob1, ob2):
                    par = h % 2
                    A = scr[:, par, 0:256].rearrange("p (q c) -> p q c", c=64)[:, 0:nqt, :]
                    Bm = scr[:, par, 256:512].rearrange("p (q c) -> p q c", c=64)[:, 0:nqt, :]
                    stag = "scr%d" % par
                    c1, c2, c3 = 16 + 12 * par, 20 + 12 * par, 24 + 12 * par
                    mtag = "smallp%d" % par

                    def p1():
                        o1 = psb[ob1][:, 0:nqt * 65].rearrange("p (q c) -> p q c", c=65)
                        o2 = psb[ob2][:, 0:nqt * 65].rearrange("p (q c) -> p q c", c=65)
                        P.op("dve", ("reciprocal", dict(out=small[:, c1:c1 + nqt], in_=o1[:, :, 64])), R=["ps%d" % ob1], W=[mtag])
                        P.op("dve", ("reciprocal", dict(out=small[:, c2:c2 + nqt], in_=o2[:, :, 64])), R=["ps%d" % ob2, mtag], W=[mtag])
                        P.op("dve", ("tensor_scalar", dict(out=small[:, c2:c2 + nqt], in0=small[:, c2:c2 + nqt], scalar1=neglam[:, 0:1],
                                                              scalar2=None, op0=ALU.mult)), R=[mtag, "neglam"], W=[mtag])
                        P.op("dve", ("tensor_tensor", dict(out=A, in0=o1[:, :, 0:64],
                                                              in1=small[:, c1:c1 + nqt].unsqueeze(2).broadcast_to([128, nqt, 64]), op=ALU.mult)),
                             R=["ps%d" % ob1, mtag], W=[stag])
                        P.op("dve", ("tensor_tensor", dict(out=Bm, in0=o2[:, :, 0:64],
                                                              in1=small[:, c2:c2 + nqt].unsqueeze(2).broadcast_to([128, nqt, 64]), op=ALU.mult)),
                             R=["ps%d" % ob2, mtag, stag], W=[stag])
                        P.op("dve", ("tensor_tensor", dict(out=A, in0=A, in1=Bm, op=ALU.add)), R=[stag], W=[stag])
                        P.op("dve", ("tensor_tensor", dict(out=Bm, in0=A, in1=A, op=ALU.mult)), R=[stag], W=[stag])
                        P.op("dve", ("tensor_reduce", dict(out=small[:, c3:c3 + nqt], in_=Bm, axis=AX.X, op=ALU.add)), R=[stag, mtag], W=[mtag])

                    def p2():
                        P.op("act", ("activation", dict(out=small[:, c3:c3 + nqt], in_=small[:, c3:c3 + nqt], func=AF.Ln, scale=1.0 / 64, bias=eps_ap)),
                             R=[mtag, "eps"], W=[mtag])
                        P.op("act", ("activation", dict(out=small[:, c3:c3 + nqt], in_=small[:, c3:c3 + nqt], func=AF.Exp, scale=-0.5)),
                             R=[mtag], W=[mtag])
                        P.op("dve", ("tensor_tensor", dict(out=A, in0=A,
                                                              in1=small[:, c3:c3 + nqt].unsqueeze(2).broadcast_to([128, nqt, 64]), op=ALU.mult)),
                             R=[stag, mtag], W=[stag])
                        P.op("dve", ("tensor_tensor", dict(out=tok[:, s * nqt:(s + 1) * nqt, h * 64:(h + 1) * 64], in0=A,
                                                              in1=dng[:].unsqueeze(1).broadcast_to([128, nqt, 64]), op=ALU.mult)),
                             R=[stag, "dng"], W=["tok"])
                    return p1, p2

                if "g" in parts:
                    for s in range(nseq):
                        if sample:
                            if not gqa_preloaded:
                                load_keys_sample("g", 0)
                            kfn = lambda kt: KT[:, kt * 128:(kt + 1) * 128]
                            ktag = "KT"
                        else:
                            for i2 in range(2):
                                i = s * 2 + i2
                                P.op("dve", ("tensor_copy", dict(
                                    out=VA[:, (gctr[0] % 8) * 2 + i2, :, 0:64], in_=TTv[:, i, 0:128].rearrange("p (h d) -> p h d", d=64))),
                                    R=TT_tags, W=[VAP[gctr[0] % 8], "VA"])
                            kfn = lambda kt, s=s: bigr[:, 4, s * 256 + kt * 128: s * 256 + (kt + 1) * 128]
                            ktag = BT(4)
                        heads = []
                        for h in range(4):
                            heads.append(dict(qch=2 + (h % 2), mcol=h // 2, vslot=h // 2, scale=0.125, obank=h % 2,
                                              kfn=kfn, ktag=ktag, post=gqa_post(h, s, h % 2)))
                        attn_group(heads, s)
                    for i in range(4):
                        for c in range(2):
                            pi = rot()
                            tr(pi, psb[pi][:, 0:128], tok[:, i, c * 128:(c + 1) * 128], R=["tok"])
                            act(bigr[:, 11 + c, i * 128:(i + 1) * 128], psb[pi][:, 0:128], AF.Copy, R=["ps%d" % pi], W=[BT(11 + c)])

                if "d" in parts:
                    dscale = 1.0 / math.sqrt(32.0)
                    for s in range(nseq):
                        for hc in range(2):
                            if sample:
                                load_keys_sample("d", hc)
                                kfn = lambda kt: KT[:, kt * 128:(kt + 1) * 128]
                                ktag = "KT"
                            else:
                                for i2 in range(2):
                                    i = s * 2 + i2
                                    P.op("dve", ("tensor_copy", dict(
                                        out=VA[:, (gctr[0] % 8) * 2 + i2, :, 0:64],
                                        in_=TTv[:, i, 128 + hc * 128: 256 + hc * 128].rearrange("p (h d) -> p h d", d=64))),
                                        R=TT_tags, W=[VAP[gctr[0] % 8], "VA"])
                                kfn = lambda kt, s=s, hc=hc: bigr[:, 7 + hc, s * 256 + kt * 128: s * 256 + (kt + 1) * 128]
                                ktag = BT(7 + hc)
                            heads = []
                            for hh in range(2):
                                h = hc * 2 + hh
                                for m in range(2):
                                    heads.append(dict(qch=5 + hc, mcol=2 + hh * 2 + m, vslot=hh, scale=dscale, obank=2 * hh + m,
                                                      kfn=kfn, ktag=ktag,
                                                      post=(diff_post(h, s, 2 * hh, 2 * hh + 1) if m == 1 else None)))
                            attn_group(heads, s)
                    for i in range(4):
                        for c in range(2):
                            pi = rot()
                            tr(pi, psb[pi][:, 0:128], tok[:, i, c * 128:(c + 1) * 128], R=["tok"])
                            act(bigr[:, 13 + c, i * 128:(i + 1) * 128], psb[pi][:, 0:128], AF.Copy, R=["ps%d" % pi], W=[BT(13 + c)])


            def d_branch():
                for i in range(4):
                    pi = rot()
                    for g in range(4):
                        mm(pi, psb[pi][:, g * 64:(g + 1) * 64], wspT[:, g, :], DvN[:, i, g * 64:(g + 1) * 64],
                           True, True, R=["wspT", "sq0", "sq1"])
                    for g in range(4):
                        P.op("dve", ("scalar_tensor_tensor", dict(
                            out=tok[:, i, g * 64:(g + 1) * 64], in0=psb[pi][:, g * 64:(g + 1) * 64], scalar=bsT[:, g:g + 1],
                            in1=TTv[:, i, 384 + g * 64: 448 + g * 64], op0=ALU.add, op1=ALU.mult)),
                            R=["ps%d" % pi, "bsT"] + TT_tags, W=["tok"])
                for i in range(4):
                    for c in range(2):
                        pi = rot()
                        tr(pi, psb[pi][:, 0:128], tok[:, i, c * 128:(c + 1) * 128], R=["tok"])
                        act(bigr[:, 15 + c, i * 128:(i + 1) * 128], psb[pi][:, 0:128], AF.Copy, R=["ps%d" % pi], W=[BT(15 + c)])


            if sample:
                w_in_groups((1, 0))
                qk_norm((4, 2, 3))
                do_rope((4, 2, 3))
                exchange_part("g")
                w_in_groups((2,))
                do_rope((7, 8, 5, 6))
                load_keys_sample("g", 0)
                exchange_part("d")
                if stop_after == "coll":
                    break
                w_in_groups((3,))
                chan_dft()
                exchange_x()
                dv_rms()
                d_branch()
                attention(gqa_preloaded=True, parts=("g",))
                attention(parts=("d",))
                coll_done.add(l)
                fourier_sample()
                out_a(lambda kc: xcs[:, kc, :], [XT(0), XT(1)])
            else:
                w_in_groups((0, 1, 2, 3))
                qk_norm((2, 3, 4))
                emit_kv()
                dv_rms()
                chan_dft()
                fourier_prompt()
                out_a(lambda kc: bigr[:, 15 + kc, :], [BT(15), BT(16)])
                attention()
                d_branch()

            for cdbg in range(8):
                dump("br%d" % cdbg, big[:, 9 + cdbg, :], [BT(9 + cdbg)])
            for n in range(4):
                for cg in range(2):
                    s0, s1, sbr = w_group(3)
                    for j in range(4):
                        for kc in range(4):
                            mm(j, psb[j][:], slabs[:, s0, kc, j * 128:(j + 1) * 128], hT[:, kc, :], kc == 0, False,
                               R=["slab%d" % s0, "h%d" % kc])
                    for j in range(4):
                        c = cg * 4 + j
                        for kc in range(4, 8):
                            mm(j, psb[j][:], slabs[:, s1, kc - 4, j * 128:(j + 1) * 128], hT[:, kc, :], False, kc == 7,
                               R=["slab%d" % s1, "h%d" % kc])
                        pb = rot()
                        for kc in range(2):
                            mm(pb, psb[pb][:], slabs[:, sbr, kc, j * 128:(j + 1) * 128], bigr[:, 9 + 2 * n + kc, :], kc == 0, kc == 1,
                               R=["slab%d" % sbr, BT(9 + 2 * n + kc)])
                        b = (n * 8 + c) % 2
                        act(scr[:, b, :], psb[j][:], AF.Sigmoid, R=["ps%d" % j], W=["scr%d" % b])
                        if n == 0:
                            P.op("dve", ("tensor_tensor", dict(out=bigr[:, c, :], in0=scr[:, b, :], in1=psb[pb][:], op=ALU.mult)),
                                 R=["scr%d" % b, "ps%d" % pb], W=[BT(c)])
                        else:
                            P.op("dve", ("tensor_tensor", dict(out=scr[:, b, :], in0=scr[:, b, :], in1=psb[pb][:], op=ALU.mult)),
                                 R=["scr%d" % b, "ps%d" % pb], W=["scr%d" % b])
                            P.op("dve", ("tensor_tensor", dict(out=bigr[:, c, :], in0=big[:, c, :], in1=scr[:, b, :], op=ALU.add)),
                                 R=["scr%d" % b, BT(c)], W=[BT(c)])
            dump("merged0", big[:, 0, :], [BT(0)])
            dump("merged7", big[:, 7, :], [BT(7)])
            for cg in range(2):
                s0, s1 = w_group(2)
                for j in range(4):
                    for kc in range(4):
                        mm(4 + j, psb[4 + j][:], slabs[:, s0, kc, j * 128:(j + 1) * 128], bigr[:, kc, :], kc == 0, False,
                           R=["slab%d" % s0, BT(kc)])
                for j in range(4):
                    c = cg * 4 + j
                    pi = 4 + j
                    for kc in range(4, 8):
                        mm(pi, psb[pi][:], slabs[:, s1, kc - 4, j * 128:(j + 1) * 128], bigr[:, kc, :], False, kc == 7,
                           R=["slab%d" % s1, BT(kc)])
                    P.op("dve", ("scalar_tensor_tensor", dict(
                        out=xT[:, u, c, :], in0=psb[pi][:], scalar=modT[:, 16 + c, ci:ci + 1], in1=xT[:, u, c, :],
                        op0=ALU.mult, op1=ALU.add)), R=["ps%d" % pi, "mod", "x%d_%d" % (u, c)], W=["x%d_%d" % (u, c)])
            dump("x1_c0", xT[:, u, 0, :], ["x%d_0" % u])
            norm_mod(u, ci, A2, 24, "A2")
            for half in range(2):
                for g in range(4):
                    s0, s1 = w_group(2)
                    for j in range(4):
                        for kc in range(4):
                            mm(4 + j, psb[4 + j][:], slabs[:, s0, kc, j * 128:(j + 1) * 128], hT[:, kc, :], kc == 0, False,
                               R=["slab%d" % s0, "h%d" % kc])
                    for j in range(4):
                        fc = g * 4 + j
                        pi = 4 + j
                        for kc in range(4, 8):
                            mm(pi, psb[pi][:], slabs[:, s1, kc - 4, j * 128:(j + 1) * 128], hT[:, kc, :], False, kc == 7,
                               R=["slab%d" % s1, "h%d" % kc])
                        if RELU2_DVE:
                            P.op("dve", ("scalar_tensor_tensor", dict(
                                out=bigr[:, fc, :], in0=psb[pi][:], scalar=0.0, in1=psb[pi][:], op0=ALU.max, op1=ALU.mult)),
                                R=["ps%d" % pi], W=[BT(fc)])
                        else:
                            b = fc % 2
                            act(scr[:, b, :], psb[pi][:], AF.Relu, R=["ps%d" % pi], W=["scr%d" % b])
                            P.op("dve", ("tensor_tensor", dict(out=bigr[:, fc, :], in0=scr[:, b, :], in1=scr[:, b, :], op=ALU.mult)),
                                 R=["scr%d" % b], W=[BT(fc)])
                for cg in range(2):
                    for kq in range(4):
                        s, = w_group(1)
                        for j in range(4):
                            for kc in range(4):
                                mm(j, psb[j][:], slabs[:, s, kc, j * 128:(j + 1) * 128], bigr[:, kq * 4 + kc, :],
                                   kq == 0 and kc == 0, kq == 3 and kc == 3, R=["slab%d" % s, BT(kq * 4 + kc)])
                    for j in range(4):
                        c = cg * 4 + j
                        P.op("dve", ("scalar_tensor_tensor", dict(
                            out=xT[:, u, c, :], in0=psb[j][:], scalar=modT[:, 40 + c, ci:ci + 1], in1=xT[:, u, c, :],
                            op0=ALU.mult, op1=ALU.add)), R=["ps%d" % j, "mod", "x%d_%d" % (u, c)], W=["x%d_%d" % (u, c)])
            if l == depth - 1 and stop_after is None:
                final_out(u)

    assert stop_after is not None or wstate["used"] == len(wq), (wstate, len(wq))
    P.emit()
    return nc


def _consts(q):
    c = {}
    c["k_ident"] = np.eye(128, dtype=np.float32)
    c["k_ones"] = np.ones((128, 128), np.float32)
    blk = np.zeros((128, 128), np.float32)
    blk[:64, :64] = 1
    blk[64:, 64:] = 1
    c["k_blk64"] = blk
    n = np.arange(64)
    ang = 2 * np.pi * np.outer(n, n) / 64
    C64 = np.cos(ang) / 8.0
    S64 = np.sin(ang) / 8.0
    cc = np.zeros((128, 256))
    cc[:64, 0:64] = C64
    cc[64:, 64:128] = C64
    cc[:64, 128:192] = S64
    cc[64:, 192:256] = S64
    c["k_ccss"] = cc.astype(np.float32)

    def rotm(hd):
        half = hd // 2
        R = np.zeros((128, 128), np.float32)
        for m in range(128):
            if m % hd < half:
                R[m + half, m] = -1.0
            else:
                R[m - half, m] = 1.0
        return R
    mk = np.zeros((128, 6), np.float32)
    mk[0:64, 0] = 1
    mk[64:128, 1] = 1
    for j in range(4):
        mk[32 * j:32 * j + 32, 2 + j] = 1
    c["k_mask"] = mk
    c["k_rotg"] = rotm(64)
    c["k_rotd"] = rotm(32)
    pos = q * 512 + np.arange(512)
    row = (pos // 64).astype(np.float64)
    col = (pos % 64).astype(np.float64)

    def tab(dim):
        quarter = dim // 4
        inv = 10000.0 ** (-np.arange(quarter, dtype=np.float32) / quarter)
        inv = inv.astype(np.float32)
        ang = np.concatenate([row[:, None].astype(np.float32) * inv, col[:, None].astype(np.float32) * inv], axis=-1)
        ang = ang.astype(np.float32)
        cos = np.cos(ang).astype(np.float32)
        sin = np.sin(ang).astype(np.float32)
        half = dim // 2
        idx = np.arange(128) % half
        return cos[:, idx].T.copy(), sin[:, idx].T.copy()
    cg, sg = tab(64)
    cd, sd = tab(32)
    c["k_rope"] = np.stack([cg, sg, cd, sd]).astype(np.float32)
    t = np.arange(256)
    a = 2 * np.pi * np.outer(t, t) / 256
    c["k_dft256"] = np.stack([np.cos(a) / 16.0, -np.sin(a) / 16.0]).astype(np.float32)
    tt = np.arange(2048, dtype=np.float64)
    a = 2 * np.pi * np.outer(tt, pos.astype(np.float64)) / 2048
    c["k_dftc"] = (np.cos(a) / math.sqrt(2048.0)).astype(np.float32)
    c["k_dfts"] = (-np.sin(a) / math.sqrt(2048.0)).astype(np.float32)
    return c


_NC_CACHE = {}
import os
import json
_DBG = {k: (tuple(v) if isinstance(v, list) else v) for k, v in json.loads(os.environ.get("KDBG", "{}")).items()}
_DBG_RUN = {}


def kernel(**inputs):
    inp = {k: np.ascontiguousarray(np.asarray(v)) for k, v in inputs.items()}
    if "nc" not in _NC_CACHE:
        _NC_CACHE["nc"] = build_nc(**_DBG)
    nc = _NC_CACHE["nc"]
    shared = ["c_ctx", "w_ada", "b_ada", "norm1_g", "norm2_g", "w_in", "w_fourier", "q_norm_g", "k_norm_g",
              "lambda_q1", "lambda_k1", "lambda_q2", "lambda_k2", "diff_norm_g", "sgu_norm_g", "w_spatial",
              "b_spatial", "w_gate", "w_branch", "w_out", "w_mlp1", "w_mlp2", "final_norm_g"]
    in_maps = []
    for core in range(8):
        b, q = core // 4, core % 4
        m = {k: inp[k] for k in shared}
        m["xp"] = inp["x_prompt"][4 * core:4 * core + 4]
        m["xs"] = inp["x_sample"][b, q * 512:(q + 1) * 512]
        m["cm"] = inp["c"][b]
        m["cgk"] = inp["cache_gqa_k"][b].reshape(DEPTH, 512, 128)
        m["cgv"] = inp["cache_gqa_v"][b].reshape(DEPTH, 512, 128)
        m["cdk"] = inp["cache_diff_k"][b].reshape(DEPTH, 512, 256)
        m["cdv"] = inp["cache_diff_v"][b].reshape(DEPTH, 512, 256)
        m.update(_consts(q))
        in_maps.append({k: np.ascontiguousarray(v) for k, v in m.items()})
    ncores = _DBG_RUN.get("cores", 8)
    res = run_bass_kernel_spmd(nc, in_maps[:ncores], core_ids=list(range(ncores)))
    R = list(res.results)
    _DBG_RUN["results"] = R
    _DBG_RUN["names"] = getattr(nc, "_dbg_names", [])
    while len(R) < 8:
        R.append(R[0])
    y_prompt = np.concatenate([R[c]["yp"] for c in range(8)], axis=0)
    y_sample = np.stack([np.concatenate([R[b * 4 + q]["ys"] for q in range(4)], axis=0) for b in range(2)], axis=0)
    ngk = np.concatenate([R[c]["ngk"] for c in range(8)], axis=0).reshape(32, DEPTH, 256, 2, 64)
    ngv = np.concatenate([R[c]["ngv"] for c in range(8)], axis=0).reshape(32, DEPTH, 256, 2, 64)
    ndk = np.concatenate([R[c]["ndk"] for c in range(8)], axis=0).reshape(32, DEPTH, 256, 4, 2, 32)
    ndv = np.concatenate([R[c]["ndv"] for c in range(8)], axis=0).reshape(32, DEPTH, 256, 4, 64)
    return (y_prompt.astype(np.float32), y_sample.astype(np.float32), ngk.astype(np.float32),
            ngv.astype(np.float32), ndk.astype(np.float32), ndv.astype(np.float32))
```

```python
import math
import numpy as np
import concourse.bass as bass
import concourse.mybir as mybir
from concourse.bass_utils import run_bass_kernel_spmd

F32 = mybir.dt.float32
F32R = mybir.dt.float32r
BF16 = mybir.dt.bfloat16
AF = mybir.ActivationFunctionType
ALU = mybir.AluOpType
AX = mybir.AxisListType

D = 1024
DEPTH = 4
T = 512
NU = 3
EPS = 1e-6
GROWS = 1280
RELU2_DVE = False


class Prog:
    def __init__(self, nc):
        self.nc = nc
        self.ops = {e: [] for e in ("pe", "act", "dve", "pool", "sp")}
        self.cnt = {e: 0 for e in self.ops}
        self.esem = {e: nc.alloc_semaphore("sem_" + e) for e in ("pe", "act", "dve", "pool")}
        self.slot_sem = {}
        self.slot_cnt = {}
        self.lastw = {}
        self.readers = {}
        self.waited = {e: {} for e in self.ops}
        self.final = []

    def _deps(self, eng, R, W, pe_acc=False):
        deps = {}

        def add(ev):
            if ev is None:
                return
            k, v, src, _sem = ev
            if pe_acc and src == "pe" and eng == "pe":
                return
            if k not in deps or deps[k][0] < v:
                deps[k] = (v, ev[3])

        for t in list(R) + list(W):
            add(self.lastw.get(t))
        for t in W:
            for ev in self.readers.get(t, []):
                add(ev)
        out = []
        for k, (v, sem) in deps.items():
            if self.waited[eng].get(k, 0) >= v:
                continue
            self.waited[eng][k] = v
            out.append((sem, v, k))
        return out

    def _commit(self, ev, R, W):
        for t in W:
            self.lastw[t] = ev
            self.readers[t] = []
        for t in R:
            self.readers.setdefault(t, []).append(ev)

    def op(self, eng, fn, R=(), W=()):
        if isinstance(fn, tuple):
            _name, _kw = fn
            fn = (lambda e, _name=_name, _kw=_kw: getattr(e, _name)(**_kw))
        waits = self._deps(eng, R, W, pe_acc=True)
        self.cnt[eng] += 1
        sem = self.esem[eng]
        ev = ("e_" + eng, self.cnt[eng], eng, sem)
        self.ops[eng].append((waits, fn, sem, 1, self.cnt[eng]))
        self._commit(ev, R, W)

    def dma(self, q, out, in_, R, W, slot, **kw):
        waits = self._deps(q, R, W)
        if slot not in self.slot_sem:
            self.slot_sem[slot] = self.nc.alloc_semaphore("ds_" + slot)
            self.slot_cnt[slot] = 0
        self.slot_cnt[slot] += 16
        sem = self.slot_sem[slot]
        ev = ("s_" + slot, self.slot_cnt[slot], "dma", sem)
        self.ops[q].append((waits, lambda e: e.dma_start(out=out, in_=in_, **kw), sem, 16, None))
        self._commit(ev, R, W)

    def coll(self, fn, R, W, slot):
        waits = self._deps("pool", R, W)
        if slot not in self.slot_sem:
            self.slot_sem[slot] = self.nc.alloc_semaphore("ds_" + slot)
            self.slot_cnt[slot] = 0
        self.slot_cnt[slot] += 1
        sem = self.slot_sem[slot]
        ev = ("s_" + slot, self.slot_cnt[slot], "dma", sem)
        self.ops["pool"].append((waits, fn, sem, None, None))
        self._commit(ev, R, W)

    def emit(self):
        nc = self.nc
        final_waits = [(self.slot_sem[s], self.slot_cnt[s]) for s in self.slot_sem]
        needed = {e: set() for e in self.esem}
        for q in self.ops:
            for (waits, fn, sem, inc, seq) in self.ops[q]:
                for (s_, v, k) in waits:
                    if k.startswith("e_"):
                        needed[k[2:]].add(v)
        for en in self.esem:
            if self.cnt[en]:
                needed[en].add(self.cnt[en])
        rank = {en: {v: i + 1 for i, v in enumerate(sorted(needed[en]))} for en in needed}
        with nc.Block() as block:
            def run(eng_name):
                def body(e):
                    for waits, fn, sem, inc, seq in self.ops[eng_name]:
                        for (s_, v, k) in waits:
                            if k.startswith("e_"):
                                e.wait_ge(s_, rank[k[2:]][v])
                            else:
                                e.wait_ge(s_, v)
                        ins = fn(e)
                        if inc is None:
                            ins.then_inc(sem)
                        elif seq is None:
                            ins.then_inc(sem, inc)
                        elif seq in rank[eng_name]:
                            ins.then_inc(sem, 1)
                    if eng_name == "sp":
                        for (s_, v) in final_waits:
                            e.wait_ge(s_, v)
                        for en in ("pe", "act", "dve", "pool"):
                            if self.cnt[en]:
                                e.wait_ge(self.esem[en], rank[en][self.cnt[en]])
                return body
            block.sync(run("sp"))
            block.tensor(run("pe"))
            block.scalar(run("act"))
            block.vector(run("dve"))
            block.gpsimd(run("pool"))


def build_nc(depth=DEPTH, units=(0, 1, 2), dbg=False, stop_after=None):
    nc = bass.Bass("TRN2", target_bir_lowering=False)
    P = Prog(nc)

    def din(name, shape, dt=F32):
        return nc.dram_tensor(name, list(shape), dt, kind="ExternalInput").ap()

    def dout(name, shape):
        return nc.dram_tensor(name, list(shape), F32, kind="ExternalOutput").ap()

    xp = din("xp", [4, 256, D])
    xs = din("xs", [512, D])
    cm = din("cm", [D])
    cctx = din("c_ctx", [D])
    cgk = din("cgk", [DEPTH, 512, 128])
    cgv = din("cgv", [DEPTH, 512, 128])
    cdk = din("cdk", [DEPTH, 512, 256])
    cdv = din("cdv", [DEPTH, 512, 256])
    w_ada = din("w_ada", [DEPTH, D, 6 * D])
    b_ada = din("b_ada", [DEPTH, 6 * D])
    norm1_g = din("norm1_g", [DEPTH, D])
    norm2_g = din("norm2_g", [DEPTH, D])
    w_in = din("w_in", [DEPTH, D, 2048])
    w_fourier = din("w_fourier", [DEPTH, 256, 256])
    q_norm_g = din("q_norm_g", [DEPTH, 64])
    k_norm_g = din("k_norm_g", [DEPTH, 64])
    lam_in = [din(n, [DEPTH, 32]) for n in ("lambda_q1", "lambda_k1", "lambda_q2", "lambda_k2")]
    diff_norm_g = din("diff_norm_g", [DEPTH, 64])
    sgu_norm_g = din("sgu_norm_g", [DEPTH, 256])
    w_spatial = din("w_spatial", [DEPTH, 4, 128, 128])
    b_spatial = din("b_spatial", [DEPTH, 4, 128])
    w_gate = din("w_gate", [DEPTH, D, 4 * D])
    w_branch = din("w_branch", [DEPTH, 4, 256, D])
    w_out = din("w_out", [DEPTH, D, D])
    w_mlp1 = din("w_mlp1", [DEPTH, D, 4 * D])
    w_mlp2 = din("w_mlp2", [DEPTH, 4 * D, D])
    final_g = din("final_norm_g", [D])
    k_ident = din("k_ident", [128, 128])
    k_ones = din("k_ones", [128, 128])
    k_blk64 = din("k_blk64", [128, 128])
    k_ccss = din("k_ccss", [128, 256])
    k_rotg = din("k_rotg", [128, 128])
    k_rotd = din("k_rotd", [128, 128])
    k_mask = din("k_mask", [128, 6])
    k_rope = din("k_rope", [4, 128, 512])
    k_dft256 = din("k_dft256", [2, 256, 256])
    k_dftc = din("k_dftc", [2048, 512])
    k_dfts = din("k_dfts", [2048, 512])
    yp = dout("yp", [4, 256, D])
    ys = dout("ys", [512, D])
    ngk = dout("ngk", [4, DEPTH, 256, 128])
    ngv = dout("ngv", [4, DEPTH, 256, 128])
    ndk = dout("ndk", [4, DEPTH, 256, 256])
    ndv = dout("ndv", [4, DEPTH, 256, 256])
    PR = {"g": 256, "d": 512, "x": 512}
    bounce = {p: [nc.dram_tensor("bounce_%s%d" % (p, l), [PR[p], 512], F32).ap() for l in range(DEPTH)] for p in PR}
    gath = {p: [nc.dram_tensor("gath_%s%d" % (p, l), [4 * PR[p], 512], F32).ap() for l in range(DEPTH)] for p in PR}

    dbg_out = dout("dbg", [16, 128, 512]) if dbg else None
    dbg_n = [0]
    dbg_names = []

    def dump(name, ap, R):
        if not dbg:
            return
        i = dbg_n[0]
        dbg_n[0] += 1
        dbg_names.append(name)
        ncol = ap.shape[-1] if len(ap.shape) == 2 else None
        P.dma("sp", dbg_out[i, :, 0:ap.shape[1]], ap, R=R, W=[], slot="dbg")
    nc._dbg_names = dbg_names

    def sb(name, shape, dt=F32):
        return nc.alloc_sbuf_tensor(name, list(shape), dt)

    xT = sb("xT", [128, NU, 8, T])
    hT = sb("hT", [128, 8, T], F32R)
    bigR = sb("bigR", [128, 17, T], F32R)
    TT = sb("TT", [128, 4, 896])
    dft256t = sb("dft256t", [128, 2, 2, 256], F32R)
    slabs = sb("slabs", [128, 4, 4, 512], F32R)
    KT = sb("KT", [128, 2560], F32R)
    VA = sb("VA", [128, 20, 130], BF16)
    PT = sb("PT", [128, 2, 512], BF16)
    tokst = KT[:, 0:1536].rearrange("p (i c) -> p i c", c=384)
    tok = sb("tok", [128, 4, 256])
    sq = sb("sq", [128, 2, T], F32R)
    scr = sb("scr", [128, 2, T])
    cstage = scr[:, 1, :].rearrange("p (i c) -> p i c", c=128)
    DvN = sq[:].rearrange("p a t -> p (a t)").rearrange("p (i c) -> p i c", c=256)
    rstd = sb("rstd", [128, T])
    small = sb("small", [128, 64])
    ident = sb("ident", [128, 128])
    ones_r = sb("ones_r", [128, 128], F32R)
    blk64_r = sb("blk64_r", [128, 128], F32R)
    ccss_r = sb("ccss_r", [128, 256], F32R)
    rotg_r = sb("rotg_r", [128, 128], F32R)
    rotd_r = sb("rotd_r", [128, 128], F32R)
    rope = sb("rope", [128, 4, 512])
    dft256 = dft256t[:]
    xcs = sb("xcs", [128, 4, 512], F32R)
    wf_r = sb("wf_r", [128, 2, 256], F32R)
    wsp = sb("wsp", [128, 4, 128])
    wspT = sb("wspT", [128, 4, 128], F32R)
    bsT = sb("bsT", [128, 4])
    condT = sb("condT", [128, 8, 2])
    scond = sb("scond", [128, 8, 2], F32R)
    modT = sb("modT", [128, 48, 2])
    badaT = sb("badaT", [128, 48])
    A1 = sb("A1", [128, 8, 2])
    A2 = sb("A2", [128, 8, 2])
    n1g = sb("n1g", [128, 8])
    n2g = sb("n2g", [128, 8])
    fng = sb("fng", [128, 8])
    qg = sb("qg", [128, 1])
    kg = sb("kg", [128, 1])
    lamv = sb("lamv", [128, 4, 32])
    neglam = sb("neglam", [128, 1])
    dng = sb("dng", [128, 64])
    sgg = sb("sgg", [128, 256])

    NPS = 8
    psb = [nc.alloc_psum_tensor("ps%d" % i, [128, 512], F32) for i in range(NPS)]
    rot_state = [0]

    def rot():
        i = 4 + rot_state[0] % 4
        rot_state[0] += 1
        return i

    def mm(ps_i, out_ap, lhsT, rhs, start, stop, R, **kw):
        P.op("pe", ("matmul", dict(out=out_ap, lhsT=lhsT, rhs=rhs, start=start, stop=stop, **kw)),
             R=R, W=["ps%d" % ps_i])

    def tr(ps_i, out_ap, in_ap, R):
        P.op("pe", ("transpose", dict(out=out_ap, in_=in_ap, identity=ident[:])), R=list(R) + ["ident"], W=["ps%d" % ps_i])

    def act(out, in_, func, R, W, **kw):
        P.op("act", ("activation", dict(out=out, in_=in_, func=func, **kw)), R=R, W=W)

    def BT(c):
        return "B%d" % c

    def XT(i):
        return ["xcs%da" % i, "xcs%db" % i]

    VAP = ["VAp%d" % i for i in range(8)]

    bigr = bigR[:]
    big = bigR[:].bitcast(F32)
    outst = tok[:].rearrange("p a t -> p (a t)").rearrange("p (b d) -> p b d", d=1024)
    OST = {0: ["tok"], 1: ["tok"]}

    wq = []
    wstate = {"issued": 0, "used": 0}
    PREF = 3

    coll_done = set()
    wneed = {}

    def w_issue():
        i = wstate["issued"]
        if i >= len(wq):
            return False
        need = wneed.get(i)
        if need is not None and need not in coll_done:
            return False
        slot = i % 4
        for (dst_fn, src) in wq[i]:
            P.dma("pool", dst_fn(slot), src, R=(["gath_x"] if need is not None else []), W=["slab%d" % slot], slot="slab%d" % slot)
        wstate["issued"] += 1
        return True

    def w_group(n):
        i = wstate["used"]
        while wstate["issued"] <= min(i + 3, len(wq) - 1):
            if not w_issue():
                break
        assert wstate["issued"] >= i + n, "weight slab not issued"
        wstate["used"] += n
        return [(i + k) % 4 for k in range(n)]

    def w_prefetch():
        i = wstate["used"]
        while wstate["issued"] <= min(i + 3, len(wq) - 1):
            if not w_issue():
                break

    def slab_std(src2d, nk=4):
        ncols = src2d.shape[1]
        return [(lambda s, nk=nk, ncols=ncols: slabs[:, s, 0:nk, 0:ncols],
                 src2d.rearrange("(k p) c -> p k c", p=128))]

    def plan_layer_unit(l, sample):
        for g in ((1, 0, 2, 3) if sample else (0, 1, 2, 3)):
            for kh in range(2):
                rows = slice(kh * 512, kh * 512 + 512)
                if g == 0:
                    ent = [(lambda s: slabs[:, s, :, 0:256],
                            w_in[l, rows, 0:256].rearrange("(k p) c -> p k c", p=128))]
                    for b in range(2):
                        for a in range(2):
                            ent.append((lambda s, b=b, a=a: slabs[:, s, :, 256 + 128 * b + 64 * a:320 + 128 * b + 64 * a],
                                        w_in[l, rows, 256 + 128 * a + 64 * b:320 + 128 * a + 64 * b].rearrange("(k p) d -> p k d", p=128)))
                    wq.append(ent)
                else:
                    wq.append(slab_std(w_in[l, rows, g * 512:(g + 1) * 512]))
        if sample:
            for s4 in range(4):
                wneed[len(wq)] = l
                wq.append([(lambda s: slabs[:, s, :, :],
                            gath["x"][l][s4 * 512:(s4 + 1) * 512, :].rearrange("(k p) c -> p k c", p=128))])
                wq.append(slab_std(k_dftc[s4 * 512:(s4 + 1) * 512, :]))
                wq.append(slab_std(k_dfts[s4 * 512:(s4 + 1) * 512, :]))
        for n in range(4):
            for cg in range(2):
                cols = slice(n * 1024 + cg * 512, n * 1024 + cg * 512 + 512)
                for kh in range(2):
                    wq.append(slab_std(w_gate[l, kh * 512:(kh + 1) * 512, cols]))
                wq.append(slab_std(w_branch[l, n, :, cg * 512:(cg + 1) * 512], nk=2))
        for cg in range(2):
            for kh in range(2):
                wq.append(slab_std(w_out[l, kh * 512:(kh + 1) * 512, cg * 512:(cg + 1) * 512]))
        for half in range(2):
            for g in range(4):
                for kh in range(2):
                    wq.append(slab_std(w_mlp1[l, kh * 512:(kh + 1) * 512, half * 2048 + g * 512: half * 2048 + (g + 1) * 512]))
            for cg in range(2):
                for kq in range(4):
                    r0 = half * 2048 + kq * 512
                    wq.append(slab_std(w_mlp2[l, r0:r0 + 512, cg * 512:(cg + 1) * 512]))

    def plan_ada(l):
        for g in range(12):
            for kh in range(2):
                wq.append(slab_std(w_ada[l, kh * 512:(kh + 1) * 512, g * 512:(g + 1) * 512]))

    for l in range(depth):
        plan_ada(l)
        for u in units:
            plan_layer_unit(l, u == 2)

    def ld(dst, src, W, slot, q="sp", **kw):
        P.dma(q, dst, src, R=[], W=W, slot=slot, **kw)

    ld(ident[:], k_ident, ["ident"], "c0")
    ld(ones_r[:], k_ones, ["ones"], "c1", q="pool")
    ld(blk64_r[:], k_blk64, ["blk64"], "c2", q="pool")
    ld(ccss_r[:], k_ccss, ["ccss"], "c3", q="pool")
    ld(rotg_r[:], k_rotg, ["rotg"], "c4", q="pool")
    ld(rotd_r[:], k_rotd, ["rotd"], "c5", q="pool")
    ld(rope[:], k_rope.rearrange("a p t -> p a t"), ["rope"], "c6")
    for a_ in range(2):
        ld(dft256[:, a_], k_dft256[a_].rearrange("(j p) t -> p j t", p=128), ["dft256"], "c7", q="pool")
    ld(fng[:], final_g.rearrange("(c p) -> p c", p=128), ["fng"], "c8", allow_slow_non_contiguous=True)
    ld(condT[:, :, 0], cctx.rearrange("(c p) -> p c", p=128), ["condT"], "c9", allow_slow_non_contiguous=True)
    ld(condT[:, :, 1], cm.rearrange("(c p) -> p c", p=128), ["condT"], "c9", allow_slow_non_contiguous=True)
    P.op("pool", lambda e: e.memset(VA[:, :, 0:1], 1.0), W=["VAones"])
    P.op("pool", lambda e: e.memset(VA[:, :, 129:130], 1.0), W=["VAones"])
    def x_load():
        for u in range(NU):
            for i in range(4):
                if u < 2:
                    src = xp[2 * u + i // 2, (i % 2) * 128:(i % 2) * 128 + 128, :]
                else:
                    src = xs[i * 128:(i + 1) * 128, :]
                b = (u * 4 + i) % 2
                stg = outst[:, 0, :] if b == 0 else scr[:].rearrange("p a t -> p (a t)")
                stags = OST[0] if b == 0 else ["scr0", "scr1"]
                ld(stg, src, stags, "ost%d" % b)
                for c in range(8):
                    pi = rot()
                    tr(pi, psb[pi][:, 0:128], stg[:, c * 128:(c + 1) * 128], R=stags)
                    eng = "dve" if c % 2 == 0 else "act"
                    dst = xT[:, u, c, i * 128:(i + 1) * 128]
                    if eng == "dve":
                        P.op("dve", ("tensor_copy", dict(out=dst, in_=psb[pi][:, 0:128])),
                             R=["ps%d" % pi], W=["x%d_%d" % (u, c)])
                    else:
                        act(dst, psb[pi][:, 0:128], AF.Copy, R=["ps%d" % pi], W=["x%d_%d" % (u, c)])
    act(scond[:], condT[:], AF.Silu, R=["condT"], W=["scond"])

    def norm_mod(u, ci, Amat, shift_c0, gtag):
        pi = rot()
        for c in range(8):
            b = c % 2
            act(sq[:, b, :], xT[:, u, c, :], AF.Square, R=["x%d_%d" % (u, c)], W=["sq%d" % b])
            mm(pi, psb[pi][:], ones_r[:], sq[:, b, :], c == 0, c == 7, R=["ones", "sq%d" % b])
        act(rstd[:], psb[pi][:], AF.Sqrt, R=["ps%d" % pi, "eps"], W=["rstd"], scale=1.0 / D, bias=eps_ap)
        P.op("dve", ("reciprocal", dict(out=rstd[:], in_=rstd[:])), R=["rstd"], W=["rstd"])
        for c in range(8):
            b = c % 2
            P.op("dve", ("scalar_tensor_tensor", dict(
                out=scr[:, b, :], in0=xT[:, u, c, :], scalar=Amat[:, c, ci:ci + 1], in1=rstd[:],
                op0=ALU.mult, op1=ALU.mult)), R=["x%d_%d" % (u, c), gtag, "rstd"], W=["scr%d" % b])
            act(hT[:, c, :], scr[:, b, :], AF.Identity, R=["scr%d" % b, "mod"], W=["h%d" % c],
                bias=modT[:, shift_c0 + c, ci:ci + 1])

    eps_t = sb("eps_t", [128, 1])
    maskc = sb("maskc", [128, 6])
    ld(maskc[:], k_mask, ["maskc"], "c10")
    P.op("pool", lambda e: e.memset(eps_t[:], EPS), W=["eps"])
    eps_ap = eps_t[:]

    def rms_rows(ps_or_ap, R_in, nparts_tag):
        pass

    def final_out(u):
        pi = rot()
        for c in range(8):
            b = c % 2
            act(sq[:, b, :], xT[:, u, c, :], AF.Square, R=["x%d_%d" % (u, c)], W=["sq%d" % b])
            mm(pi, psb[pi][:], ones_r[:], sq[:, b, :], c == 0, c == 7, R=["ones", "sq%d" % b])
        act(rstd[:], psb[pi][:], AF.Sqrt, R=["ps%d" % pi, "eps"], W=["rstd"], scale=1.0 / D, bias=eps_ap)
        P.op("dve", ("reciprocal", dict(out=rstd[:], in_=rstd[:])), R=["rstd"], W=["rstd"])
        scrv = scr[:].rearrange("p a t -> p (a t)").rearrange("p (c q) -> p c q", q=128)
        for i in range(4):
            for c in range(8):
                P.op("dve", ("scalar_tensor_tensor", dict(
                    out=scrv[:, c, :], in0=xT[:, u, c, i * 128:(i + 1) * 128], scalar=fng[:, c:c + 1], in1=rstd[:, i * 128:(i + 1) * 128],
                    op0=ALU.mult, op1=ALU.mult)), R=["x%d_%d" % (u, c), "fng", "rstd"], W=["scr0", "scr1"])
            for c in range(8):
                pi = rot()
                tr(pi, psb[pi][:, 0:128], scrv[:, c, :], R=["scr0", "scr1"])
                if c % 2 == 0:
                    P.op("dve", ("tensor_copy", dict(out=outst[:, 0, c * 128:(c + 1) * 128], in_=psb[pi][:, 0:128])),
                         R=["ps%d" % pi], W=["tok"])
                else:
                    act(outst[:, 0, c * 128:(c + 1) * 128], psb[pi][:, 0:128], AF.Copy, R=["ps%d" % pi], W=["tok"])
            if u < 2:
                dst = yp[2 * u + i // 2, (i % 2) * 128:(i % 2) * 128 + 128, :]
            else:
                dst = ys[i * 128:(i + 1) * 128, :]
            P.dma("sp", dst, outst[:, 0, :], R=["tok"], W=[], slot="oy")


    for l in range(depth):
        lam_init = 0.8 - 0.6 * math.exp(-0.3 * l)
        lam_scale = 1.0 - lam_init
        ld(badaT[:], b_ada[l].rearrange("(c p) -> p c", p=128), ["badaT"], "l0", allow_slow_non_contiguous=True)
        ld(n1g[:], norm1_g[l].rearrange("(c p) -> p c", p=128), ["n1g"], "l1", allow_slow_non_contiguous=True)
        ld(n2g[:], norm2_g[l].rearrange("(c p) -> p c", p=128), ["n2g"], "l2", allow_slow_non_contiguous=True)
        for hh in range(2):
            ld(qg[hh * 64:(hh + 1) * 64, :], q_norm_g[l].rearrange("(p o) -> p o", o=1), ["qg"], "l3", allow_slow_non_contiguous=True)
            ld(kg[hh * 64:(hh + 1) * 64, :], k_norm_g[l].rearrange("(p o) -> p o", o=1), ["kg"], "l4", allow_slow_non_contiguous=True)
        for j in range(4):
            ld(lamv[:, j, :], lam_in[j][l].partition_broadcast(128), ["lamv"], "l5")
        ld(dng[:], diff_norm_g[l].partition_broadcast(128), ["dng"], "l6")
        ld(sgg[:], sgu_norm_g[l].partition_broadcast(128), ["sgg"], "l7")
        ld(wsp[:], w_spatial[l].rearrange("g q p -> q g p"), ["wsp"], "l8")
        ld(bsT[:], b_spatial[l].rearrange("g q -> q g"), ["bsT"], "l9", allow_slow_non_contiguous=True)
        ld(wf_r[:], w_fourier[l].rearrange("(k p) c -> p k c", p=128), ["wf"], "l10", q="pool")
        P.op("dve", ("tensor_tensor", dict(out=small[:, 0:32], in0=lamv[:, 0, :], in1=lamv[:, 1, :], op=ALU.mult)), R=["lamv"], W=["small"])
        P.op("dve", ("tensor_tensor", dict(out=small[:, 32:64], in0=lamv[:, 2, :], in1=lamv[:, 3, :], op=ALU.mult)), R=["lamv", "small"], W=["small"])
        P.op("dve", ("tensor_reduce", dict(out=neglam[:], in_=small[:, 0:32], axis=AX.X, op=ALU.add)), R=["small"], W=["neglam"])
        P.op("dve", ("tensor_reduce", dict(out=small[:, 0:1], in_=small[:, 32:64], axis=AX.X, op=ALU.add)), R=["small", "neglam"], W=["small"])
        act(neglam[:], neglam[:], AF.Exp, R=["neglam"], W=["neglam"])
        act(small[:, 1:2], small[:, 0:1], AF.Exp, R=["small"], W=["small"])
        P.op("dve", ("tensor_tensor", dict(out=neglam[:], in0=small[:, 1:2], in1=neglam[:], op=ALU.subtract)), R=["small", "neglam"], W=["neglam"])
        P.op("dve", ("tensor_scalar_add", dict(out=neglam[:], in0=neglam[:], scalar1=-lam_init)), R=["neglam"], W=["neglam"])
        P.op("dve", ("tensor_scalar_mul", dict(out=dng[:], in0=dng[:], scalar1=lam_scale)), R=["dng"], W=["dng"])
        for g in range(4):
            pi = rot()
            tr(pi, psb[pi][:, 0:128], wsp[:, g, :], R=["wsp"])
            P.op("dve", ("tensor_copy", dict(out=wspT[:, g, :], in_=psb[pi][:, 0:128])), R=["ps%d" % pi], W=["wspT"])
        for g in range(12):
            s0, s1 = w_group(2)
            pi = rot()
            for kc in range(8):
                sl_ = s0 if kc < 4 else s1
                mm(pi, psb[pi][0:2, :], scond[:, kc, :], slabs[:, sl_, kc % 4, :], kc == 0, kc == 7,
                   R=["slab%d" % sl_, "scond"])
            P.op("dve", ("tensor_copy", dict(out=rstd[0:2, :], in_=psb[pi][0:2, :])), R=["ps%d" % pi], W=["rstd"])
            for j in range(4):
                pj = rot()
                P.op("pe", ("transpose", dict(out=psb[pj][:, 0:2], in_=rstd[0:2, j * 128:(j + 1) * 128], identity=ident[0:2, 0:2])),
                     R=["rstd", "ident"], W=["ps%d" % pj])
                cc = g * 4 + j
                P.op("dve", ("tensor_scalar", dict(
                    out=modT[:, cc, :], in0=psb[pj][:, 0:2], scalar1=badaT[:, cc:cc + 1], scalar2=None, op0=ALU.add)),
                    R=["ps%d" % pj, "badaT"], W=["mod"])
        if l == 0:
            x_load()
        for ci in range(2):
            P.op("dve", ("scalar_tensor_tensor", dict(
                out=A1[:, :, ci], in0=modT[:, 8:16, ci], scalar=1.0, in1=n1g[:], op0=ALU.add, op1=ALU.mult)),
                R=["mod", "n1g"], W=["A1"])
            P.op("dve", ("scalar_tensor_tensor", dict(
                out=A2[:, :, ci], in0=modT[:, 32:40, ci], scalar=1.0, in1=n2g[:], op0=ALU.add, op1=ALU.mult)),
                R=["mod", "n2g"], W=["A2"])

        for u in units:
            sample = (u == 2)
            ci = 1 if sample else 0
            nseq = 1 if sample else 2
            sl = T // nseq
            norm_mod(u, ci, A1, 0, "A1")
            hR = ["h%d" % c for c in range(8)]
            TTv = TT[:]
            TT_tags = ["TT"]
            def w_in_groups(gs):
                for g in gs:
                    s0, s1 = w_group(2)

                    def wsl(kc, c0, c1):
                        s = s0 if kc < 4 else s1
                        return slabs[:, s, kc % 4, c0:c1], "slab%d" % s

                    def ftype(c0, dst_chunk, evac):
                        pi = rot()
                        for kc in range(8):
                            w_ap, wt = wsl(kc, c0, c0 + 128)
                            mm(pi, psb[pi][:], w_ap, hT[:, kc, :], kc == 0, kc == 7, R=[wt, "h%d" % kc])
                        evac(pi, dst_chunk)

                    def ttype(c0, ncols, toff, post=None):
                        for i in range(4):
                            pi = rot()
                            for kc in range(8):
                                w_ap, wt = wsl(kc, c0, c0 + ncols)
                                mm(pi, psb[pi][:, 0:ncols], hT[:, kc, i * 128:(i + 1) * 128], w_ap, kc == 0, kc == 7,
                                   R=[wt, "h%d" % kc])
                            dst = TTv[:, i, toff:toff + ncols]
                            if post is None:
                                P.op("dve", ("tensor_copy", dict(out=dst, in_=psb[pi][:, 0:ncols])),
                                     R=["ps%d" % pi], W=TT_tags)
                            else:
                                act(dst, psb[pi][:, 0:ncols], post, R=["ps%d" % pi], W=TT_tags)

                    def ev_copy_r(pi, ch):
                        act(bigr[:, ch, :], psb[pi][:], AF.Copy, R=["ps%d" % pi], W=[BT(ch)])

                    def ev_copy_f(pi, ch):
                        P.op("dve", ("tensor_copy", dict(out=bigr[:, ch, :], in_=psb[pi][:])), R=["ps%d" % pi], W=[BT(ch)])

                    if g == 0:
                        ftype(0, 0, ev_copy_r)
                        ftype(128, 1, ev_copy_r)
                        ftype(256, 2, ev_copy_f)
                        ftype(384, 3, ev_copy_f)
                    elif g == 1:
                        ftype(0, 4, ev_copy_f)
                        ttype(128, 128, 0)
                        ftype(256, 5, ev_copy_f if sample else ev_copy_r)
                        ftype(384, 6, ev_copy_f if sample else ev_copy_r)
                    elif g == 2:
                        ftype(0, 7, ev_copy_f if sample else ev_copy_r)
                        ftype(128, 8, ev_copy_f if sample else ev_copy_r)
                        ttype(256, 256, 128)
                    else:
                        ttype(0, 256, 384, post=AF.Gelu_apprx_tanh)
                        ttype(256, 256, 640, post=AF.Gelu_apprx_tanh)

            def qk_norm(chs):
                for ch in chs:
                    gvec, gt = (kg, "kg") if ch == 4 else (qg, "qg")
                    act(sq[:, 0, :], big[:, ch, :], AF.Square, R=[BT(ch)], W=["sq0"])
                    pi = rot()
                    mm(pi, psb[pi][:], blk64_r[:], sq[:, 0, :], True, True, R=["blk64", "sq0"])
                    act(rstd[:], psb[pi][:], AF.Sqrt, R=["ps%d" % pi, "eps"], W=["rstd"], scale=1.0 / 64, bias=eps_ap)
                    P.op("dve", ("reciprocal", dict(out=rstd[:], in_=rstd[:])), R=["rstd"], W=["rstd"])
                    P.op("dve", ("scalar_tensor_tensor", dict(
                        out=bigr[:, ch, :], in0=big[:, ch, :], scalar=gvec[:, 0:1], in1=rstd[:], op0=ALU.mult, op1=ALU.mult)),
                        R=[BT(ch), gt, "rstd"], W=[BT(ch)])

            def do_rope(chs):
                for ch in chs:
                    isg = ch <= 4
                    rm = rotg_r if isg else rotd_r
                    rt = "rotg" if isg else "rotd"
                    cosi, sini = (0, 1) if isg else (2, 3)
                    act(sq[:, 1, :], big[:, ch, :], AF.Copy, R=[BT(ch)], W=["sq1"])
                    pi = rot()
                    mm(pi, psb[pi][:], rm[:], sq[:, 1, :], True, True, R=[rt, "sq1"])
                    P.op("dve", ("tensor_tensor", dict(out=scr[:, 0, :], in0=psb[pi][:], in1=rope[:, sini, :], op=ALU.mult)),
                         R=["ps%d" % pi, "rope"], W=["scr0"])
                    P.op("pool", ("tensor_tensor", dict(out=scr[:, 1, :], in0=big[:, ch, :], in1=rope[:, cosi, :], op=ALU.mult)),
                         R=[BT(ch), "rope"], W=["scr1"])
                    P.op("dve", ("tensor_tensor", dict(out=bigr[:, ch, :], in0=scr[:, 0, :], in1=scr[:, 1, :], op=ALU.add)),
                         R=["scr0", "scr1"], W=[BT(ch)])

            def emit_kv():
                if True:
                    for i in range(4):
                        for (ch, off) in ((4, 0), (7, 128), (8, 256)):
                            pi = rot()
                            tr(pi, psb[pi][:, 0:128], big[:, ch, i * 128:(i + 1) * 128], R=[BT(ch)])
                            P.op("dve", ("tensor_copy", dict(out=tokst[:, i, off:off + 128], in_=psb[pi][:, 0:128])),
                                 R=["ps%d" % pi], W=["KT"])
                    sq0 = 2 * u
                    for s_ in range(2):
                        P.dma("sp", ngk[sq0 + s_, l].rearrange("(j p) f -> p j f", p=128),
                              tokst[:, 2 * s_:2 * s_ + 2, 0:128].bitcast(F32), R=["KT"], W=[], slot="o_ngk%d" % s_)
                        P.dma("sp", ndk[sq0 + s_, l].rearrange("(j p) f -> p j f", p=128),
                              tokst[:, 2 * s_:2 * s_ + 2, 128:384].bitcast(F32), R=["KT"], W=[], slot="o_ndk%d" % s_)
                        P.dma("sp", ngv[sq0 + s_, l].rearrange("(j p) f -> p j f", p=128),
                              TTv[:, 2 * s_:2 * s_ + 2, 0:128], R=TT_tags, W=[], slot="o_ngv%d" % s_)
                        P.dma("sp", ndv[sq0 + s_, l].rearrange("(j p) f -> p j f", p=128),
                              TTv[:, 2 * s_:2 * s_ + 2, 128:384], R=TT_tags, W=[], slot="o_ndv%d" % s_)

            def dv_rms():
                for i in range(4):
                    dv_ap = TTv[:, i, 640:896]
                    act(scr[:, 0, 0:256], dv_ap, AF.Square, R=TT_tags, W=["scr0"], accum_out=small[:, 8 + i:9 + i])
                P.op("act", ("activation", dict(out=small[:, 12:16], in_=small[:, 8:12], func=AF.Sqrt, scale=1.0 / 256, bias=eps_ap)),
                     R=["scr0", "eps"], W=["small2"])
                P.op("dve", ("reciprocal", dict(out=small[:, 12:16], in_=small[:, 12:16])), R=["small2"], W=["small2"])
                for i in range(4):
                    dv_ap = TTv[:, i, 640:896]
                    P.op("dve", ("scalar_tensor_tensor", dict(
                        out=DvN[:, i, :], in0=dv_ap, scalar=small[:, 12 + i:13 + i], in1=sgg[:], op0=ALU.mult, op1=ALU.mult)),
                        R=TT_tags + ["small2", "sgg"], W=["sq0", "sq1"])


            def chan_dft():
                for i in range(4):
                    pi = rot()
                    for j in range(2):
                        mm(pi, psb[pi][:, j * 256:(j + 1) * 256], bigr[:, j, i * 128:(i + 1) * 128], ccss_r[:], True, True,
                           R=[BT(j), "ccss"])
                    act(xcs[:, i, :], psb[pi][:], AF.Copy, R=["ps%d" % pi], W=XT(i))

            def fourier_prompt():
                for s in range(2):
                    for j in range(2):
                        pi = rot()
                        n = 0
                        for i2 in range(2):
                            i = s * 2 + i2
                            for part in range(2):
                                mm(pi, psb[pi][:, 0:256], xcs[:, i, j * 256 + part * 128: j * 256 + part * 128 + 128],
                                   dft256[:, part, i2, :], n == 0, n == 3, R=XT(i) + ["dft256"])
                                n += 1
                        act(bigr[:, 15 + j, s * 256:(s + 1) * 256], psb[pi][:, 0:256], AF.Copy, R=["ps%d" % pi], W=[BT(15 + j)])

            def exchange_part(p_):
                bz = bounce[p_][l]
                if p_ == "g":
                    P.dma("sp", bz[0:128, :], big[:, 4, :], R=[BT(4)], W=["bounce_g"], slot="bzg")
                    P.dma("sp", bz[128:256, :].rearrange("r (a c) -> (r a) c", c=128).rearrange("(j p) c -> p j c", p=128),
                          TTv[:, :, 0:128], R=TT_tags, W=["bounce_g"], slot="bzg")
                else:
                    P.dma("sp", bz[0:256, :].rearrange("(c p) t -> p c t", p=128), big[:, 7:9, :], R=[BT(7), BT(8)], W=["bounce_d"], slot="bzd")
                    P.dma("sp", bz[256:512, :].rearrange("r (a c) -> (r a) c", c=256).rearrange("(j p) c -> p j c", p=128),
                          TTv[:, :, 128:384], R=TT_tags, W=["bounce_d"], slot="bzd")
                w_prefetch()
                P.coll(lambda e, l=l, p_=p_: e.collective_compute(
                    "AllGather", ALU.bypass, replica_groups=[[0, 1, 2, 3], [4, 5, 6, 7]],
                    ins=[bounce[p_][l].opt()], outs=[gath[p_][l].opt()]), R=["bounce_" + p_], W=["gath_" + p_], slot="cc_" + p_)

            def exchange_x():
                bx_ = bounce["x"][l]
                P.dma("sp", bx_.rearrange("(j p) c -> p j c", p=128), xcs[:].bitcast(F32), R=XT(0) + XT(1) + XT(2) + XT(3), W=["bounce_x"], slot="bzx")
                w_prefetch()
                P.coll(lambda e, l=l: e.collective_compute(
                    "AllGather", ALU.bypass, replica_groups=[[0, 1, 2, 3], [4, 5, 6, 7]],
                    ins=[bounce["x"][l].opt()], outs=[gath["x"][l].opt()]), R=["bounce_x"], W=["gath_x"], slot="cc_x")

            def fourier_sample():
                acc = [rot(), rot()]
                n = 0
                for s4 in range(4):
                    sx, sc, ss = w_group(3)
                    for i in range(4):
                        for j in range(2):
                            mm(acc[j], psb[acc[j]][:], slabs[:, sx, i, j * 256:j * 256 + 128], slabs[:, sc, i, :],
                               n == 0, False, R=["slab%d" % sx, "slab%d" % sc])
                            mm(acc[j], psb[acc[j]][:], slabs[:, sx, i, j * 256 + 128:j * 256 + 256], slabs[:, ss, i, :],
                               False, n == 15, R=["slab%d" % sx, "slab%d" % ss])
                        n += 1
                for j in range(2):
                    act(xcs[:, j, :], psb[acc[j]][:], AF.Copy, R=["ps%d" % acc[j]], W=XT(j))

            def out_a(ysrc, ytags):
                for c in range(2):
                    pi = rot()
                    for kc in range(2):
                        mm(pi, psb[pi][:], wf_r[:, kc, c * 128:(c + 1) * 128], ysrc(kc), kc == 0, kc == 1, R=["wf"] + (ytags[kc] if isinstance(ytags[kc], list) else [ytags[kc]]))
                    act(bigr[:, 9 + c, :], psb[pi][:], AF.Copy, R=["ps%d" % pi], W=[BT(9 + c)])

            def load_keys_sample(kind, hc):
                if kind == "g":
                    gp, gtag, grows = gath["g"][l], "gath_g", 256
                    kcache, kcols, krow0 = cgk[l], slice(0, 128), 0
                    vcache, vcols, vrow0, vw = cgv[l], slice(0, 128), 128, 128
                else:
                    gp, gtag, grows = gath["d"][l], "gath_d", 512
                    kcache, kcols, krow0 = cdk[l], slice(hc * 128, hc * 128 + 128), hc * 128
                    vcache, vcols, vrow0, vw = cdv[l], slice(hc * 128, hc * 128 + 128), 256, 256
                gl = gp.rearrange("(r x) c -> x r c", x=grows)
                w_prefetch()
                P.dma("sp", cstage[:, :, :], kcache[:, kcols].rearrange("(j p) c -> p j c", p=128), R=[], W=["scr1"], slot="cst")
                for j in range(4):
                    pi = rot()
                    tr(pi, psb[pi][:, 0:128], cstage[:, j, :], R=["scr1"])
                    act(KT[:, j * 128:(j + 1) * 128], psb[pi][:, 0:128], AF.Copy, R=["ps%d" % pi], W=["KT"])
                P.dma("pool", KT[:, 512:2560].rearrange("p (r t) -> p r t", r=4), gl[krow0:krow0 + 128, :, :],
                      R=[gtag], W=["KT"], slot="ktl")
                P.dma("pool", VA[:, 0:4, 1:129], vcache[:, vcols].rearrange("(j p) d -> p j d", p=128),
                      R=[], W=["VA"] + VAP, slot="val")
                for r in range(4):
                    src = gp[r * grows + vrow0: r * grows + vrow0 + vw, :].rearrange("x (a c) -> (x a) c", c=vw)
                    src = src[:, vcols].rearrange("(j p) d -> p j d", p=128)
                    P.dma("pool", VA[:, 4 + 4 * r: 8 + 4 * r, 1:129], src, R=[gtag], W=["VA"] + VAP, slot="val")

            def attention(gqa_preloaded=False, parts=("g", "d")):
                nk = 2560 if sample else 256
                nkt = nk // 128
                nq = sl
                nqt = nq // 128

                gctr = [0]

                def attn_group(heads, s):
                    nh = len(heads)
                    g_ = gctr[0]
                    gctr[0] += 1
                    if sample:
                        qoff, vb, vtag = 0, 0, "VA"
                        qtag = lambda hi: XT(hi)
                    else:
                        qoff, vb, vtag = (g_ % 2) * 256, (g_ % 8) * 2, VAP[g_ % 8]
                        qtag = lambda hi: [XT(hi)[g_ % 2]]
                    for hi, hd in enumerate(heads):
                        act(xcs[:, hi, qoff:qoff + nq], big[:, hd["qch"], s * sl: s * sl + nq], AF.Copy,
                            R=[BT(hd["qch"]), "maskc"], W=qtag(hi), scale=maskc[:, hd["mcol"]:hd["mcol"] + 1])
                    kpb = 2 if (nkt == 2 and nq * 2 <= 512) else 1
                    steps = [(hi, kt) for hi in range(nh) for kt in range(0, nkt, kpb)]
                    pending = []
                    sbank = {}

                    def issue_s(i):
                        hi, kt = steps[i]
                        hd = heads[hi]
                        pi = rot()
                        for kk in range(kpb):
                            mm(pi, psb[pi][:, kk * nq:(kk + 1) * nq], hd["kfn"](kt + kk), xcs[:, hi, qoff:qoff + nq], True, True,
                               R=[hd["ktag"]] + qtag(hi), skip_group_check=True)
                        sbank[i] = pi

                    issue_s(0)
                    for i in range(len(steps)):
                        if i + 1 < len(steps):
                            issue_s(i + 1)
                        hi, kt = steps[i]
                        hd = heads[hi]
                        pi = sbank[i]
                        b = i % 2
                        act(PT[:, b, 0:kpb * nq], psb[pi][:, 0:kpb * nq], AF.Exp, R=["ps%d" % pi], W=["PT%d" % b], scale=hd["scale"])
                        ob = hd["obank"]
                        for kk in range(kpb):
                            for qt in range(nqt):
                                ktt = kt + kk
                                mm(ob, psb[ob][:, qt * 65:(qt + 1) * 65], PT[:, b, kk * nq + qt * 128: kk * nq + (qt + 1) * 128],
                                   VA[:, vb + ktt, 65 * hd["vslot"]:65 * hd["vslot"] + 65], ktt == 0 and qt == 0, ktt == nkt - 1 and qt == nqt - 1,
                                   R=["PT%d" % b, vtag, "VAones"], skip_group_check=True)
                        for pd in [p for p in pending if p[0] <= i]:
                            pending.remove(pd)
                            pd[1]()
                        if kt + kpb - 1 == nkt - 1 and hd["post"] is not None:
                            p1, p2 = hd["post"]
                            p1()
                            if p2 is not None:
                                pending.append((i + 3, p2))
                    for pd in pending:
                        pd[1]()

                def gqa_post(h, s, ob):
                    def p1():
                        ov = psb[ob][:, 0:nqt * 65].rearrange("p (q c) -> p q c", c=65)
                        c0 = 16 + 4 * (h % 2)
                        d0, v0 = (0, 1) if (h // 2) == 0 else (64, 0)
                        P.op("dve", ("reciprocal", dict(out=small[:, c0:c0 + nqt], in_=ov[:, :, d0])), R=["ps%d" % ob], W=["small3"])
                        P.op("dve", ("tensor_tensor", dict(
                            out=tok[:, s * nqt:(s + 1) * nqt, h * 64:(h + 1) * 64], in0=ov[:, :, v0:v0 + 64],
                            in1=small[:, c0:c0 + nqt].unsqueeze(2).broadcast_to([128, nqt, 64]), op=ALU.mult)),
                            R=["ps%d" % ob, "small3"], W=["tok"])
                    return p1, None

                def diff_post(h, s, ob1, ob2):
                    par = h % 2
                    d0, v0 = (0, 1) if (h % 2) == 0 else (64, 0)
                    A = scr[:, par, 0:256].rearrange("p (q c) -> p q c", c=64)[:, 0:nqt, :]
                    Bm = scr[:, par, 256:512].rearrange("p (q c) -> p q c", c=64)[:, 0:nqt, :]
                    stag = "scr%d" % par
                    c1, c2, c3 = 16 + 12 * par, 20 + 12 * par, 24 + 12 * par
                    mtag = "smallp%d" % par

                    def p1():
                        o1 = psb[ob1][:, 0:nqt * 65].rearrange("p (q c) -> p q c", c=65)
                        o2 = psb[ob2][:, 0:nqt * 65].rearrange("p (q c) -> p q c", c=65)
                        P.op("dve", ("reciprocal", dict(out=small[:, c1:c1 + nqt], in_=o1[:, :, d0])), R=["ps%d" % ob1], W=[mtag])
                        P.op("dve", ("reciprocal", dict(out=small[:, c2:c2 + nqt], in_=o2[:, :, d0])), R=["ps%d" % ob2, mtag], W=[mtag])
                        P.op("dve", ("tensor_scalar", dict(out=small[:, c2:c2 + nqt], in0=small[:, c2:c2 + nqt], scalar1=neglam[:, 0:1],
                                                              scalar2=None, op0=ALU.mult)), R=[mtag, "neglam"], W=[mtag])
                        P.op("dve", ("tensor_tensor", dict(out=A, in0=o1[:, :, v0:v0 + 64],
                                                              in1=small[:, c1:c1 + nqt].unsqueeze(2).broadcast_to([128, nqt, 64]), op=ALU.mult)),
                             R=["ps%d" % ob1, mtag], W=[stag])
                        P.op("dve", ("tensor_tensor", dict(out=Bm, in0=o2[:, :, v0:v0 + 64],
                                                              in1=small[:, c2:c2 + nqt].unsqueeze(2).broadcast_to([128, nqt, 64]), op=ALU.mult)),
                             R=["ps%d" % ob2, mtag, stag], W=[stag])
                        P.op("dve", ("tensor_tensor", dict(out=A, in0=A, in1=Bm, op=ALU.add)), R=[stag], W=[stag])
                        P.op("dve", ("tensor_tensor", dict(out=Bm, in0=A, in1=A, op=ALU.mult)), R=[stag], W=[stag])
                        P.op("dve", ("tensor_reduce", dict(out=small[:, c3:c3 + nqt], in_=Bm, axis=AX.X, op=ALU.add)), R=[stag, mtag], W=[mtag])

                    def p2():
                        P.op("act", ("activation", dict(out=small[:, c3:c3 + nqt], in_=small[:, c3:c3 + nqt], func=AF.Ln, scale=1.0 / 64, bias=eps_ap)),
                             R=[mtag, "eps"], W=[mtag])
                        P.op("act", ("activation", dict(out=small[:, c3:c3 + nqt], in_=small[:, c3:c3 + nqt], func=AF.Exp, scale=-0.5)),
                             R=[mtag], W=[mtag])
                        P.op("dve", ("tensor_tensor", dict(out=A, in0=A,
                                                              in1=small[:, c3:c3 + nqt].unsqueeze(2).broadcast_to([128, nqt, 64]), op=ALU.mult)),
                             R=[stag, mtag], W=[stag])
                        P.op("dve", ("tensor_tensor", dict(out=tok[:, s * nqt:(s + 1) * nqt, h * 64:(h + 1) * 64], in0=A,
                                                              in1=dng[:].unsqueeze(1).broadcast_to([128, nqt, 64]), op=ALU.mult)),
                             R=[stag, "dng"], W=["tok"])
                    return p1, p2

                if "g" in parts:
                    for s in range(nseq):
                        if sample:
                            if not gqa_preloaded:
                                load_keys_sample("g", 0)
                            kfn = lambda kt: KT[:, kt * 128:(kt + 1) * 128]
                            ktag = "KT"
                        else:
                            for i2 in range(2):
                                i = s * 2 + i2
                                P.op("dve", ("tensor_copy", dict(
                                    out=VA[:, (gctr[0] % 8) * 2 + i2, 1:129], in_=TTv[:, i, 0:128])),
                                    R=TT_tags, W=[VAP[gctr[0] % 8], "VA"])
                            kfn = lambda kt, s=s: bigr[:, 4, s * 256 + kt * 128: s * 256 + (kt + 1) * 128]
                            ktag = BT(4)
                        heads = []
                        for h in range(4):
                            heads.append(dict(qch=2 + (h % 2), mcol=h // 2, vslot=h // 2, scale=0.125, obank=h % 2,
                                              kfn=kfn, ktag=ktag, post=gqa_post(h, s, h % 2)))
                        attn_group(heads, s)
                    for i in range(4):
                        for c in range(2):
                            pi = rot()
                            tr(pi, psb[pi][:, 0:128], tok[:, i, c * 128:(c + 1) * 128], R=["tok"])
                            act(bigr[:, 11 + c, i * 128:(i + 1) * 128], psb[pi][:, 0:128], AF.Copy, R=["ps%d" % pi], W=[BT(11 + c)])

                if "d" in parts:
                    dscale = 1.0 / math.sqrt(32.0)
                    for s in range(nseq):
                        for hc in range(2):
                            if sample:
                                load_keys_sample("d", hc)
                                kfn = lambda kt: KT[:, kt * 128:(kt + 1) * 128]
                                ktag = "KT"
                            else:
                                for i2 in range(2):
                                    i = s * 2 + i2
                                    P.op("dve", ("tensor_copy", dict(
                                        out=VA[:, (gctr[0] % 8) * 2 + i2, 1:129],
                                        in_=TTv[:, i, 128 + hc * 128: 256 + hc * 128])),
                                        R=TT_tags, W=[VAP[gctr[0] % 8], "VA"])
                                kfn = lambda kt, s=s, hc=hc: bigr[:, 7 + hc, s * 256 + kt * 128: s * 256 + (kt + 1) * 128]
                                ktag = BT(7 + hc)
                            heads = []
                            for hh in range(2):
                                h = hc * 2 + hh
                                for m in range(2):
                                    heads.append(dict(qch=5 + hc, mcol=2 + hh * 2 + m, vslot=hh, scale=dscale, obank=2 * hh + m,
                                                      kfn=kfn, ktag=ktag,
                                                      post=(diff_post(h, s, 2 * hh, 2 * hh + 1) if m == 1 else None)))
                            attn_group(heads, s)
                    for i in range(4):
                        for c in range(2):
                            pi = rot()
                            tr(pi, psb[pi][:, 0:128], tok[:, i, c * 128:(c + 1) * 128], R=["tok"])
                            act(bigr[:, 13 + c, i * 128:(i + 1) * 128], psb[pi][:, 0:128], AF.Copy, R=["ps%d" % pi], W=[BT(13 + c)])


            def d_branch():
                for i in range(4):
                    pi = rot()
                    for g in range(4):
                        mm(pi, psb[pi][:, g * 64:(g + 1) * 64], wspT[:, g, :], DvN[:, i, g * 64:(g + 1) * 64],
                           True, True, R=["wspT", "sq0", "sq1"])
                    for g in range(4):
                        P.op("dve", ("scalar_tensor_tensor", dict(
                            out=tok[:, i, g * 64:(g + 1) * 64], in0=psb[pi][:, g * 64:(g + 1) * 64], scalar=bsT[:, g:g + 1],
                            in1=TTv[:, i, 384 + g * 64: 448 + g * 64], op0=ALU.add, op1=ALU.mult)),
                            R=["ps%d" % pi, "bsT"] + TT_tags, W=["tok"])
                for i in range(4):
                    for c in range(2):
                        pi = rot()
                        tr(pi, psb[pi][:, 0:128], tok[:, i, c * 128:(c + 1) * 128], R=["tok"])
                        act(bigr[:, 15 + c, i * 128:(i + 1) * 128], psb[pi][:, 0:128], AF.Copy, R=["ps%d" % pi], W=[BT(15 + c)])


            if sample:
                w_in_groups((1, 0))
                qk_norm((4, 2, 3))
                do_rope((4, 2, 3))
                exchange_part("g")
                w_in_groups((2,))
                do_rope((7, 8, 5, 6))
                load_keys_sample("g", 0)
                exchange_part("d")
                if stop_after == "coll":
                    break
                w_in_groups((3,))
                chan_dft()
                exchange_x()
                dv_rms()
                d_branch()
                attention(gqa_preloaded=True, parts=("g",))
                attention(parts=("d",))
                coll_done.add(l)
                fourier_sample()
                out_a(lambda kc: xcs[:, kc, :], [XT(0), XT(1)])
            else:
                w_in_groups((0, 1, 2, 3))
                qk_norm((2, 3, 4))
                emit_kv()
                dv_rms()
                chan_dft()
                fourier_prompt()
                out_a(lambda kc: bigr[:, 15 + kc, :], [BT(15), BT(16)])
                attention()
                d_branch()

            for cdbg in range(8):
                dump("br%d" % cdbg, big[:, 9 + cdbg, :], [BT(9 + cdbg)])
            for n in range(4):
                for cg in range(2):
                    s0, s1, sbr = w_group(3)
                    for j in range(4):
                        for kc in range(4):
                            mm(j, psb[j][:], slabs[:, s0, kc, j * 128:(j + 1) * 128], hT[:, kc, :], kc == 0, False,
                               R=["slab%d" % s0, "h%d" % kc])
                    for j in range(4):
                        c = cg * 4 + j
                        for kc in range(4, 8):
                            mm(j, psb[j][:], slabs[:, s1, kc - 4, j * 128:(j + 1) * 128], hT[:, kc, :], False, kc == 7,
                               R=["slab%d" % s1, "h%d" % kc])
                        pb = rot()
                        for kc in range(2):
                            mm(pb, psb[pb][:], slabs[:, sbr, kc, j * 128:(j + 1) * 128], bigr[:, 9 + 2 * n + kc, :], kc == 0, kc == 1,
                               R=["slab%d" % sbr, BT(9 + 2 * n + kc)])
                        b = (n * 8 + c) % 2
                        act(scr[:, b, :], psb[j][:], AF.Sigmoid, R=["ps%d" % j], W=["scr%d" % b])
                        if n == 0:
                            P.op("dve", ("tensor_tensor", dict(out=bigr[:, c, :], in0=scr[:, b, :], in1=psb[pb][:], op=ALU.mult)),
                                 R=["scr%d" % b, "ps%d" % pb], W=[BT(c)])
                        else:
                            P.op("dve", ("tensor_tensor", dict(out=scr[:, b, :], in0=scr[:, b, :], in1=psb[pb][:], op=ALU.mult)),
                                 R=["scr%d" % b, "ps%d" % pb], W=["scr%d" % b])
                            P.op("dve", ("tensor_tensor", dict(out=bigr[:, c, :], in0=big[:, c, :], in1=scr[:, b, :], op=ALU.add)),
                                 R=["scr%d" % b, BT(c)], W=[BT(c)])
            dump("merged0", big[:, 0, :], [BT(0)])
            dump("merged7", big[:, 7, :], [BT(7)])
            for cg in range(2):
                s0, s1 = w_group(2)
                for j in range(4):
                    for kc in range(4):
                        mm(4 + j, psb[4 + j][:], slabs[:, s0, kc, j * 128:(j + 1) * 128], bigr[:, kc, :], kc == 0, False,
                           R=["slab%d" % s0, BT(kc)])
                for j in range(4):
                    c = cg * 4 + j
                    pi = 4 + j
                    for kc in range(4, 8):
                        mm(pi, psb[pi][:], slabs[:, s1, kc - 4, j * 128:(j + 1) * 128], bigr[:, kc, :], False, kc == 7,
                           R=["slab%d" % s1, BT(kc)])
                    P.op("dve", ("scalar_tensor_tensor", dict(
                        out=xT[:, u, c, :], in0=psb[pi][:], scalar=modT[:, 16 + c, ci:ci + 1], in1=xT[:, u, c, :],
                        op0=ALU.mult, op1=ALU.add)), R=["ps%d" % pi, "mod", "x%d_%d" % (u, c)], W=["x%d_%d" % (u, c)])
            dump("x1_c0", xT[:, u, 0, :], ["x%d_0" % u])
            norm_mod(u, ci, A2, 24, "A2")
            for half in range(2):
                for g in range(4):
                    s0, s1 = w_group(2)
                    for j in range(4):
                        for kc in range(4):
                            mm(4 + j, psb[4 + j][:], slabs[:, s0, kc, j * 128:(j + 1) * 128], hT[:, kc, :], kc == 0, False,
                               R=["slab%d" % s0, "h%d" % kc])
                    for j in range(4):
                        fc = g * 4 + j
                        pi = 4 + j
                        for kc in range(4, 8):
                            mm(pi, psb[pi][:], slabs[:, s1, kc - 4, j * 128:(j + 1) * 128], hT[:, kc, :], False, kc == 7,
                               R=["slab%d" % s1, "h%d" % kc])
                        if RELU2_DVE:
                            P.op("dve", ("scalar_tensor_tensor", dict(
                                out=bigr[:, fc, :], in0=psb[pi][:], scalar=0.0, in1=psb[pi][:], op0=ALU.max, op1=ALU.mult)),
                                R=["ps%d" % pi], W=[BT(fc)])
                        else:
                            b = fc % 2
                            act(scr[:, b, :], psb[pi][:], AF.Relu, R=["ps%d" % pi], W=["scr%d" % b])
                            P.op("dve", ("tensor_tensor", dict(out=bigr[:, fc, :], in0=scr[:, b, :], in1=scr[:, b, :], op=ALU.mult)),
                                 R=["scr%d" % b], W=[BT(fc)])
                for cg in range(2):
                    for kq in range(4):
                        s, = w_group(1)
                        for j in range(4):
                            for kc in range(4):
                                mm(j, psb[j][:], slabs[:, s, kc, j * 128:(j + 1) * 128], bigr[:, kq * 4 + kc, :],
                                   kq == 0 and kc == 0, kq == 3 and kc == 3, R=["slab%d" % s, BT(kq * 4 + kc)])
                    for j in range(4):
                        c = cg * 4 + j
                        P.op("dve", ("scalar_tensor_tensor", dict(
                            out=xT[:, u, c, :], in0=psb[j][:], scalar=modT[:, 40 + c, ci:ci + 1], in1=xT[:, u, c, :],
                            op0=ALU.mult, op1=ALU.add)), R=["ps%d" % j, "mod", "x%d_%d" % (u, c)], W=["x%d_%d" % (u, c)])
            if l == depth - 1 and stop_after is None:
                final_out(u)

    assert stop_after is not None or wstate["used"] == len(wq), (wstate, len(wq))
    P.emit()
    return nc


def _consts(q):
    c = {}
    c["k_ident"] = np.eye(128, dtype=np.float32)
    c["k_ones"] = np.ones((128, 128), np.float32)
    blk = np.zeros((128, 128), np.float32)
    blk[:64, :64] = 1
    blk[64:, 64:] = 1
    c["k_blk64"] = blk
    n = np.arange(64)
    ang = 2 * np.pi * np.outer(n, n) / 64
    C64 = np.cos(ang) / 8.0
    S64 = np.sin(ang) / 8.0
    cc = np.zeros((128, 256))
    cc[:64, 0:64] = C64
    cc[64:, 64:128] = C64
    cc[:64, 128:192] = S64
    cc[64:, 192:256] = S64
    c["k_ccss"] = cc.astype(np.float32)

    def rotm(hd):
        half = hd // 2
        R = np.zeros((128, 128), np.float32)
        for m in range(128):
            if m % hd < half:
                R[m + half, m] = -1.0
            else:
                R[m - half, m] = 1.0
        return R
    mk = np.zeros((128, 6), np.float32)
    mk[0:64, 0] = 1
    mk[64:128, 1] = 1
    for j in range(4):
        mk[32 * j:32 * j + 32, 2 + j] = 1
    c["k_mask"] = mk
    c["k_rotg"] = rotm(64)
    c["k_rotd"] = rotm(32)
    pos = q * 512 + np.arange(512)
    row = (pos // 64).astype(np.float64)
    col = (pos % 64).astype(np.float64)

    def tab(dim):
        quarter = dim // 4
        inv = 10000.0 ** (-np.arange(quarter, dtype=np.float32) / quarter)
        inv = inv.astype(np.float32)
        ang = np.concatenate([row[:, None].astype(np.float32) * inv, col[:, None].astype(np.float32) * inv], axis=-1)
        ang = ang.astype(np.float32)
        cos = np.cos(ang).astype(np.float32)
        sin = np.sin(ang).astype(np.float32)
        half = dim // 2
        idx = np.arange(128) % half
        return cos[:, idx].T.copy(), sin[:, idx].T.copy()
    cg, sg = tab(64)
    cd, sd = tab(32)
    c["k_rope"] = np.stack([cg, sg, cd, sd]).astype(np.float32)
    t = np.arange(256)
    a = 2 * np.pi * np.outer(t, t) / 256
    c["k_dft256"] = np.stack([np.cos(a) / 16.0, -np.sin(a) / 16.0]).astype(np.float32)
    tt = np.arange(2048, dtype=np.float64)
    a = 2 * np.pi * np.outer(tt, pos.astype(np.float64)) / 2048
    c["k_dftc"] = (np.cos(a) / math.sqrt(2048.0)).astype(np.float32)
    c["k_dfts"] = (-np.sin(a) / math.sqrt(2048.0)).astype(np.float32)
    return c


_NC_CACHE = {}
import os
import json
_DBG = {k: (tuple(v) if isinstance(v, list) else v) for k, v in json.loads(os.environ.get("KDBG", "{}")).items()}
_DBG_RUN = {}


def kernel(**inputs):
    inp = {k: np.ascontiguousarray(np.asarray(v)) for k, v in inputs.items()}
    if "nc" not in _NC_CACHE:
        _NC_CACHE["nc"] = build_nc(**_DBG)
    nc = _NC_CACHE["nc"]
    shared = ["c_ctx", "w_ada", "b_ada", "norm1_g", "norm2_g", "w_in", "w_fourier", "q_norm_g", "k_norm_g",
              "lambda_q1", "lambda_k1", "lambda_q2", "lambda_k2", "diff_norm_g", "sgu_norm_g", "w_spatial",
              "b_spatial", "w_gate", "w_branch", "w_out", "w_mlp1", "w_mlp2", "final_norm_g"]
    in_maps = []
    for core in range(8):
        b, q = core // 4, core % 4
        m = {k: inp[k] for k in shared}
        m["xp"] = inp["x_prompt"][4 * core:4 * core + 4]
        m["xs"] = inp["x_sample"][b, q * 512:(q + 1) * 512]
        m["cm"] = inp["c"][b]
        m["cgk"] = inp["cache_gqa_k"][b].reshape(DEPTH, 512, 128)
        m["cgv"] = inp["cache_gqa_v"][b].reshape(DEPTH, 512, 128)
        m["cdk"] = inp["cache_diff_k"][b].reshape(DEPTH, 512, 256)
        m["cdv"] = inp["cache_diff_v"][b].reshape(DEPTH, 512, 256)
        m.update(_consts(q))
        in_maps.append({k: np.ascontiguousarray(v) for k, v in m.items()})
    ncores = _DBG_RUN.get("cores", 8)
    res = run_bass_kernel_spmd(nc, in_maps[:ncores], core_ids=list(range(ncores)))
    R = list(res.results)
    _DBG_RUN["results"] = R
    _DBG_RUN["names"] = getattr(nc, "_dbg_names", [])
    while len(R) < 8:
        R.append(R[0])
    y_prompt = np.concatenate([R[c]["yp"] for c in range(8)], axis=0)
    y_sample = np.stack([np.concatenate([R[b * 4 + q]["ys"] for q in range(4)], axis=0) for b in range(2)], axis=0)
    ngk = np.concatenate([R[c]["ngk"] for c in range(8)], axis=0).reshape(32, DEPTH, 256, 2, 64)
    ngv = np.concatenate([R[c]["ngv"] for c in range(8)], axis=0).reshape(32, DEPTH, 256, 2, 64)
    ndk = np.concatenate([R[c]["ndk"] for c in range(8)], axis=0).reshape(32, DEPTH, 256, 4, 2, 32)
    ndv = np.concatenate([R[c]["ndv"] for c in range(8)], axis=0).reshape(32, DEPTH, 256, 4, 64)
    return (y_prompt.astype(np.float32), y_sample.astype(np.float32), ngk.astype(np.float32),
            ngv.astype(np.float32), ndk.astype(np.float32), ndv.astype(np.float32))
```

```python
import math
import numpy as np
import concourse.bass as bass
import concourse.mybir as mybir
from concourse.bass_utils import run_bass_kernel_spmd

F32 = mybir.dt.float32
F32R = mybir.dt.float32r
BF16 = mybir.dt.bfloat16
AF = mybir.ActivationFunctionType
ALU = mybir.AluOpType
AX = mybir.AxisListType

D = 1024
DEPTH = 4
T = 512
NU = 3
EPS = 1e-6
GROWS = 1280
RELU2_DVE = False


class Prog:
    def __init__(self, nc):
        self.nc = nc
        self.ops = {e: [] for e in ("pe", "act", "dve", "pool", "sp")}
        self.cnt = {e: 0 for e in self.ops}
        self.esem = {e: nc.alloc_semaphore("sem_" + e) for e in ("pe", "act", "dve", "pool")}
        self.slot_sem = {}
        self.slot_cnt = {}
        self.lastw = {}
        self.readers = {}
        self.waited = {e: {} for e in self.ops}
        self.final = []

    def _deps(self, eng, R, W, pe_acc=False):
        deps = {}

        def add(ev):
            if ev is None:
                return
            k, v, src, _sem = ev
            if pe_acc and src == "pe" and eng == "pe":
                return
            if k not in deps or deps[k][0] < v:
                deps[k] = (v, ev[3])

        for t in list(R) + list(W):
            add(self.lastw.get(t))
        for t in W:
            for ev in self.readers.get(t, []):
                add(ev)
        out = []
        for k, (v, sem) in deps.items():
            if self.waited[eng].get(k, 0) >= v:
                continue
            self.waited[eng][k] = v
            out.append((sem, v, k))
        return out

    def _commit(self, ev, R, W):
        for t in W:
            self.lastw[t] = ev
            self.readers[t] = []
        for t in R:
            self.readers.setdefault(t, []).append(ev)

    def op(self, eng, fn, R=(), W=()):
        if isinstance(fn, tuple):
            _name, _kw = fn
            fn = (lambda e, _name=_name, _kw=_kw: getattr(e, _name)(**_kw))
        waits = self._deps(eng, R, W, pe_acc=True)
        self.cnt[eng] += 1
        sem = self.esem[eng]
        ev = ("e_" + eng, self.cnt[eng], eng, sem)
        self.ops[eng].append((waits, fn, sem, 1, self.cnt[eng]))
        self._commit(ev, R, W)

    def dma(self, q, out, in_, R, W, slot, **kw):
        waits = self._deps(q, R, W)
        if slot not in self.slot_sem:
            self.slot_sem[slot] = self.nc.alloc_semaphore("ds_" + slot)
            self.slot_cnt[slot] = 0
        self.slot_cnt[slot] += 16
        sem = self.slot_sem[slot]
        ev = ("s_" + slot, self.slot_cnt[slot], "dma", sem)
        self.ops[q].append((waits, lambda e: e.dma_start(out=out, in_=in_, **kw), sem, 16, None))
        self._commit(ev, R, W)

    def coll(self, fn, R, W, slot):
        waits = self._deps("pool", R, W)
        if slot not in self.slot_sem:
            self.slot_sem[slot] = self.nc.alloc_semaphore("ds_" + slot)
            self.slot_cnt[slot] = 0
        self.slot_cnt[slot] += 1
        sem = self.slot_sem[slot]
        ev = ("s_" + slot, self.slot_cnt[slot], "dma", sem)
        self.ops["pool"].append((waits, fn, sem, None, None))
        self._commit(ev, R, W)

    def emit(self):
        nc = self.nc
        final_waits = [(self.slot_sem[s], self.slot_cnt[s]) for s in self.slot_sem]
        needed = {e: set() for e in self.esem}
        for q in self.ops:
            for (waits, fn, sem, inc, seq) in self.ops[q]:
                for (s_, v, k) in waits:
                    if k.startswith("e_"):
                        needed[k[2:]].add(v)
        for en in self.esem:
            if self.cnt[en]:
                needed[en].add(self.cnt[en])
        rank = {en: {v: i + 1 for i, v in enumerate(sorted(needed[en]))} for en in needed}
        with nc.Block() as block:
            def run(eng_name):
                def body(e):
                    for waits, fn, sem, inc, seq in self.ops[eng_name]:
                        for (s_, v, k) in waits:
                            if k.startswith("e_"):
                                e.wait_ge(s_, rank[k[2:]][v])
                            else:
                                e.wait_ge(s_, v)
                        ins = fn(e)
                        if inc is None:
                            ins.then_inc(sem)
                        elif seq is None:
                            ins.then_inc(sem, inc)
                        elif seq in rank[eng_name]:
                            ins.then_inc(sem, 1)
                    if eng_name == "sp":
                        for (s_, v) in final_waits:
                            e.wait_ge(s_, v)
                        for en in ("pe", "act", "dve", "pool"):
                            if self.cnt[en]:
                                e.wait_ge(self.esem[en], rank[en][self.cnt[en]])
                return body
            block.sync(run("sp"))
            block.tensor(run("pe"))
            block.scalar(run("act"))
            block.vector(run("dve"))
            block.gpsimd(run("pool"))


def build_nc(depth=DEPTH, units=(0, 1, 2), dbg=False, stop_after=None):
    nc = bass.Bass("TRN2", target_bir_lowering=False)
    P = Prog(nc)

    def din(name, shape, dt=F32):
        return nc.dram_tensor(name, list(shape), dt, kind="ExternalInput").ap()

    def dout(name, shape):
        return nc.dram_tensor(name, list(shape), F32, kind="ExternalOutput").ap()

    xp = din("xp", [4, 256, D])
    xs = din("xs", [512, D])
    cm = din("cm", [D])
    cctx = din("c_ctx", [D])
    cgk = din("cgk", [DEPTH, 512, 128])
    cgv = din("cgv", [DEPTH, 512, 128])
    cdk = din("cdk", [DEPTH, 512, 256])
    cdv = din("cdv", [DEPTH, 512, 256])
    w_ada = din("w_ada", [DEPTH, D, 6 * D])
    b_ada = din("b_ada", [DEPTH, 6 * D])
    norm1_g = din("norm1_g", [DEPTH, D])
    norm2_g = din("norm2_g", [DEPTH, D])
    w_in = din("w_in", [DEPTH, D, 2048])
    w_fourier = din("w_fourier", [DEPTH, 256, 256])
    q_norm_g = din("q_norm_g", [DEPTH, 64])
    k_norm_g = din("k_norm_g", [DEPTH, 64])
    lam_in = [din(n, [DEPTH, 32]) for n in ("lambda_q1", "lambda_k1", "lambda_q2", "lambda_k2")]
    diff_norm_g = din("diff_norm_g", [DEPTH, 64])
    sgu_norm_g = din("sgu_norm_g", [DEPTH, 256])
    w_spatial = din("w_spatial", [DEPTH, 4, 128, 128])
    b_spatial = din("b_spatial", [DEPTH, 4, 128])
    w_gate = din("w_gate", [DEPTH, D, 4 * D])
    w_branch = din("w_branch", [DEPTH, 4, 256, D])
    w_out = din("w_out", [DEPTH, D, D])
    w_mlp1 = din("w_mlp1", [DEPTH, D, 4 * D])
    w_mlp2 = din("w_mlp2", [DEPTH, 4 * D, D])
    final_g = din("final_norm_g", [D])
    k_ident = din("k_ident", [128, 128])
    k_ones = din("k_ones", [128, 128])
    k_blk64 = din("k_blk64", [128, 128])
    k_ccss = din("k_ccss", [128, 256])
    k_rotg = din("k_rotg", [128, 128])
    k_rotd = din("k_rotd", [128, 128])
    k_mask = din("k_mask", [128, 6])
    k_rope = din("k_rope", [4, 128, 512])
    k_dft256 = din("k_dft256", [2, 256, 256])
    k_dftc = din("k_dftc", [2048, 512])
    k_dfts = din("k_dfts", [2048, 512])
    yp = dout("yp", [4, 256, D])
    ys = dout("ys", [512, D])
    ngk = dout("ngk", [4, DEPTH, 256, 128])
    ngv = dout("ngv", [4, DEPTH, 256, 128])
    ndk = dout("ndk", [4, DEPTH, 256, 256])
    ndv = dout("ndv", [4, DEPTH, 256, 256])
    PR = {"g": 256, "d": 512, "x": 512}
    bounce = {p: [nc.dram_tensor("bounce_%s%d" % (p, l), [PR[p], 512], F32).ap() for l in range(DEPTH)] for p in PR}
    gath = {p: [nc.dram_tensor("gath_%s%d" % (p, l), [4 * PR[p], 512], F32).ap() for l in range(DEPTH)] for p in PR}

    dbg_out = dout("dbg", [16, 128, 512]) if dbg else None
    dbg_n = [0]
    dbg_names = []

    def dump(name, ap, R):
        if not dbg:
            return
        i = dbg_n[0]
        dbg_n[0] += 1
        dbg_names.append(name)
        ncol = ap.shape[-1] if len(ap.shape) == 2 else None
        P.dma("sp", dbg_out[i, :, 0:ap.shape[1]], ap, R=R, W=[], slot="dbg")
    nc._dbg_names = dbg_names

    def sb(name, shape, dt=F32):
        return nc.alloc_sbuf_tensor(name, list(shape), dt)

    xT = sb("xT", [128, NU, 8, T])
    hT = sb("hT", [128, 8, T], F32R)
    bigR = sb("bigR", [128, 17, T], F32R)
    TT = sb("TT", [128, 4, 896])
    dft256t = sb("dft256t", [128, 2, 2, 256], F32R)
    slabs = sb("slabs", [128, 4, 4, 512], F32R)
    KT = sb("KT", [128, 2560], F32R)
    VA = sb("VA", [128, 20, 130], BF16)
    PT = sb("PT", [128, 2, 512], BF16)
    tokst = KT[:, 0:1536].rearrange("p (i c) -> p i c", c=384)
    tok = sb("tok", [128, 4, 256])
    sq = sb("sq", [128, 2, T], F32R)
    scr = sb("scr", [128, 2, T])
    cstage = scr[:, 1, :].rearrange("p (i c) -> p i c", c=128)
    DvN = sq[:].rearrange("p a t -> p (a t)").rearrange("p (i c) -> p i c", c=256)
    rstd = sb("rstd", [128, T])
    small = sb("small", [128, 64])
    ident = sb("ident", [128, 128])
    ones_r = sb("ones_r", [128, 128], F32R)
    blk64_r = sb("blk64_r", [128, 128], F32R)
    ccss_r = sb("ccss_r", [128, 256], F32R)
    rotg_r = sb("rotg_r", [128, 128], F32R)
    rotd_r = sb("rotd_r", [128, 128], F32R)
    rope = sb("rope", [128, 4, 512])
    dft256 = dft256t[:]
    xcs = sb("xcs", [128, 4, 512], F32R)
    wf_r = sb("wf_r", [128, 2, 256], F32R)
    wsp = sb("wsp", [128, 4, 128])
    wspT = sb("wspT", [128, 4, 128], F32R)
    bsT = sb("bsT", [128, 4])
    condT = sb("condT", [128, 8, 2])
    scond = sb("scond", [128, 8, 2], F32R)
    modT = sb("modT", [128, 48, 2])
    badaT = sb("badaT", [128, 48])
    A1 = sb("A1", [128, 8, 2])
    A2 = sb("A2", [128, 8, 2])
    n1g = sb("n1g", [128, 8])
    n2g = sb("n2g", [128, 8])
    fng = sb("fng", [128, 8])
    qg = sb("qg", [128, 1])
    kg = sb("kg", [128, 1])
    lamv = sb("lamv", [128, 4, 32])
    neglam = sb("neglam", [128, 1])
    dng = sb("dng", [128, 64])
    sgg = sb("sgg", [128, 256])

    NPS = 8
    psb = [nc.alloc_psum_tensor("ps%d" % i, [128, 512], F32) for i in range(NPS)]
    rot_state = [0]

    def rot():
        i = 4 + rot_state[0] % 4
        rot_state[0] += 1
        return i

    def mm(ps_i, out_ap, lhsT, rhs, start, stop, R, **kw):
        P.op("pe", ("matmul", dict(out=out_ap, lhsT=lhsT, rhs=rhs, start=start, stop=stop, **kw)),
             R=R, W=["ps%d" % ps_i])

    def tr(ps_i, out_ap, in_ap, R):
        P.op("pe", ("transpose", dict(out=out_ap, in_=in_ap, identity=ident[:])), R=list(R) + ["ident"], W=["ps%d" % ps_i])

    def act(out, in_, func, R, W, **kw):
        P.op("act", ("activation", dict(out=out, in_=in_, func=func, **kw)), R=R, W=W)

    def BT(c):
        return "B%d" % c

    def XT(i):
        return ["xcs%da" % i, "xcs%db" % i]

    VAP = ["VAp%d" % i for i in range(8)]

    bigr = bigR[:]
    big = bigR[:].bitcast(F32)
    outst = tok[:].rearrange("p a t -> p (a t)").rearrange("p (b d) -> p b d", d=1024)
    OST = {0: ["tok"], 1: ["tok"]}

    wq = []
    wstate = {"issued": 0, "used": 0}
    PREF = 3

    coll_done = set()
    wneed = {}

    def w_issue():
        i = wstate["issued"]
        if i >= len(wq):
            return False
        need = wneed.get(i)
        if need is not None and need not in coll_done:
            return False
        slot = i % 4
        for (dst_fn, src) in wq[i]:
            P.dma("pool", dst_fn(slot), src, R=(["gath_x"] if need is not None else []), W=["slab%d" % slot], slot="slab%d" % slot)
        wstate["issued"] += 1
        return True

    def w_group(n):
        i = wstate["used"]
        while wstate["issued"] <= min(i + 3, len(wq) - 1):
            if not w_issue():
                break
        assert wstate["issued"] >= i + n, "weight slab not issued"
        wstate["used"] += n
        return [(i + k) % 4 for k in range(n)]

    def w_prefetch():
        i = wstate["used"]
        while wstate["issued"] <= min(i + 3, len(wq) - 1):
            if not w_issue():
                break

    def slab_std(src2d, nk=4):
        ncols = src2d.shape[1]
        return [(lambda s, nk=nk, ncols=ncols: slabs[:, s, 0:nk, 0:ncols],
                 src2d.rearrange("(k p) c -> p k c", p=128))]

    def plan_layer_unit(l, sample):
        for g in ((1, 0, 2, 3) if sample else (0, 1, 2, 3)):
            for kh in range(2):
                rows = slice(kh * 512, kh * 512 + 512)
                if g == 0:
                    ent = [(lambda s: slabs[:, s, :, 0:256],
                            w_in[l, rows, 0:256].rearrange("(k p) c -> p k c", p=128))]
                    for b in range(2):
                        for a in range(2):
                            ent.append((lambda s, b=b, a=a: slabs[:, s, :, 256 + 128 * b + 64 * a:320 + 128 * b + 64 * a],
                                        w_in[l, rows, 256 + 128 * a + 64 * b:320 + 128 * a + 64 * b].rearrange("(k p) d -> p k d", p=128)))
                    wq.append(ent)
                else:
                    wq.append(slab_std(w_in[l, rows, g * 512:(g + 1) * 512]))
        if sample:
            for s4 in range(4):
                wneed[len(wq)] = l
                wq.append([(lambda s: slabs[:, s, :, :],
                            gath["x"][l][s4 * 512:(s4 + 1) * 512, :].rearrange("(k p) c -> p k c", p=128))])
                wq.append(slab_std(k_dftc[s4 * 512:(s4 + 1) * 512, :]))
                wq.append(slab_std(k_dfts[s4 * 512:(s4 + 1) * 512, :]))
        for n in range(4):
            for cg in range(2):
                cols = slice(n * 1024 + cg * 512, n * 1024 + cg * 512 + 512)
                for kh in range(2):
                    wq.append(slab_std(w_gate[l, kh * 512:(kh + 1) * 512, cols]))
                wq.append(slab_std(w_branch[l, n, :, cg * 512:(cg + 1) * 512], nk=2))
        for cg in range(2):
            for kh in range(2):
                wq.append(slab_std(w_out[l, kh * 512:(kh + 1) * 512, cg * 512:(cg + 1) * 512]))
        for half in range(2):
            for g in range(4):
                for kh in range(2):
                    wq.append(slab_std(w_mlp1[l, kh * 512:(kh + 1) * 512, half * 2048 + g * 512: half * 2048 + (g + 1) * 512]))
            for cg in range(2):
                for kq in range(4):
                    r0 = half * 2048 + kq * 512
                    wq.append(slab_std(w_mlp2[l, r0:r0 + 512, cg * 512:(cg + 1) * 512]))

    def plan_ada(l):
        for g in range(12):
            for kh in range(2):
                wq.append(slab_std(w_ada[l, kh * 512:(kh + 1) * 512, g * 512:(g + 1) * 512]))

    for l in range(depth):
        plan_ada(l)
        for u in units:
            plan_layer_unit(l, u == 2)

    def ld(dst, src, W, slot, q="sp", **kw):
        P.dma(q, dst, src, R=[], W=W, slot=slot, **kw)

    ld(ident[:], k_ident, ["ident"], "c0")
    ld(ones_r[:], k_ones, ["ones"], "c1", q="pool")
    ld(blk64_r[:], k_blk64, ["blk64"], "c2", q="pool")
    ld(ccss_r[:], k_ccss, ["ccss"], "c3", q="pool")
    ld(rotg_r[:], k_rotg, ["rotg"], "c4", q="pool")
    ld(rotd_r[:], k_rotd, ["rotd"], "c5", q="pool")
    ld(rope[:], k_rope.rearrange("a p t -> p a t"), ["rope"], "c6")
    for a_ in range(2):
        ld(dft256[:, a_], k_dft256[a_].rearrange("(j p) t -> p j t", p=128), ["dft256"], "c7", q="pool")
    ld(fng[:], final_g.rearrange("(c p) -> p c", p=128), ["fng"], "c8", allow_slow_non_contiguous=True)
    ld(condT[:, :, 0], cctx.rearrange("(c p) -> p c", p=128), ["condT"], "c9", allow_slow_non_contiguous=True)
    ld(condT[:, :, 1], cm.rearrange("(c p) -> p c", p=128), ["condT"], "c9", allow_slow_non_contiguous=True)
    P.op("pool", lambda e: e.memset(VA[:, :, 0:1], 1.0), W=["VAones"])
    P.op("pool", lambda e: e.memset(VA[:, :, 129:130], 1.0), W=["VAones"])
    def x_load():
        for u in range(NU):
            for i in range(4):
                if u < 2:
                    src = xp[2 * u + i // 2, (i % 2) * 128:(i % 2) * 128 + 128, :]
                else:
                    src = xs[i * 128:(i + 1) * 128, :]
                b = (u * 4 + i) % 2
                stg = outst[:, 0, :] if b == 0 else scr[:].rearrange("p a t -> p (a t)")
                stags = OST[0] if b == 0 else ["scr0", "scr1"]
                ld(stg, src, stags, "ost%d" % b)
                for c in range(8):
                    pi = rot()
                    tr(pi, psb[pi][:, 0:128], stg[:, c * 128:(c + 1) * 128], R=stags)
                    eng = "dve" if c % 2 == 0 else "act"
                    dst = xT[:, u, c, i * 128:(i + 1) * 128]
                    if eng == "dve":
                        P.op("dve", ("tensor_copy", dict(out=dst, in_=psb[pi][:, 0:128])),
                             R=["ps%d" % pi], W=["x%d_%d" % (u, c)])
                    else:
                        act(dst, psb[pi][:, 0:128], AF.Copy, R=["ps%d" % pi], W=["x%d_%d" % (u, c)])
    act(scond[:], condT[:], AF.Silu, R=["condT"], W=["scond"])

    def norm_mod(u, ci, Amat, shift_c0, gtag):
        pi = rot()
        for c in range(8):
            b = c % 2
            act(sq[:, b, :], xT[:, u, c, :], AF.Square, R=["x%d_%d" % (u, c)], W=["sq%d" % b])
            mm(pi, psb[pi][:], ones_r[:], sq[:, b, :], c == 0, c == 7, R=["ones", "sq%d" % b])
        act(rstd[:], psb[pi][:], AF.Sqrt, R=["ps%d" % pi, "eps"], W=["rstd"], scale=1.0 / D, bias=eps_ap)
        P.op("dve", ("reciprocal", dict(out=rstd[:], in_=rstd[:])), R=["rstd"], W=["rstd"])
        for c in range(8):
            b = c % 2
            P.op("dve", ("scalar_tensor_tensor", dict(
                out=scr[:, b, :], in0=xT[:, u, c, :], scalar=Amat[:, c, ci:ci + 1], in1=rstd[:],
                op0=ALU.mult, op1=ALU.mult)), R=["x%d_%d" % (u, c), gtag, "rstd"], W=["scr%d" % b])
            act(hT[:, c, :], scr[:, b, :], AF.Identity, R=["scr%d" % b, "mod"], W=["h%d" % c],
                bias=modT[:, shift_c0 + c, ci:ci + 1])

    eps_t = sb("eps_t", [128, 1])
    maskc = sb("maskc", [128, 6])
    ld(maskc[:], k_mask, ["maskc"], "c10")
    P.op("pool", lambda e: e.memset(eps_t[:], EPS), W=["eps"])
    eps_ap = eps_t[:]

    def rms_rows(ps_or_ap, R_in, nparts_tag):
        pass

    def final_out(u):
        pi = rot()
        for c in range(8):
            b = c % 2
            act(sq[:, b, :], xT[:, u, c, :], AF.Square, R=["x%d_%d" % (u, c)], W=["sq%d" % b])
            mm(pi, psb[pi][:], ones_r[:], sq[:, b, :], c == 0, c == 7, R=["ones", "sq%d" % b])
        act(rstd[:], psb[pi][:], AF.Sqrt, R=["ps%d" % pi, "eps"], W=["rstd"], scale=1.0 / D, bias=eps_ap)
        P.op("dve", ("reciprocal", dict(out=rstd[:], in_=rstd[:])), R=["rstd"], W=["rstd"])
        scrv = scr[:].rearrange("p a t -> p (a t)").rearrange("p (c q) -> p c q", q=128)
        for i in range(4):
            for c in range(8):
                P.op("dve", ("scalar_tensor_tensor", dict(
                    out=scrv[:, c, :], in0=xT[:, u, c, i * 128:(i + 1) * 128], scalar=fng[:, c:c + 1], in1=rstd[:, i * 128:(i + 1) * 128],
                    op0=ALU.mult, op1=ALU.mult)), R=["x%d_%d" % (u, c), "fng", "rstd"], W=["scr0", "scr1"])
            for c in range(8):
                pi = rot()
                tr(pi, psb[pi][:, 0:128], scrv[:, c, :], R=["scr0", "scr1"])
                if c % 2 == 0:
                    P.op("dve", ("tensor_copy", dict(out=outst[:, 0, c * 128:(c + 1) * 128], in_=psb[pi][:, 0:128])),
                         R=["ps%d" % pi], W=["tok"])
                else:
                    act(outst[:, 0, c * 128:(c + 1) * 128], psb[pi][:, 0:128], AF.Copy, R=["ps%d" % pi], W=["tok"])
            if u < 2:
                dst = yp[2 * u + i // 2, (i % 2) * 128:(i % 2) * 128 + 128, :]
            else:
                dst = ys[i * 128:(i + 1) * 128, :]
            P.dma("sp", dst, outst[:, 0, :], R=["tok"], W=[], slot="oy")


    for l in range(depth):
        lam_init = 0.8 - 0.6 * math.exp(-0.3 * l)
        lam_scale = 1.0 - lam_init
        ld(badaT[:], b_ada[l].rearrange("(c p) -> p c", p=128), ["badaT"], "l0", allow_slow_non_contiguous=True)
        ld(n1g[:], norm1_g[l].rearrange("(c p) -> p c", p=128), ["n1g"], "l1", allow_slow_non_contiguous=True)
        ld(n2g[:], norm2_g[l].rearrange("(c p) -> p c", p=128), ["n2g"], "l2", allow_slow_non_contiguous=True)
        for hh in range(2):
            ld(qg[hh * 64:(hh + 1) * 64, :], q_norm_g[l].rearrange("(p o) -> p o", o=1), ["qg"], "l3", allow_slow_non_contiguous=True)
            ld(kg[hh * 64:(hh + 1) * 64, :], k_norm_g[l].rearrange("(p o) -> p o", o=1), ["kg"], "l4", allow_slow_non_contiguous=True)
        for j in range(4):
            ld(lamv[:, j, :], lam_in[j][l].partition_broadcast(128), ["lamv"], "l5")
        ld(dng[:], diff_norm_g[l].partition_broadcast(128), ["dng"], "l6")
        ld(sgg[:], sgu_norm_g[l].partition_broadcast(128), ["sgg"], "l7")
        ld(wsp[:], w_spatial[l].rearrange("g q p -> q g p"), ["wsp"], "l8")
        ld(bsT[:], b_spatial[l].rearrange("g q -> q g"), ["bsT"], "l9", allow_slow_non_contiguous=True)
        ld(wf_r[:], w_fourier[l].rearrange("(k p) c -> p k c", p=128), ["wf"], "l10", q="pool")
        P.op("dve", ("tensor_tensor", dict(out=small[:, 0:32], in0=lamv[:, 0, :], in1=lamv[:, 1, :], op=ALU.mult)), R=["lamv"], W=["small"])
        P.op("dve", ("tensor_tensor", dict(out=small[:, 32:64], in0=lamv[:, 2, :], in1=lamv[:, 3, :], op=ALU.mult)), R=["lamv", "small"], W=["small"])
        P.op("dve", ("tensor_reduce", dict(out=neglam[:], in_=small[:, 0:32], axis=AX.X, op=ALU.add)), R=["small"], W=["neglam"])
        P.op("dve", ("tensor_reduce", dict(out=small[:, 0:1], in_=small[:, 32:64], axis=AX.X, op=ALU.add)), R=["small", "neglam"], W=["small"])
        act(neglam[:], neglam[:], AF.Exp, R=["neglam"], W=["neglam"])
        act(small[:, 1:2], small[:, 0:1], AF.Exp, R=["small"], W=["small"])
        P.op("dve", ("tensor_tensor", dict(out=neglam[:], in0=small[:, 1:2], in1=neglam[:], op=ALU.subtract)), R=["small", "neglam"], W=["neglam"])
        P.op("dve", ("tensor_scalar_add", dict(out=neglam[:], in0=neglam[:], scalar1=-lam_init)), R=["neglam"], W=["neglam"])
        P.op("dve", ("tensor_scalar_mul", dict(out=dng[:], in0=dng[:], scalar1=lam_scale)), R=["dng"], W=["dng"])
        for g in range(4):
            pi = rot()
            tr(pi, psb[pi][:, 0:128], wsp[:, g, :], R=["wsp"])
            P.op("dve", ("tensor_copy", dict(out=wspT[:, g, :], in_=psb[pi][:, 0:128])), R=["ps%d" % pi], W=["wspT"])
        for g in range(12):
            s0, s1 = w_group(2)
            pi = rot()
            for kc in range(8):
                sl_ = s0 if kc < 4 else s1
                mm(pi, psb[pi][0:2, :], scond[:, kc, :], slabs[:, sl_, kc % 4, :], kc == 0, kc == 7,
                   R=["slab%d" % sl_, "scond"])
            P.op("dve", ("tensor_copy", dict(out=rstd[0:2, :], in_=psb[pi][0:2, :])), R=["ps%d" % pi], W=["rstd"])
            for j in range(4):
                pj = rot()
                P.op("pe", ("transpose", dict(out=psb[pj][:, 0:2], in_=rstd[0:2, j * 128:(j + 1) * 128], identity=ident[0:2, 0:2])),
                     R=["rstd", "ident"], W=["ps%d" % pj])
                cc = g * 4 + j
                P.op("dve", ("tensor_scalar", dict(
                    out=modT[:, cc, :], in0=psb[pj][:, 0:2], scalar1=badaT[:, cc:cc + 1], scalar2=None, op0=ALU.add)),
                    R=["ps%d" % pj, "badaT"], W=["mod"])
        if l == 0:
            x_load()
        for ci in range(2):
            P.op("dve", ("scalar_tensor_tensor", dict(
                out=A1[:, :, ci], in0=modT[:, 8:16, ci], scalar=1.0, in1=n1g[:], op0=ALU.add, op1=ALU.mult)),
                R=["mod", "n1g"], W=["A1"])
            P.op("dve", ("scalar_tensor_tensor", dict(
                out=A2[:, :, ci], in0=modT[:, 32:40, ci], scalar=1.0, in1=n2g[:], op0=ALU.add, op1=ALU.mult)),
                R=["mod", "n2g"], W=["A2"])

        for u in units:
            sample = (u == 2)
            ci = 1 if sample else 0
            nseq = 1 if sample else 2
            sl = T // nseq
            norm_mod(u, ci, A1, 0, "A1")
            hR = ["h%d" % c for c in range(8)]
            TTv = TT[:]
            TT_tags = ["TT"]
            def w_in_groups(gs):
                for g in gs:
                    s0, s1 = w_group(2)

                    def wsl(kc, c0, c1):
                        s = s0 if kc < 4 else s1
                        return slabs[:, s, kc % 4, c0:c1], "slab%d" % s

                    def ftype(c0, dst_chunk, evac):
                        pi = rot()
                        for kc in range(8):
                            w_ap, wt = wsl(kc, c0, c0 + 128)
                            mm(pi, psb[pi][:], w_ap, hT[:, kc, :], kc == 0, kc == 7, R=[wt, "h%d" % kc])
                        evac(pi, dst_chunk)

                    def ttype(c0, ncols, toff, post=None):
                        for i in range(4):
                            pi = rot()
                            for kc in range(8):
                                w_ap, wt = wsl(kc, c0, c0 + ncols)
                                mm(pi, psb[pi][:, 0:ncols], hT[:, kc, i * 128:(i + 1) * 128], w_ap, kc == 0, kc == 7,
                                   R=[wt, "h%d" % kc])
                            dst = TTv[:, i, toff:toff + ncols]
                            if post is None:
                                P.op("dve", ("tensor_copy", dict(out=dst, in_=psb[pi][:, 0:ncols])),
                                     R=["ps%d" % pi], W=TT_tags)
                            else:
                                act(dst, psb[pi][:, 0:ncols], post, R=["ps%d" % pi], W=TT_tags)

                    def ev_copy_r(pi, ch):
                        act(bigr[:, ch, :], psb[pi][:], AF.Copy, R=["ps%d" % pi], W=[BT(ch)])

                    def ev_copy_f(pi, ch):
                        P.op("dve", ("tensor_copy", dict(out=bigr[:, ch, :], in_=psb[pi][:])), R=["ps%d" % pi], W=[BT(ch)])

                    if g == 0:
                        ftype(0, 0, ev_copy_r)
                        ftype(128, 1, ev_copy_r)
                        ftype(256, 2, ev_copy_f)
                        ftype(384, 3, ev_copy_f)
                    elif g == 1:
                        ftype(0, 4, ev_copy_f)
                        ttype(128, 128, 0)
                        ftype(256, 5, ev_copy_f if sample else ev_copy_r)
                        ftype(384, 6, ev_copy_f if sample else ev_copy_r)
                    elif g == 2:
                        ftype(0, 7, ev_copy_f if sample else ev_copy_r)
                        ftype(128, 8, ev_copy_f if sample else ev_copy_r)
                        ttype(256, 256, 128)
                    else:
                        ttype(0, 256, 384, post=AF.Gelu_apprx_tanh)
                        ttype(256, 256, 640, post=AF.Gelu_apprx_tanh)

            def qk_norm(chs):
                for ch in chs:
                    gvec, gt = (kg, "kg") if ch == 4 else (qg, "qg")
                    act(sq[:, 0, :], big[:, ch, :], AF.Square, R=[BT(ch)], W=["sq0"])
                    pi = rot()
                    mm(pi, psb[pi][:], blk64_r[:], sq[:, 0, :], True, True, R=["blk64", "sq0"])
                    act(rstd[:], psb[pi][:], AF.Sqrt, R=["ps%d" % pi, "eps"], W=["rstd"], scale=1.0 / 64, bias=eps_ap)
                    P.op("dve", ("reciprocal", dict(out=rstd[:], in_=rstd[:])), R=["rstd"], W=["rstd"])
                    P.op("dve", ("scalar_tensor_tensor", dict(
                        out=bigr[:, ch, :], in0=big[:, ch, :], scalar=gvec[:, 0:1], in1=rstd[:], op0=ALU.mult, op1=ALU.mult)),
                        R=[BT(ch), gt, "rstd"], W=[BT(ch)])

            def do_rope(chs):
                for ch in chs:
                    isg = ch <= 4
                    rm = rotg_r if isg else rotd_r
                    rt = "rotg" if isg else "rotd"
                    cosi, sini = (0, 1) if isg else (2, 3)
                    act(sq[:, 1, :], big[:, ch, :], AF.Copy, R=[BT(ch)], W=["sq1"])
                    pi = rot()
                    mm(pi, psb[pi][:], rm[:], sq[:, 1, :], True, True, R=[rt, "sq1"])
                    P.op("dve", ("tensor_tensor", dict(out=scr[:, 0, :], in0=psb[pi][:], in1=rope[:, sini, :], op=ALU.mult)),
                         R=["ps%d" % pi, "rope"], W=["scr0"])
                    P.op("pool", ("tensor_tensor", dict(out=scr[:, 1, :], in0=big[:, ch, :], in1=rope[:, cosi, :], op=ALU.mult)),
                         R=[BT(ch), "rope"], W=["scr1"])
                    P.op("dve", ("tensor_tensor", dict(out=bigr[:, ch, :], in0=scr[:, 0, :], in1=scr[:, 1, :], op=ALU.add)),
                         R=["scr0", "scr1"], W=[BT(ch)])

            def emit_kv():
                if True:
                    for i in range(4):
                        for (ch, off) in ((4, 0), (7, 128), (8, 256)):
                            pi = rot()
                            tr(pi, psb[pi][:, 0:128], big[:, ch, i * 128:(i + 1) * 128], R=[BT(ch)])
                            P.op("dve", ("tensor_copy", dict(out=tokst[:, i, off:off + 128], in_=psb[pi][:, 0:128])),
                                 R=["ps%d" % pi], W=["KT"])
                    sq0 = 2 * u
                    for s_ in range(2):
                        P.dma("sp", ngk[sq0 + s_, l].rearrange("(j p) f -> p j f", p=128),
                              tokst[:, 2 * s_:2 * s_ + 2, 0:128].bitcast(F32), R=["KT"], W=[], slot="o_ngk%d" % s_)
                        P.dma("sp", ndk[sq0 + s_, l].rearrange("(j p) f -> p j f", p=128),
                              tokst[:, 2 * s_:2 * s_ + 2, 128:384].bitcast(F32), R=["KT"], W=[], slot="o_ndk%d" % s_)
                        P.dma("sp", ngv[sq0 + s_, l].rearrange("(j p) f -> p j f", p=128),
                              TTv[:, 2 * s_:2 * s_ + 2, 0:128], R=TT_tags, W=[], slot="o_ngv%d" % s_)
                        P.dma("sp", ndv[sq0 + s_, l].rearrange("(j p) f -> p j f", p=128),
                              TTv[:, 2 * s_:2 * s_ + 2, 128:384], R=TT_tags, W=[], slot="o_ndv%d" % s_)

            def dv_rms():
                for i in range(4):
                    dv_ap = TTv[:, i, 640:896]
                    act(scr[:, 0, 0:256], dv_ap, AF.Square, R=TT_tags, W=["scr0"], accum_out=small[:, 8 + i:9 + i])
                P.op("act", ("activation", dict(out=small[:, 12:16], in_=small[:, 8:12], func=AF.Sqrt, scale=1.0 / 256, bias=eps_ap)),
                     R=["scr0", "eps"], W=["small2"])
                P.op("dve", ("reciprocal", dict(out=small[:, 12:16], in_=small[:, 12:16])), R=["small2"], W=["small2"])
                for i in range(4):
                    dv_ap = TTv[:, i, 640:896]
                    P.op("dve", ("scalar_tensor_tensor", dict(
                        out=DvN[:, i, :], in0=dv_ap, scalar=small[:, 12 + i:13 + i], in1=sgg[:], op0=ALU.mult, op1=ALU.mult)),
                        R=TT_tags + ["small2", "sgg"], W=["sq0", "sq1"])


            def chan_dft():
                for i in range(4):
                    pi = rot()
                    for j in range(2):
                        mm(pi, psb[pi][:, j * 256:(j + 1) * 256], bigr[:, j, i * 128:(i + 1) * 128], ccss_r[:], True, True,
                           R=[BT(j), "ccss"])
                    act(xcs[:, i, :], psb[pi][:], AF.Copy, R=["ps%d" % pi], W=XT(i))

            def fourier_prompt():
                for s in range(2):
                    for j in range(2):
                        pi = rot()
                        n = 0
                        for i2 in range(2):
                            i = s * 2 + i2
                            for part in range(2):
                                mm(pi, psb[pi][:, 0:256], xcs[:, i, j * 256 + part * 128: j * 256 + part * 128 + 128],
                                   dft256[:, part, i2, :], n == 0, n == 3, R=XT(i) + ["dft256"])
                                n += 1
                        act(bigr[:, 15 + j, s * 256:(s + 1) * 256], psb[pi][:, 0:256], AF.Copy, R=["ps%d" % pi], W=[BT(15 + j)])

            def exchange_part(p_):
                bz = bounce[p_][l]
                if p_ == "g":
                    P.dma("sp", bz[0:128, :], big[:, 4, :], R=[BT(4)], W=["bounce_g"], slot="bzg")
                    P.dma("sp", bz[128:256, :].rearrange("r (a c) -> (r a) c", c=128).rearrange("(j p) c -> p j c", p=128),
                          TTv[:, :, 0:128], R=TT_tags, W=["bounce_g"], slot="bzg")
                else:
                    P.dma("sp", bz[0:256, :].rearrange("(c p) t -> p c t", p=128), big[:, 7:9, :], R=[BT(7), BT(8)], W=["bounce_d"], slot="bzd")
                    P.dma("sp", bz[256:512, :].rearrange("r (a c) -> (r a) c", c=256).rearrange("(j p) c -> p j c", p=128),
                          TTv[:, :, 128:384], R=TT_tags, W=["bounce_d"], slot="bzd")
                w_prefetch()
                P.coll(lambda e, l=l, p_=p_: e.collective_compute(
                    "AllGather", ALU.bypass, replica_groups=[[0, 1, 2, 3], [4, 5, 6, 7]],
                    ins=[bounce[p_][l].opt()], outs=[gath[p_][l].opt()]), R=["bounce_" + p_], W=["gath_" + p_], slot="cc_" + p_)

            def exchange_x():
                bx_ = bounce["x"][l]
                P.dma("sp", bx_.rearrange("(j p) c -> p j c", p=128), xcs[:].bitcast(F32), R=XT(0) + XT(1) + XT(2) + XT(3), W=["bounce_x"], slot="bzx")
                w_prefetch()
                P.coll(lambda e, l=l: e.collective_compute(
                    "AllGather", ALU.bypass, replica_groups=[[0, 1, 2, 3], [4, 5, 6, 7]],
                    ins=[bounce["x"][l].opt()], outs=[gath["x"][l].opt()]), R=["bounce_x"], W=["gath_x"], slot="cc_x")

            def fourier_sample():
                acc = [rot(), rot()]
                n = 0
                for s4 in range(4):
                    sx, sc, ss = w_group(3)
                    for i in range(4):
                        for j in range(2):
                            mm(acc[j], psb[acc[j]][:], slabs[:, sx, i, j * 256:j * 256 + 128], slabs[:, sc, i, :],
                               n == 0, False, R=["slab%d" % sx, "slab%d" % sc])
                            mm(acc[j], psb[acc[j]][:], slabs[:, sx, i, j * 256 + 128:j * 256 + 256], slabs[:, ss, i, :],
                               False, n == 15, R=["slab%d" % sx, "slab%d" % ss])
                        n += 1
                for j in range(2):
                    act(xcs[:, j, :], psb[acc[j]][:], AF.Copy, R=["ps%d" % acc[j]], W=XT(j))

            def out_a(ysrc, ytags):
                for c in range(2):
                    pi = rot()
                    for kc in range(2):
                        mm(pi, psb[pi][:], wf_r[:, kc, c * 128:(c + 1) * 128], ysrc(kc), kc == 0, kc == 1, R=["wf"] + (ytags[kc] if isinstance(ytags[kc], list) else [ytags[kc]]))
                    act(bigr[:, 9 + c, :], psb[pi][:], AF.Copy, R=["ps%d" % pi], W=[BT(9 + c)])

            def load_keys_sample(kind, hc):
                if kind == "g":
                    gp, gtag, grows = gath["g"][l], "gath_g", 256
                    kcache, kcols, krow0 = cgk[l], slice(0, 128), 0
                    vcache, vcols, vrow0, vw = cgv[l], slice(0, 128), 128, 128
                else:
                    gp, gtag, grows = gath["d"][l], "gath_d", 512
                    kcache, kcols, krow0 = cdk[l], slice(hc * 128, hc * 128 + 128), hc * 128
                    vcache, vcols, vrow0, vw = cdv[l], slice(hc * 128, hc * 128 + 128), 256, 256
                gl = gp.rearrange("(r x) c -> x r c", x=grows)
                w_prefetch()
                P.dma("sp", cstage[:, :, :], kcache[:, kcols].rearrange("(j p) c -> p j c", p=128), R=[], W=["scr1"], slot="cst")
                for j in range(4):
                    pi = rot()
                    tr(pi, psb[pi][:, 0:128], cstage[:, j, :], R=["scr1"])
                    act(KT[:, j * 128:(j + 1) * 128], psb[pi][:, 0:128], AF.Copy, R=["ps%d" % pi], W=["KT"])
                P.dma("pool", KT[:, 512:2560].rearrange("p (r t) -> p r t", r=4), gl[krow0:krow0 + 128, :, :],
                      R=[gtag], W=["KT"], slot="ktl")
                P.dma("pool", VA[:, 0:4, 1:129], vcache[:, vcols].rearrange("(j p) d -> p j d", p=128),
                      R=[], W=["VA"] + VAP, slot="val")
                for r in range(4):
                    src = gp[r * grows + vrow0: r * grows + vrow0 + vw, :].rearrange("x (a c) -> (x a) c", c=vw)
                    src = src[:, vcols].rearrange("(j p) d -> p j d", p=128)
                    P.dma("pool", VA[:, 4 + 4 * r: 8 + 4 * r, 1:129], src, R=[gtag], W=["VA"] + VAP, slot="val")

            def attention(gqa_preloaded=False, parts=("g", "d")):
                nk = 2560 if sample else 256
                nkt = nk // 128
                nq = sl
                nqt = nq // 128

                gctr = [0]

                def attn_group(heads, s):
                    nh = len(heads)
                    g_ = gctr[0]
                    gctr[0] += 1
                    if sample:
                        qoff, vb, vtag = 0, 0, "VA"
                        qtag = lambda hi: XT(hi)
                    else:
                        qoff, vb, vtag = (g_ % 2) * 256, (g_ % 8) * 2, VAP[g_ % 8]
                        qtag = lambda hi: [XT(hi)[g_ % 2]]
                    for hi, hd in enumerate(heads):
                        act(xcs[:, hi, qoff:qoff + nq], big[:, hd["qch"], s * sl: s * sl + nq], AF.Copy,
                            R=[BT(hd["qch"]), "maskc"], W=qtag(hi), scale=maskc[:, hd["mcol"]:hd["mcol"] + 1])
                    kpb = 2 if (nkt == 2 and nq * 2 <= 512) else 1
                    steps = [(hi, kt) for hi in range(nh) for kt in range(0, nkt, kpb)]
                    pending = []
                    sbank = {}

                    def issue_s(i):
                        hi, kt = steps[i]
                        hd = heads[hi]
                        pi = rot()
                        for kk in range(kpb):
                            mm(pi, psb[pi][:, kk * nq:(kk + 1) * nq], hd["kfn"](kt + kk), xcs[:, hi, qoff:qoff + nq], True, True,
                               R=[hd["ktag"]] + qtag(hi), skip_group_check=True)
                        sbank[i] = pi

                    issue_s(0)
                    for i in range(len(steps)):
                        if i + 1 < len(steps):
                            issue_s(i + 1)
                        hi, kt = steps[i]
                        hd = heads[hi]
                        pi = sbank[i]
                        b = i % 2
                        act(PT[:, b, 0:kpb * nq], psb[pi][:, 0:kpb * nq], AF.Exp, R=["ps%d" % pi], W=["PT%d" % b], scale=hd["scale"])
                        ob = hd["obank"]
                        for kk in range(kpb):
                            for qt in range(nqt):
                                ktt = kt + kk
                                mm(ob, psb[ob][:, qt * 65:(qt + 1) * 65], PT[:, b, kk * nq + qt * 128: kk * nq + (qt + 1) * 128],
                                   VA[:, vb + ktt, 65 * hd["vslot"]:65 * hd["vslot"] + 65], ktt == 0 and qt == 0, ktt == nkt - 1 and qt == nqt - 1,
                                   R=["PT%d" % b, vtag, "VAones"], skip_group_check=True)
                        for pd in [p for p in pending if p[0] <= i]:
                            pending.remove(pd)
                            pd[1]()
                        if kt + kpb - 1 == nkt - 1 and hd["post"] is not None:
                            p1, p2 = hd["post"]
                            p1()
                            if p2 is not None:
                                pending.append((i + 3, p2))
                    for pd in pending:
                        pd[1]()

                def gqa_post(h, s, ob):
                    def p1():
                        ov = psb[ob][:, 0:nqt * 65].rearrange("p (q c) -> p q c", c=65)
                        c0 = 16 + 4 * (h % 2)
                        d0, v0 = (0, 1) if (h // 2) == 0 else (64, 0)
                        P.op("dve", ("reciprocal", dict(out=small[:, c0:c0 + nqt], in_=ov[:, :, d0])), R=["ps%d" % ob], W=["small3"])
                        P.op("dve", ("tensor_tensor", dict(
                            out=tok[:, s * nqt:(s + 1) * nqt, h * 64:(h + 1) * 64], in0=ov[:, :, v0:v0 + 64],
                            in1=small[:, c0:c0 + nqt].unsqueeze(2).broadcast_to([128, nqt, 64]), op=ALU.mult)),
                            R=["ps%d" % ob, "small3"], W=["tok"])
                    return p1, None

                def diff_post(h, s, ob1, ob2):
                    par = h % 2
                    d0, v0 = (0, 1) if (h % 2) == 0 else (64, 0)
                    A = scr[:, par, 0:256].rearrange("p (q c) -> p q c", c=64)[:, 0:nqt, :]
                    Bm = scr[:, par, 256:512].rearrange("p (q c) -> p q c", c=64)[:, 0:nqt, :]
                    stag = "scr%d" % par
                    c1, c2, c3 = 16 + 12 * par, 20 + 12 * par, 24 + 12 * par
                    mtag = "smallp%d" % par

                    def p1():
                        o1 = psb[ob1][:, 0:nqt * 65].rearrange("p (q c) -> p q c", c=65)
                        o2 = psb[ob2][:, 0:nqt * 65].rearrange("p (q c) -> p q c", c=65)
                        P.op("dve", ("reciprocal", dict(out=small[:, c1:c1 + nqt], in_=o1[:, :, d0])), R=["ps%d" % ob1], W=[mtag])
                        P.op("dve", ("reciprocal", dict(out=small[:, c2:c2 + nqt], in_=o2[:, :, d0])), R=["ps%d" % ob2, mtag], W=[mtag])
                        P.op("dve", ("tensor_scalar", dict(out=small[:, c2:c2 + nqt], in0=small[:, c2:c2 + nqt], scalar1=neglam[:, 0:1],
                                                              scalar2=None, op0=ALU.mult)), R=[mtag, "neglam"], W=[mtag])
                        P.op("dve", ("tensor_tensor", dict(out=A, in0=o1[:, :, v0:v0 + 64],
                                                              in1=small[:, c1:c1 + nqt].unsqueeze(2).broadcast_to([128, nqt, 64]), op=ALU.mult)),
                             R=["ps%d" % ob1, mtag], W=[stag])
                        P.op("dve", ("tensor_tensor", dict(out=Bm, in0=o2[:, :, v0:v0 + 64],
                                                              in1=small[:, c2:c2 + nqt].unsqueeze(2).broadcast_to([128, nqt, 64]), op=ALU.mult)),
                             R=["ps%d" % ob2, mtag, stag], W=[stag])
                        P.op("dve", ("tensor_tensor", dict(out=A, in0=A, in1=Bm, op=ALU.add)), R=[stag], W=[stag])
                        P.op("dve", ("tensor_tensor", dict(out=Bm, in0=A, in1=A, op=ALU.mult)), R=[stag], W=[stag])
                        P.op("dve", ("tensor_reduce", dict(out=small[:, c3:c3 + nqt], in_=Bm, axis=AX.X, op=ALU.add)), R=[stag, mtag], W=[mtag])

                    def p2():
                        P.op("act", ("activation", dict(out=small[:, c3:c3 + nqt], in_=small[:, c3:c3 + nqt], func=AF.Ln, scale=1.0 / 64, bias=eps_ap)),
                             R=[mtag, "eps"], W=[mtag])
                        P.op("act", ("activation", dict(out=small[:, c3:c3 + nqt], in_=small[:, c3:c3 + nqt], func=AF.Exp, scale=-0.5)),
                             R=[mtag], W=[mtag])
                        P.op("dve", ("tensor_tensor", dict(out=A, in0=A,
                                                              in1=small[:, c3:c3 + nqt].unsqueeze(2).broadcast_to([128, nqt, 64]), op=ALU.mult)),
                             R=[stag, mtag], W=[stag])
                        P.op("dve", ("tensor_tensor", dict(out=tok[:, s * nqt:(s + 1) * nqt, h * 64:(h + 1) * 64], in0=A,
                                                              in1=dng[:].unsqueeze(1).broadcast_to([128, nqt, 64]), op=ALU.mult)),
                             R=[stag, "dng"], W=["tok"])
                    return p1, p2

                if "g" in parts:
                    for s in range(nseq):
                        if sample:
                            if not gqa_preloaded:
                                load_keys_sample("g", 0)
                            kfn = lambda kt: KT[:, kt * 128:(kt + 1) * 128]
                            ktag = "KT"
                        else:
                            for i2 in range(2):
                                i = s * 2 + i2
                                P.op("dve", ("tensor_copy", dict(
                                    out=VA[:, (gctr[0] % 8) * 2 + i2, 1:129], in_=TTv[:, i, 0:128])),
                                    R=TT_tags, W=[VAP[gctr[0] % 8], "VA"])
                            kfn = lambda kt, s=s: bigr[:, 4, s * 256 + kt * 128: s * 256 + (kt + 1) * 128]
                            ktag = BT(4)
                        heads = []
                        for h in range(4):
                            heads.append(dict(qch=2 + (h % 2), mcol=h // 2, vslot=h // 2, scale=0.125, obank=h % 2,
                                              kfn=kfn, ktag=ktag, post=gqa_post(h, s, h % 2)))
                        attn_group(heads, s)
                    for i in range(4):
                        for c in range(2):
                            pi = rot()
                            tr(pi, psb[pi][:, 0:128], tok[:, i, c * 128:(c + 1) * 128], R=["tok"])
                            act(bigr[:, 11 + c, i * 128:(i + 1) * 128], psb[pi][:, 0:128], AF.Copy, R=["ps%d" % pi], W=[BT(11 + c)])

                if "d" in parts:
                    dscale = 1.0 / math.sqrt(32.0)
                    for s in range(nseq):
                        for hc in range(2):
                            if sample:
                                load_keys_sample("d", hc)
                                kfn = lambda kt: KT[:, kt * 128:(kt + 1) * 128]
                                ktag = "KT"
                            else:
                                for i2 in range(2):
                                    i = s * 2 + i2
                                    P.op("dve", ("tensor_copy", dict(
                                        out=VA[:, (gctr[0] % 8) * 2 + i2, 1:129],
                                        in_=TTv[:, i, 128 + hc * 128: 256 + hc * 128])),
                                        R=TT_tags, W=[VAP[gctr[0] % 8], "VA"])
                                kfn = lambda kt, s=s, hc=hc: bigr[:, 7 + hc, s * 256 + kt * 128: s * 256 + (kt + 1) * 128]
                                ktag = BT(7 + hc)
                            heads = []
                            for hh in range(2):
                                h = hc * 2 + hh
                                for m in range(2):
                                    heads.append(dict(qch=5 + hc, mcol=2 + hh * 2 + m, vslot=hh, scale=dscale, obank=2 * hh + m,
                                                      kfn=kfn, ktag=ktag,
                                                      post=(diff_post(h, s, 2 * hh, 2 * hh + 1) if m == 1 else None)))
                            attn_group(heads, s)
                    for i in range(4):
                        for c in range(2):
                            pi = rot()
                            tr(pi, psb[pi][:, 0:128], tok[:, i, c * 128:(c + 1) * 128], R=["tok"])
                            act(bigr[:, 13 + c, i * 128:(i + 1) * 128], psb[pi][:, 0:128], AF.Copy, R=["ps%d" % pi], W=[BT(13 + c)])


            def d_branch():
                for i in range(4):
                    pi = rot()
                    for g in range(4):
                        mm(pi, psb[pi][:, g * 64:(g + 1) * 64], wspT[:, g, :], DvN[:, i, g * 64:(g + 1) * 64],
                           True, True, R=["wspT", "sq0", "sq1"])
                    for g in range(4):
                        P.op("dve", ("scalar_tensor_tensor", dict(
                            out=tok[:, i, g * 64:(g + 1) * 64], in0=psb[pi][:, g * 64:(g + 1) * 64], scalar=bsT[:, g:g + 1],
                            in1=TTv[:, i, 384 + g * 64: 448 + g * 64], op0=ALU.add, op1=ALU.mult)),
                            R=["ps%d" % pi, "bsT"] + TT_tags, W=["tok"])
                for i in range(4):
                    for c in range(2):
                        pi = rot()
                        tr(pi, psb[pi][:, 0:128], tok[:, i, c * 128:(c + 1) * 128], R=["tok"])
                        act(bigr[:, 15 + c, i * 128:(i + 1) * 128], psb[pi][:, 0:128], AF.Copy, R=["ps%d" % pi], W=[BT(15 + c)])


            if sample:
                w_in_groups((1, 0))
                qk_norm((4, 2, 3))
                do_rope((4, 2, 3))
                exchange_part("g")
                w_in_groups((2,))
                do_rope((7, 8, 5, 6))
                load_keys_sample("g", 0)
                exchange_part("d")
                if stop_after == "coll":
                    break
                w_in_groups((3,))
                chan_dft()
                exchange_x()
                dv_rms()
                d_branch()
                attention(gqa_preloaded=True, parts=("g",))
                attention(parts=("d",))
                coll_done.add(l)
                fourier_sample()
                out_a(lambda kc: xcs[:, kc, :], [XT(0), XT(1)])
            else:
                w_in_groups((0, 1, 2, 3))
                qk_norm((2, 3, 4))
                dv_rms()
                chan_dft()
                fourier_prompt()
                out_a(lambda kc: bigr[:, 15 + kc, :], [BT(15), BT(16)])
                attention()
                d_branch()
                emit_kv()

            for cdbg in range(8):
                dump("br%d" % cdbg, big[:, 9 + cdbg, :], [BT(9 + cdbg)])
            for n in range(4):
                for cg in range(2):
                    s0, s1, sbr = w_group(3)
                    for j in range(4):
                        for kc in range(4):
                            mm(j, psb[j][:], slabs[:, s0, kc, j * 128:(j + 1) * 128], hT[:, kc, :], kc == 0, False,
                               R=["slab%d" % s0, "h%d" % kc])
                    for j in range(4):
                        c = cg * 4 + j
                        for kc in range(4, 8):
                            mm(j, psb[j][:], slabs[:, s1, kc - 4, j * 128:(j + 1) * 128], hT[:, kc, :], False, kc == 7,
                               R=["slab%d" % s1, "h%d" % kc])
                        pb = rot()
                        for kc in range(2):
                            mm(pb, psb[pb][:], slabs[:, sbr, kc, j * 128:(j + 1) * 128], bigr[:, 9 + 2 * n + kc, :], kc == 0, kc == 1,
                               R=["slab%d" % sbr, BT(9 + 2 * n + kc)])
                        b = (n * 8 + c) % 2
                        act(scr[:, b, :], psb[j][:], AF.Sigmoid, R=["ps%d" % j], W=["scr%d" % b])
                        if n == 0:
                            P.op("dve", ("tensor_tensor", dict(out=bigr[:, c, :], in0=scr[:, b, :], in1=psb[pb][:], op=ALU.mult)),
                                 R=["scr%d" % b, "ps%d" % pb], W=[BT(c)])
                        else:
                            P.op("dve", ("tensor_tensor", dict(out=scr[:, b, :], in0=scr[:, b, :], in1=psb[pb][:], op=ALU.mult)),
                                 R=["scr%d" % b, "ps%d" % pb], W=["scr%d" % b])
                            P.op("dve", ("tensor_tensor", dict(out=bigr[:, c, :], in0=big[:, c, :], in1=scr[:, b, :], op=ALU.add)),
                                 R=["scr%d" % b, BT(c)], W=[BT(c)])
            dump("merged0", big[:, 0, :], [BT(0)])
            dump("merged7", big[:, 7, :], [BT(7)])
            for cg in range(2):
                s0, s1 = w_group(2)
                for j in range(4):
                    for kc in range(4):
                        mm(4 + j, psb[4 + j][:], slabs[:, s0, kc, j * 128:(j + 1) * 128], bigr[:, kc, :], kc == 0, False,
                           R=["slab%d" % s0, BT(kc)])
                for j in range(4):
                    c = cg * 4 + j
                    pi = 4 + j
                    for kc in range(4, 8):
                        mm(pi, psb[pi][:], slabs[:, s1, kc - 4, j * 128:(j + 1) * 128], bigr[:, kc, :], False, kc == 7,
                           R=["slab%d" % s1, BT(kc)])
                    P.op("dve", ("scalar_tensor_tensor", dict(
                        out=xT[:, u, c, :], in0=psb[pi][:], scalar=modT[:, 16 + c, ci:ci + 1], in1=xT[:, u, c, :],
                        op0=ALU.mult, op1=ALU.add)), R=["ps%d" % pi, "mod", "x%d_%d" % (u, c)], W=["x%d_%d" % (u, c)])
            dump("x1_c0", xT[:, u, 0, :], ["x%d_0" % u])
            norm_mod(u, ci, A2, 24, "A2")
            for half in range(2):
                for g in range(4):
                    s0, s1 = w_group(2)
                    for j in range(4):
                        for kc in range(4):
                            mm(4 + j, psb[4 + j][:], slabs[:, s0, kc, j * 128:(j + 1) * 128], hT[:, kc, :], kc == 0, False,
                               R=["slab%d" % s0, "h%d" % kc])
                    for j in range(4):
                        fc = g * 4 + j
                        pi = 4 + j
                        for kc in range(4, 8):
                            mm(pi, psb[pi][:], slabs[:, s1, kc - 4, j * 128:(j + 1) * 128], hT[:, kc, :], False, kc == 7,
                               R=["slab%d" % s1, "h%d" % kc])
                        if RELU2_DVE:
                            P.op("dve", ("scalar_tensor_tensor", dict(
                                out=bigr[:, fc, :], in0=psb[pi][:], scalar=0.0, in1=psb[pi][:], op0=ALU.max, op1=ALU.mult)),
                                R=["ps%d" % pi], W=[BT(fc)])
                        else:
                            b = fc % 2
                            act(scr[:, b, :], psb[pi][:], AF.Relu, R=["ps%d" % pi], W=["scr%d" % b])
                            P.op("dve", ("tensor_tensor", dict(out=bigr[:, fc, :], in0=scr[:, b, :], in1=scr[:, b, :], op=ALU.mult)),
                                 R=["scr%d" % b], W=[BT(fc)])
                for cg in range(2):
                    for kq in range(4):
                        s, = w_group(1)
                        for j in range(4):
                            for kc in range(4):
                                mm(j, psb[j][:], slabs[:, s, kc, j * 128:(j + 1) * 128], bigr[:, kq * 4 + kc, :],
                                   kq == 0 and kc == 0, kq == 3 and kc == 3, R=["slab%d" % s, BT(kq * 4 + kc)])
                    for j in range(4):
                        c = cg * 4 + j
                        P.op("dve", ("scalar_tensor_tensor", dict(
                            out=xT[:, u, c, :], in0=psb[j][:], scalar=modT[:, 40 + c, ci:ci + 1], in1=xT[:, u, c, :],
                            op0=ALU.mult, op1=ALU.add)), R=["ps%d" % j, "mod", "x%d_%d" % (u, c)], W=["x%d_%d" % (u, c)])
            if l == depth - 1 and stop_after is None:
                final_out(u)

    assert stop_after is not None or wstate["used"] == len(wq), (wstate, len(wq))
    P.emit()
    return nc


def _consts(q):
    c = {}
    c["k_ident"] = np.eye(128, dtype=np.float32)
    c["k_ones"] = np.ones((128, 128), np.float32)
    blk = np.zeros((128, 128), np.float32)
    blk[:64, :64] = 1
    blk[64:, 64:] = 1
    c["k_blk64"] = blk
    n = np.arange(64)
    ang = 2 * np.pi * np.outer(n, n) / 64
    C64 = np.cos(ang) / 8.0
    S64 = np.sin(ang) / 8.0
    cc = np.zeros((128, 256))
    cc[:64, 0:64] = C64
    cc[64:, 64:128] = C64
    cc[:64, 128:192] = S64
    cc[64:, 192:256] = S64
    c["k_ccss"] = cc.astype(np.float32)

    def rotm(hd):
        half = hd // 2
        R = np.zeros((128, 128), np.float32)
        for m in range(128):
            if m % hd < half:
                R[m + half, m] = -1.0
            else:
                R[m - half, m] = 1.0
        return R
    mk = np.zeros((128, 6), np.float32)
    mk[0:64, 0] = 1
    mk[64:128, 1] = 1
    for j in range(4):
        mk[32 * j:32 * j + 32, 2 + j] = 1
    c["k_mask"] = mk
    c["k_rotg"] = rotm(64)
    c["k_rotd"] = rotm(32)
    pos = q * 512 + np.arange(512)
    row = (pos // 64).astype(np.float64)
    col = (pos % 64).astype(np.float64)

    def tab(dim):
        quarter = dim // 4
        inv = 10000.0 ** (-np.arange(quarter, dtype=np.float32) / quarter)
        inv = inv.astype(np.float32)
        ang = np.concatenate([row[:, None].astype(np.float32) * inv, col[:, None].astype(np.float32) * inv], axis=-1)
        ang = ang.astype(np.float32)
        cos = np.cos(ang).astype(np.float32)
        sin = np.sin(ang).astype(np.float32)
        half = dim // 2
        idx = np.arange(128) % half
        return cos[:, idx].T.copy(), sin[:, idx].T.copy()
    cg, sg = tab(64)
    cd, sd = tab(32)
    c["k_rope"] = np.stack([cg, sg, cd, sd]).astype(np.float32)
    t = np.arange(256)
    a = 2 * np.pi * np.outer(t, t) / 256
    c["k_dft256"] = np.stack([np.cos(a) / 16.0, -np.sin(a) / 16.0]).astype(np.float32)
    tt = np.arange(2048, dtype=np.float64)
    a = 2 * np.pi * np.outer(tt, pos.astype(np.float64)) / 2048
    c["k_dftc"] = (np.cos(a) / math.sqrt(2048.0)).astype(np.float32)
    c["k_dfts"] = (-np.sin(a) / math.sqrt(2048.0)).astype(np.float32)
    return c


_NC_CACHE = {}
import os
import json
_DBG = {k: (tuple(v) if isinstance(v, list) else v) for k, v in json.loads(os.environ.get("KDBG", "{}")).items()}
_DBG_RUN = {}


def kernel(**inputs):
    inp = {k: np.ascontiguousarray(np.asarray(v)) for k, v in inputs.items()}
    if "nc" not in _NC_CACHE:
        _NC_CACHE["nc"] = build_nc(**_DBG)
    nc = _NC_CACHE["nc"]
    shared = ["c_ctx", "w_ada", "b_ada", "norm1_g", "norm2_g", "w_in", "w_fourier", "q_norm_g", "k_norm_g",
              "lambda_q1", "lambda_k1", "lambda_q2", "lambda_k2", "diff_norm_g", "sgu_norm_g", "w_spatial",
              "b_spatial", "w_gate", "w_branch", "w_out", "w_mlp1", "w_mlp2", "final_norm_g"]
    in_maps = []
    for core in range(8):
        b, q = core // 4, core % 4
        m = {k: inp[k] for k in shared}
        m["xp"] = inp["x_prompt"][4 * core:4 * core + 4]
        m["xs"] = inp["x_sample"][b, q * 512:(q + 1) * 512]
        m["cm"] = inp["c"][b]
        m["cgk"] = inp["cache_gqa_k"][b].reshape(DEPTH, 512, 128)
        m["cgv"] = inp["cache_gqa_v"][b].reshape(DEPTH, 512, 128)
        m["cdk"] = inp["cache_diff_k"][b].reshape(DEPTH, 512, 256)
        m["cdv"] = inp["cache_diff_v"][b].reshape(DEPTH, 512, 256)
        m.update(_consts(q))
        in_maps.append({k: np.ascontiguousarray(v) for k, v in m.items()})
    ncores = _DBG_RUN.get("cores", 8)
    res = run_bass_kernel_spmd(nc, in_maps[:ncores], core_ids=list(range(ncores)))
    R = list(res.results)
    _DBG_RUN["results"] = R
    _DBG_RUN["names"] = getattr(nc, "_dbg_names", [])
    while len(R) < 8:
        R.append(R[0])
    y_prompt = np.concatenate([R[c]["yp"] for c in range(8)], axis=0)
    y_sample = np.stack([np.concatenate([R[b * 4 + q]["ys"] for q in range(4)], axis=0) for b in range(2)], axis=0)
    ngk = np.concatenate([R[c]["ngk"] for c in range(8)], axis=0).reshape(32, DEPTH, 256, 2, 64)
    ngv = np.concatenate([R[c]["ngv"] for c in range(8)], axis=0).reshape(32, DEPTH, 256, 2, 64)
    ndk = np.concatenate([R[c]["ndk"] for c in range(8)], axis=0).reshape(32, DEPTH, 256, 4, 2, 32)
    ndv = np.concatenate([R[c]["ndv"] for c in range(8)], axis=0).reshape(32, DEPTH, 256, 4, 64)
    return (y_prompt.astype(np.float32), y_sample.astype(np.float32), ngk.astype(np.float32),
            ngv.astype(np.float32), ndk.astype(np.float32), ndv.astype(np.float32))
```

```python
import math
import numpy as np
import concourse.bass as bass
import concourse.mybir as mybir
from concourse.bass_utils import run_bass_kernel_spmd

F32 = mybir.dt.float32
F32R = mybir.dt.float32r
BF16 = mybir.dt.bfloat16
AF = mybir.ActivationFunctionType
ALU = mybir.AluOpType
AX = mybir.AxisListType

D = 1024
DEPTH = 4
T = 512
NU = 3
EPS = 1e-6
GROWS = 1280
RELU2_DVE = False


class Prog:
    def __init__(self, nc):
        self.nc = nc
        self.ops = {e: [] for e in ("pe", "act", "dve", "pool", "sp")}
        self.cnt = {e: 0 for e in self.ops}
        self.esem = {e: nc.alloc_semaphore("sem_" + e) for e in ("pe", "act", "dve", "pool")}
        self.slot_sem = {}
        self.slot_cnt = {}
        self.lastw = {}
        self.readers = {}
        self.waited = {e: {} for e in self.ops}
        self.final = []

    def _deps(self, eng, R, W, pe_acc=False):
        deps = {}

        def add(ev):
            if ev is None:
                return
            k, v, src, _sem = ev
            if pe_acc and src == "pe" and eng == "pe":
                return
            if k not in deps or deps[k][0] < v:
                deps[k] = (v, ev[3])

        for t in list(R) + list(W):
            add(self.lastw.get(t))
        for t in W:
            for ev in self.readers.get(t, []):
                add(ev)
        out = []
        for k, (v, sem) in deps.items():
            if self.waited[eng].get(k, 0) >= v:
                continue
            self.waited[eng][k] = v
            out.append((sem, v, k))
        return out

    def _commit(self, ev, R, W):
        for t in W:
            self.lastw[t] = ev
            self.readers[t] = []
        for t in R:
            self.readers.setdefault(t, []).append(ev)

    def op(self, eng, fn, R=(), W=()):
        if isinstance(fn, tuple):
            _name, _kw = fn
            fn = (lambda e, _name=_name, _kw=_kw: getattr(e, _name)(**_kw))
        waits = self._deps(eng, R, W, pe_acc=True)
        self.cnt[eng] += 1
        sem = self.esem[eng]
        ev = ("e_" + eng, self.cnt[eng], eng, sem)
        self.ops[eng].append((waits, fn, sem, 1, self.cnt[eng]))
        self._commit(ev, R, W)

    def dma(self, q, out, in_, R, W, slot, **kw):
        waits = self._deps(q, R, W)
        if slot not in self.slot_sem:
            self.slot_sem[slot] = self.nc.alloc_semaphore("ds_" + slot)
            self.slot_cnt[slot] = 0
        self.slot_cnt[slot] += 16
        sem = self.slot_sem[slot]
        ev = ("s_" + slot, self.slot_cnt[slot], "dma", sem)
        self.ops[q].append((waits, lambda e: e.dma_start(out=out, in_=in_, **kw), sem, 16, None))
        self._commit(ev, R, W)

    def coll(self, fn, R, W, slot):
        waits = self._deps("pool", R, W)
        if slot not in self.slot_sem:
            self.slot_sem[slot] = self.nc.alloc_semaphore("ds_" + slot)
            self.slot_cnt[slot] = 0
        self.slot_cnt[slot] += 1
        sem = self.slot_sem[slot]
        ev = ("s_" + slot, self.slot_cnt[slot], "dma", sem)
        self.ops["pool"].append((waits, fn, sem, None, None))
        self._commit(ev, R, W)

    def emit(self):
        nc = self.nc
        final_waits = [(self.slot_sem[s], self.slot_cnt[s]) for s in self.slot_sem]
        needed = {e: set() for e in self.esem}
        for q in self.ops:
            for (waits, fn, sem, inc, seq) in self.ops[q]:
                for (s_, v, k) in waits:
                    if k.startswith("e_"):
                        needed[k[2:]].add(v)
        for en in self.esem:
            if self.cnt[en]:
                needed[en].add(self.cnt[en])
        rank = {en: {v: i + 1 for i, v in enumerate(sorted(needed[en]))} for en in needed}
        with nc.Block() as block:
            def run(eng_name):
                def body(e):
                    for waits, fn, sem, inc, seq in self.ops[eng_name]:
                        for (s_, v, k) in waits:
                            if k.startswith("e_"):
                                e.wait_ge(s_, rank[k[2:]][v])
                            else:
                                e.wait_ge(s_, v)
                        ins = fn(e)
                        if inc is None:
                            ins.then_inc(sem)
                        elif seq is None:
                            ins.then_inc(sem, inc)
                        elif seq in rank[eng_name]:
                            ins.then_inc(sem, 1)
                    if eng_name == "sp":
                        for (s_, v) in final_waits:
                            e.wait_ge(s_, v)
                        for en in ("pe", "act", "dve", "pool"):
                            if self.cnt[en]:
                                e.wait_ge(self.esem[en], rank[en][self.cnt[en]])
                return body
            block.sync(run("sp"))
            block.tensor(run("pe"))
            block.scalar(run("act"))
            block.vector(run("dve"))
            block.gpsimd(run("pool"))


def build_nc(depth=DEPTH, units=(0, 1, 2), dbg=False, stop_after=None):
    nc = bass.Bass("TRN2", target_bir_lowering=False)
    P = Prog(nc)

    def din(name, shape, dt=F32):
        return nc.dram_tensor(name, list(shape), dt, kind="ExternalInput").ap()

    def dout(name, shape):
        return nc.dram_tensor(name, list(shape), F32, kind="ExternalOutput").ap()

    xp = din("xp", [4, 256, D])
    xs = din("xs", [512, D])
    cm = din("cm", [D])
    cctx = din("c_ctx", [D])
    cgk = din("cgk", [DEPTH, 512, 128])
    cgv = din("cgv", [DEPTH, 512, 128])
    cdk = din("cdk", [DEPTH, 512, 256])
    cdv = din("cdv", [DEPTH, 512, 256])
    w_ada = din("w_ada", [DEPTH, D, 6 * D])
    b_ada = din("b_ada", [DEPTH, 6 * D])
    norm1_g = din("norm1_g", [DEPTH, D])
    norm2_g = din("norm2_g", [DEPTH, D])
    w_in = din("w_in", [DEPTH, D, 2048])
    w_fourier = din("w_fourier", [DEPTH, 256, 256])
    q_norm_g = din("q_norm_g", [DEPTH, 64])
    k_norm_g = din("k_norm_g", [DEPTH, 64])
    lam_in = [din(n, [DEPTH, 32]) for n in ("lambda_q1", "lambda_k1", "lambda_q2", "lambda_k2")]
    diff_norm_g = din("diff_norm_g", [DEPTH, 64])
    sgu_norm_g = din("sgu_norm_g", [DEPTH, 256])
    w_spatial = din("w_spatial", [DEPTH, 4, 128, 128])
    b_spatial = din("b_spatial", [DEPTH, 4, 128])
    w_gate = din("w_gate", [DEPTH, D, 4 * D])
    w_branch = din("w_branch", [DEPTH, 4, 256, D])
    w_out = din("w_out", [DEPTH, D, D])
    w_mlp1 = din("w_mlp1", [DEPTH, D, 4 * D])
    w_mlp2 = din("w_mlp2", [DEPTH, 4 * D, D])
    final_g = din("final_norm_g", [D])
    k_ident = din("k_ident", [128, 128])
    k_ones = din("k_ones", [128, 128])
    k_blk64 = din("k_blk64", [128, 128])
    k_ccss = din("k_ccss", [128, 256])
    k_rotg = din("k_rotg", [128, 128])
    k_rotd = din("k_rotd", [128, 128])
    k_mask = din("k_mask", [128, 6])
    k_rope = din("k_rope", [4, 128, 512])
    k_dft256 = din("k_dft256", [2, 256, 256])
    k_dftc = din("k_dftc", [2048, 512])
    k_dfts = din("k_dfts", [2048, 512])
    yp = dout("yp", [4, 256, D])
    ys = dout("ys", [512, D])
    ngk = dout("ngk", [4, DEPTH, 256, 128])
    ngv = dout("ngv", [4, DEPTH, 256, 128])
    ndk = dout("ndk", [4, DEPTH, 256, 256])
    ndv = dout("ndv", [4, DEPTH, 256, 256])
    PR = {"g": 256, "d": 512, "x": 512}
    bounce = {p: [nc.dram_tensor("bounce_%s%d" % (p, l), [PR[p], 512], F32).ap() for l in range(DEPTH)] for p in PR}
    gath = {p: [nc.dram_tensor("gath_%s%d" % (p, l), [4 * PR[p], 512], F32).ap() for l in range(DEPTH)] for p in PR}

    dbg_out = dout("dbg", [16, 128, 512]) if dbg else None
    dbg_n = [0]
    dbg_names = []

    def dump(name, ap, R):
        if not dbg:
            return
        i = dbg_n[0]
        dbg_n[0] += 1
        dbg_names.append(name)
        ncol = ap.shape[-1] if len(ap.shape) == 2 else None
        P.dma("sp", dbg_out[i, :, 0:ap.shape[1]], ap, R=R, W=[], slot="dbg")
    nc._dbg_names = dbg_names

    def sb(name, shape, dt=F32):
        return nc.alloc_sbuf_tensor(name, list(shape), dt)

    xT = sb("xT", [128, NU, 8, T])
    hT = sb("hT", [128, 8, T], F32R)
    bigR = sb("bigR", [128, 17, T], F32R)
    TT = sb("TT", [128, 4, 896])
    dft256t = sb("dft256t", [128, 2, 2, 256], F32R)
    slabs = sb("slabs", [128, 4, 4, 512], F32R)
    KT = sb("KT", [128, 2560], F32R)
    VA = sb("VA", [128, 20, 130], BF16)
    PT = sb("PT", [128, 2, 512], BF16)
    tokst = KT[:, 0:1536].rearrange("p (i c) -> p i c", c=384)
    tok = sb("tok", [128, 4, 256])
    sq = sb("sq", [128, 2, T], F32R)
    scr = sb("scr", [128, 2, T])
    cstage = scr[:, 1, :].rearrange("p (i c) -> p i c", c=128)
    DvN = sq[:].rearrange("p a t -> p (a t)").rearrange("p (i c) -> p i c", c=256)
    rstd = sb("rstd", [128, T])
    small = sb("small", [128, 64])
    ident = sb("ident", [128, 128])
    ones_r = sb("ones_r", [128, 128], F32R)
    blk64_r = sb("blk64_r", [128, 128], F32R)
    ccss_r = sb("ccss_r", [128, 256], F32R)
    rotg_r = sb("rotg_r", [128, 128], F32R)
    rotd_r = sb("rotd_r", [128, 128], F32R)
    rope = sb("rope", [128, 4, 512])
    dft256 = dft256t[:]
    xcs = sb("xcs", [128, 4, 512], F32R)
    wf_r = sb("wf_r", [128, 2, 256], F32R)
    wsp = sb("wsp", [128, 4, 128])
    wspT = sb("wspT", [128, 4, 128], F32R)
    bsT = sb("bsT", [128, 4])
    condT = sb("condT", [128, 8, 2])
    scond = sb("scond", [128, 8, 2], F32R)
    modT = sb("modT", [128, 48, 2])
    badaT = sb("badaT", [128, 48])
    A1 = sb("A1", [128, 8, 2])
    A2 = sb("A2", [128, 8, 2])
    n1g = sb("n1g", [128, 8])
    n2g = sb("n2g", [128, 8])
    fng = sb("fng", [128, 8])
    qg = sb("qg", [128, 1])
    kg = sb("kg", [128, 1])
    lamv = sb("lamv", [128, 4, 32])
    neglam = sb("neglam", [128, 1])
    dng = sb("dng", [128, 64])
    sgg = sb("sgg", [128, 256])

    NPS = 8
    psb = [nc.alloc_psum_tensor("ps%d" % i, [128, 512], F32) for i in range(NPS)]
    rot_state = [0]

    def rot():
        i = 4 + rot_state[0] % 4
        rot_state[0] += 1
        return i

    def mm(ps_i, out_ap, lhsT, rhs, start, stop, R, **kw):
        P.op("pe", ("matmul", dict(out=out_ap, lhsT=lhsT, rhs=rhs, start=start, stop=stop, **kw)),
             R=R, W=["ps%d" % ps_i])

    def tr(ps_i, out_ap, in_ap, R):
        P.op("pe", ("transpose", dict(out=out_ap, in_=in_ap, identity=ident[:])), R=list(R) + ["ident"], W=["ps%d" % ps_i])

    def act(out, in_, func, R, W, **kw):
        P.op("act", ("activation", dict(out=out, in_=in_, func=func, **kw)), R=R, W=W)

    def BT(c):
        return "B%d" % c

    def XT(i):
        return ["xcs%da" % i, "xcs%db" % i]

    VAP = ["VAp%d" % i for i in range(8)]

    bigr = bigR[:]
    big = bigR[:].bitcast(F32)
    outst = tok[:].rearrange("p a t -> p (a t)").rearrange("p (b d) -> p b d", d=1024)
    OST = {0: ["tok"], 1: ["tok"]}

    wq = []
    wstate = {"issued": 0, "used": 0}
    PREF = 3

    coll_done = set()
    wneed = {}

    def w_issue():
        i = wstate["issued"]
        if i >= len(wq):
            return False
        need = wneed.get(i)
        if need is not None and need not in coll_done:
            return False
        slot = i % 4
        for (dst_fn, src) in wq[i]:
            P.dma("pool", dst_fn(slot), src, R=(["gath_x"] if need is not None else []), W=["slab%d" % slot], slot="slab%d" % slot)
        wstate["issued"] += 1
        return True

    def w_group(n):
        i = wstate["used"]
        while wstate["issued"] <= min(i + 3, len(wq) - 1):
            if not w_issue():
                break
        assert wstate["issued"] >= i + n, "weight slab not issued"
        wstate["used"] += n
        return [(i + k) % 4 for k in range(n)]

    def w_prefetch():
        i = wstate["used"]
        while wstate["issued"] <= min(i + 3, len(wq) - 1):
            if not w_issue():
                break

    def slab_std(src2d, nk=4):
        ncols = src2d.shape[1]
        return [(lambda s, nk=nk, ncols=ncols: slabs[:, s, 0:nk, 0:ncols],
                 src2d.rearrange("(k p) c -> p k c", p=128))]

    def plan_layer_unit(l, sample):
        for g in ((1, 0, 2, 3) if sample else (0, 1, 2, 3)):
            for kh in range(2):
                rows = slice(kh * 512, kh * 512 + 512)
                if g == 0:
                    ent = [(lambda s: slabs[:, s, :, 0:256],
                            w_in[l, rows, 0:256].rearrange("(k p) c -> p k c", p=128))]
                    for b in range(2):
                        for a in range(2):
                            ent.append((lambda s, b=b, a=a: slabs[:, s, :, 256 + 128 * b + 64 * a:320 + 128 * b + 64 * a],
                                        w_in[l, rows, 256 + 128 * a + 64 * b:320 + 128 * a + 64 * b].rearrange("(k p) d -> p k d", p=128)))
                    wq.append(ent)
                else:
                    wq.append(slab_std(w_in[l, rows, g * 512:(g + 1) * 512]))
        if sample:
            for s4 in range(4):
                wneed[len(wq)] = l
                wq.append([(lambda s: slabs[:, s, :, :],
                            gath["x"][l][s4 * 512:(s4 + 1) * 512, :].rearrange("(k p) c -> p k c", p=128))])
                wq.append(slab_std(k_dftc[s4 * 512:(s4 + 1) * 512, :]))
                wq.append(slab_std(k_dfts[s4 * 512:(s4 + 1) * 512, :]))
        for n in range(4):
            for cg in range(2):
                cols = slice(n * 1024 + cg * 512, n * 1024 + cg * 512 + 512)
                for kh in range(2):
                    wq.append(slab_std(w_gate[l, kh * 512:(kh + 1) * 512, cols]))
                wq.append(slab_std(w_branch[l, n, :, cg * 512:(cg + 1) * 512], nk=2))
        for cg in range(2):
            for kh in range(2):
                wq.append(slab_std(w_out[l, kh * 512:(kh + 1) * 512, cg * 512:(cg + 1) * 512]))
        for half in range(2):
            for g in range(4):
                for kh in range(2):
                    wq.append(slab_std(w_mlp1[l, kh * 512:(kh + 1) * 512, half * 2048 + g * 512: half * 2048 + (g + 1) * 512]))
            for cg in range(2):
                for kq in range(4):
                    r0 = half * 2048 + kq * 512
                    wq.append(slab_std(w_mlp2[l, r0:r0 + 512, cg * 512:(cg + 1) * 512]))

    def plan_ada(l):
        for g in range(12):
            for kh in range(2):
                wq.append(slab_std(w_ada[l, kh * 512:(kh + 1) * 512, g * 512:(g + 1) * 512]))

    for l in range(depth):
        plan_ada(l)
        for u in units:
            plan_layer_unit(l, u == 2)

    def ld(dst, src, W, slot, q="sp", **kw):
        P.dma(q, dst, src, R=[], W=W, slot=slot, **kw)

    ld(ident[:], k_ident, ["ident"], "c0")
    ld(ones_r[:], k_ones, ["ones"], "c1", q="pool")
    ld(blk64_r[:], k_blk64, ["blk64"], "c2", q="pool")
    ld(ccss_r[:], k_ccss, ["ccss"], "c3", q="pool")
    ld(rotg_r[:], k_rotg, ["rotg"], "c4", q="pool")
    ld(rotd_r[:], k_rotd, ["rotd"], "c5", q="pool")
    ld(rope[:], k_rope.rearrange("a p t -> p a t"), ["rope"], "c6")
    for a_ in range(2):
        ld(dft256[:, a_], k_dft256[a_].rearrange("(j p) t -> p j t", p=128), ["dft256"], "c7", q="pool")
    ld(fng[:], final_g.rearrange("(c p) -> p c", p=128), ["fng"], "c8", allow_slow_non_contiguous=True)
    ld(condT[:, :, 0], cctx.rearrange("(c p) -> p c", p=128), ["condT"], "c9", allow_slow_non_contiguous=True)
    ld(condT[:, :, 1], cm.rearrange("(c p) -> p c", p=128), ["condT"], "c9", allow_slow_non_contiguous=True)
    P.op("pool", lambda e: e.memset(VA[:, :, 0:1], 1.0), W=["VAones"])
    P.op("pool", lambda e: e.memset(VA[:, :, 129:130], 1.0), W=["VAones"])
    def x_load():
        for u in range(NU):
            for i in range(4):
                if u < 2:
                    src = xp[2 * u + i // 2, (i % 2) * 128:(i % 2) * 128 + 128, :]
                else:
                    src = xs[i * 128:(i + 1) * 128, :]
                b = (u * 4 + i) % 2
                stg = outst[:, 0, :] if b == 0 else scr[:].rearrange("p a t -> p (a t)")
                stags = OST[0] if b == 0 else ["scr0", "scr1"]
                ld(stg, src, stags, "ost%d" % b)
                for c in range(8):
                    pi = rot()
                    tr(pi, psb[pi][:, 0:128], stg[:, c * 128:(c + 1) * 128], R=stags)
                    eng = "dve" if c % 2 == 0 else "act"
                    dst = xT[:, u, c, i * 128:(i + 1) * 128]
                    if eng == "dve":
                        P.op("dve", ("tensor_copy", dict(out=dst, in_=psb[pi][:, 0:128])),
                             R=["ps%d" % pi], W=["x%d_%d" % (u, c)])
                    else:
                        act(dst, psb[pi][:, 0:128], AF.Copy, R=["ps%d" % pi], W=["x%d_%d" % (u, c)])
    act(scond[:], condT[:], AF.Silu, R=["condT"], W=["scond"])

    def norm_mod(u, ci, Amat, shift_c0, gtag):
        pi = rot()
        for c in range(8):
            b = c % 2
            act(sq[:, b, :], xT[:, u, c, :], AF.Square, R=["x%d_%d" % (u, c)], W=["sq%d" % b])
            mm(pi, psb[pi][:], ones_r[:], sq[:, b, :], c == 0, c == 7, R=["ones", "sq%d" % b])
        act(rstd[:], psb[pi][:], AF.Sqrt, R=["ps%d" % pi, "eps"], W=["rstd"], scale=1.0 / D, bias=eps_ap)
        P.op("dve", ("reciprocal", dict(out=rstd[:], in_=rstd[:])), R=["rstd"], W=["rstd"])
        for c in range(8):
            b = c % 2
            P.op("dve", ("scalar_tensor_tensor", dict(
                out=scr[:, b, :], in0=xT[:, u, c, :], scalar=Amat[:, c, ci:ci + 1], in1=rstd[:],
                op0=ALU.mult, op1=ALU.mult)), R=["x%d_%d" % (u, c), gtag, "rstd"], W=["scr%d" % b])
            act(hT[:, c, :], scr[:, b, :], AF.Identity, R=["scr%d" % b, "mod"], W=["h%d" % c],
                bias=modT[:, shift_c0 + c, ci:ci + 1])

    eps_t = sb("eps_t", [128, 1])
    maskc = sb("maskc", [128, 6])
    ld(maskc[:], k_mask, ["maskc"], "c10")
    P.op("pool", lambda e: e.memset(eps_t[:], EPS), W=["eps"])
    eps_ap = eps_t[:]

    def rms_rows(ps_or_ap, R_in, nparts_tag):
        pass

    def final_out(u):
        pi = rot()
        for c in range(8):
            b = c % 2
            act(sq[:, b, :], xT[:, u, c, :], AF.Square, R=["x%d_%d" % (u, c)], W=["sq%d" % b])
            mm(pi, psb[pi][:], ones_r[:], sq[:, b, :], c == 0, c == 7, R=["ones", "sq%d" % b])
        act(rstd[:], psb[pi][:], AF.Sqrt, R=["ps%d" % pi, "eps"], W=["rstd"], scale=1.0 / D, bias=eps_ap)
        P.op("dve", ("reciprocal", dict(out=rstd[:], in_=rstd[:])), R=["rstd"], W=["rstd"])
        scrv = scr[:].rearrange("p a t -> p (a t)").rearrange("p (c q) -> p c q", q=128)
        for i in range(4):
            for c in range(8):
                P.op("dve", ("scalar_tensor_tensor", dict(
                    out=scrv[:, c, :], in0=xT[:, u, c, i * 128:(i + 1) * 128], scalar=fng[:, c:c + 1], in1=rstd[:, i * 128:(i + 1) * 128],
                    op0=ALU.mult, op1=ALU.mult)), R=["x%d_%d" % (u, c), "fng", "rstd"], W=["scr0", "scr1"])
            for c in range(8):
                pi = rot()
                tr(pi, psb[pi][:, 0:128], scrv[:, c, :], R=["scr0", "scr1"])
                if c % 2 == 0:
                    P.op("dve", ("tensor_copy", dict(out=outst[:, 0, c * 128:(c + 1) * 128], in_=psb[pi][:, 0:128])),
                         R=["ps%d" % pi], W=["tok"])
                else:
                    act(outst[:, 0, c * 128:(c + 1) * 128], psb[pi][:, 0:128], AF.Copy, R=["ps%d" % pi], W=["tok"])
            if u < 2:
                dst = yp[2 * u + i // 2, (i % 2) * 128:(i % 2) * 128 + 128, :]
            else:
                dst = ys[i * 128:(i + 1) * 128, :]
            P.dma("sp", dst, outst[:, 0, :], R=["tok"], W=[], slot="oy")


    for l in range(depth):
        lam_init = 0.8 - 0.6 * math.exp(-0.3 * l)
        lam_scale = 1.0 - lam_init
        ld(badaT[:], b_ada[l].rearrange("(c p) -> p c", p=128), ["badaT"], "l0", allow_slow_non_contiguous=True)
        ld(n1g[:], norm1_g[l].rearrange("(c p) -> p c", p=128), ["n1g"], "l1", allow_slow_non_contiguous=True)
        ld(n2g[:], norm2_g[l].rearrange("(c p) -> p c", p=128), ["n2g"], "l2", allow_slow_non_contiguous=True)
        for hh in range(2):
            ld(qg[hh * 64:(hh + 1) * 64, :], q_norm_g[l].rearrange("(p o) -> p o", o=1), ["qg"], "l3", allow_slow_non_contiguous=True)
            ld(kg[hh * 64:(hh + 1) * 64, :], k_norm_g[l].rearrange("(p o) -> p o", o=1), ["kg"], "l4", allow_slow_non_contiguous=True)
        for j in range(4):
            ld(lamv[:, j, :], lam_in[j][l].partition_broadcast(128), ["lamv"], "l5")
        ld(dng[:], diff_norm_g[l].partition_broadcast(128), ["dng"], "l6")
        ld(sgg[:], sgu_norm_g[l].partition_broadcast(128), ["sgg"], "l7")
        ld(wsp[:], w_spatial[l].rearrange("g q p -> q g p"), ["wsp"], "l8")
        ld(bsT[:], b_spatial[l].rearrange("g q -> q g"), ["bsT"], "l9", allow_slow_non_contiguous=True)
        ld(wf_r[:], w_fourier[l].rearrange("(k p) c -> p k c", p=128), ["wf"], "l10", q="pool")
        P.op("dve", ("tensor_tensor", dict(out=small[:, 0:32], in0=lamv[:, 0, :], in1=lamv[:, 1, :], op=ALU.mult)), R=["lamv"], W=["small"])
        P.op("dve", ("tensor_tensor", dict(out=small[:, 32:64], in0=lamv[:, 2, :], in1=lamv[:, 3, :], op=ALU.mult)), R=["lamv", "small"], W=["small"])
        P.op("dve", ("tensor_reduce", dict(out=neglam[:], in_=small[:, 0:32], axis=AX.X, op=ALU.add)), R=["small"], W=["neglam"])
        P.op("dve", ("tensor_reduce", dict(out=small[:, 0:1], in_=small[:, 32:64], axis=AX.X, op=ALU.add)), R=["small", "neglam"], W=["small"])
        act(neglam[:], neglam[:], AF.Exp, R=["neglam"], W=["neglam"])
        act(small[:, 1:2], small[:, 0:1], AF.Exp, R=["small"], W=["small"])
        P.op("dve", ("tensor_tensor", dict(out=neglam[:], in0=small[:, 1:2], in1=neglam[:], op=ALU.subtract)), R=["small", "neglam"], W=["neglam"])
        P.op("dve", ("tensor_scalar_add", dict(out=neglam[:], in0=neglam[:], scalar1=-lam_init)), R=["neglam"], W=["neglam"])
        P.op("dve", ("tensor_scalar_mul", dict(out=dng[:], in0=dng[:], scalar1=lam_scale)), R=["dng"], W=["dng"])
        for g in range(4):
            pi = rot()
            tr(pi, psb[pi][:, 0:128], wsp[:, g, :], R=["wsp"])
            P.op("dve", ("tensor_copy", dict(out=wspT[:, g, :], in_=psb[pi][:, 0:128])), R=["ps%d" % pi], W=["wspT"])
        for g in range(12):
            s0, s1 = w_group(2)
            pi = rot()
            for kc in range(8):
                sl_ = s0 if kc < 4 else s1
                mm(pi, psb[pi][0:2, :], scond[:, kc, :], slabs[:, sl_, kc % 4, :], kc == 0, kc == 7,
                   R=["slab%d" % sl_, "scond"])
            P.op("dve", ("tensor_copy", dict(out=rstd[0:2, :], in_=psb[pi][0:2, :])), R=["ps%d" % pi], W=["rstd"])
            for j in range(4):
                pj = rot()
                P.op("pe", ("transpose", dict(out=psb[pj][:, 0:2], in_=rstd[0:2, j * 128:(j + 1) * 128], identity=ident[0:2, 0:2])),
                     R=["rstd", "ident"], W=["ps%d" % pj])
                cc = g * 4 + j
                P.op("dve", ("tensor_scalar", dict(
                    out=modT[:, cc, :], in0=psb[pj][:, 0:2], scalar1=badaT[:, cc:cc + 1], scalar2=None, op0=ALU.add)),
                    R=["ps%d" % pj, "badaT"], W=["mod"])
        if l == 0:
            x_load()
        for ci in range(2):
            P.op("dve", ("scalar_tensor_tensor", dict(
                out=A1[:, :, ci], in0=modT[:, 8:16, ci], scalar=1.0, in1=n1g[:], op0=ALU.add, op1=ALU.mult)),
                R=["mod", "n1g"], W=["A1"])
            P.op("dve", ("scalar_tensor_tensor", dict(
                out=A2[:, :, ci], in0=modT[:, 32:40, ci], scalar=1.0, in1=n2g[:], op0=ALU.add, op1=ALU.mult)),
                R=["mod", "n2g"], W=["A2"])

        for u in units:
            sample = (u == 2)
            ci = 1 if sample else 0
            nseq = 1 if sample else 2
            sl = T // nseq
            norm_mod(u, ci, A1, 0, "A1")
            hR = ["h%d" % c for c in range(8)]
            TTv = TT[:]
            TT_tags = ["TT"]
            def w_in_groups(gs):
                for g in gs:
                    s0, s1 = w_group(2)

                    def wsl(kc, c0, c1):
                        s = s0 if kc < 4 else s1
                        return slabs[:, s, kc % 4, c0:c1], "slab%d" % s

                    def ftype(c0, dst_chunk, evac):
                        pi = rot()
                        for kc in range(8):
                            w_ap, wt = wsl(kc, c0, c0 + 128)
                            mm(pi, psb[pi][:], w_ap, hT[:, kc, :], kc == 0, kc == 7, R=[wt, "h%d" % kc])
                        evac(pi, dst_chunk)

                    def ttype(c0, ncols, toff, post=None):
                        for i in range(4):
                            pi = rot()
                            for kc in range(8):
                                w_ap, wt = wsl(kc, c0, c0 + ncols)
                                mm(pi, psb[pi][:, 0:ncols], hT[:, kc, i * 128:(i + 1) * 128], w_ap, kc == 0, kc == 7,
                                   R=[wt, "h%d" % kc])
                            dst = TTv[:, i, toff:toff + ncols]
                            if post is None:
                                P.op("dve", ("tensor_copy", dict(out=dst, in_=psb[pi][:, 0:ncols])),
                                     R=["ps%d" % pi], W=TT_tags)
                            else:
                                act(dst, psb[pi][:, 0:ncols], post, R=["ps%d" % pi], W=TT_tags)

                    def ev_copy_r(pi, ch):
                        act(bigr[:, ch, :], psb[pi][:], AF.Copy, R=["ps%d" % pi], W=[BT(ch)])

                    def ev_copy_f(pi, ch):
                        P.op("dve", ("tensor_copy", dict(out=bigr[:, ch, :], in_=psb[pi][:])), R=["ps%d" % pi], W=[BT(ch)])

                    if g == 0:
                        for j in range(4):
                            for kc in range(4):
                                mm(j, psb[j][:], slabs[:, s0, kc, j * 128:(j + 1) * 128], hT[:, kc, :], kc == 0, False,
                                   R=["slab%d" % s0, "h%d" % kc])
                        for j in range(4):
                            for kc in range(4, 8):
                                mm(j, psb[j][:], slabs[:, s1, kc - 4, j * 128:(j + 1) * 128], hT[:, kc, :], False, kc == 7,
                                   R=["slab%d" % s1, "h%d" % kc])
                            (ev_copy_r if j < 2 else ev_copy_f)(j, j)
                    elif g == 1:
                        ftype(0, 4, ev_copy_f)
                        ttype(128, 128, 0)
                        ftype(256, 5, ev_copy_f if sample else ev_copy_r)
                        ftype(384, 6, ev_copy_f if sample else ev_copy_r)
                    elif g == 2:
                        ftype(0, 7, ev_copy_f if sample else ev_copy_r)
                        ftype(128, 8, ev_copy_f if sample else ev_copy_r)
                        ttype(256, 256, 128)
                    else:
                        ttype(0, 256, 384, post=AF.Gelu_apprx_tanh)
                        ttype(256, 256, 640, post=AF.Gelu_apprx_tanh)

            def qk_norm(chs):
                for ch in chs:
                    gvec, gt = (kg, "kg") if ch == 4 else (qg, "qg")
                    act(sq[:, 0, :], big[:, ch, :], AF.Square, R=[BT(ch)], W=["sq0"])
                    pi = rot()
                    mm(pi, psb[pi][:], blk64_r[:], sq[:, 0, :], True, True, R=["blk64", "sq0"])
                    act(rstd[:], psb[pi][:], AF.Sqrt, R=["ps%d" % pi, "eps"], W=["rstd"], scale=1.0 / 64, bias=eps_ap)
                    P.op("dve", ("reciprocal", dict(out=rstd[:], in_=rstd[:])), R=["rstd"], W=["rstd"])
                    P.op("dve", ("scalar_tensor_tensor", dict(
                        out=bigr[:, ch, :], in0=big[:, ch, :], scalar=gvec[:, 0:1], in1=rstd[:], op0=ALU.mult, op1=ALU.mult)),
                        R=[BT(ch), gt, "rstd"], W=[BT(ch)])

            def do_rope(chs):
                for ch in chs:
                    isg = ch <= 4
                    rm = rotg_r if isg else rotd_r
                    rt = "rotg" if isg else "rotd"
                    cosi, sini = (0, 1) if isg else (2, 3)
                    act(sq[:, 1, :], big[:, ch, :], AF.Copy, R=[BT(ch)], W=["sq1"])
                    pi = rot()
                    mm(pi, psb[pi][:], rm[:], sq[:, 1, :], True, True, R=[rt, "sq1"])
                    P.op("dve", ("tensor_tensor", dict(out=scr[:, 0, :], in0=psb[pi][:], in1=rope[:, sini, :], op=ALU.mult)),
                         R=["ps%d" % pi, "rope"], W=["scr0"])
                    P.op("pool", ("tensor_tensor", dict(out=scr[:, 1, :], in0=big[:, ch, :], in1=rope[:, cosi, :], op=ALU.mult)),
                         R=[BT(ch), "rope"], W=["scr1"])
                    P.op("dve", ("tensor_tensor", dict(out=bigr[:, ch, :], in0=scr[:, 0, :], in1=scr[:, 1, :], op=ALU.add)),
                         R=["scr0", "scr1"], W=[BT(ch)])

            def emit_kv():
                if True:
                    for i in range(4):
                        for (ch, off) in ((4, 0), (7, 128), (8, 256)):
                            pi = rot()
                            tr(pi, psb[pi][:, 0:128], big[:, ch, i * 128:(i + 1) * 128], R=[BT(ch)])
                            P.op("dve", ("tensor_copy", dict(out=tokst[:, i, off:off + 128], in_=psb[pi][:, 0:128])),
                                 R=["ps%d" % pi], W=["KT"])
                    sq0 = 2 * u
                    for s_ in range(2):
                        P.dma("sp", ngk[sq0 + s_, l].rearrange("(j p) f -> p j f", p=128),
                              tokst[:, 2 * s_:2 * s_ + 2, 0:128].bitcast(F32), R=["KT"], W=[], slot="o_ngk%d" % s_)
                        P.dma("sp", ndk[sq0 + s_, l].rearrange("(j p) f -> p j f", p=128),
                              tokst[:, 2 * s_:2 * s_ + 2, 128:384].bitcast(F32), R=["KT"], W=[], slot="o_ndk%d" % s_)
                        P.dma("sp", ngv[sq0 + s_, l].rearrange("(j p) f -> p j f", p=128),
                              TTv[:, 2 * s_:2 * s_ + 2, 0:128], R=TT_tags, W=[], slot="o_ngv%d" % s_)
                        P.dma("sp", ndv[sq0 + s_, l].rearrange("(j p) f -> p j f", p=128),
                              TTv[:, 2 * s_:2 * s_ + 2, 128:384], R=TT_tags, W=[], slot="o_ndv%d" % s_)

            def dv_rms():
                for i in range(4):
                    dv_ap = TTv[:, i, 640:896]
                    act(scr[:, 0, 0:256], dv_ap, AF.Square, R=TT_tags, W=["scr0"], accum_out=small[:, 8 + i:9 + i])
                P.op("act", ("activation", dict(out=small[:, 12:16], in_=small[:, 8:12], func=AF.Sqrt, scale=1.0 / 256, bias=eps_ap)),
                     R=["scr0", "eps"], W=["small2"])
                P.op("dve", ("reciprocal", dict(out=small[:, 12:16], in_=small[:, 12:16])), R=["small2"], W=["small2"])
                for i in range(4):
                    dv_ap = TTv[:, i, 640:896]
                    P.op("dve", ("scalar_tensor_tensor", dict(
                        out=DvN[:, i, :], in0=dv_ap, scalar=small[:, 12 + i:13 + i], in1=sgg[:], op0=ALU.mult, op1=ALU.mult)),
                        R=TT_tags + ["small2", "sgg"], W=["sq0", "sq1"])


            def chan_dft():
                for i in range(4):
                    pi = rot()
                    for j in range(2):
                        mm(pi, psb[pi][:, j * 256:(j + 1) * 256], bigr[:, j, i * 128:(i + 1) * 128], ccss_r[:], True, True,
                           R=[BT(j), "ccss"])
                    act(xcs[:, i, :], psb[pi][:], AF.Copy, R=["ps%d" % pi], W=XT(i))

            def fourier_prompt():
                for s in range(2):
                    for j in range(2):
                        pi = rot()
                        n = 0
                        for i2 in range(2):
                            i = s * 2 + i2
                            for part in range(2):
                                mm(pi, psb[pi][:, 0:256], xcs[:, i, j * 256 + part * 128: j * 256 + part * 128 + 128],
                                   dft256[:, part, i2, :], n == 0, n == 3, R=XT(i) + ["dft256"])
                                n += 1
                        act(bigr[:, 15 + j, s * 256:(s + 1) * 256], psb[pi][:, 0:256], AF.Copy, R=["ps%d" % pi], W=[BT(15 + j)])

            def exchange_part(p_):
                bz = bounce[p_][l]
                if p_ == "g":
                    P.dma("sp", bz[0:128, :], big[:, 4, :], R=[BT(4)], W=["bounce_g"], slot="bzg")
                    P.dma("sp", bz[128:256, :].rearrange("r (a c) -> (r a) c", c=128).rearrange("(j p) c -> p j c", p=128),
                          TTv[:, :, 0:128], R=TT_tags, W=["bounce_g"], slot="bzg")
                else:
                    P.dma("sp", bz[0:256, :].rearrange("(c p) t -> p c t", p=128), big[:, 7:9, :], R=[BT(7), BT(8)], W=["bounce_d"], slot="bzd")
                    P.dma("sp", bz[256:512, :].rearrange("r (a c) -> (r a) c", c=256).rearrange("(j p) c -> p j c", p=128),
                          TTv[:, :, 128:384], R=TT_tags, W=["bounce_d"], slot="bzd")
                w_prefetch()
                P.coll(lambda e, l=l, p_=p_: e.collective_compute(
                    "AllGather", ALU.bypass, replica_groups=[[0, 1, 2, 3], [4, 5, 6, 7]],
                    ins=[bounce[p_][l].opt()], outs=[gath[p_][l].opt()]), R=["bounce_" + p_], W=["gath_" + p_], slot="cc_" + p_)

            def exchange_x():
                bx_ = bounce["x"][l]
                P.dma("sp", bx_.rearrange("(j p) c -> p j c", p=128), xcs[:].bitcast(F32), R=XT(0) + XT(1) + XT(2) + XT(3), W=["bounce_x"], slot="bzx")
                w_prefetch()
                P.coll(lambda e, l=l: e.collective_compute(
                    "AllGather", ALU.bypass, replica_groups=[[0, 1, 2, 3], [4, 5, 6, 7]],
                    ins=[bounce["x"][l].opt()], outs=[gath["x"][l].opt()]), R=["bounce_x"], W=["gath_x"], slot="cc_x")

            def fourier_sample():
                acc = [rot(), rot()]
                n = 0
                for s4 in range(4):
                    sx, sc, ss = w_group(3)
                    for i in range(4):
                        for j in range(2):
                            mm(acc[j], psb[acc[j]][:], slabs[:, sx, i, j * 256:j * 256 + 128], slabs[:, sc, i, :],
                               n == 0, False, R=["slab%d" % sx, "slab%d" % sc])
                            mm(acc[j], psb[acc[j]][:], slabs[:, sx, i, j * 256 + 128:j * 256 + 256], slabs[:, ss, i, :],
                               False, n == 15, R=["slab%d" % sx, "slab%d" % ss])
                        n += 1
                for j in range(2):
                    act(xcs[:, j, :], psb[acc[j]][:], AF.Copy, R=["ps%d" % acc[j]], W=XT(j))

            def out_a(ysrc, ytags):
                for c in range(2):
                    pi = rot()
                    for kc in range(2):
                        mm(pi, psb[pi][:], wf_r[:, kc, c * 128:(c + 1) * 128], ysrc(kc), kc == 0, kc == 1, R=["wf"] + (ytags[kc] if isinstance(ytags[kc], list) else [ytags[kc]]))
                    act(bigr[:, 9 + c, :], psb[pi][:], AF.Copy, R=["ps%d" % pi], W=[BT(9 + c)])

            def load_keys_sample(kind, hc):
                if kind == "g":
                    gp, gtag, grows = gath["g"][l], "gath_g", 256
                    kcache, kcols, krow0 = cgk[l], slice(0, 128), 0
                    vcache, vcols, vrow0, vw = cgv[l], slice(0, 128), 128, 128
                else:
                    gp, gtag, grows = gath["d"][l], "gath_d", 512
                    kcache, kcols, krow0 = cdk[l], slice(hc * 128, hc * 128 + 128), hc * 128
                    vcache, vcols, vrow0, vw = cdv[l], slice(hc * 128, hc * 128 + 128), 256, 256
                gl = gp.rearrange("(r x) c -> x r c", x=grows)
                w_prefetch()
                P.dma("sp", cstage[:, :, :], kcache[:, kcols].rearrange("(j p) c -> p j c", p=128), R=[], W=["scr1"], slot="cst")
                for j in range(4):
                    pi = rot()
                    tr(pi, psb[pi][:, 0:128], cstage[:, j, :], R=["scr1"])
                    act(KT[:, j * 128:(j + 1) * 128], psb[pi][:, 0:128], AF.Copy, R=["ps%d" % pi], W=["KT"])
                P.dma("pool", KT[:, 512:2560].rearrange("p (r t) -> p r t", r=4), gl[krow0:krow0 + 128, :, :],
                      R=[gtag], W=["KT"], slot="ktl")
                P.dma("pool", VA[:, 0:4, 1:129], vcache[:, vcols].rearrange("(j p) d -> p j d", p=128),
                      R=[], W=["VA"] + VAP, slot="val")
                for r in range(4):
                    src = gp[r * grows + vrow0: r * grows + vrow0 + vw, :].rearrange("x (a c) -> (x a) c", c=vw)
                    src = src[:, vcols].rearrange("(j p) d -> p j d", p=128)
                    P.dma("pool", VA[:, 4 + 4 * r: 8 + 4 * r, 1:129], src, R=[gtag], W=["VA"] + VAP, slot="val")

            def attention(gqa_preloaded=False, parts=("g", "d")):
                nk = 2560 if sample else 256
                nkt = nk // 128
                nq = sl
                nqt = nq // 128

                gctr = [0]

                def attn_group(heads, s):
                    nh = len(heads)
                    g_ = gctr[0]
                    gctr[0] += 1
                    if sample:
                        qoff, vb, vtag = 0, 0, "VA"
                        qtag = lambda hi: XT(hi)
                    else:
                        qoff, vb, vtag = (g_ % 2) * 256, (g_ % 8) * 2, VAP[g_ % 8]
                        qtag = lambda hi: [XT(hi)[g_ % 2]]
                    for hi, hd in enumerate(heads):
                        act(xcs[:, hi, qoff:qoff + nq], big[:, hd["qch"], s * sl: s * sl + nq], AF.Copy,
                            R=[BT(hd["qch"]), "maskc"], W=qtag(hi), scale=maskc[:, hd["mcol"]:hd["mcol"] + 1])
                    kpb = 2 if (nkt == 2 and nq * 2 <= 512) else 1
                    steps = [(hi, kt) for hi in range(nh) for kt in range(0, nkt, kpb)]
                    pending = []
                    sbank = {}

                    def issue_s(i):
                        hi, kt = steps[i]
                        hd = heads[hi]
                        pi = rot()
                        for kk in range(kpb):
                            mm(pi, psb[pi][:, kk * nq:(kk + 1) * nq], hd["kfn"](kt + kk), xcs[:, hi, qoff:qoff + nq], True, True,
                               R=[hd["ktag"]] + qtag(hi), skip_group_check=True)
                        sbank[i] = pi

                    issue_s(0)
                    for i in range(len(steps)):
                        if i + 1 < len(steps):
                            issue_s(i + 1)
                        hi, kt = steps[i]
                        hd = heads[hi]
                        pi = sbank[i]
                        b = i % 2
                        act(PT[:, b, 0:kpb * nq], psb[pi][:, 0:kpb * nq], AF.Exp, R=["ps%d" % pi], W=["PT%d" % b], scale=hd["scale"])
                        ob = hd["obank"]
                        for kk in range(kpb):
                            for qt in range(nqt):
                                ktt = kt + kk
                                mm(ob, psb[ob][:, qt * 65:(qt + 1) * 65], PT[:, b, kk * nq + qt * 128: kk * nq + (qt + 1) * 128],
                                   VA[:, vb + ktt, 65 * hd["vslot"]:65 * hd["vslot"] + 65], ktt == 0 and qt == 0, ktt == nkt - 1 and qt == nqt - 1,
                                   R=["PT%d" % b, vtag, "VAones"], skip_group_check=True)
                        for pd in [p for p in pending if p[0] <= i]:
                            pending.remove(pd)
                            pd[1]()
                        if kt + kpb - 1 == nkt - 1 and hd["post"] is not None:
                            p1, p2 = hd["post"]
                            p1()
                            if p2 is not None:
                                pending.append((i + 3, p2))
                    for pd in pending:
                        pd[1]()

                def gqa_post(h, s, ob):
                    def p1():
                        ov = psb[ob][:, 0:nqt * 65].rearrange("p (q c) -> p q c", c=65)
                        c0 = 16 + 4 * (h % 2)
                        d0, v0 = (0, 1) if (h // 2) == 0 else (64, 0)
                        P.op("dve", ("reciprocal", dict(out=small[:, c0:c0 + nqt], in_=ov[:, :, d0])), R=["ps%d" % ob], W=["small3"])
                        P.op("dve", ("tensor_tensor", dict(
                            out=tok[:, s * nqt:(s + 1) * nqt, h * 64:(h + 1) * 64], in0=ov[:, :, v0:v0 + 64],
                            in1=small[:, c0:c0 + nqt].unsqueeze(2).broadcast_to([128, nqt, 64]), op=ALU.mult)),
                            R=["ps%d" % ob, "small3"], W=["tok"])
                    return p1, None

                def diff_post(h, s, ob1, ob2):
                    par = h % 2
                    d0, v0 = (0, 1) if (h % 2) == 0 else (64, 0)
                    A = scr[:, par, 0:256].rearrange("p (q c) -> p q c", c=64)[:, 0:nqt, :]
                    Bm = scr[:, par, 256:512].rearrange("p (q c) -> p q c", c=64)[:, 0:nqt, :]
                    stag = "scr%d" % par
                    c1, c2, c3 = 16 + 12 * par, 20 + 12 * par, 24 + 12 * par
                    mtag = "smallp%d" % par

                    def p1():
                        o1 = psb[ob1][:, 0:nqt * 65].rearrange("p (q c) -> p q c", c=65)
                        o2 = psb[ob2][:, 0:nqt * 65].rearrange("p (q c) -> p q c", c=65)
                        P.op("dve", ("reciprocal", dict(out=small[:, c1:c1 + nqt], in_=o1[:, :, d0])), R=["ps%d" % ob1], W=[mtag])
                        P.op("dve", ("reciprocal", dict(out=small[:, c2:c2 + nqt], in_=o2[:, :, d0])), R=["ps%d" % ob2, mtag], W=[mtag])
                        P.op("dve", ("tensor_scalar", dict(out=small[:, c2:c2 + nqt], in0=small[:, c2:c2 + nqt], scalar1=neglam[:, 0:1],
                                                              scalar2=None, op0=ALU.mult)), R=[mtag, "neglam"], W=[mtag])
                        P.op("dve", ("tensor_tensor", dict(out=A, in0=o1[:, :, v0:v0 + 64],
                                                              in1=small[:, c1:c1 + nqt].unsqueeze(2).broadcast_to([128, nqt, 64]), op=ALU.mult)),
                             R=["ps%d" % ob1, mtag], W=[stag])
                        P.op("dve", ("tensor_tensor", dict(out=Bm, in0=o2[:, :, v0:v0 + 64],
                                                              in1=small[:, c2:c2 + nqt].unsqueeze(2).broadcast_to([128, nqt, 64]), op=ALU.mult)),
                             R=["ps%d" % ob2, mtag, stag], W=[stag])
                        P.op("dve", ("tensor_tensor", dict(out=A, in0=A, in1=Bm, op=ALU.add)), R=[stag], W=[stag])
                        P.op("dve", ("tensor_tensor", dict(out=Bm, in0=A, in1=A, op=ALU.mult)), R=[stag], W=[stag])
                        P.op("dve", ("tensor_reduce", dict(out=small[:, c3:c3 + nqt], in_=Bm, axis=AX.X, op=ALU.add)), R=[stag, mtag], W=[mtag])

                    def p2():
                        P.op("act", ("activation", dict(out=small[:, c3:c3 + nqt], in_=small[:, c3:c3 + nqt], func=AF.Ln, scale=1.0 / 64, bias=eps_ap)),
                             R=[mtag, "eps"], W=[mtag])
                        P.op("act", ("activation", dict(out=small[:, c3:c3 + nqt], in_=small[:, c3:c3 + nqt], func=AF.Exp, scale=-0.5)),
                             R=[mtag], W=[mtag])
                        P.op("dve", ("tensor_tensor", dict(out=A, in0=A,
                                                              in1=small[:, c3:c3 + nqt].unsqueeze(2).broadcast_to([128, nqt, 64]), op=ALU.mult)),
                             R=[stag, mtag], W=[stag])
                        P.op("dve", ("tensor_tensor", dict(out=tok[:, s * nqt:(s + 1) * nqt, h * 64:(h + 1) * 64], in0=A,
                                                              in1=dng[:].unsqueeze(1).broadcast_to([128, nqt, 64]), op=ALU.mult)),
                             R=[stag, "dng"], W=["tok"])
                    return p1, p2

                if "g" in parts:
                    for s in range(nseq):
                        if sample:
                            if not gqa_preloaded:
                                load_keys_sample("g", 0)
                            kfn = lambda kt: KT[:, kt * 128:(kt + 1) * 128]
                            ktag = "KT"
                        else:
                            for i2 in range(2):
                                i = s * 2 + i2
                                P.op("dve", ("tensor_copy", dict(
                                    out=VA[:, (gctr[0] % 8) * 2 + i2, 1:129], in_=TTv[:, i, 0:128])),
                                    R=TT_tags, W=[VAP[gctr[0] % 8], "VA"])
                            kfn = lambda kt, s=s: bigr[:, 4, s * 256 + kt * 128: s * 256 + (kt + 1) * 128]
                            ktag = BT(4)
                        heads = []
                        for h in range(4):
                            heads.append(dict(qch=2 + (h % 2), mcol=h // 2, vslot=h // 2, scale=0.125, obank=h % 2,
                                              kfn=kfn, ktag=ktag, post=gqa_post(h, s, h % 2)))
                        attn_group(heads, s)
                    for i in range(4):
                        for c in range(2):
                            pi = rot()
                            tr(pi, psb[pi][:, 0:128], tok[:, i, c * 128:(c + 1) * 128], R=["tok"])
                            act(bigr[:, 11 + c, i * 128:(i + 1) * 128], psb[pi][:, 0:128], AF.Copy, R=["ps%d" % pi], W=[BT(11 + c)])

                if "d" in parts:
                    dscale = 1.0 / math.sqrt(32.0)
                    for s in range(nseq):
                        for hc in range(2):
                            if sample:
                                load_keys_sample("d", hc)
                                kfn = lambda kt: KT[:, kt * 128:(kt + 1) * 128]
                                ktag = "KT"
                            else:
                                for i2 in range(2):
                                    i = s * 2 + i2
                                    P.op("dve", ("tensor_copy", dict(
                                        out=VA[:, (gctr[0] % 8) * 2 + i2, 1:129],
                                        in_=TTv[:, i, 128 + hc * 128: 256 + hc * 128])),
                                        R=TT_tags, W=[VAP[gctr[0] % 8], "VA"])
                                kfn = lambda kt, s=s, hc=hc: bigr[:, 7 + hc, s * 256 + kt * 128: s * 256 + (kt + 1) * 128]
                                ktag = BT(7 + hc)
                            heads = []
                            for hh in range(2):
                                h = hc * 2 + hh
                                for m in range(2):
                                    heads.append(dict(qch=5 + hc, mcol=2 + hh * 2 + m, vslot=hh, scale=dscale, obank=2 * hh + m,
                                                      kfn=kfn, ktag=ktag,
                                                      post=(diff_post(h, s, 2 * hh, 2 * hh + 1) if m == 1 else None)))
                            attn_group(heads, s)
                    for i in range(4):
                        for c in range(2):
                            pi = rot()
                            tr(pi, psb[pi][:, 0:128], tok[:, i, c * 128:(c + 1) * 128], R=["tok"])
                            act(bigr[:, 13 + c, i * 128:(i + 1) * 128], psb[pi][:, 0:128], AF.Copy, R=["ps%d" % pi], W=[BT(13 + c)])


            def d_branch():
                for i in range(4):
                    pi = rot()
                    for g in range(4):
                        mm(pi, psb[pi][:, g * 64:(g + 1) * 64], wspT[:, g, :], DvN[:, i, g * 64:(g + 1) * 64],
                           True, True, R=["wspT", "sq0", "sq1"])
                    for g in range(4):
                        P.op("dve", ("scalar_tensor_tensor", dict(
                            out=tok[:, i, g * 64:(g + 1) * 64], in0=psb[pi][:, g * 64:(g + 1) * 64], scalar=bsT[:, g:g + 1],
                            in1=TTv[:, i, 384 + g * 64: 448 + g * 64], op0=ALU.add, op1=ALU.mult)),
                            R=["ps%d" % pi, "bsT"] + TT_tags, W=["tok"])
                for i in range(4):
                    for c in range(2):
                        pi = rot()
                        tr(pi, psb[pi][:, 0:128], tok[:, i, c * 128:(c + 1) * 128], R=["tok"])
                        act(bigr[:, 15 + c, i * 128:(i + 1) * 128], psb[pi][:, 0:128], AF.Copy, R=["ps%d" % pi], W=[BT(15 + c)])


            if sample:
                w_in_groups((1, 0))
                qk_norm((4, 2, 3))
                do_rope((4, 2, 3))
                exchange_part("g")
                w_in_groups((2,))
                do_rope((7, 8, 5, 6))
                load_keys_sample("g", 0)
                exchange_part("d")
                if stop_after == "coll":
                    break
                w_in_groups((3,))
                chan_dft()
                exchange_x()
                dv_rms()
                d_branch()
                attention(gqa_preloaded=True, parts=("g",))
                attention(parts=("d",))
                coll_done.add(l)
                fourier_sample()
                out_a(lambda kc: xcs[:, kc, :], [XT(0), XT(1)])
            else:
                w_in_groups((0, 1, 2, 3))
                qk_norm((2, 3, 4))
                emit_kv()
                dv_rms()
                chan_dft()
                fourier_prompt()
                out_a(lambda kc: bigr[:, 15 + kc, :], [BT(15), BT(16)])
                attention()
                d_branch()

            for cdbg in range(8):
                dump("br%d" % cdbg, big[:, 9 + cdbg, :], [BT(9 + cdbg)])
            for n in range(4):
                for cg in range(2):
                    s0, s1, sbr = w_group(3)
                    for j in range(4):
                        for kc in range(4):
                            mm(j, psb[j][:], slabs[:, s0, kc, j * 128:(j + 1) * 128], hT[:, kc, :], kc == 0, False,
                               R=["slab%d" % s0, "h%d" % kc])
                    for j in range(4):
                        c = cg * 4 + j
                        for kc in range(4, 8):
                            mm(j, psb[j][:], slabs[:, s1, kc - 4, j * 128:(j + 1) * 128], hT[:, kc, :], False, kc == 7,
                               R=["slab%d" % s1, "h%d" % kc])
                        pb = rot()
                        for kc in range(2):
                            mm(pb, psb[pb][:], slabs[:, sbr, kc, j * 128:(j + 1) * 128], bigr[:, 9 + 2 * n + kc, :], kc == 0, kc == 1,
                               R=["slab%d" % sbr, BT(9 + 2 * n + kc)])
                        b = (n * 8 + c) % 2
                        act(scr[:, b, :], psb[j][:], AF.Sigmoid, R=["ps%d" % j], W=["scr%d" % b])
                        if n == 0:
                            P.op("dve", ("tensor_tensor", dict(out=bigr[:, c, :], in0=scr[:, b, :], in1=psb[pb][:], op=ALU.mult)),
                                 R=["scr%d" % b, "ps%d" % pb], W=[BT(c)])
                        else:
                            P.op("dve", ("tensor_tensor", dict(out=scr[:, b, :], in0=scr[:, b, :], in1=psb[pb][:], op=ALU.mult)),
                                 R=["scr%d" % b, "ps%d" % pb], W=["scr%d" % b])
                            P.op("dve", ("tensor_tensor", dict(out=bigr[:, c, :], in0=big[:, c, :], in1=scr[:, b, :], op=ALU.add)),
                                 R=["scr%d" % b, BT(c)], W=[BT(c)])
            dump("merged0", big[:, 0, :], [BT(0)])
            dump("merged7", big[:, 7, :], [BT(7)])
            for cg in range(2):
                s0, s1 = w_group(2)
                for j in range(4):
                    for kc in range(4):
                        mm(4 + j, psb[4 + j][:], slabs[:, s0, kc, j * 128:(j + 1) * 128], bigr[:, kc, :], kc == 0, False,
                           R=["slab%d" % s0, BT(kc)])
                for j in range(4):
                    c = cg * 4 + j
                    pi = 4 + j
                    for kc in range(4, 8):
                        mm(pi, psb[pi][:], slabs[:, s1, kc - 4, j * 128:(j + 1) * 128], bigr[:, kc, :], False, kc == 7,
                           R=["slab%d" % s1, BT(kc)])
                    P.op("dve", ("scalar_tensor_tensor", dict(
                        out=xT[:, u, c, :], in0=psb[pi][:], scalar=modT[:, 16 + c, ci:ci + 1], in1=xT[:, u, c, :],
                        op0=ALU.mult, op1=ALU.add)), R=["ps%d" % pi, "mod", "x%d_%d" % (u, c)], W=["x%d_%d" % (u, c)])
            dump("x1_c0", xT[:, u, 0, :], ["x%d_0" % u])
            norm_mod(u, ci, A2, 24, "A2")
            for half in range(2):
                for g in range(4):
                    s0, s1 = w_group(2)
                    for j in range(4):
                        for kc in range(4):
                            mm(4 + j, psb[4 + j][:], slabs[:, s0, kc, j * 128:(j + 1) * 128], hT[:, kc, :], kc == 0, False,
                               R=["slab%d" % s0, "h%d" % kc])
                    for j in range(4):
                        fc = g * 4 + j
                        pi = 4 + j
                        for kc in range(4, 8):
                            mm(pi, psb[pi][:], slabs[:, s1, kc - 4, j * 128:(j + 1) * 128], hT[:, kc, :], False, kc == 7,
                               R=["slab%d" % s1, "h%d" % kc])
                        if RELU2_DVE:
                            P.op("dve", ("scalar_tensor_tensor", dict(
                                out=bigr[:, fc, :], in0=psb[pi][:], scalar=0.0, in1=psb[pi][:], op0=ALU.max, op1=ALU.mult)),
                                R=["ps%d" % pi], W=[BT(fc)])
                        else:
                            b = fc % 2
                            act(scr[:, b, :], psb[pi][:], AF.Relu, R=["ps%d" % pi], W=["scr%d" % b])
                            P.op("dve", ("tensor_tensor", dict(out=bigr[:, fc, :], in0=scr[:, b, :], in1=scr[:, b, :], op=ALU.mult)),
                                 R=["scr%d" % b], W=[BT(fc)])
                for cg in range(2):
                    for kq in range(4):
                        s, = w_group(1)
                        for j in range(4):
                            for kc in range(4):
                                mm(j, psb[j][:], slabs[:, s, kc, j * 128:(j + 1) * 128], bigr[:, kq * 4 + kc, :],
                                   kq == 0 and kc == 0, kq == 3 and kc == 3, R=["slab%d" % s, BT(kq * 4 + kc)])
                    for j in range(4):
                        c = cg * 4 + j
                        P.op("dve", ("scalar_tensor_tensor", dict(
                            out=xT[:, u, c, :], in0=psb[j][:], scalar=modT[:, 40 + c, ci:ci + 1], in1=xT[:, u, c, :],
                            op0=ALU.mult, op1=ALU.add)), R=["ps%d" % j, "mod", "x%d_%d" % (u, c)], W=["x%d_%d" % (u, c)])
            if l == depth - 1 and stop_after is None:
                final_out(u)

    assert stop_after is not None or wstate["used"] == len(wq), (wstate, len(wq))
    P.emit()
    return nc


def _consts(q):
    c = {}
    c["k_ident"] = np.eye(128, dtype=np.float32)
    c["k_ones"] = np.ones((128, 128), np.float32)
    blk = np.zeros((128, 128), np.float32)
    blk[:64, :64] = 1
    blk[64:, 64:] = 1
    c["k_blk64"] = blk
    n = np.arange(64)
    ang = 2 * np.pi * np.outer(n, n) / 64
    C64 = np.cos(ang) / 8.0
    S64 = np.sin(ang) / 8.0
    cc = np.zeros((128, 256))
    cc[:64, 0:64] = C64
    cc[64:, 64:128] = C64
    cc[:64, 128:192] = S64
    cc[64:, 192:256] = S64
    c["k_ccss"] = cc.astype(np.float32)

    def rotm(hd):
        half = hd // 2
        R = np.zeros((128, 128), np.float32)
        for m in range(128):
            if m % hd < half:
                R[m + half, m] = -1.0
            else:
                R[m - half, m] = 1.0
        return R
    mk = np.zeros((128, 6), np.float32)
    mk[0:64, 0] = 1
    mk[64:128, 1] = 1
    for j in range(4):
        mk[32 * j:32 * j + 32, 2 + j] = 1
    c["k_mask"] = mk
    c["k_rotg"] = rotm(64)
    c["k_rotd"] = rotm(32)
    pos = q * 512 + np.arange(512)
    row = (pos // 64).astype(np.float64)
    col = (pos % 64).astype(np.float64)

    def tab(dim):
        quarter = dim // 4
        inv = 10000.0 ** (-np.arange(quarter, dtype=np.float32) / quarter)
        inv = inv.astype(np.float32)
        ang = np.concatenate([row[:, None].astype(np.float32) * inv, col[:, None].astype(np.float32) * inv], axis=-1)
        ang = ang.astype(np.float32)
        cos = np.cos(ang).astype(np.float32)
        sin = np.sin(ang).astype(np.float32)
        half = dim // 2
        idx = np.arange(128) % half
        return cos[:, idx].T.copy(), sin[:, idx].T.copy()
    cg, sg = tab(64)
    cd, sd = tab(32)
    c["k_rope"] = np.stack([cg, sg, cd, sd]).astype(np.float32)
    t = np.arange(256)
    a = 2 * np.pi * np.outer(t, t) / 256
    c["k_dft256"] = np.stack([np.cos(a) / 16.0, -np.sin(a) / 16.0]).astype(np.float32)
    tt = np.arange(2048, dtype=np.float64)
    a = 2 * np.pi * np.outer(tt, pos.astype(np.float64)) / 2048
    c["k_dftc"] = (np.cos(a) / math.sqrt(2048.0)).astype(np.float32)
    c["k_dfts"] = (-np.sin(a) / math.sqrt(2048.0)).astype(np.float32)
    return c


_NC_CACHE = {}
import os
import json
_DBG = {k: (tuple(v) if isinstance(v, list) else v) for k, v in json.loads(os.environ.get("KDBG", "{}")).items()}
_DBG_RUN = {}


def kernel(**inputs):
    inp = {k: np.ascontiguousarray(np.asarray(v)) for k, v in inputs.items()}
    if "nc" not in _NC_CACHE:
        _NC_CACHE["nc"] = build_nc(**_DBG)
    nc = _NC_CACHE["nc"]
    shared = ["c_ctx", "w_ada", "b_ada", "norm1_g", "norm2_g", "w_in", "w_fourier", "q_norm_g", "k_norm_g",
              "lambda_q1", "lambda_k1", "lambda_q2", "lambda_k2", "diff_norm_g", "sgu_norm_g", "w_spatial",
              "b_spatial", "w_gate", "w_branch", "w_out", "w_mlp1", "w_mlp2", "final_norm_g"]
    in_maps = []
    for core in range(8):
        b, q = core // 4, core % 4
        m = {k: inp[k] for k in shared}
        m["xp"] = inp["x_prompt"][4 * core:4 * core + 4]
        m["xs"] = inp["x_sample"][b, q * 512:(q + 1) * 512]
        m["cm"] = inp["c"][b]
        m["cgk"] = inp["cache_gqa_k"][b].reshape(DEPTH, 512, 128)
        m["cgv"] = inp["cache_gqa_v"][b].reshape(DEPTH, 512, 128)
        m["cdk"] = inp["cache_diff_k"][b].reshape(DEPTH, 512, 256)
        m["cdv"] = inp["cache_diff_v"][b].reshape(DEPTH, 512, 256)
        m.update(_consts(q))
        in_maps.append({k: np.ascontiguousarray(v) for k, v in m.items()})
    ncores = _DBG_RUN.get("cores", 8)
    res = run_bass_kernel_spmd(nc, in_maps[:ncores], core_ids=list(range(ncores)))
    R = list(res.results)
    _DBG_RUN["results"] = R
    _DBG_RUN["names"] = getattr(nc, "_dbg_names", [])
    while len(R) < 8:
        R.append(R[0])
    y_prompt = np.concatenate([R[c]["yp"] for c in range(8)], axis=0)
    y_sample = np.stack([np.concatenate([R[b * 4 + q]["ys"] for q in range(4)], axis=0) for b in range(2)], axis=0)
    ngk = np.concatenate([R[c]["ngk"] for c in range(8)], axis=0).reshape(32, DEPTH, 256, 2, 64)
    ngv = np.concatenate([R[c]["ngv"] for c in range(8)], axis=0).reshape(32, DEPTH, 256, 2, 64)
    ndk = np.concatenate([R[c]["ndk"] for c in range(8)], axis=0).reshape(32, DEPTH, 256, 4, 2, 32)
    ndv = np.concatenate([R[c]["ndv"] for c in range(8)], axis=0).reshape(32, DEPTH, 256, 4, 64)
    return (y_prompt.astype(np.float32), y_sample.astype(np.float32), ngk.astype(np.float32),
            ngv.astype(np.float32), ndk.astype(np.float32), ndv.astype(np.float32))
```
